# Optimizing a Trainium2 kernel written in Bass

```python
import math
import jax
import jax.numpy as jnp
from jax import lax
import numpy as np

D_MODEL = 1024
BATCH = 8
SEQ = 4096
DEPTH = 2

N_A_LAYERS = DEPTH // 2
N_B_LAYERS = DEPTH - N_A_LAYERS
DN_ALPHA = (2.0 * DEPTH) ** 0.25
DN_BETA = (8.0 * DEPTH) ** -0.25
LN_EPS = 1e-5
HEAD_NORM_EPS = 1e-6
MIX_WIDTH = D_MODEL

MEM_LEN = 256
MEM_HEADS = 4
MEM_HEAD_DIM = D_MODEL // 16
MEM_WIDTH = MEM_HEADS * MEM_HEAD_DIM

GLA_HEADS = 4
GLA_DV = (MIX_WIDTH - MEM_WIDTH) // GLA_HEADS
GLA_DK = GLA_DV // 2
GLA_GATE_RANK = 16
GLA_TAU = 16.0
GLA_CHUNK = 64
GLA_QK_WIDTH = GLA_HEADS * GLA_DK
GLA_V_WIDTH = GLA_HEADS * GLA_DV

DIL_PAIRS = ((128, 1), (512, 4), (2048, 16))
DIL_GROUPS = len(DIL_PAIRS)
DIL_SLOTS = 6
DIL_HEAD_DIM = (MIX_WIDTH - MEM_WIDTH) // DIL_SLOTS
DIL_BLOCK = 128
DIL_Q_WIDTH = DIL_GROUPS * DIL_SLOTS * DIL_HEAD_DIM
DIL_KV_WIDTH = DIL_SLOTS * DIL_HEAD_DIM

PEER_N_KEYS = 128
PEER_EXPERTS = PEER_N_KEYS * PEER_N_KEYS
PEER_HEADS = 8
PEER_TOPK = 16
PEER_QDIM = 256
PEER_HALF = PEER_QDIM // 2
PEER_TOKEN_BLOCK = 128

A_Q0, A_Q1 = 0, GLA_QK_WIDTH
A_K0, A_K1 = A_Q1, A_Q1 + GLA_QK_WIDTH
A_V0, A_V1 = A_K1, A_K1 + GLA_V_WIDTH
A_R0, A_R1 = A_V1, A_V1 + GLA_V_WIDTH
A_G0, A_G1 = A_R1, A_R1 + GLA_GATE_RANK
A_M0, A_M1 = A_G1, A_G1 + MEM_WIDTH
A_IN_WIDTH = A_M1
B_Q0, B_Q1 = 0, DIL_Q_WIDTH
B_M0, B_M1 = B_Q1, B_Q1 + MEM_WIDTH
B_IN_WIDTH = B_M1

kernel_name = "hybrid_gla_dilated_peer_deepnorm"


def _layer_norm(x, g, b):
    xf = x.astype(jnp.float32)
    mu = jnp.mean(xf, axis=-1, keepdims=True)
    var = jnp.mean(jnp.square(xf - mu), axis=-1, keepdims=True)
    y = (xf - mu) * lax.rsqrt(var + LN_EPS)
    return (y * g.astype(jnp.float32) + b.astype(jnp.float32)).astype(x.dtype)


def _gla(q, k, v, log_a):
    B, S, H, K = q.shape
    nc = S // GLA_CHUNK

    def chunks(t):
        return t.astype(jnp.float32).reshape(B, nc, GLA_CHUNK, H, -1).transpose(1, 0, 3, 2, 4)

    q_c = chunks(q) * (K ** -0.5)
    k_c, v_c, g_c = chunks(k), chunks(v), chunks(log_a)
    b = jnp.cumsum(g_c, axis=3)
    b_last = b[:, :, :, -1:, :]
    q_t = q_c * jnp.exp(b)
    k_t = k_c * jnp.exp(-b)
    k_end = k_c * jnp.exp(b_last - b)
    causal = jnp.tril(jnp.ones((GLA_CHUNK, GLA_CHUNK), dtype=bool))
    attn = jnp.where(causal, jnp.einsum('nbhck,nbhsk->nbhcs', q_t, k_t), 0.0)
    o_intra = jnp.einsum('nbhcs,nbhsv->nbhcv', attn, v_c)

    def step(state, inp):
        qt, ke, vc, dec = inp
        o = jnp.einsum('bhck,bhkv->bhcv', qt, state)
        state = dec[..., None] * state + jnp.einsum('bhck,bhcv->bhkv', ke, vc)
        return state, o

    s0 = jnp.zeros((B, H, K, v.shape[-1]), jnp.float32)
    _, o_inter = lax.scan(step, s0, (q_t, k_end, v_c, jnp.exp(b_last[:, :, :, 0, :])))
    return (o_intra + o_inter).transpose(1, 0, 3, 2, 4).reshape(B, S, H, -1)


def _dilated_group(q, k, v, window, dilation):
    B, S, H, E = q.shape
    L = S // dilation
    nb = -(-L // DIL_BLOCK)
    Lp = nb * DIL_BLOCK
    span = window // dilation

    def sub(t):
        return t.reshape(B, L, dilation, H, E).transpose(0, 2, 3, 1, 4)

    qs = jnp.pad(sub(q), ((0, 0), (0, 0), (0, 0), (0, Lp - L), (0, 0)))
    qs = qs.reshape(B, dilation, H, nb, DIL_BLOCK, E)

    def kv_blocks(t):
        tp = jnp.pad(sub(t), ((0, 0), (0, 0), (0, 0), (DIL_BLOCK, Lp - L), (0, 0)))
        prev = tp[:, :, :, :Lp].reshape(B, dilation, H, nb, DIL_BLOCK, E)
        cur = tp[:, :, :, DIL_BLOCK:].reshape(B, dilation, H, nb, DIL_BLOCK, E)
        return jnp.concatenate([prev, cur], axis=4)

    kb, vb = kv_blocks(k), kv_blocks(v)
    s = jnp.einsum('bdhnqe,bdhnke->bdhnqk', qs, kb,
                   preferred_element_type=jnp.float32) * (E ** -0.5)
    qi = jnp.arange(DIL_BLOCK)[:, None]
    kj = jnp.arange(2 * DIL_BLOCK)[None, :]
    rel = DIL_BLOCK + qi - kj
    band = (rel >= 0) & (rel <= span)
    not_front = (jnp.arange(nb)[:, None, None] > 0) | (kj[None] >= DIL_BLOCK)
    mask = band[None] & not_front
    s = jnp.where(mask, s, -jnp.inf)
    lse = jax.nn.logsumexp(s, axis=-1)
    p = jnp.exp(s - lse[..., None])
    o = jnp.einsum('bdhnqk,bdhnke->bdhnqe', p.astype(vb.dtype), vb,
                   preferred_element_type=jnp.float32)
    o = o.reshape(B, dilation, H, Lp, E)[:, :, :, :L].transpose(0, 3, 1, 2, 4).reshape(B, S, H, E)
    lse = lse.reshape(B, dilation, H, Lp)[..., :L].transpose(0, 3, 1, 2).reshape(B, S, H)
    return o, lse


def _dilated_attention(q_all, k, v):
    outs, lses = [], []
    for g, (window, dilation) in enumerate(DIL_PAIRS):
        o, l = _dilated_group(q_all[:, :, g], k, v, window, dilation)
        outs.append(o)
        lses.append(l)
    wts = jax.nn.softmax(jnp.stack(lses, axis=0), axis=0)
    return jnp.einsum('gbsh,gbshe->bshe', wts, jnp.stack(outs, axis=0))


def _mem_attn(qm, km, vm):
    s = jnp.einsum('bshe,bmhe->bhsm', qm, km,
                   preferred_element_type=jnp.float32) * (qm.shape[-1] ** -0.5)
    p = jax.nn.softmax(s, axis=-1)
    return jnp.einsum('bhsm,bmhe->bshe', p.astype(vm.dtype), vm)


def _peer(x, w_q, sub_keys, u, v):
    B, S, D = x.shape
    T = B * S
    xt = x.reshape(T, D)
    q = (xt @ w_q).reshape(T, PEER_HEADS, 2, PEER_HALF)
    s = jnp.einsum('thpe,pne->thpn', q, sub_keys, preferred_element_type=jnp.float32)
    top_s, top_i = lax.top_k(s, PEER_TOPK)
    cand = (top_s[:, :, 0, :, None] + top_s[:, :, 1, None, :]).reshape(
        T, PEER_HEADS, PEER_TOPK * PEER_TOPK)
    best_s, best_j = lax.top_k(cand, PEER_TOPK)
    idx_a = jnp.take_along_axis(top_i[:, :, 0], best_j // PEER_TOPK, axis=-1)
    idx_b = jnp.take_along_axis(top_i[:, :, 1], best_j % PEER_TOPK, axis=-1)
    expert = idx_a * PEER_N_KEYS + idx_b
    gate = jax.nn.softmax(best_s, axis=-1)
    nblk = T // PEER_TOKEN_BLOCK

    def block(args):
        xb, eb, gb = args
        h = jnp.einsum('td,thkd->thk', xb, u[eb], preferred_element_type=jnp.float32)
        a = (jax.nn.gelu(h, approximate=False) * gb).astype(xb.dtype)
        return jnp.einsum('thk,thkd->td', a, v[eb])

    out = lax.map(block, (xt.reshape(nblk, PEER_TOKEN_BLOCK, D),
                          expert.reshape(nblk, PEER_TOKEN_BLOCK, PEER_HEADS, PEER_TOPK),
                          gate.reshape(nblk, PEER_TOKEN_BLOCK, PEER_HEADS, PEER_TOPK)))
    return out.reshape(B, S, D).astype(x.dtype)


def setup_inputs(seed: int = 0) -> dict:
    key = jax.random.key(seed)
    ks = jax.random.split(key, 20)
    n = jax.random.normal
    D = D_MODEL
    return {
        'x': n(ks[0], (BATCH, SEQ, D), jnp.float32),
        'mem': n(ks[1], (BATCH, MEM_LEN, D), jnp.float32),
        'a_w_in': n(ks[2], (N_A_LAYERS, D, A_IN_WIDTH), jnp.float32) * D ** -0.5,
        'a_w_gate2': n(ks[3], (N_A_LAYERS, GLA_GATE_RANK, GLA_QK_WIDTH), jnp.float32) * GLA_GATE_RANK ** -0.5,
        'a_b_gate': 0.02 * n(ks[4], (N_A_LAYERS, GLA_QK_WIDTH), jnp.float32),
        'a_norm_g': 1.0 + 0.02 * n(ks[5], (N_A_LAYERS, GLA_V_WIDTH), jnp.float32),
        'b_w_in': n(ks[6], (N_B_LAYERS, D, B_IN_WIDTH), jnp.float32) * D ** -0.5,
        'shared_w_kv': n(ks[7], (D, 2 * DIL_KV_WIDTH), jnp.float32) * D ** -0.5,
        'w_mem_kv': n(ks[8], (DEPTH, D, 2 * MEM_WIDTH), jnp.float32) * D ** -0.5,
        'w_out': n(ks[9], (DEPTH, MIX_WIDTH, D), jnp.float32) * (MIX_WIDTH ** -0.5 * DN_BETA),
        'ln_mix_g': 1.0 + 0.02 * n(ks[10], (DEPTH, D), jnp.float32),
        'ln_mix_b': 0.02 * n(ks[11], (DEPTH, D), jnp.float32),
        'ln_ffn_g': 1.0 + 0.02 * n(ks[12], (DEPTH, D), jnp.float32),
        'ln_ffn_b': 0.02 * n(ks[13], (DEPTH, D), jnp.float32),
        'peer_w_q': n(ks[14], (DEPTH, D, PEER_HEADS * PEER_QDIM), jnp.float32) * D ** -0.5,
        'peer_sub_keys': n(ks[15], (DEPTH, 2, PEER_N_KEYS, PEER_HALF), jnp.float32) * PEER_HALF ** -0.5,
        'peer_u': n(ks[16], (DEPTH, PEER_EXPERTS, D), jnp.float32) * D ** -0.5,
        'peer_v': n(ks[17], (DEPTH, PEER_EXPERTS, D), jnp.float32) * (DN_BETA * PEER_HEADS ** -0.5),
    }


def reference(x, mem, a_w_in, a_w_gate2, a_b_gate, a_norm_g, b_w_in, shared_w_kv,
              w_mem_kv, w_out, ln_mix_g, ln_mix_b, ln_ffn_g, ln_ffn_b,
              peer_w_q, peer_sub_keys, peer_u, peer_v):
    B, S, _ = x.shape
    shared_k = None
    shared_v = None
    for l in range(DEPTH):
        mkv = (mem @ w_mem_kv[l]).reshape(B, MEM_LEN, 2, MEM_HEADS, MEM_HEAD_DIM)
        if l < N_A_LAYERS:
            h = x @ a_w_in[l]
            q = h[..., A_Q0:A_Q1].reshape(B, S, GLA_HEADS, GLA_DK)
            k = h[..., A_K0:A_K1].reshape(B, S, GLA_HEADS, GLA_DK)
            v = h[..., A_V0:A_V1].reshape(B, S, GLA_HEADS, GLA_DV)
            r = h[..., A_R0:A_R1]
            g_pre = (h[..., A_G0:A_G1] @ a_w_gate2[l] + a_b_gate[l]).astype(jnp.float32)
            log_a = (jax.nn.log_sigmoid(g_pre) / GLA_TAU).reshape(B, S, GLA_HEADS, GLA_DK)
            o = _gla(q, k, v, log_a)
            o = o * lax.rsqrt(jnp.mean(jnp.square(o), axis=-1, keepdims=True) + HEAD_NORM_EPS)
            o = o.reshape(B, S, GLA_V_WIDTH) * a_norm_g[l].astype(jnp.float32)
            mix = (jax.nn.silu(r.astype(jnp.float32)) * o).astype(x.dtype)
            qm = h[..., A_M0:A_M1]
        else:
            h = x @ b_w_in[l - N_A_LAYERS]
            qd = h[..., B_Q0:B_Q1].reshape(B, S, DIL_GROUPS, DIL_SLOTS, DIL_HEAD_DIM)
            mix = _dilated_attention(qd, shared_k, shared_v).reshape(B, S, DIL_KV_WIDTH).astype(x.dtype)
            qm = h[..., B_M0:B_M1]
        mo = _mem_attn(qm.reshape(B, S, MEM_HEADS, MEM_HEAD_DIM),
                       mkv[:, :, 0], mkv[:, :, 1]).reshape(B, S, MEM_WIDTH).astype(x.dtype)
        y = jnp.concatenate([mix, mo], axis=-1) @ w_out[l]
        x = _layer_norm(DN_ALPHA * x + y, ln_mix_g[l], ln_mix_b[l])
        f = _peer(x, peer_w_q[l], peer_sub_keys[l], peer_u[l], peer_v[l])
        x = _layer_norm(DN_ALPHA * x + f, ln_ffn_g[l], ln_ffn_b[l])
        if l == N_A_LAYERS - 1:
            kv = (x @ shared_w_kv).reshape(B, S, 2, DIL_SLOTS, DIL_HEAD_DIM)
            shared_k = kv[:, :, 0]
            shared_v = kv[:, :, 1]
    return x
```

```python
from contextlib import ExitStack
import numpy as np
import concourse.bass as bass
import concourse.mybir as mybir
from concourse.bass_utils import run_bass_kernel_spmd

F32 = mybir.dt.float32
BF16 = mybir.dt.bfloat16
ALU = mybir.AluOpType
AF = mybir.ActivationFunctionType
AX = mybir.AxisListType

D = 1024
SEQ = 4096
NT = SEQ // 128
DN_ALPHA = 4.0 ** 0.25
LN_EPS = 1e-5
HN_EPS = 1e-6
A_W = 2576
B_W = 2560


class Sched:
    def __init__(self, nc, stack):
        self.nc = nc
        self.E = {'pe': nc.tensor, 'dve': nc.vector, 'act': nc.scalar, 'pool': nc.gpsimd, 'sp': nc.sync}
        self.sem = {e: stack.enter_context(nc.semaphore('s_' + e)) for e in self.E}
        self.cnt = {e: 0 for e in self.E}
        self.seen = {e: {} for e in self.E}
        self.NDS = 32
        self.dsem = [stack.enter_context(nc.semaphore('d%d' % i)) for i in range(self.NDS)]
        self.dcnt = [0] * self.NDS
        self.dnext = 0
        self.tiles = {}
        self.nins = 0

    def _st(self, key):
        if key not in self.tiles:
            self.tiles[key] = {'w': None, 'r': {}}
        return self.tiles[key]

    def _semobj(self, sk):
        return self.sem[sk] if isinstance(sk, str) else self.dsem[sk]

    def _wait(self, eng, sk, val):
        if self.seen[eng].get(sk, 0) >= val:
            return
        self.E[eng].wait_ge(self._semobj(sk), val)
        self.seen[eng][sk] = val
        self.nins += 1

    def _deps(self, eng, reads, writes):
        for k in reads:
            st = self._st(k)
            if st['w'] is not None:
                self._wait(eng, *st['w'])
        for k in writes:
            st = self._st(k)
            if st['w'] is not None:
                self._wait(eng, *st['w'])
            for sk, v in st['r'].items():
                self._wait(eng, sk, v)

    def _mark(self, sk, val, reads, writes):
        for k in reads:
            st = self._st(k)
            st['r'][sk] = max(st['r'].get(sk, 0), val)
        for k in writes:
            st = self._st(k)
            st['w'] = (sk, val)
            st['r'] = {}

    def op(self, eng, fn, reads=(), writes=()):
        self._deps(eng, reads, writes)
        ins = fn(self.E[eng])
        self.cnt[eng] += 1
        ins.then_inc(self.sem[eng], 1)
        self._mark(eng, self.cnt[eng], reads, writes)
        self.nins += 1
        return ins

    def dma(self, eng, out, in_, reads=(), writes=(), **kw):
        s = self.dnext
        self.dnext = (self.dnext + 1) % self.NDS
        if self.dcnt[s] > 0:
            self._wait(eng, s, self.dcnt[s])
        self._deps(eng, reads, writes)
        ins = self.E[eng].dma_start(out=out, in_=in_, **kw)
        self.dcnt[s] += 16
        ins.then_inc(self.dsem[s], 16)
        self._mark(s, self.dcnt[s], reads, writes)
        self.nins += 1
        return ins

    def barrier(self):
        for e in self.E:
            for s in range(self.NDS):
                if self.dcnt[s] > 0:
                    self._wait(e, s, self.dcnt[s])
            for o in self.E:
                if o != e and self.cnt[o] > 0:
                    self._wait(e, o, self.cnt[o])

    def finish(self, eng='sp'):
        for s in range(self.NDS):
            if self.dcnt[s] > 0:
                self._wait(eng, s, self.dcnt[s])
        for e in self.E:
            if e != eng and self.cnt[e] > 0:
                self._wait(eng, e, self.cnt[e])


class Ctx:
    pass


def _consts_host():
    idx = np.arange(128)
    same = (idx[:, None] // 64) == (idx[None, :] // 64)
    M2 = (same & (idx[:, None] <= idx[None, :])).astype(np.float32)
    U2 = (same & (idx[:, None] > idx[None, :])).astype(np.float32)
    mp = (idx[:, None] >= idx[None, :]).astype(np.float32)
    mc = (idx[:, None] <= idx[None, :]).astype(np.float32)
    return {
        'c_ident': np.eye(128, dtype=np.float32),
        'c_m2': M2, 'c_u2': U2,
        'c_dmask': np.ascontiguousarray(np.stack([mp, mc], axis=1)),
        'c_ones': np.ones((128, 128), dtype=np.float32),
    }


def build(phases=('A', 'P0', 'B', 'P1'), dbg=False):
    nc = bass.Bass("TRN2", target_bir_lowering=False)
    C = Ctx()
    C.nc = nc
    st = ExitStack()
    C.st = st
    S = Sched(nc, st)
    C.S = S

    def dram(name, shape, dt=F32, kind="ExternalInput"):
        return nc.dram_tensor(name, list(shape), dt, kind=kind).ap()
    C.dram = dram

    def sb(name, shape, dt=F32):
        return st.enter_context(nc.sbuf_tensor(name, list(shape), dt))
    C.sb = sb

    C.B = [st.enter_context(nc.psum_tensor("bank%d" % i, [128, 512], F32)) for i in range(8)]

    C.ident = sb("ident", [128, 128], BF16)
    C.m2 = sb("m2", [128, 128], F32)
    C.u2 = sb("u2", [128, 128], F32)
    C.ones_f = sb("ones_f", [128, 128], F32)
    C.ones_b = sb("ones_b", [128, 128], BF16)
    C.dmask = sb("dmask", [128, 2, 128], BF16)
    S.dma('pool', C.ident[:], dram('c_ident', [128, 128]), writes=['ident'])
    S.dma('sp', C.m2[:], dram('c_m2', [128, 128]), writes=['m2'])
    S.dma('sp', C.u2[:], dram('c_u2', [128, 128]), writes=['u2'])
    c_ones = dram('c_ones', [128, 128])
    S.dma('sp', C.ones_f[:], c_ones, writes=['ones_f'])
    S.dma('pool', C.ones_b[:], c_ones, writes=['ones_b'])
    S.dma('pool', C.dmask[:], dram('c_dmask', [128, 2, 128]), writes=['dmask'])

    ext_in = "ExternalInput"
    inter = "ExternalOutput" if dbg else "Internal"
    C.out = None
    names = {'A': 'xm0', 'P0': 'xf0', 'B': 'xm1'}
    prev = {'P0': 'xm0', 'B': 'xf0', 'P1': 'xm1'}
    for ph in ('A', 'P0', 'B', 'P1'):
        if ph not in phases:
            continue
        if ph in prev and not hasattr(C, prev[ph]):
            setattr(C, prev[ph], dram(prev[ph], [SEQ, D], F32, ext_in))
            setattr(C, prev[ph] + 'T', dram(prev[ph] + 'T', [D, SEQ], BF16, ext_in))
        if ph in names:
            setattr(C, names[ph], dram(names[ph], [SEQ, D], F32, inter))
            setattr(C, names[ph] + 'T', dram(names[ph] + 'T', [D, SEQ], BF16, inter))
        if ph == 'A':
            phase_A(C)
        elif ph == 'P0':
            phase_P(C, 0, C.xm0, C.xm0T, C.xf0, C.xf0T)
        elif ph == 'B':
            phase_B(C, C.xf0, C.xf0T, C.xm1, C.xm1T)
        else:
            C.out = dram('out', [SEQ, D], F32, "ExternalOutput")
            phase_P(C, 1, C.xm1, C.xm1T, C.out, None)
        S.barrier()
    S.finish('sp')
    st.close()
    return nc


def ln_epilogue(C, pre, key_pre, g_rep, b_rep, out_dram, outT_dram, i, tag, bank):
    S = C.S
    stt, mv, rs, xn, xnb, xnT = C.ln_st, C.ln_mv, C.ln_rs, C.ln_xn, C.ln_xnb, C.ln_xnT
    for hlf in range(2):
        S.op('dve', lambda e: e.bn_stats(out=stt[:, hlf, :], in_=pre[:, hlf * 512:(hlf + 1) * 512]), reads=[key_pre], writes=['ln_st'])
    S.op('dve', lambda e: e.bn_aggr(out=mv[:], in_=stt[:].rearrange("p a b -> p (a b)")), reads=['ln_st'], writes=['ln_mv'])
    S.op('act', lambda e: e.activation(out=rs[:], in_=mv[:, 1:2], func=AF.Sqrt, bias=LN_EPS, scale=1.0), reads=['ln_mv'], writes=['ln_rs'])
    S.op('dve', lambda e: e.reciprocal(out=rs[:], in_=rs[:]), reads=['ln_rs'], writes=['ln_rs'])
    S.op('dve', lambda e: e.tensor_scalar(out=xn[:], in0=pre[:], scalar1=mv[:, 0:1], scalar2=rs[:, 0:1], op0=ALU.subtract, op1=ALU.mult),
         reads=[key_pre, 'ln_mv', 'ln_rs'], writes=['ln_xn'])
    S.op('pool', lambda e: e.tensor_tensor(out=xn[:], in0=xn[:], in1=g_rep[:], op=ALU.mult), reads=['ln_xn', 'lnp'], writes=['ln_xn'])
    S.op('pool', lambda e: e.tensor_tensor(out=xn[:], in0=xn[:], in1=b_rep[:], op=ALU.add), reads=['ln_xn', 'lnp'], writes=['ln_xn'])
    S.dma('sp', out_dram[i * 128:(i + 1) * 128, :], xn[:], reads=['ln_xn'])
    if outT_dram is not None:
        S.op('act', lambda e: e.copy(out=xnb[:], in_=xn[:]), reads=['ln_xn'], writes=['ln_xnb'])
        pb = C.B[bank][:].bitcast(BF16)
        for c in range(8):
            S.op('pe', lambda e: e.transpose(pb[:, c * 128:(c + 1) * 128], xnb[:, c * 128:(c + 1) * 128], C.ident[:]),
                 reads=['ln_xnb', 'ident'], writes=['B%d' % bank])
        S.op('act', lambda e: e.copy(out=xnT[:].rearrange("p c t -> p (c t)"), in_=pb[:, :]), reads=['B%d' % bank], writes=['ln_xnT'])
        S.dma('sp', outT_dram[:, i * 128:(i + 1) * 128].rearrange("(c p) t -> p c t", p=128), xnT[:], reads=['ln_xnT'])


def alloc_ln(C):
    sb = C.sb
    C.ln_st = sb("ln_st", [128, 2, 6])
    C.ln_mv = sb("ln_mv", [128, 2])
    C.ln_rs = sb("ln_rs", [128, 1])
    C.ln_xn = sb("ln_xn", [128, 1024])
    C.ln_xnb = sb("ln_xnb", [128, 1024], BF16)
    C.ln_xnT = sb("ln_xnT", [128, 8, 128], BF16)


def load_w_bf(C, name, dram_ap, ncols, key):
    t = C.sb(name, [128, 8, ncols], BF16)
    for c in range(8):
        C.S.dma('pool', t[:, c, :], dram_ap[c * 128:(c + 1) * 128, :], writes=[key])
    return t


def load_rep(C, name, dram_ap, n, key):
    t = C.sb(name, [128, n], F32)
    C.S.dma('sp', t[:], dram_ap.partition_broadcast(128), writes=[key])
    return t


def mem_kv(C, memT_d, wmkv_d, tag):
    S, sb, B = C.S, C.sb, C.B
    memT = load_w_bf(C, "memT" + tag, memT_d, 256, 'memT' + tag)
    wm = load_w_bf(C, "wmkv" + tag, wmkv_d, 512, 'wmkv' + tag)
    kmT = sb("kmT" + tag, [64, 4, 256], BF16)
    vmx = sb("vmx" + tag, [128, 2, 4, 65], BF16)
    S.op('pool', lambda e: e.memset(vmx[:].rearrange("p a b c -> p (a b c)"), 1.0), writes=['vmx' + tag])
    for h in range(4):
        for c in range(8):
            S.op('pe', lambda e: e.matmul(B[h % 2][0:64, (h // 2) * 256:(h // 2) * 256 + 256],
                                          wm[:, c, h * 64:(h + 1) * 64], memT[:, c, :], start=(c == 0), stop=(c == 7)),
                 reads=['memT' + tag, 'wmkv' + tag], writes=['B%d' % (h % 2)])
        S.op('act', lambda e: e.copy(out=kmT[:, h, :], in_=B[h % 2][0:64, (h // 2) * 256:(h // 2) * 256 + 256]), reads=['B%d' % (h % 2)], writes=['kmT' + tag])
    for j in range(2):
        for c in range(8):
            S.op('pe', lambda e: e.matmul(B[2 + j][:, 0:256], memT[:, c, j * 128:(j + 1) * 128], wm[:, c, 256:512], start=(c == 0), stop=(c == 7)),
                 reads=['memT' + tag, 'wmkv' + tag], writes=['B%d' % (2 + j)])
        S.op('act', lambda e: e.copy(out=vmx[:, j, :, 0:64], in_=B[2 + j][:, 0:256].rearrange("p (h e) -> p h e", h=4)),
             reads=['B%d' % (2 + j)], writes=['vmx' + tag])
    return kmT, vmx


def mem_attn_tile(C, qmT, key_qmT, kmT, vmx, tag, cat, key_cat, bs, bm):
    S, B = C.S, C.B
    pT = C.ma_pT
    for h in range(4):
        bk = bs[h // 2]
        for j in range(2):
            col = ((h % 2) * 2 + j) * 128
            S.op('pe', lambda e: e.matmul(B[bk][:, col:col + 128], kmT[:, h, j * 128:(j + 1) * 128], qmT[:, h, :], start=True, stop=True),
                 reads=['kmT' + tag, key_qmT], writes=['B%d' % bk])
    for hh in range(2):
        S.op('act', lambda e: e.activation(out=pT[:, hh * 512:(hh + 1) * 512], in_=B[bs[hh]][:, :], func=AF.Exp, scale=0.125),
             reads=['B%d' % bs[hh]], writes=['ma_pT'])
    for h in range(4):
        for j in range(2):
            col = (h * 2 + j) * 128
            S.op('pe', lambda e: e.matmul(B[bm][:, h * 65:h * 65 + 65], pT[:, col:col + 128], vmx[:, j, h, :], start=(j == 0), stop=(j == 1)),
                 reads=['ma_pT', 'vmx' + tag], writes=['B%d' % bm])
    mo = B[bm][:, 0:260].rearrange("p (h e) -> p h e", h=4)
    S.op('dve', lambda e: e.reciprocal(out=C.ma_rd[:], in_=mo[:, :, 64]), reads=['B%d' % bm], writes=['ma_rd'])
    S.op('dve', lambda e: e.tensor_tensor(out=cat[:, 768:1024].rearrange("p (h e) -> p h e", h=4), in0=mo[:, :, 0:64],
                                          in1=C.ma_rd[:].unsqueeze(2).to_broadcast([128, 4, 64]), op=ALU.mult),
         reads=['B%d' % bm, 'ma_rd'], writes=[key_cat])


def outproj_tile(C, cat, key_cat, wout, key_wout, XR, key_XR, bt, by, pre, chunks=range(8)):
    S, B = C.S, C.B
    pb = B[bt][:].bitcast(BF16)
    chunks = list(chunks)
    for c in chunks:
        S.op('pe', lambda e: e.transpose(pb[:, c * 128:(c + 1) * 128], cat[:, c * 128:(c + 1) * 128], C.ident[:]),
             reads=[key_cat, 'ident'], writes=['B%d' % bt])
    c0, c1 = chunks[0], chunks[-1] + 1
    S.op('act', lambda e: e.copy(out=C.catT[:, c0:c1, :].rearrange("p c t -> p (c t)"), in_=pb[:, c0 * 128:c1 * 128]), reads=['B%d' % bt], writes=['catT'])
    for hlf in range(2):
        for c in range(8):
            S.op('pe', lambda e: e.matmul(B[by[hlf]][:, :], C.catT[:, c, :], wout[:, c, hlf * 512:(hlf + 1) * 512], start=(c == 0), stop=(c == 7)),
                 reads=['catT', key_wout], writes=['B%d' % by[hlf]])
        S.op('dve', lambda e: e.scalar_tensor_tensor(out=pre[:, hlf * 512:(hlf + 1) * 512], in0=XR[:, hlf * 512:(hlf + 1) * 512], scalar=DN_ALPHA,
                                                     in1=B[by[hlf]][:, :], op0=ALU.mult, op1=ALU.add),
             reads=[key_XR, 'B%d' % by[hlf]], writes=['pre'])


def phase_A(C):
    S, B, dram, nc = C.S, C.B, C.dram, C.nc
    st = ExitStack()
    sb = lambda name, shape, dt=F32: st.enter_context(nc.sbuf_tensor("a_" + name, list(shape), dt))
    C.sb_save = C.sb
    C.sb = sb
    x_d = dram('x', [SEQ, D]); xT_d = dram('xT', [D, SEQ])
    C.memT_d = dram('memT', [D, 256])
    W = load_w_bf(C, "awin", dram('a_w_in', [D, A_W]), A_W, 'awin')
    wout = load_w_bf(C, "wout0", dram('w_out0', [D, D]), D, 'wout0')
    wg2 = sb("wg2", [16, 384]); S.dma('sp', wg2[:], dram('a_w_gate2', [16, 384]), writes=['wg2'])
    bg = sb("bg", [1, 384]); S.dma('sp', bg[:], dram('a_b_gate', [1, 384]), writes=['bg'])
    ng = load_rep(C, "ng", dram('a_norm_g', [1, 768]), 768, 'ng')
    lng = load_rep(C, "lng", dram('ln_mix_g0', [1, D]), D, 'lnp')
    lnb = load_rep(C, "lnb", dram('ln_mix_b0', [1, D]), D, 'lnp')
    kmT, vmx = mem_kv(C, C.memT_d, dram('w_mem_kv0', [D, 512]), '0')
    alloc_ln(C)
    C.ma_pT = sb("ma_pT", [128, 1024], BF16)
    C.ma_rd = sb("ma_rd", [128, 4])
    C.catT = sb("catT", [128, 8, 128], BF16)
    XT = [sb("XT%d" % j, [128, 8, 128], BF16) for j in range(2)]
    XR = [sb("XR%d" % j, [128, 1024]) for j in range(2)]
    hgT = sb("hgT", [16, 128])
    qmT = sb("qmT", [64, 4, 128], BF16)
    t1 = sb("a_t1", [128, 384]); la = sb("a_la", [128, 384])
    expb = sb("expb", [96, 4, 128]); expnb = sb("expnb", [96, 4, 128]); expE = sb("expE", [128, 384])
    qt = sb("qt", [96, 4, 128], BF16); kt = sb("kt", [96, 4, 128], BF16)
    kend = sb("kend", [128, 384], BF16); vb = sb("vb", [128, 768], BF16)
    gate = sb("gate", [128, 768])
    attnTb = sb("attnTb", [128, 4, 128], BF16)
    St = sb("St", [96, 4, 192]); SbA = [sb("SbA%d" % q, [96, 4, 192], BF16) for q in range(2)]; SbB = sb("SbB", [96, 4, 192], BF16)
    sq = sb("sq", [128, 768]); ssq = sb("ssq", [128, 4]); to = sb("to", [128, 768])
    cat = sb("cat", [128, 1024], BF16)
    pre = sb("pre", [128, 1024])
    S.op('pool', lambda e: e.memset(St[:].rearrange("p a b -> p (a b)"), 0.0), writes=['St'])
    S.op('pool', lambda e: e.memset(SbA[0][:].rearrange("p a b -> p (a b)"), 0.0), writes=['SbA0'])
    QS = 96 ** -0.5

    def load(i):
        j = i % 2
        S.dma('pool', XT[j][:], xT_d[:, i * 128:(i + 1) * 128].rearrange("(c p) t -> p c t", p=128), writes=['XT%d' % j])
        S.dma('sp', XR[j][:], x_d[i * 128:(i + 1) * 128, :], writes=['XR%d' % j])

    load(0)
    for i in range(NT):
        j = i % 2
        if i + 1 < NT:
            load(i + 1)
        xt = XT[j]; kx = 'XT%d' % j
        for h in range(4):
            for c in range(8):
                S.op('pe', lambda e: e.matmul(B[0][0:96, h * 128:(h + 1) * 128], W[:, c, h * 96:(h + 1) * 96], xt[:, c, :], start=(c == 0), stop=(c == 7)),
                     reads=['awin', kx], writes=['B0'])
        for h in range(4):
            for c in range(8):
                S.op('pe', lambda e: e.matmul(B[1][0:96, h * 128:(h + 1) * 128], W[:, c, 384 + h * 96:384 + (h + 1) * 96], xt[:, c, :], start=(c == 0), stop=(c == 7)),
                     reads=['awin', kx], writes=['B1'])
        for (bk, c0, n) in ((2, 384, 384), (3, 768, 512), (4, 1280, 512), (5, 1792, 512)):
            for c in range(8):
                S.op('pe', lambda e: e.matmul(B[bk][:, 0:n], xt[:, c, :], W[:, c, c0:c0 + n], start=(c == 0), stop=(c == 7)),
                     reads=['awin', kx], writes=['B%d' % bk])
        for h in range(4):
            for c in range(8):
                S.op('pe', lambda e: e.matmul(B[6][0:64, h * 128:(h + 1) * 128], W[:, c, 2320 + h * 64:2320 + (h + 1) * 64], xt[:, c, :], start=(c == 0), stop=(c == 7)),
                     reads=['awin', kx], writes=['B6'])
        for c in range(8):
            S.op('pe', lambda e: e.matmul(B[7][0:16, 0:128], W[:, c, 2304:2320], xt[:, c, :], start=(c == 0), stop=(c == 7)),
                 reads=['awin', kx], writes=['B7'])
        S.op('act', lambda e: e.copy(out=qmT[:].rearrange("p h t -> p (h t)"), in_=B[6][0:64, :]), reads=['B6'], writes=['qmT'])
        S.op('act', lambda e: e.copy(out=hgT[:], in_=B[7][0:16, 0:128]), reads=['B7'], writes=['hgT'])
        S.op('pe', lambda e: e.matmul(B[7][:, 0:384], hgT[:], wg2[:], start=True, stop=False), reads=['hgT', 'wg2'], writes=['B7'])
        S.op('pe', lambda e: e.matmul(B[7][:, 0:384], C.ones_f[0:1, :], bg[:], start=False, stop=True), reads=['ones_f', 'bg'], writes=['B7'])
        S.op('act', lambda e: e.activation(out=t1[:], in_=B[7][:, 0:384], func=AF.Exp, scale=-1.0), reads=['B7'], writes=['a_t1'])
        S.op('act', lambda e: e.activation(out=t1[:], in_=t1[:], func=AF.Ln, bias=1.0, scale=1.0), reads=['a_t1'], writes=['a_t1'])
        S.op('act', lambda e: e.mul(out=la[:], in_=t1[:], mul=-1.0 / 16.0), reads=['a_t1'], writes=['a_la'])
        for h in range(4):
            S.op('pe', lambda e: e.matmul(B[6][0:96, h * 128:(h + 1) * 128], la[:, h * 96:(h + 1) * 96], C.m2[:], start=True, stop=True),
                 reads=['a_la', 'm2'], writes=['B6'])
        S.op('pe', lambda e: e.matmul(B[7][:, 0:384], C.u2[:], la[:], start=True, stop=True), reads=['a_la', 'u2'], writes=['B7'])
        S.op('act', lambda e: e.activation(out=expb[:].rearrange("p h t -> p (h t)"), in_=B[6][0:96, :], func=AF.Exp), reads=['B6'], writes=['expb'])
        S.op('act', lambda e: e.activation(out=expnb[:].rearrange("p h t -> p (h t)"), in_=B[6][0:96, :], func=AF.Exp, scale=-1.0), reads=['B6'], writes=['expnb'])
        S.op('act', lambda e: e.activation(out=expE[:], in_=B[7][:, 0:384], func=AF.Exp), reads=['B7'], writes=['expE'])
        S.op('dve', lambda e: e.scalar_tensor_tensor(out=qt[:].rearrange("p h t -> p (h t)"), in0=B[0][0:96, :], scalar=QS, in1=expb[:].rearrange("p h t -> p (h t)"),
                                                     op0=ALU.mult, op1=ALU.mult), reads=['B0', 'expb'], writes=['qt'])
        S.op('dve', lambda e: e.tensor_tensor(out=kt[:].rearrange("p h t -> p (h t)"), in0=B[1][0:96, :], in1=expnb[:].rearrange("p h t -> p (h t)"), op=ALU.mult),
             reads=['B1', 'expnb'], writes=['kt'])
        S.op('dve', lambda e: e.tensor_tensor(out=kend[:], in0=B[2][:, 0:384], in1=expE[:], op=ALU.mult), reads=['B2', 'expE'], writes=['kend'])
        S.op('act', lambda e: e.copy(out=vb[:, 0:512], in_=B[3][:, :]), reads=['B3'], writes=['vb'])
        S.op('act', lambda e: e.copy(out=vb[:, 512:768], in_=B[4][:, 0:256]), reads=['B4'], writes=['vb'])
        S.op('act', lambda e: e.activation(out=gate[:, 0:256], in_=B[4][:, 256:512], func=AF.Silu), reads=['B4'], writes=['gate'])
        S.op('act', lambda e: e.activation(out=gate[:, 256:768], in_=B[5][:, :], func=AF.Silu), reads=['B5'], writes=['gate'])
        S.op('pool', lambda e: e.tensor_tensor(out=gate[:], in0=gate[:], in1=ng[:], op=ALU.mult), reads=['gate', 'ng'], writes=['gate'])
        for h in range(4):
            S.op('pe', lambda e: e.matmul(B[0][:, h * 128:(h + 1) * 128], kt[:, h, :], qt[:, h, :], start=True, stop=True), reads=['kt', 'qt'], writes=['B0'])
        S.op('dve', lambda e: e.tensor_tensor(out=attnTb[:], in0=B[0][:, :].rearrange("p (h t) -> p h t", h=4),
                                              in1=C.m2[:].unsqueeze(1).to_broadcast([128, 4, 128]), op=ALU.mult), reads=['B0', 'm2'], writes=['attnTb'])
        for ch in range(2):
            for h in range(4):
                bk = 2 + ch * 2 + h // 2
                col = (h % 2) * 192
                S.op('pe', lambda e: e.matmul(B[bk][0:96, col:col + 192], kend[ch * 64:(ch + 1) * 64, h * 96:(h + 1) * 96], vb[ch * 64:(ch + 1) * 64, h * 192:(h + 1) * 192],
                                              start=True, stop=True), reads=['kend', 'vb'], writes=['B%d' % bk])
        def o_ap(h, lo, hi):
            bk = 1 if h < 2 else 6
            col = (h % 2) * 192
            return B[bk][lo:hi, col:col + 192], 'B%d' % bk
        SbS = SbA[i % 2]; kS = 'SbA%d' % (i % 2)
        SbN = SbA[(i + 1) % 2]; kN = 'SbA%d' % ((i + 1) % 2)
        for ch in range(2):
            for h in range(4):
                bk = 2 + ch * 2 + h // 2
                col = (h % 2) * 192
                S.op('dve', lambda e: e.scalar_tensor_tensor(out=St[:, h, :], in0=St[:, h, :], scalar=expb[:, h, ch * 64 + 63:ch * 64 + 64], in1=B[bk][0:96, col:col + 192],
                                                             op0=ALU.mult, op1=ALU.add), reads=['St', 'expb', 'B%d' % bk], writes=['St'])
            Sb_dst, kd = (SbB, 'SbB') if ch == 0 else (SbN, kN)
            S.op('act', lambda e: e.copy(out=Sb_dst[:].rearrange("p a b -> p (a b)"), in_=St[:].rearrange("p a b -> p (a b)")),
                 reads=['St'], writes=[kd])
        for h in range(4):
            oap, ok = o_ap(h, 0, 128)
            S.op('pe', lambda e: e.matmul(oap, attnTb[:, h, :], vb[:, h * 192:(h + 1) * 192], start=True, stop=False), reads=['attnTb', 'vb'], writes=[ok])
            oap0, _ = o_ap(h, 0, 64)
            S.op('pe', lambda e: e.matmul(oap0, qt[:, h, 0:64], SbS[:, h, :], start=False, stop=False), reads=['qt', kS], writes=[ok])
            oap1, _ = o_ap(h, 64, 128)
            S.op('pe', lambda e: e.matmul(oap1, qt[:, h, 64:128], SbB[:, h, :], start=False, stop=True), reads=['qt', 'SbB'], writes=[ok])
        for hh in range(2):
            bk = 1 if hh == 0 else 6
            S.op('act', lambda e: e.activation(out=sq[:, hh * 384:(hh + 1) * 384], in_=B[bk][:, 0:384], func=AF.Square), reads=['B%d' % bk], writes=['sq'])
        S.op('dve', lambda e: e.tensor_reduce(out=ssq[:], in_=sq[:].rearrange("p (h v) -> p h v", h=4), axis=AX.X, op=ALU.add), reads=['sq'], writes=['ssq'])
        S.op('act', lambda e: e.activation(out=ssq[:], in_=ssq[:], func=AF.Sqrt, bias=HN_EPS, scale=1.0 / 192.0), reads=['ssq'], writes=['ssq'])
        S.op('dve', lambda e: e.reciprocal(out=ssq[:], in_=ssq[:]), reads=['ssq'], writes=['ssq'])
        for hh in range(2):
            bk = 1 if hh == 0 else 6
            S.op('dve', lambda e: e.tensor_tensor(out=to[:, hh * 384:(hh + 1) * 384].rearrange("p (h v) -> p h v", h=2),
                                                  in0=B[bk][:, 0:384].rearrange("p (h v) -> p h v", h=2),
                                                  in1=ssq[:, hh * 2:hh * 2 + 2].unsqueeze(2).to_broadcast([128, 2, 192]), op=ALU.mult),
                 reads=['B%d' % bk, 'ssq'], writes=['to'])
        S.op('dve', lambda e: e.tensor_tensor(out=cat[:, 0:768], in0=to[:], in1=gate[:], op=ALU.mult), reads=['to', 'gate'], writes=['cat'])
        mem_attn_tile(C, qmT, 'qmT', kmT, vmx, '0', cat, 'cat', (2, 3), 4)
        outproj_tile(C, cat, 'cat', wout, 'wout0', XR[j], 'XR%d' % j, 5, (7, 0), pre)
        ln_epilogue(C, pre, 'pre', lng, lnb, C.xm0, C.xm0T, i, 'A', 5)
    C.sb = C.sb_save
    st.close()


def phase_P(C, l, xm_d, xmT_d, xf_d, xfT_d, eng_w8='dve'):
    S, B, dram, nc = C.S, C.B, C.dram, C.nc
    st = ExitStack()
    sb = lambda name, shape, dt=F32: st.enter_context(nc.sbuf_tensor("p%d_%s" % (l, name), list(shape), dt))
    P = 'p%d_' % l
    wq = sb("wq", [128, 8, 2048], BF16)
    wq_d = dram('peer_w_q%d' % l, [D, 2048])
    for c in range(8):
        S.dma('pool', wq[:, c, :], wq_d[c * 128:(c + 1) * 128, :], writes=[P + 'wq'])
    keysT = sb("keysT", [128, 2, 128])
    kd = dram('peer_keysT%d' % l, [2, 128, 128])
    for p in range(2):
        S.dma('sp', keysT[:, p, :], kd[p], writes=[P + 'keysT'])
    uT_d = dram('peer_uT%d' % l, [D, 16384])
    v_d = dram('peer_v%d' % l, [16384, D])
    lng = sb("lng", [128, D]); lnb = sb("lnb", [128, D])
    S.dma('sp', lng[:], dram('ln_ffn_g%d' % l, [1, D]).partition_broadcast(128), writes=['lnp'])
    S.dma('sp', lnb[:], dram('ln_ffn_b%d' % l, [1, D]).partition_broadcast(128), writes=['lnp'])
    xmT = sb("xmT", [128, 8, 512], BF16)
    xr = sb("xr", [128, 1024])
    acc = sb("acc", [128, 4, 1024])
    qT = sb("qT", [128, 16, 128])
    s_sb = sb("s_sb", [128, 16, 128])
    tmp = sb("tmp", [128, 4, 256])
    top = sb("top", [128, 8, 2, 16])
    cand = sb("cand", [128, 8, 16, 16])
    c16 = sb("c16", [128, 8, 16])
    dd = sb("dd", [128, 8, 16])
    zz = sb("zz", [128, 8])
    cs = sb("cs", [128, 8, 2])
    E = sb("E", [128, 4, 16, 128])
    gsc = sb("gsc", [128, 4, 8])
    uT = [sb("uT%d" % q, [128, 8, 512], BF16) for q in range(2)]
    vv = [sb("vv%d" % q, [128, 4, 1024], BF16) for q in range(2)]
    gel = [sb("gel%d" % q, [128, 512]) for q in range(2)]
    W8 = sb("W8", [128, 8, 4, 128])
    G = sb("G", [128, 4, 128])
    A = [sb("A%d" % q, [128, 512], BF16) for q in range(2)]
    AT = [sb("AT%d" % q, [128, 4, 128], BF16) for q in range(2)]
    pre = sb("pre", [128, 1024])
    C.ln_st = sb("ln_st", [128, 2, 6]); C.ln_mv = sb("ln_mv", [128, 2]); C.ln_rs = sb("ln_rs", [128, 1])
    C.ln_xn = sb("ln_xn", [128, 1024]); C.ln_xnb = sb("ln_xnb", [128, 1024], BF16); C.ln_xnT = sb("ln_xnT", [128, 8, 128], BF16)
    DELTA = 2e-4
    NEG = -1e30
    cnt = [0]

    def load_chunk(k):
        q = k % 2
        for c in range(8):
            S.dma('pool', uT[q][:, c, :], uT_d[c * 128:(c + 1) * 128, k * 512:(k + 1) * 512], writes=[P + 'uT%d' % q])
        S.dma('pool', vv[q][:], v_d[k * 512:(k + 1) * 512, :].rearrange("(a b) d -> b a d", b=128), writes=[P + 'vv%d' % q])

    for sti in range(SEQ // 512):
        t0 = sti * 512
        S.dma('sp', xmT[:], xmT_d[:, t0:t0 + 512].rearrange("(c p) t -> p c t", p=128), writes=[P + 'xmT'])
        load_chunk(0)
        for tt in range(4):
            for hp in range(16):
                bk = 4 + (hp % 4)
                for c in range(8):
                    S.op('pe', lambda e: e.matmul(B[bk][:, 0:128], wq[:, c, hp * 128:(hp + 1) * 128], xmT[:, c, tt * 128:(tt + 1) * 128], start=(c == 0), stop=(c == 7)),
                         reads=[P + 'wq', P + 'xmT'], writes=['B%d' % bk])
                S.op('act', lambda e: e.copy(out=qT[:, hp, :], in_=B[bk][:, 0:128]), reads=['B%d' % bk], writes=[P + 'qT'])
            for hp in range(16):
                bk = hp // 4
                col = (hp % 4) * 128
                S.op('pe', lambda e: e.matmul(B[bk][:, col:col + 128], qT[:, hp, :], keysT[:, hp % 2, :], start=True, stop=True),
                     reads=[P + 'qT', P + 'keysT'], writes=['B%d' % bk])
            for bk in range(4):
                S.op('act', lambda e: e.copy(out=s_sb[:, bk * 4:(bk + 1) * 4, :].rearrange("p a b -> p (a b)"), in_=B[bk][:, :]), reads=['B%d' % bk], writes=[P + 's_sb'])
            for hp in range(16):
                h, p = hp // 2, hp % 2
                S.op('dve', lambda e: e.max(out=top[:, h, p, 0:8], in_=s_sb[:, hp, :]), reads=[P + 's_sb'], writes=[P + 'top%d' % hp])
            for hp in range(16):
                h, p = hp // 2, hp % 2
                S.op('dve', lambda e: e.match_replace(out=tmp[:, hp % 4, 0:128], in_to_replace=top[:, h, p, 0:8], in_values=s_sb[:, hp, :], imm_value=NEG),
                     reads=[P + 's_sb', P + 'top%d' % hp], writes=[P + 'tmp%d' % (hp % 4)])
                S.op('dve', lambda e: e.max(out=top[:, h, p, 8:16], in_=tmp[:, hp % 4, 0:128]), reads=[P + 'tmp%d' % (hp % 4)], writes=[P + 'top%d' % hp])
            allt = [P + 'top%d' % hp for hp in range(16)]
            S.op('dve', lambda e: e.tensor_tensor(out=cand[:], in0=top[:, :, 0, :].unsqueeze(3).to_broadcast([128, 8, 16, 16]),
                                                  in1=top[:, :, 1, :].unsqueeze(2).to_broadcast([128, 8, 16, 16]), op=ALU.add),
                 reads=allt, writes=[P + 'cand'])
            for h in range(8):
                S.op('dve', lambda e: e.max(out=c16[:, h, 0:8], in_=cand[:, h, :, :].rearrange("p a b -> p (a b)")), reads=[P + 'cand'], writes=[P + 'c16_%d' % h])
            for h in range(8):
                S.op('dve', lambda e: e.match_replace(out=tmp[:, h % 4, :], in_to_replace=c16[:, h, 0:8], in_values=cand[:, h, :, :].rearrange("p a b -> p (a b)"), imm_value=NEG),
                     reads=[P + 'cand', P + 'c16_%d' % h], writes=[P + 'tmp%d' % (h % 4)])
                S.op('dve', lambda e: e.max(out=c16[:, h, 8:16], in_=tmp[:, h % 4, :]), reads=[P + 'tmp%d' % (h % 4)], writes=[P + 'c16_%d' % h])
            allc = [P + 'c16_%d' % h for h in range(8)]
            S.op('dve', lambda e: e.tensor_tensor(out=dd[:], in0=c16[:], in1=c16[:, :, 0:1].to_broadcast([128, 8, 16]), op=ALU.subtract), reads=allc, writes=[P + 'dd'])
            S.op('act', lambda e: e.activation(out=dd[:].rearrange("p a b -> p (a b)"), in_=dd[:].rearrange("p a b -> p (a b)"), func=AF.Exp), reads=[P + 'dd'], writes=[P + 'dd'])
            S.op('dve', lambda e: e.tensor_reduce(out=zz[:], in_=dd[:], axis=AX.X, op=ALU.add), reads=[P + 'dd'], writes=[P + 'zz'])
            S.op('dve', lambda e: e.reciprocal(out=zz[:], in_=zz[:]), reads=[P + 'zz'], writes=[P + 'zz'])
            S.op('dve', lambda e: e.scalar_tensor_tensor(out=gsc[:, tt, :], in0=dd[:, :, 15], scalar=float(np.exp(-DELTA)), in1=zz[:], op0=ALU.mult, op1=ALU.mult),
                 reads=[P + 'dd', P + 'zz'], writes=[P + 'gsc'])
            S.op('dve', lambda e: e.tensor_copy(out=cs[:, :, 0], in_=top[:, :, 0, 0]), reads=allt, writes=[P + 'cs'])
            S.op('dve', lambda e: e.scalar_tensor_tensor(out=cs[:, :, 1], in0=c16[:, :, 15], scalar=-DELTA, in1=top[:, :, 0, 0], op0=ALU.add, op1=ALU.subtract),
                 reads=allc + allt, writes=[P + 'cs'])
            S.op('dve', lambda e: e.tensor_tensor(out=E[:, tt, :, :], in0=s_sb[:], in1=cs[:].rearrange("p h q -> p (h q)").unsqueeze(2).to_broadcast([128, 16, 128]), op=ALU.subtract),
                 reads=[P + 's_sb', P + 'cs'], writes=[P + 'E'])
            S.op('act', lambda e: e.activation(out=E[:, tt, :, :].rearrange("p a b -> p (a b)"), in_=E[:, tt, :, :].rearrange("p a b -> p (a b)"), func=AF.Exp),
                 reads=[P + 'E'], writes=[P + 'E'])
        for k in range(32):
            q = k % 2
            if k + 1 < 32:
                load_chunk(k + 1)
            for tt in range(4):
                z = cnt[0] % 2
                cnt[0] += 1
                bh = z
                for c in range(8):
                    S.op('pe', lambda e: e.matmul(B[bh][:, :], xmT[:, c, tt * 128:(tt + 1) * 128], uT[q][:, c, :], start=(c == 0), stop=(c == 7)),
                         reads=[P + 'xmT', P + 'uT%d' % q], writes=['B%d' % bh])
                S.op('act', lambda e: e.activation(out=gel[z][:], in_=B[bh][:, :], func=AF.Gelu), reads=['B%d' % bh], writes=[P + 'gel%d' % z])
                Ev = E[:, tt, :, :].rearrange("p (h q) n -> p h q n", q=2)
                S.op(eng_w8, lambda e: e.tensor_tensor(out=W8[:], in0=Ev[:, :, 0, k * 4:(k + 1) * 4].unsqueeze(3).to_broadcast([128, 8, 4, 128]),
                                                       in1=Ev[:, :, 1, :].unsqueeze(2).to_broadcast([128, 8, 4, 128]), op=ALU.mult),
                     reads=[P + 'E'], writes=[P + 'W8'])
                W8f = W8[:].rearrange("p h a b -> p (h a b)")
                S.op('dve', lambda e: e.scalar_tensor_tensor(out=W8f, in0=W8f, scalar=1.0, in1=W8f, op0=ALU.is_ge, op1=ALU.mult), reads=[P + 'W8'], writes=[P + 'W8'])
                S.op('dve', lambda e: e.tensor_tensor(out=W8[:].rearrange("p h a b -> p h (a b)"), in0=W8[:].rearrange("p h a b -> p h (a b)"),
                                                      in1=gsc[:, tt, :].unsqueeze(2).to_broadcast([128, 8, 512]), op=ALU.mult),
                     reads=[P + 'W8', P + 'gsc'], writes=[P + 'W8'])
                S.op('dve', lambda e: e.tensor_reduce(out=G[:].rearrange("p a b -> p (a b)"), in_=W8[:].rearrange("p h a b -> p (a b) h"), axis=AX.X, op=ALU.add),
                     reads=[P + 'W8'], writes=[P + 'G'])
                S.op('dve', lambda e: e.tensor_tensor(out=A[z][:], in0=gel[z][:], in1=G[:].rearrange("p a b -> p (a b)"), op=ALU.mult),
                     reads=[P + 'gel%d' % z, P + 'G'], writes=[P + 'A%d' % z])
                bt = 2 + z
                pb = B[bt][:].bitcast(BF16)
                for a in range(4):
                    S.op('pe', lambda e: e.transpose(pb[:, a * 128:(a + 1) * 128], A[z][:, a * 128:(a + 1) * 128], C.ident[:]),
                         reads=[P + 'A%d' % z, 'ident'], writes=['B%d' % bt])
                S.op('act', lambda e: e.copy(out=AT[z][:].rearrange("p a t -> p (a t)"), in_=pb[:, 0:512]), reads=['B%d' % bt], writes=[P + 'AT%d' % z])
                for hlf in range(2):
                    bo = 4 + z * 2 + hlf
                    for a in range(4):
                        S.op('pe', lambda e: e.matmul(B[bo][:, :], AT[z][:, a, :], vv[q][:, a, hlf * 512:(hlf + 1) * 512], start=(a == 0), stop=(a == 3)),
                             reads=[P + 'AT%d' % z, P + 'vv%d' % q], writes=['B%d' % bo])
                    if k == 0:
                        S.op('act', lambda e: e.copy(out=acc[:, tt, hlf * 512:(hlf + 1) * 512], in_=B[bo][:, :]), reads=['B%d' % bo], writes=[P + 'acc%d' % tt])
                    else:
                        S.op('dve', lambda e: e.tensor_tensor(out=acc[:, tt, hlf * 512:(hlf + 1) * 512], in0=acc[:, tt, hlf * 512:(hlf + 1) * 512], in1=B[bo][:, :], op=ALU.add),
                             reads=['B%d' % bo, P + 'acc%d' % tt], writes=[P + 'acc%d' % tt])
        for tt in range(4):
            i = sti * 4 + tt
            S.dma('sp', xr[:], xm_d[i * 128:(i + 1) * 128, :], writes=[P + 'xr'])
            S.op('dve', lambda e: e.scalar_tensor_tensor(out=pre[:], in0=xr[:], scalar=DN_ALPHA, in1=acc[:, tt, :], op0=ALU.mult, op1=ALU.add),
                 reads=[P + 'xr', P + 'acc%d' % tt], writes=['pre'])
            ln_epilogue(C, pre, 'pre', lng, lnb, xf_d, xfT_d, i, 'P%d' % l, 3)
    st.close()


def phase_B(C, xf_d, xfT_d, xm1_d, xm1T_d):
    S, B, dram, nc = C.S, C.B, C.dram, C.nc
    st = ExitStack()
    sb = lambda name, shape, dt=F32: st.enter_context(nc.sbuf_tensor("b_" + name, list(shape), dt))
    C.sb_save = C.sb
    C.sb = sb
    bw_d = dram('b_w_in', [D, B_W])
    kv_d = dram('shared_w_kv', [D, 1536])
    catT_d = dram('catT1', [768, SEQ], BF16, "Internal")
    XF = sb("XF", [128, 8, SEQ], BF16)
    for c in range(8):
        S.dma('sp' if c % 2 == 0 else 'act', XF[:, c, :], xfT_d[c * 128:(c + 1) * 128, :], writes=['XF'])
    st1 = ExitStack()
    sb_outer = sb
    sb = lambda name, shape, dt=F32: st1.enter_context(nc.sbuf_tensor("b1_" + name, list(shape), dt))
    acc = sb("acc", [128, 2, SEQ])
    mixT = sb("mixT", [128, SEQ], BF16)
    KT = sb("KT", [128, SEQ], BF16)
    QT = sb("QT", [128, SEQ], BF16)
    V = sb("V", [128, 32, 128], BF16)
    Wq = sb("Wq", [128, 8, 3, 128], BF16)
    Wk = sb("Wk", [128, 8, 128], BF16)
    Wv = sb("Wv", [128, 8, 128], BF16)
    PT = [sb("PT%d" % q, [128, 2, 128], BF16) for q in range(2)]
    SC = 128 ** -0.5
    DIL = (1, 4, 16)
    cnt = [0]

    def proj_T(dst, key_dst, wsel, key_w):
        for tg in range(8):
            bk = 4 + tg % 4
            for c in range(8):
                S.op('pe', lambda e: e.matmul(B[bk][:, :], wsel(c), XF[:, c, tg * 512:(tg + 1) * 512], start=(c == 0), stop=(c == 7)),
                     reads=[key_w, 'XF'], writes=['B%d' % bk])
            S.op('act', lambda e: e.copy(out=dst[:, tg * 512:(tg + 1) * 512], in_=B[bk][:, :]), reads=['B%d' % bk], writes=[key_dst])

    for s_ in range(6):
        for c in range(8):
            S.dma('pool', Wq[:, c, :, :], bw_d[c * 128:(c + 1) * 128, 0:2304].rearrange("p (g s e) -> p g s e", g=3, s=6)[:, :, s_, :], writes=['b_Wq'])
            S.dma('pool', Wk[:, c, :], kv_d[c * 128:(c + 1) * 128, s_ * 128:(s_ + 1) * 128], writes=['b_Wk'])
            S.dma('pool', Wv[:, c, :], kv_d[c * 128:(c + 1) * 128, 768 + s_ * 128:768 + (s_ + 1) * 128], writes=['b_Wv'])
        proj_T(KT, 'b_KT', lambda c: Wk[:, c, :], 'b_Wk')
        for g in range(3):
            d = DIL[g]
            nb = 32 // d
            proj_T(QT, 'b_QT', lambda c: Wq[:, c, g, :], 'b_Wq')
            for b4 in range(8):
                bk = 4 + b4 % 4
                for u in range(4):
                    blk = b4 * 4 + u
                    r, n = blk // nb, blk % nb
                    t0 = d * 128 * n + r
                    for c in range(8):
                        S.op('pe', lambda e: e.matmul(B[bk][:, u * 128:(u + 1) * 128], XF[:, c, t0:t0 + 127 * d + 1:d], Wv[:, c, :], start=(c == 0), stop=(c == 7)),
                             reads=['XF', 'b_Wv'], writes=['B%d' % bk])
                S.op('act', lambda e: e.copy(out=V[:, b4 * 4:(b4 + 1) * 4, :].rearrange("p a b -> p (a b)"), in_=B[bk][:, :]), reads=['B%d' % bk], writes=['b_V'])
            for blk in range(32):
                r, n = blk // nb, blk % nb
                z = cnt[0] % 2
                cnt[0] += 1
                tq = d * 128 * n + r
                qs = slice(tq, tq + 127 * d + 1, d)
                kbs = [1] if n == 0 else [0, 1]
                bs, bo = z, 2 + z
                for kb in kbs:
                    tk = d * 128 * (n - 1 + kb) + r
                    S.op('pe', lambda e: e.matmul(B[bs][:, kb * 128:(kb + 1) * 128], KT[:, tk:tk + 127 * d + 1:d], QT[:, qs], start=True, stop=True),
                         reads=['b_KT', 'b_QT'], writes=['B%d' % bs])
                lo = kbs[0]
                S.op('act', lambda e: e.activation(out=PT[z][:, lo:2, :].rearrange("p a b -> p (a b)"), in_=B[bs][:, lo * 128:256], func=AF.Exp, scale=SC),
                     reads=['B%d' % bs], writes=['b_PT%d' % z])
                S.op('dve', lambda e: e.tensor_tensor(out=PT[z][:, lo:2, :], in0=PT[z][:, lo:2, :], in1=C.dmask[:, lo:2, :], op=ALU.mult),
                     reads=['b_PT%d' % z, 'dmask'], writes=['b_PT%d' % z])
                for which in range(2):
                    for kb in kbs:
                        lhs = V[:, blk - 1 + kb, :] if which == 0 else C.ones_b[:]
                        S.op('pe', lambda e: e.matmul(B[bo][:, which * 128:(which + 1) * 128], lhs, PT[z][:, kb, :], start=(kb == kbs[0]), stop=(kb == 1)),
                             reads=['b_V', 'ones_b', 'b_PT%d' % z], writes=['B%d' % bo])
                pv = B[bo][:, 0:256].rearrange("p (a b) -> p a b", a=2)
                if g == 0:
                    S.op('act', lambda e: e.copy(out=acc[:, :, qs], in_=pv), reads=['B%d' % bo], writes=['b_acc'])
                else:
                    S.op('dve', lambda e: e.tensor_tensor(out=acc[:, :, qs], in0=acc[:, :, qs], in1=pv, op=ALU.add), reads=['B%d' % bo, 'b_acc'], writes=['b_acc'])
        S.op('dve', lambda e: e.reciprocal(out=acc[:, 1, :], in_=acc[:, 1, :]), reads=['b_acc'], writes=['b_acc'])
        S.op('dve', lambda e: e.tensor_tensor(out=mixT[:], in0=acc[:, 0, :], in1=acc[:, 1, :], op=ALU.mult), reads=['b_acc'], writes=['b_mixT'])
        S.dma('sp', catT_d[s_ * 128:(s_ + 1) * 128, :], mixT[:], reads=['b_mixT'], writes=['catT1_d'])
    S.barrier()
    st1.close()
    sb = sb_outer
    Wm = sb("Wm", [128, 8, 256], BF16)
    for c in range(8):
        S.dma('pool', Wm[:, c, :], bw_d[c * 128:(c + 1) * 128, 2304:2560], writes=['b_Wm'])
    wout = load_w_bf(C, "wout1", dram('w_out1', [D, D]), D, 'wout1')
    lng = load_rep(C, "lng", dram('ln_mix_g1', [1, D]), D, 'lnp')
    lnb = load_rep(C, "lnb", dram('ln_mix_b1', [1, D]), D, 'lnp')
    kmT, vmx = mem_kv(C, dram('memT', [D, 256]) if not hasattr(C, 'memT_d') else C.memT_d, dram('w_mem_kv1', [D, 512]), '1')
    alloc_ln(C)
    C.ma_pT = sb("ma_pT", [128, 1024], BF16)
    C.ma_rd = sb("ma_rd", [128, 4])
    C.catT = sb("catT", [128, 8, 128], BF16)
    qmT = sb("qmT", [64, 4, 128], BF16)
    cat = sb("cat", [128, 1024], BF16)
    pre = sb("pre", [128, 1024])
    XR = [sb("XR%d" % j, [128, 1024]) for j in range(2)]
    for i in range(NT):
        j = i % 2
        S.dma('sp', XR[j][:], xf_d[i * 128:(i + 1) * 128, :], writes=['bXR%d' % j])
        S.dma('sp', C.catT[:, 0:6, :], catT_d[:, i * 128:(i + 1) * 128].rearrange("(c p) t -> p c t", p=128), reads=['catT1_d'], writes=['catT'])
        for h in range(4):
            for c in range(8):
                S.op('pe', lambda e: e.matmul(B[6][0:64, h * 128:(h + 1) * 128], Wm[:, c, h * 64:(h + 1) * 64], XF[:, c, i * 128:(i + 1) * 128], start=(c == 0), stop=(c == 7)),
                     reads=['b_Wm', 'XF'], writes=['B6'])
        S.op('act', lambda e: e.copy(out=qmT[:].rearrange("p h t -> p (h t)"), in_=B[6][0:64, :]), reads=['B6'], writes=['b_qmT'])
        mem_attn_tile(C, qmT, 'b_qmT', kmT, vmx, '1', cat, 'b_cat', (2, 3), 4)
        outproj_tile(C, cat, 'b_cat', wout, 'wout1', XR[j], 'bXR%d' % j, 5, (7, 0), pre, chunks=(6, 7))
        ln_epilogue(C, pre, 'pre', lng, lnb, xm1_d, xm1T_d, i, 'B', 5)
    C.sb = C.sb_save
    st.close()


_PROG_CACHE = {}


def _prep_inputs(inputs, b):
    g = lambda k: np.asarray(inputs[k], dtype=np.float32)
    m = {}
    m['x'] = np.ascontiguousarray(g('x')[b])
    m['xT'] = np.ascontiguousarray(g('x')[b].T)
    m['memT'] = np.ascontiguousarray(g('mem')[b].T)
    m['a_w_in'] = np.ascontiguousarray(g('a_w_in')[0])
    m['a_w_gate2'] = np.ascontiguousarray(g('a_w_gate2')[0])
    m['a_b_gate'] = np.ascontiguousarray(g('a_b_gate')[0][None, :])
    m['a_norm_g'] = np.ascontiguousarray(g('a_norm_g')[0][None, :])
    for l in range(2):
        m['w_mem_kv%d' % l] = np.ascontiguousarray(g('w_mem_kv')[l])
        m['w_out%d' % l] = np.ascontiguousarray(g('w_out')[l])
        for nm in ('ln_mix_g', 'ln_mix_b', 'ln_ffn_g', 'ln_ffn_b'):
            m['%s%d' % (nm, l)] = np.ascontiguousarray(g(nm)[l][None, :])
    m['b_w_in'] = np.ascontiguousarray(g('b_w_in')[0])
    m['shared_w_kv'] = np.ascontiguousarray(g('shared_w_kv'))
    for l in range(2):
        m['peer_w_q%d' % l] = np.ascontiguousarray(g('peer_w_q')[l])
        m['peer_keysT%d' % l] = np.ascontiguousarray(g('peer_sub_keys')[l].transpose(0, 2, 1))
        m['peer_uT%d' % l] = np.ascontiguousarray(g('peer_u')[l].T)
        m['peer_v%d' % l] = np.ascontiguousarray(g('peer_v')[l])
    m.update(_consts_host())
    return m


def kernel(**inputs):
    if 'nc' not in _PROG_CACHE:
        _PROG_CACHE['nc'] = build()
    nc = _PROG_CACHE['nc']
    in_maps = [_prep_inputs(inputs, b) for b in range(8)]
    res = run_bass_kernel_spmd(nc, in_maps, core_ids=list(range(8)))
    out = np.stack([np.asarray(r['out'], dtype=np.float32) for r in res.results], axis=0)
    return out
```

```python
from contextlib import ExitStack
import numpy as np
import concourse.bass as bass
import concourse.mybir as mybir
from concourse.bass_utils import run_bass_kernel_spmd

F32 = mybir.dt.float32
BF16 = mybir.dt.bfloat16
ALU = mybir.AluOpType
AF = mybir.ActivationFunctionType
AX = mybir.AxisListType

D = 1024
SEQ = 4096
NT = SEQ // 128
DN_ALPHA = 4.0 ** 0.25
LN_EPS = 1e-5
HN_EPS = 1e-6
A_W = 2576
B_W = 2560


class Sched:
    def __init__(self, nc, stack):
        self.nc = nc
        self.E = {'pe': nc.tensor, 'dve': nc.vector, 'act': nc.scalar, 'pool': nc.gpsimd, 'sp': nc.sync}
        self.sem = {e: stack.enter_context(nc.semaphore('s_' + e)) for e in self.E}
        self.cnt = {e: 0 for e in self.E}
        self.seen = {e: {} for e in self.E}
        self.NDS = 32
        self.dsem = [stack.enter_context(nc.semaphore('d%d' % i)) for i in range(self.NDS)]
        self.dcnt = [0] * self.NDS
        self.dnext = 0
        self.tiles = {}
        self.nins = 0

    def _st(self, key):
        if key not in self.tiles:
            self.tiles[key] = {'w': None, 'r': {}}
        return self.tiles[key]

    def _semobj(self, sk):
        return self.sem[sk] if isinstance(sk, str) else self.dsem[sk]

    def _wait(self, eng, sk, val):
        if self.seen[eng].get(sk, 0) >= val:
            return
        self.E[eng].wait_ge(self._semobj(sk), val)
        self.seen[eng][sk] = val
        self.nins += 1

    def _deps(self, eng, reads, writes):
        for k in reads:
            st = self._st(k)
            if st['w'] is not None:
                self._wait(eng, *st['w'])
        for k in writes:
            st = self._st(k)
            if st['w'] is not None:
                self._wait(eng, *st['w'])
            for sk, v in st['r'].items():
                self._wait(eng, sk, v)

    def _mark(self, sk, val, reads, writes):
        for k in reads:
            st = self._st(k)
            st['r'][sk] = max(st['r'].get(sk, 0), val)
        for k in writes:
            st = self._st(k)
            st['w'] = (sk, val)
            st['r'] = {}

    def op(self, eng, fn, reads=(), writes=()):
        self._deps(eng, reads, writes)
        ins = fn(self.E[eng])
        self.cnt[eng] += 1
        ins.then_inc(self.sem[eng], 1)
        self._mark(eng, self.cnt[eng], reads, writes)
        self.nins += 1
        return ins

    def dma(self, eng, out, in_, reads=(), writes=(), **kw):
        s = self.dnext
        self.dnext = (self.dnext + 1) % self.NDS
        if self.dcnt[s] > 0:
            self._wait(eng, s, self.dcnt[s])
        self._deps(eng, reads, writes)
        ins = self.E[eng].dma_start(out=out, in_=in_, **kw)
        self.dcnt[s] += 16
        ins.then_inc(self.dsem[s], 16)
        self._mark(s, self.dcnt[s], reads, writes)
        self.nins += 1
        return ins

    def barrier(self):
        for e in self.E:
            for s in range(self.NDS):
                if self.dcnt[s] > 0:
                    self._wait(e, s, self.dcnt[s])
            for o in self.E:
                if o != e and self.cnt[o] > 0:
                    self._wait(e, o, self.cnt[o])

    def finish(self, eng='sp'):
        for s in range(self.NDS):
            if self.dcnt[s] > 0:
                self._wait(eng, s, self.dcnt[s])
        for e in self.E:
            if e != eng and self.cnt[e] > 0:
                self._wait(eng, e, self.cnt[e])


class Ctx:
    pass


def _consts_host():
    idx = np.arange(128)
    same = (idx[:, None] // 64) == (idx[None, :] // 64)
    M2 = (same & (idx[:, None] <= idx[None, :])).astype(np.float32)
    U2 = (same & (idx[:, None] > idx[None, :])).astype(np.float32)
    mp = (idx[:, None] >= idx[None, :]).astype(np.float32)
    mc = (idx[:, None] <= idx[None, :]).astype(np.float32)
    return {
        'c_ident': np.eye(128, dtype=np.float32),
        'c_m2': M2, 'c_u2': U2,
        'c_dmask': np.ascontiguousarray(np.stack([mp, mc], axis=1)),
        'c_ones': np.ones((128, 128), dtype=np.float32),
    }


def build(phases=('A', 'P0', 'B', 'P1'), dbg=False, n_super=SEQ // 512):
    nc = bass.Bass("TRN2", target_bir_lowering=False)
    C = Ctx()
    C.nc = nc
    st = ExitStack()
    C.st = st
    S = Sched(nc, st)
    C.S = S

    def dram(name, shape, dt=F32, kind="ExternalInput"):
        return nc.dram_tensor(name, list(shape), dt, kind=kind).ap()
    C.dram = dram

    def sb(name, shape, dt=F32):
        return st.enter_context(nc.sbuf_tensor(name, list(shape), dt))
    C.sb = sb

    C.B = [st.enter_context(nc.psum_tensor("bank%d" % i, [128, 512], F32)) for i in range(8)]

    C.ident = sb("ident", [128, 128], BF16)
    C.m2 = sb("m2", [128, 128], F32)
    C.u2 = sb("u2", [128, 128], F32)
    C.ones_f = sb("ones_f", [128, 128], F32)
    C.ones_b = sb("ones_b", [128, 128], BF16)
    C.dmask = sb("dmask", [128, 2, 128], BF16)
    S.dma('pool', C.ident[:], dram('c_ident', [128, 128]), writes=['ident'])
    S.dma('sp', C.m2[:], dram('c_m2', [128, 128]), writes=['m2'])
    S.dma('sp', C.u2[:], dram('c_u2', [128, 128]), writes=['u2'])
    c_ones = dram('c_ones', [128, 128])
    S.dma('sp', C.ones_f[:], c_ones, writes=['ones_f'])
    S.dma('pool', C.ones_b[:], c_ones, writes=['ones_b'])
    S.dma('pool', C.dmask[:], dram('c_dmask', [128, 2, 128]), writes=['dmask'])

    ext_in = "ExternalInput"
    inter = "ExternalOutput" if dbg else "Internal"
    C.out = None
    names = {'A': 'xm0', 'P0': 'xf0', 'B': 'xm1'}
    prev = {'P0': 'xm0', 'B': 'xf0', 'P1': 'xm1'}
    for l in (0, 1):
        if ('P%d' % l) in phases:
            peer_convert(C, l)
    for ph in ('A', 'P0', 'B', 'P1'):
        if ph not in phases:
            continue
        if ph in prev and not hasattr(C, prev[ph]):
            setattr(C, prev[ph], dram(prev[ph], [SEQ, D], F32, ext_in))
            setattr(C, prev[ph] + 'T', dram(prev[ph] + 'T', [D, SEQ], BF16, ext_in))
        if ph in names:
            setattr(C, names[ph], dram(names[ph], [SEQ, D], F32, inter))
            setattr(C, names[ph] + 'T', dram(names[ph] + 'T', [D, SEQ], BF16, inter))
        if ph == 'A':
            phase_A(C)
        elif ph == 'P0':
            phase_P(C, 0, C.xm0, C.xm0T, C.xf0, C.xf0T, n_super)
        elif ph == 'B':
            phase_B(C, C.xf0, C.xf0T, C.xm1, C.xm1T)
        else:
            C.out = dram('out', [SEQ, D], F32, "ExternalOutput")
            phase_P(C, 1, C.xm1, C.xm1T, C.out, None, n_super)
        S.barrier()
    S.finish('sp')
    st.close()
    return nc


def ln_epilogue(C, pre, key_pre, g_rep, b_rep, out_dram, outT_dram, i, tag, bank):
    S = C.S
    stt, mv, rs, xn, xnb, xnT = C.ln_st, C.ln_mv, C.ln_rs, C.ln_xn, C.ln_xnb, C.ln_xnT
    for hlf in range(2):
        S.op('dve', lambda e: e.bn_stats(out=stt[:, hlf, :], in_=pre[:, hlf * 512:(hlf + 1) * 512]), reads=[key_pre], writes=['ln_st'])
    S.op('dve', lambda e: e.bn_aggr(out=mv[:], in_=stt[:].rearrange("p a b -> p (a b)")), reads=['ln_st'], writes=['ln_mv'])
    S.op('act', lambda e: e.activation(out=rs[:], in_=mv[:, 1:2], func=AF.Sqrt, bias=LN_EPS, scale=1.0), reads=['ln_mv'], writes=['ln_rs'])
    S.op('dve', lambda e: e.reciprocal(out=rs[:], in_=rs[:]), reads=['ln_rs'], writes=['ln_rs'])
    S.op('dve', lambda e: e.tensor_scalar(out=xn[:], in0=pre[:], scalar1=mv[:, 0:1], scalar2=rs[:, 0:1], op0=ALU.subtract, op1=ALU.mult),
         reads=[key_pre, 'ln_mv', 'ln_rs'], writes=['ln_xn'])
    S.op('pool', lambda e: e.tensor_tensor(out=xn[:], in0=xn[:], in1=g_rep[:], op=ALU.mult), reads=['ln_xn', 'lnp'], writes=['ln_xn'])
    S.op('pool', lambda e: e.tensor_tensor(out=xn[:], in0=xn[:], in1=b_rep[:], op=ALU.add), reads=['ln_xn', 'lnp'], writes=['ln_xn'])
    S.dma('sp', out_dram[i * 128:(i + 1) * 128, :], xn[:], reads=['ln_xn'])
    if outT_dram is not None:
        S.op('act', lambda e: e.copy(out=xnb[:], in_=xn[:]), reads=['ln_xn'], writes=['ln_xnb'])
        pb = C.B[bank][:].bitcast(BF16)
        for c in range(8):
            S.op('pe', lambda e: e.transpose(pb[:, c * 128:(c + 1) * 128], xnb[:, c * 128:(c + 1) * 128], C.ident[:]),
                 reads=['ln_xnb', 'ident'], writes=['B%d' % bank])
        S.op('act', lambda e: e.copy(out=xnT[:].rearrange("p c t -> p (c t)"), in_=pb[:, :]), reads=['B%d' % bank], writes=['ln_xnT'])
        S.dma('sp', outT_dram[:, i * 128:(i + 1) * 128].rearrange("(c p) t -> p c t", p=128), xnT[:], reads=['ln_xnT'])


def alloc_ln(C):
    sb = C.sb
    C.ln_st = sb("ln_st", [128, 2, 6])
    C.ln_mv = sb("ln_mv", [128, 2])
    C.ln_rs = sb("ln_rs", [128, 1])
    C.ln_xn = sb("ln_xn", [128, 1024])
    C.ln_xnb = sb("ln_xnb", [128, 1024], BF16)
    C.ln_xnT = sb("ln_xnT", [128, 8, 128], BF16)


def load_w_bf(C, name, dram_ap, ncols, key):
    t = C.sb(name, [128, 8, ncols], BF16)
    for c in range(8):
        C.S.dma('pool', t[:, c, :], dram_ap[c * 128:(c + 1) * 128, :], writes=[key])
    return t


def load_rep(C, name, dram_ap, n, key):
    t = C.sb(name, [128, n], F32)
    C.S.dma('sp', t[:], dram_ap.partition_broadcast(128), writes=[key])
    return t


def mem_kv(C, memT_d, wmkv_d, tag):
    S, sb, B = C.S, C.sb, C.B
    memT = load_w_bf(C, "memT" + tag, memT_d, 256, 'memT' + tag)
    wm = load_w_bf(C, "wmkv" + tag, wmkv_d, 512, 'wmkv' + tag)
    kmT = sb("kmT" + tag, [64, 4, 256], BF16)
    vmx = sb("vmx" + tag, [128, 2, 4, 65], BF16)
    S.op('pool', lambda e: e.memset(vmx[:].rearrange("p a b c -> p (a b c)"), 1.0), writes=['vmx' + tag])
    for h in range(4):
        for c in range(8):
            S.op('pe', lambda e: e.matmul(B[h % 2][0:64, (h // 2) * 256:(h // 2) * 256 + 256],
                                          wm[:, c, h * 64:(h + 1) * 64], memT[:, c, :], start=(c == 0), stop=(c == 7)),
                 reads=['memT' + tag, 'wmkv' + tag], writes=['B%d' % (h % 2)])
        S.op('act', lambda e: e.copy(out=kmT[:, h, :], in_=B[h % 2][0:64, (h // 2) * 256:(h // 2) * 256 + 256]), reads=['B%d' % (h % 2)], writes=['kmT' + tag])
    for j in range(2):
        for c in range(8):
            S.op('pe', lambda e: e.matmul(B[2 + j][:, 0:256], memT[:, c, j * 128:(j + 1) * 128], wm[:, c, 256:512], start=(c == 0), stop=(c == 7)),
                 reads=['memT' + tag, 'wmkv' + tag], writes=['B%d' % (2 + j)])
        S.op('act', lambda e: e.copy(out=vmx[:, j, :, 0:64], in_=B[2 + j][:, 0:256].rearrange("p (h e) -> p h e", h=4)),
             reads=['B%d' % (2 + j)], writes=['vmx' + tag])
    return kmT, vmx


def mem_attn_tile(C, qmT, key_qmT, kmT, vmx, tag, cat, key_cat, bs, bm):
    S, B = C.S, C.B
    pT = C.ma_pT
    for h in range(4):
        bk = bs[h // 2]
        for j in range(2):
            col = ((h % 2) * 2 + j) * 128
            S.op('pe', lambda e: e.matmul(B[bk][:, col:col + 128], kmT[:, h, j * 128:(j + 1) * 128], qmT[:, h, :], start=True, stop=True),
                 reads=['kmT' + tag, key_qmT], writes=['B%d' % bk])
    for hh in range(2):
        S.op('act', lambda e: e.activation(out=pT[:, hh * 512:(hh + 1) * 512], in_=B[bs[hh]][:, :], func=AF.Exp, scale=0.125),
             reads=['B%d' % bs[hh]], writes=['ma_pT'])
    for h in range(4):
        for j in range(2):
            col = (h * 2 + j) * 128
            S.op('pe', lambda e: e.matmul(B[bm][:, h * 65:h * 65 + 65], pT[:, col:col + 128], vmx[:, j, h, :], start=(j == 0), stop=(j == 1)),
                 reads=['ma_pT', 'vmx' + tag], writes=['B%d' % bm])
    mo = B[bm][:, 0:260].rearrange("p (h e) -> p h e", h=4)
    S.op('dve', lambda e: e.reciprocal(out=C.ma_rd[:], in_=mo[:, :, 64]), reads=['B%d' % bm], writes=['ma_rd'])
    S.op('dve', lambda e: e.tensor_tensor(out=cat[:, 768:1024].rearrange("p (h e) -> p h e", h=4), in0=mo[:, :, 0:64],
                                          in1=C.ma_rd[:].unsqueeze(2).to_broadcast([128, 4, 64]), op=ALU.mult),
         reads=['B%d' % bm, 'ma_rd'], writes=[key_cat])


def outproj_tile(C, cat, key_cat, wout, key_wout, XR, key_XR, bt, by, pre, chunks=range(8)):
    S, B = C.S, C.B
    pb = B[bt][:].bitcast(BF16)
    chunks = list(chunks)
    for c in chunks:
        S.op('pe', lambda e: e.transpose(pb[:, c * 128:(c + 1) * 128], cat[:, c * 128:(c + 1) * 128], C.ident[:]),
             reads=[key_cat, 'ident'], writes=['B%d' % bt])
    c0, c1 = chunks[0], chunks[-1] + 1
    S.op('act', lambda e: e.copy(out=C.catT[:, c0:c1, :].rearrange("p c t -> p (c t)"), in_=pb[:, c0 * 128:c1 * 128]), reads=['B%d' % bt], writes=['catT'])
    for hlf in range(2):
        for c in range(8):
            S.op('pe', lambda e: e.matmul(B[by[hlf]][:, :], C.catT[:, c, :], wout[:, c, hlf * 512:(hlf + 1) * 512], start=(c == 0), stop=(c == 7)),
                 reads=['catT', key_wout], writes=['B%d' % by[hlf]])
        S.op('dve', lambda e: e.scalar_tensor_tensor(out=pre[:, hlf * 512:(hlf + 1) * 512], in0=XR[:, hlf * 512:(hlf + 1) * 512], scalar=DN_ALPHA,
                                                     in1=B[by[hlf]][:, :], op0=ALU.mult, op1=ALU.add),
             reads=[key_XR, 'B%d' % by[hlf]], writes=['pre'])


def phase_A(C):
    S, B, dram, nc = C.S, C.B, C.dram, C.nc
    st = ExitStack()
    sb = lambda name, shape, dt=F32: st.enter_context(nc.sbuf_tensor("a_" + name, list(shape), dt))
    C.sb_save = C.sb
    C.sb = sb
    x_d = dram('x', [SEQ, D]); xT_d = dram('xT', [D, SEQ])
    C.memT_d = dram('memT', [D, 256])
    W = load_w_bf(C, "awin", dram('a_w_in', [D, A_W]), A_W, 'awin')
    wout = load_w_bf(C, "wout0", dram('w_out0', [D, D]), D, 'wout0')
    wg2 = sb("wg2", [16, 384]); S.dma('sp', wg2[:], dram('a_w_gate2', [16, 384]), writes=['wg2'])
    bg = sb("bg", [1, 384]); S.dma('sp', bg[:], dram('a_b_gate', [1, 384]), writes=['bg'])
    ng = load_rep(C, "ng", dram('a_norm_g', [1, 768]), 768, 'ng')
    lng = load_rep(C, "lng", dram('ln_mix_g0', [1, D]), D, 'lnp')
    lnb = load_rep(C, "lnb", dram('ln_mix_b0', [1, D]), D, 'lnp')
    kmT, vmx = mem_kv(C, C.memT_d, dram('w_mem_kv0', [D, 512]), '0')
    alloc_ln(C)
    C.ma_pT = sb("ma_pT", [128, 1024], BF16)
    C.ma_rd = sb("ma_rd", [128, 4])
    C.catT = sb("catT", [128, 8, 128], BF16)
    XT = [sb("XT%d" % j, [128, 8, 128], BF16) for j in range(2)]
    XR = [sb("XR%d" % j, [128, 1024]) for j in range(2)]
    hgT = sb("hgT", [16, 128])
    qmT = sb("qmT", [64, 4, 128], BF16)
    t1 = sb("a_t1", [128, 384]); la = sb("a_la", [128, 384])
    expb = sb("expb", [96, 4, 128]); expnb = sb("expnb", [96, 4, 128]); expE = sb("expE", [128, 384])
    qt = sb("qt", [96, 4, 128], BF16); kt = sb("kt", [96, 4, 128], BF16)
    kend = sb("kend", [128, 384], BF16); vb = sb("vb", [128, 768], BF16)
    gate = sb("gate", [128, 768])
    attnTb = sb("attnTb", [128, 4, 128], BF16)
    St = sb("St", [96, 4, 192]); SbA = [sb("SbA%d" % q, [96, 4, 192], BF16) for q in range(2)]; SbB = sb("SbB", [96, 4, 192], BF16)
    sq = sb("sq", [128, 768]); ssq = sb("ssq", [128, 4]); to = sb("to", [128, 768])
    cat = sb("cat", [128, 1024], BF16)
    pre = sb("pre", [128, 1024])
    S.op('pool', lambda e: e.memset(St[:].rearrange("p a b -> p (a b)"), 0.0), writes=['St'])
    S.op('pool', lambda e: e.memset(SbA[0][:].rearrange("p a b -> p (a b)"), 0.0), writes=['SbA0'])
    QS = 96 ** -0.5

    def load(i):
        j = i % 2
        S.dma('pool', XT[j][:], xT_d[:, i * 128:(i + 1) * 128].rearrange("(c p) t -> p c t", p=128), writes=['XT%d' % j])
        S.dma('sp', XR[j][:], x_d[i * 128:(i + 1) * 128, :], writes=['XR%d' % j])

    load(0)
    for i in range(NT):
        j = i % 2
        if i + 1 < NT:
            load(i + 1)
        xt = XT[j]; kx = 'XT%d' % j
        for h in range(4):
            for c in range(8):
                S.op('pe', lambda e: e.matmul(B[0][0:96, h * 128:(h + 1) * 128], W[:, c, h * 96:(h + 1) * 96], xt[:, c, :], start=(c == 0), stop=(c == 7)),
                     reads=['awin', kx], writes=['B0'])
        for h in range(4):
            for c in range(8):
                S.op('pe', lambda e: e.matmul(B[1][0:96, h * 128:(h + 1) * 128], W[:, c, 384 + h * 96:384 + (h + 1) * 96], xt[:, c, :], start=(c == 0), stop=(c == 7)),
                     reads=['awin', kx], writes=['B1'])
        for (bk, c0, n) in ((2, 384, 384), (3, 768, 512), (4, 1280, 512), (5, 1792, 512)):
            for c in range(8):
                S.op('pe', lambda e: e.matmul(B[bk][:, 0:n], xt[:, c, :], W[:, c, c0:c0 + n], start=(c == 0), stop=(c == 7)),
                     reads=['awin', kx], writes=['B%d' % bk])
        for h in range(4):
            for c in range(8):
                S.op('pe', lambda e: e.matmul(B[6][0:64, h * 128:(h + 1) * 128], W[:, c, 2320 + h * 64:2320 + (h + 1) * 64], xt[:, c, :], start=(c == 0), stop=(c == 7)),
                     reads=['awin', kx], writes=['B6'])
        for c in range(8):
            S.op('pe', lambda e: e.matmul(B[7][0:16, 0:128], W[:, c, 2304:2320], xt[:, c, :], start=(c == 0), stop=(c == 7)),
                 reads=['awin', kx], writes=['B7'])
        S.op('act', lambda e: e.copy(out=qmT[:].rearrange("p h t -> p (h t)"), in_=B[6][0:64, :]), reads=['B6'], writes=['qmT'])
        S.op('act', lambda e: e.copy(out=hgT[:], in_=B[7][0:16, 0:128]), reads=['B7'], writes=['hgT'])
        S.op('pe', lambda e: e.matmul(B[7][:, 0:384], hgT[:], wg2[:], start=True, stop=False), reads=['hgT', 'wg2'], writes=['B7'])
        S.op('pe', lambda e: e.matmul(B[7][:, 0:384], C.ones_f[0:1, :], bg[:], start=False, stop=True), reads=['ones_f', 'bg'], writes=['B7'])
        S.op('act', lambda e: e.activation(out=t1[:], in_=B[7][:, 0:384], func=AF.Exp, scale=-1.0), reads=['B7'], writes=['a_t1'])
        S.op('act', lambda e: e.activation(out=t1[:], in_=t1[:], func=AF.Ln, bias=1.0, scale=1.0), reads=['a_t1'], writes=['a_t1'])
        S.op('act', lambda e: e.mul(out=la[:], in_=t1[:], mul=-1.0 / 16.0), reads=['a_t1'], writes=['a_la'])
        for h in range(4):
            S.op('pe', lambda e: e.matmul(B[6][0:96, h * 128:(h + 1) * 128], la[:, h * 96:(h + 1) * 96], C.m2[:], start=True, stop=True),
                 reads=['a_la', 'm2'], writes=['B6'])
        S.op('pe', lambda e: e.matmul(B[7][:, 0:384], C.u2[:], la[:], start=True, stop=True), reads=['a_la', 'u2'], writes=['B7'])
        S.op('act', lambda e: e.activation(out=expb[:].rearrange("p h t -> p (h t)"), in_=B[6][0:96, :], func=AF.Exp), reads=['B6'], writes=['expb'])
        S.op('act', lambda e: e.activation(out=expnb[:].rearrange("p h t -> p (h t)"), in_=B[6][0:96, :], func=AF.Exp, scale=-1.0), reads=['B6'], writes=['expnb'])
        S.op('act', lambda e: e.activation(out=expE[:], in_=B[7][:, 0:384], func=AF.Exp), reads=['B7'], writes=['expE'])
        S.op('dve', lambda e: e.scalar_tensor_tensor(out=qt[:].rearrange("p h t -> p (h t)"), in0=B[0][0:96, :], scalar=QS, in1=expb[:].rearrange("p h t -> p (h t)"),
                                                     op0=ALU.mult, op1=ALU.mult), reads=['B0', 'expb'], writes=['qt'])
        S.op('dve', lambda e: e.tensor_tensor(out=kt[:].rearrange("p h t -> p (h t)"), in0=B[1][0:96, :], in1=expnb[:].rearrange("p h t -> p (h t)"), op=ALU.mult),
             reads=['B1', 'expnb'], writes=['kt'])
        S.op('dve', lambda e: e.tensor_tensor(out=kend[:], in0=B[2][:, 0:384], in1=expE[:], op=ALU.mult), reads=['B2', 'expE'], writes=['kend'])
        S.op('act', lambda e: e.copy(out=vb[:, 0:512], in_=B[3][:, :]), reads=['B3'], writes=['vb'])
        S.op('act', lambda e: e.copy(out=vb[:, 512:768], in_=B[4][:, 0:256]), reads=['B4'], writes=['vb'])
        S.op('act', lambda e: e.activation(out=gate[:, 0:256], in_=B[4][:, 256:512], func=AF.Silu), reads=['B4'], writes=['gate'])
        S.op('act', lambda e: e.activation(out=gate[:, 256:768], in_=B[5][:, :], func=AF.Silu), reads=['B5'], writes=['gate'])
        S.op('pool', lambda e: e.tensor_tensor(out=gate[:], in0=gate[:], in1=ng[:], op=ALU.mult), reads=['gate', 'ng'], writes=['gate'])
        for h in range(4):
            S.op('pe', lambda e: e.matmul(B[0][:, h * 128:(h + 1) * 128], kt[:, h, :], qt[:, h, :], start=True, stop=True), reads=['kt', 'qt'], writes=['B0'])
        S.op('dve', lambda e: e.tensor_tensor(out=attnTb[:], in0=B[0][:, :].rearrange("p (h t) -> p h t", h=4),
                                              in1=C.m2[:].unsqueeze(1).to_broadcast([128, 4, 128]), op=ALU.mult), reads=['B0', 'm2'], writes=['attnTb'])
        for ch in range(2):
            for h in range(4):
                bk = 2 + ch * 2 + h // 2
                col = (h % 2) * 192
                S.op('pe', lambda e: e.matmul(B[bk][0:96, col:col + 192], kend[ch * 64:(ch + 1) * 64, h * 96:(h + 1) * 96], vb[ch * 64:(ch + 1) * 64, h * 192:(h + 1) * 192],
                                              start=True, stop=True), reads=['kend', 'vb'], writes=['B%d' % bk])
        def o_ap(h, lo, hi):
            bk = 1 if h < 2 else 6
            col = (h % 2) * 192
            return B[bk][lo:hi, col:col + 192], 'B%d' % bk
        SbS = SbA[i % 2]; kS = 'SbA%d' % (i % 2)
        SbN = SbA[(i + 1) % 2]; kN = 'SbA%d' % ((i + 1) % 2)
        for ch in range(2):
            for h in range(4):
                bk = 2 + ch * 2 + h // 2
                col = (h % 2) * 192
                S.op('dve', lambda e: e.scalar_tensor_tensor(out=St[:, h, :], in0=St[:, h, :], scalar=expb[:, h, ch * 64 + 63:ch * 64 + 64], in1=B[bk][0:96, col:col + 192],
                                                             op0=ALU.mult, op1=ALU.add), reads=['St', 'expb', 'B%d' % bk], writes=['St'])
            Sb_dst, kd = (SbB, 'SbB') if ch == 0 else (SbN, kN)
            S.op('act', lambda e: e.copy(out=Sb_dst[:].rearrange("p a b -> p (a b)"), in_=St[:].rearrange("p a b -> p (a b)")),
                 reads=['St'], writes=[kd])
        for h in range(4):
            oap, ok = o_ap(h, 0, 128)
            S.op('pe', lambda e: e.matmul(oap, attnTb[:, h, :], vb[:, h * 192:(h + 1) * 192], start=True, stop=False), reads=['attnTb', 'vb'], writes=[ok])
            oap0, _ = o_ap(h, 0, 64)
            S.op('pe', lambda e: e.matmul(oap0, qt[:, h, 0:64], SbS[:, h, :], start=False, stop=False), reads=['qt', kS], writes=[ok])
            oap1, _ = o_ap(h, 64, 128)
            S.op('pe', lambda e: e.matmul(oap1, qt[:, h, 64:128], SbB[:, h, :], start=False, stop=True), reads=['qt', 'SbB'], writes=[ok])
        for hh in range(2):
            bk = 1 if hh == 0 else 6
            S.op('act', lambda e: e.activation(out=sq[:, hh * 384:(hh + 1) * 384], in_=B[bk][:, 0:384], func=AF.Square), reads=['B%d' % bk], writes=['sq'])
        S.op('dve', lambda e: e.tensor_reduce(out=ssq[:], in_=sq[:].rearrange("p (h v) -> p h v", h=4), axis=AX.X, op=ALU.add), reads=['sq'], writes=['ssq'])
        S.op('act', lambda e: e.activation(out=ssq[:], in_=ssq[:], func=AF.Sqrt, bias=HN_EPS, scale=1.0 / 192.0), reads=['ssq'], writes=['ssq'])
        S.op('dve', lambda e: e.reciprocal(out=ssq[:], in_=ssq[:]), reads=['ssq'], writes=['ssq'])
        for hh in range(2):
            bk = 1 if hh == 0 else 6
            S.op('dve', lambda e: e.tensor_tensor(out=to[:, hh * 384:(hh + 1) * 384].rearrange("p (h v) -> p h v", h=2),
                                                  in0=B[bk][:, 0:384].rearrange("p (h v) -> p h v", h=2),
                                                  in1=ssq[:, hh * 2:hh * 2 + 2].unsqueeze(2).to_broadcast([128, 2, 192]), op=ALU.mult),
                 reads=['B%d' % bk, 'ssq'], writes=['to'])
        S.op('dve', lambda e: e.tensor_tensor(out=cat[:, 0:768], in0=to[:], in1=gate[:], op=ALU.mult), reads=['to', 'gate'], writes=['cat'])
        mem_attn_tile(C, qmT, 'qmT', kmT, vmx, '0', cat, 'cat', (2, 3), 4)
        outproj_tile(C, cat, 'cat', wout, 'wout0', XR[j], 'XR%d' % j, 5, (7, 0), pre)
        ln_epilogue(C, pre, 'pre', lng, lnb, C.xm0, C.xm0T, i, 'A', 5)
    C.sb = C.sb_save
    st.close()


def peer_convert(C, l):
    S, dram = C.S, C.dram
    uT_d = dram('peer_uT%d' % l, [D, 16384])
    v_d = dram('peer_v%d' % l, [16384, D])
    uTb = dram('peer_uTb%d' % l, [D, 16384], BF16, "Internal")
    vb = dram('peer_vb%d' % l, [16384, D], BF16, "Internal")
    for c in range(8):
        for q in range(4):
            S.dma('pool', uTb[c * 128:(c + 1) * 128, q * 4096:(q + 1) * 4096], uT_d[c * 128:(c + 1) * 128, q * 4096:(q + 1) * 4096], writes=['uTb%d' % l])
    for c in range(32):
        S.dma('pool', vb[c * 512:(c + 1) * 512, :].rearrange("(p a) d -> p a d", p=128), v_d[c * 512:(c + 1) * 512, :].rearrange("(p a) d -> p a d", p=128), writes=['vb%d' % l])
    C.peer_w = getattr(C, 'peer_w', {})
    C.peer_w[l] = (uTb, vb)


def phase_P(C, l, xm_d, xmT_d, xf_d, xfT_d, n_super=SEQ // 512):
    S, B, dram, nc = C.S, C.B, C.dram, C.nc
    st = ExitStack()
    sb = lambda name, shape, dt=F32: st.enter_context(nc.sbuf_tensor("p%d_%s" % (l, name), list(shape), dt))
    P = 'p%d_' % l
    wq = sb("wq", [128, 8, 2048], BF16)
    wq_d = dram('peer_w_q%d' % l, [D, 2048])
    for c in range(8):
        S.dma('pool', wq[:, c, :], wq_d[c * 128:(c + 1) * 128, :], writes=[P + 'wq'])
    keysT = sb("keysT", [128, 2, 128])
    kd = dram('peer_keysT%d' % l, [2, 128, 128])
    for p in range(2):
        S.dma('sp', keysT[:, p, :], kd[p], writes=[P + 'keysT'])
    if l not in getattr(C, 'peer_w', {}):
        peer_convert(C, l)
    uT_d, v_d = C.peer_w[l]
    lng = sb("lng", [128, D]); lnb = sb("lnb", [128, D])
    S.dma('sp', lng[:], dram('ln_ffn_g%d' % l, [1, D]).partition_broadcast(128), writes=['lnp'])
    S.dma('sp', lnb[:], dram('ln_ffn_b%d' % l, [1, D]).partition_broadcast(128), writes=['lnp'])
    xmT = sb("xmT", [128, 8, 512], BF16)
    xr = sb("xr", [128, 1024])
    acc = sb("acc", [128, 4, 1024])
    top = sb("top", [128, 8, 2, 16])
    c16 = sb("c16", [128, 8, 16])
    dd = sb("dd", [128, 8, 16])
    zz = sb("zz", [128, 8])
    cs = sb("cs", [128, 8, 2])
    E = sb("E", [128, 4, 16, 128])
    gsc = sb("gsc", [128, 4, 8])
    uT = [sb("uT%d" % q, [128, 8, 512], BF16) for q in range(2)]
    vv = [sb("vv%d" % q, [128, 4, 1024], BF16) for q in range(2)]
    gel = [sb("gel%d" % q, [128, 512]) for q in range(2)]
    W8 = [sb("W8_%d" % q, [128, 8, 4, 128]) for q in range(2)]
    w0 = W8[0][:].rearrange("p h a b -> p (h a b)")
    w1 = W8[1][:].rearrange("p h a b -> p (h a b)")
    qT = w0[:, 0:2048].rearrange("p (a b) -> p a b", a=16)
    s_sb = w0[:, 2048:4096].rearrange("p (a b) -> p a b", a=16)
    cand = w1[:, 0:2048].rearrange("p (h a b) -> p h a b", h=8, a=16)
    tmp = w1[:, 2048:3072].rearrange("p (a b) -> p a b", a=4)
    Ms = [sb("Ms%d" % q, [128, 8, 512], BF16) for q in range(1)]
    tmpo = [sb("tmpo%d" % q, [128, 1024]) for q in range(2)]
    G = sb("G", [128, 512], BF16)
    A = [sb("A%d" % q, [128, 512], BF16) for q in range(2)]
    AT = [sb("AT%d" % q, [128, 4, 128], BF16) for q in range(2)]
    pre = sb("pre", [128, 1024])
    C.ln_st = sb("ln_st", [128, 2, 6]); C.ln_mv = sb("ln_mv", [128, 2]); C.ln_rs = sb("ln_rs", [128, 1])
    C.ln_xn = sb("ln_xn", [128, 1024]); C.ln_xnb = sb("ln_xnb", [128, 1024], BF16); C.ln_xnT = sb("ln_xnT", [128, 8, 128], BF16)
    DELTA = 2e-4
    NEG = -1e30
    cnt = [0]

    def load_chunk(k):
        q = k % 2
        S.dma('sp', uT[q][:], uT_d[:, k * 512:(k + 1) * 512].rearrange("(c p) e -> p c e", p=128), reads=['uTb%d' % l], writes=[P + 'uT%d' % q])
        S.dma('sp', vv[q][:], v_d[k * 512:(k + 1) * 512, :].rearrange("(a b) d -> b a d", b=128), reads=['vb%d' % l], writes=[P + 'vv%d' % q])

    for sti in range(n_super):
        t0 = sti * 512
        S.dma('sp', xmT[:], xmT_d[:, t0:t0 + 512].rearrange("(c p) t -> p c t", p=128), writes=[P + 'xmT'])
        load_chunk(0)
        load_chunk(1)
        for tt in range(4):
            for hp in range(16):
                bk = 4 + (hp % 4)
                for c in range(8):
                    S.op('pe', lambda e: e.matmul(B[bk][:, 0:128], wq[:, c, hp * 128:(hp + 1) * 128], xmT[:, c, tt * 128:(tt + 1) * 128], start=(c == 0), stop=(c == 7)),
                         reads=[P + 'wq', P + 'xmT'], writes=['B%d' % bk])
                S.op('act', lambda e: e.copy(out=qT[:, hp, :], in_=B[bk][:, 0:128]), reads=['B%d' % bk], writes=[P + 'qT'])
            for hp in range(16):
                bk = hp // 4
                col = (hp % 4) * 128
                S.op('pe', lambda e: e.matmul(B[bk][:, col:col + 128], qT[:, hp, :], keysT[:, hp % 2, :], start=True, stop=True),
                     reads=[P + 'qT', P + 'keysT'], writes=['B%d' % bk])
            for bk in range(4):
                S.op('act', lambda e: e.copy(out=s_sb[:, bk * 4:(bk + 1) * 4, :].rearrange("p a b -> p (a b)"), in_=B[bk][:, :]), reads=['B%d' % bk], writes=[P + 's_sb'])
            for hp in range(16):
                h, p = hp // 2, hp % 2
                S.op('dve', lambda e: e.max(out=top[:, h, p, 0:8], in_=s_sb[:, hp, :]), reads=[P + 's_sb'], writes=[P + 'top%d' % hp])
            for hp in range(16):
                h, p = hp // 2, hp % 2
                S.op('dve', lambda e: e.match_replace(out=tmp[:, hp % 4, 0:128], in_to_replace=top[:, h, p, 0:8], in_values=s_sb[:, hp, :], imm_value=NEG),
                     reads=[P + 's_sb', P + 'top%d' % hp], writes=[P + 'tmp%d' % (hp % 4)])
                S.op('dve', lambda e: e.max(out=top[:, h, p, 8:16], in_=tmp[:, hp % 4, 0:128]), reads=[P + 'tmp%d' % (hp % 4)], writes=[P + 'top%d' % hp])
            allt = [P + 'top%d' % hp for hp in range(16)]
            S.op('dve', lambda e: e.tensor_tensor(out=cand, in0=top[:, :, 0, :].unsqueeze(3).to_broadcast([128, 8, 16, 16]),
                                                  in1=top[:, :, 1, :].unsqueeze(2).to_broadcast([128, 8, 16, 16]), op=ALU.add),
                 reads=allt, writes=[P + 'cand'])
            for h in range(8):
                S.op('dve', lambda e: e.max(out=c16[:, h, 0:8], in_=cand[:, h, :, :].rearrange("p a b -> p (a b)")), reads=[P + 'cand'], writes=[P + 'c16_%d' % h])
            for h in range(8):
                S.op('dve', lambda e: e.match_replace(out=tmp[:, h % 4, :], in_to_replace=c16[:, h, 0:8], in_values=cand[:, h, :, :].rearrange("p a b -> p (a b)"), imm_value=NEG),
                     reads=[P + 'cand', P + 'c16_%d' % h], writes=[P + 'tmp%d' % (h % 4)])
                S.op('dve', lambda e: e.max(out=c16[:, h, 8:16], in_=tmp[:, h % 4, :]), reads=[P + 'tmp%d' % (h % 4)], writes=[P + 'c16_%d' % h])
            allc = [P + 'c16_%d' % h for h in range(8)]
            S.op('dve', lambda e: e.tensor_tensor(out=dd[:], in0=c16[:], in1=c16[:, :, 0:1].to_broadcast([128, 8, 16]), op=ALU.subtract), reads=allc, writes=[P + 'dd'])
            S.op('act', lambda e: e.activation(out=dd[:].rearrange("p a b -> p (a b)"), in_=dd[:].rearrange("p a b -> p (a b)"), func=AF.Exp), reads=[P + 'dd'], writes=[P + 'dd'])
            S.op('dve', lambda e: e.tensor_reduce(out=zz[:], in_=dd[:], axis=AX.X, op=ALU.add), reads=[P + 'dd'], writes=[P + 'zz'])
            S.op('dve', lambda e: e.reciprocal(out=zz[:], in_=zz[:]), reads=[P + 'zz'], writes=[P + 'zz'])
            S.op('dve', lambda e: e.scalar_tensor_tensor(out=gsc[:, tt, :], in0=dd[:, :, 15], scalar=float(np.exp(-DELTA)), in1=zz[:], op0=ALU.mult, op1=ALU.mult),
                 reads=[P + 'dd', P + 'zz'], writes=[P + 'gsc'])
            S.op('dve', lambda e: e.tensor_copy(out=cs[:, :, 0], in_=top[:, :, 0, 0]), reads=allt, writes=[P + 'cs'])
            S.op('dve', lambda e: e.scalar_tensor_tensor(out=cs[:, :, 1], in0=c16[:, :, 15], scalar=-DELTA, in1=top[:, :, 0, 0], op0=ALU.add, op1=ALU.subtract),
                 reads=allc + allt, writes=[P + 'cs'])
            S.op('dve', lambda e: e.tensor_tensor(out=E[:, tt, :, :], in0=s_sb, in1=cs[:].rearrange("p h q -> p (h q)").unsqueeze(2).to_broadcast([128, 16, 128]), op=ALU.subtract),
                 reads=[P + 's_sb', P + 'cs'], writes=[P + 'E'])
            S.op('act', lambda e: e.activation(out=E[:, tt, :, :].rearrange("p a b -> p (a b)"), in_=E[:, tt, :, :].rearrange("p a b -> p (a b)"), func=AF.Exp),
                 reads=[P + 'E'], writes=[P + 'E'])
            Ea = E[:, tt, :, :].rearrange("p (h q) n -> p h q n", q=2)[:, :, 0, :]
            S.op('dve', lambda e: e.tensor_tensor(out=Ea, in0=Ea, in1=gsc[:, tt, :].unsqueeze(2).to_broadcast([128, 8, 128]), op=ALU.mult),
                 reads=[P + 'E', P + 'gsc'], writes=[P + 'E'])
        S.barrier()
        its = [(k, tt) for k in range(32) for tt in range(4)]
        NI = len(its)

        def st_P1(n):
            k, tt = its[n]; z = n % 2; q = k % 2
            for c in range(8):
                S.op('pe', lambda e: e.matmul(B[z][:, :], xmT[:, c, tt * 128:(tt + 1) * 128], uT[q][:, c, :], start=(c == 0), stop=(c == 7)),
                     reads=[P + 'xmT', P + 'uT%d' % q], writes=['B%d' % z])
            S.op('act', lambda e: e.activation(out=gel[z][:], in_=B[z][:, :], func=AF.Gelu), reads=['B%d' % z], writes=[P + 'gel%d' % z])

        def st_D1(n):
            k, tt = its[n]; z = n % 2
            kW = P + 'W8_0'
            kM = P + 'Ms0'
            Ev = E[:, tt, :, :].rearrange("p (h q) n -> p h q n", q=2)
            S.op('dve', lambda e: e.tensor_tensor(out=W8[0][:], in0=Ev[:, :, 0, k * 4:(k + 1) * 4].unsqueeze(3).to_broadcast([128, 8, 4, 128]),
                                                  in1=Ev[:, :, 1, :].unsqueeze(2).to_broadcast([128, 8, 4, 128]), op=ALU.mult),
                 reads=[P + 'E'], writes=[kW])
            W8h = W8[0][:].rearrange("p h a b -> p h (a b)")
            for h in range(8):
                S.op('dve', lambda e: e.scalar_tensor_tensor(out=Ms[0][:, h, :], in0=W8h[:, h, :], scalar=gsc[:, tt, h:h + 1], in1=W8h[:, h, :], op0=ALU.is_ge, op1=ALU.mult),
                     reads=[kW, P + 'gsc'], writes=[kM + '_%d' % h])
            S.op('dve', lambda e: e.tensor_tensor(out=Ms[0][:, 0:4, :], in0=Ms[0][:, 0:4, :], in1=Ms[0][:, 4:8, :], op=ALU.add), reads=[kM + '_%d' % h for h in range(8)], writes=[kM] + [kM + '_%d' % h for h in range(8)])
            S.op('dve', lambda e: e.tensor_tensor(out=Ms[0][:, 0:2, :], in0=Ms[0][:, 0:2, :], in1=Ms[0][:, 2:4, :], op=ALU.add), reads=[kM], writes=[kM])
            S.op('dve', lambda e: e.tensor_tensor(out=G[:], in0=Ms[0][:, 0, :], in1=Ms[0][:, 1, :], op=ALU.add), reads=[kM], writes=[P + 'G'])
            S.op('dve', lambda e: e.tensor_tensor(out=A[z][:], in0=gel[z][:], in1=G[:], op=ALU.mult),
                 reads=[P + 'gel%d' % z, P + 'G'], writes=[P + 'A%d' % z])

        def st_P2(n):
            z = n % 2
            bt = 2 + z
            pb = B[bt][:].bitcast(BF16)
            for a in range(4):
                S.op('pe', lambda e: e.transpose(pb[:, a * 128:(a + 1) * 128], A[z][:, a * 128:(a + 1) * 128], C.ident[:]),
                     reads=[P + 'A%d' % z, 'ident'], writes=['B%d' % bt])
            S.op('act', lambda e: e.copy(out=AT[z][:].rearrange("p a t -> p (a t)"), in_=pb[:, 0:512]), reads=['B%d' % bt], writes=[P + 'AT%d' % z])

        def st_P3(n):
            k, tt = its[n]; z = n % 2; q = k % 2
            for hlf in range(2):
                bo = 4 + z * 2 + hlf
                for a in range(4):
                    S.op('pe', lambda e: e.matmul(B[bo][:, :], AT[z][:, a, :], vv[q][:, a, hlf * 512:(hlf + 1) * 512], start=(a == 0), stop=(a == 3)),
                         reads=[P + 'AT%d' % z, P + 'vv%d' % q], writes=['B%d' % bo])

        def st_D2(n):
            k, tt = its[n]; z = n % 2
            for hlf in range(2):
                bo = 4 + z * 2 + hlf
                if k == 0:
                    S.op('act', lambda e: e.copy(out=acc[:, tt, hlf * 512:(hlf + 1) * 512], in_=B[bo][:, :]), reads=['B%d' % bo], writes=[P + 'acc%d' % tt])
                else:
                    S.op('act', lambda e: e.copy(out=tmpo[z][:, hlf * 512:(hlf + 1) * 512], in_=B[bo][:, :]), reads=['B%d' % bo], writes=[P + 'tmpo%d' % z])
            if k > 0:
                S.op('pool', lambda e: e.tensor_tensor(out=acc[:, tt, :], in0=acc[:, tt, :], in1=tmpo[z][:], op=ALU.add),
                     reads=[P + 'tmpo%d' % z, P + 'acc%d' % tt], writes=[P + 'acc%d' % tt])

        st_P1(0)
        for n in range(NI + 1):
            if n + 1 < NI:
                st_P1(n + 1)
            if n < NI:
                st_D1(n)
            if n >= 1:
                st_P3(n - 1)
                k_prev, tt_prev = its[n - 1]
                if tt_prev == 3 and k_prev + 2 < 32:
                    load_chunk(k_prev + 2)
            if n < NI:
                st_P2(n)
            if n >= 1:
                st_D2(n - 1)
        S.barrier()
        for tt in range(4):
            i = sti * 4 + tt
            S.dma('sp', xr[:], xm_d[i * 128:(i + 1) * 128, :], writes=[P + 'xr'])
            S.op('dve', lambda e: e.scalar_tensor_tensor(out=pre[:], in0=xr[:], scalar=DN_ALPHA, in1=acc[:, tt, :], op0=ALU.mult, op1=ALU.add),
                 reads=[P + 'xr', P + 'acc%d' % tt], writes=['pre'])
            ln_epilogue(C, pre, 'pre', lng, lnb, xf_d, xfT_d, i, 'P%d' % l, 3)
    st.close()


def phase_B(C, xf_d, xfT_d, xm1_d, xm1T_d):
    S, B, dram, nc = C.S, C.B, C.dram, C.nc
    st = ExitStack()
    sb = lambda name, shape, dt=F32: st.enter_context(nc.sbuf_tensor("b_" + name, list(shape), dt))
    C.sb_save = C.sb
    C.sb = sb
    bw_d = dram('b_w_in', [D, B_W])
    kv_d = dram('shared_w_kv', [D, 1536])
    catT_d = dram('catT1', [768, SEQ], BF16, "Internal")
    XF = sb("XF", [128, 8, SEQ], BF16)
    for c in range(8):
        S.dma('sp' if c % 2 == 0 else 'act', XF[:, c, :], xfT_d[c * 128:(c + 1) * 128, :], writes=['XF'])
    st1 = ExitStack()
    sb_outer = sb
    sb = lambda name, shape, dt=F32: st1.enter_context(nc.sbuf_tensor("b1_" + name, list(shape), dt))
    acc = sb("acc", [128, 2, SEQ])
    mixT = sb("mixT", [128, SEQ], BF16)
    KT = sb("KT", [128, SEQ], BF16)
    QT = sb("QT", [128, SEQ], BF16)
    V = sb("V", [128, 32, 128], BF16)
    Wq = sb("Wq", [128, 8, 3, 128], BF16)
    Wk = sb("Wk", [128, 8, 128], BF16)
    Wv = sb("Wv", [128, 8, 128], BF16)
    PT = [sb("PT%d" % q, [128, 2, 128], BF16) for q in range(2)]
    SC = 128 ** -0.5
    DIL = (1, 4, 16)
    cnt = [0]

    def proj_T(dst, key_dst, wsel, key_w):
        for tg in range(8):
            bk = 4 + tg % 4
            for c in range(8):
                S.op('pe', lambda e: e.matmul(B[bk][:, :], wsel(c), XF[:, c, tg * 512:(tg + 1) * 512], start=(c == 0), stop=(c == 7)),
                     reads=[key_w, 'XF'], writes=['B%d' % bk])
            S.op('act', lambda e: e.copy(out=dst[:, tg * 512:(tg + 1) * 512], in_=B[bk][:, :]), reads=['B%d' % bk], writes=[key_dst])

    for s_ in range(6):
        for c in range(8):
            S.dma('pool', Wq[:, c, :, :], bw_d[c * 128:(c + 1) * 128, 0:2304].rearrange("p (g s e) -> p g s e", g=3, s=6)[:, :, s_, :], writes=['b_Wq'])
            S.dma('pool', Wk[:, c, :], kv_d[c * 128:(c + 1) * 128, s_ * 128:(s_ + 1) * 128], writes=['b_Wk'])
            S.dma('pool', Wv[:, c, :], kv_d[c * 128:(c + 1) * 128, 768 + s_ * 128:768 + (s_ + 1) * 128], writes=['b_Wv'])
        proj_T(KT, 'b_KT', lambda c: Wk[:, c, :], 'b_Wk')
        for g in range(3):
            d = DIL[g]
            nb = 32 // d
            proj_T(QT, 'b_QT', lambda c: Wq[:, c, g, :], 'b_Wq')
            for b4 in range(8):
                bk = 4 + b4 % 4
                for u in range(4):
                    blk = b4 * 4 + u
                    r, n = blk // nb, blk % nb
                    t0 = d * 128 * n + r
                    for c in range(8):
                        S.op('pe', lambda e: e.matmul(B[bk][:, u * 128:(u + 1) * 128], XF[:, c, t0:t0 + 127 * d + 1:d], Wv[:, c, :], start=(c == 0), stop=(c == 7)),
                             reads=['XF', 'b_Wv'], writes=['B%d' % bk])
                S.op('act', lambda e: e.copy(out=V[:, b4 * 4:(b4 + 1) * 4, :].rearrange("p a b -> p (a b)"), in_=B[bk][:, :]), reads=['B%d' % bk], writes=['b_V'])
            for blk in range(32):
                r, n = blk // nb, blk % nb
                z = cnt[0] % 2
                cnt[0] += 1
                tq = d * 128 * n + r
                qs = slice(tq, tq + 127 * d + 1, d)
                kbs = [1] if n == 0 else [0, 1]
                bs, bo = z, 2 + z
                for kb in kbs:
                    tk = d * 128 * (n - 1 + kb) + r
                    S.op('pe', lambda e: e.matmul(B[bs][:, kb * 128:(kb + 1) * 128], KT[:, tk:tk + 127 * d + 1:d], QT[:, qs], start=True, stop=True),
                         reads=['b_KT', 'b_QT'], writes=['B%d' % bs])
                lo = kbs[0]
                S.op('act', lambda e: e.activation(out=PT[z][:, lo:2, :].rearrange("p a b -> p (a b)"), in_=B[bs][:, lo * 128:256], func=AF.Exp, scale=SC),
                     reads=['B%d' % bs], writes=['b_PT%d' % z])
                S.op('dve', lambda e: e.tensor_tensor(out=PT[z][:, lo:2, :], in0=PT[z][:, lo:2, :], in1=C.dmask[:, lo:2, :], op=ALU.mult),
                     reads=['b_PT%d' % z, 'dmask'], writes=['b_PT%d' % z])
                for which in range(2):
                    for kb in kbs:
                        lhs = V[:, blk - 1 + kb, :] if which == 0 else C.ones_b[:]
                        S.op('pe', lambda e: e.matmul(B[bo][:, which * 128:(which + 1) * 128], lhs, PT[z][:, kb, :], start=(kb == kbs[0]), stop=(kb == 1)),
                             reads=['b_V', 'ones_b', 'b_PT%d' % z], writes=['B%d' % bo])
                pv = B[bo][:, 0:256].rearrange("p (a b) -> p a b", a=2)
                if g == 0:
                    S.op('act', lambda e: e.copy(out=acc[:, :, qs], in_=pv), reads=['B%d' % bo], writes=['b_acc'])
                else:
                    S.op('dve', lambda e: e.tensor_tensor(out=acc[:, :, qs], in0=acc[:, :, qs], in1=pv, op=ALU.add), reads=['B%d' % bo, 'b_acc'], writes=['b_acc'])
        S.op('dve', lambda e: e.reciprocal(out=acc[:, 1, :], in_=acc[:, 1, :]), reads=['b_acc'], writes=['b_acc'])
        S.op('dve', lambda e: e.tensor_tensor(out=mixT[:], in0=acc[:, 0, :], in1=acc[:, 1, :], op=ALU.mult), reads=['b_acc'], writes=['b_mixT'])
        S.dma('sp', catT_d[s_ * 128:(s_ + 1) * 128, :], mixT[:], reads=['b_mixT'], writes=['catT1_d'])
    S.barrier()
    st1.close()
    sb = sb_outer
    Wm = sb("Wm", [128, 8, 256], BF16)
    for c in range(8):
        S.dma('pool', Wm[:, c, :], bw_d[c * 128:(c + 1) * 128, 2304:2560], writes=['b_Wm'])
    wout = load_w_bf(C, "wout1", dram('w_out1', [D, D]), D, 'wout1')
    lng = load_rep(C, "lng", dram('ln_mix_g1', [1, D]), D, 'lnp')
    lnb = load_rep(C, "lnb", dram('ln_mix_b1', [1, D]), D, 'lnp')
    kmT, vmx = mem_kv(C, dram('memT', [D, 256]) if not hasattr(C, 'memT_d') else C.memT_d, dram('w_mem_kv1', [D, 512]), '1')
    alloc_ln(C)
    C.ma_pT = sb("ma_pT", [128, 1024], BF16)
    C.ma_rd = sb("ma_rd", [128, 4])
    C.catT = sb("catT", [128, 8, 128], BF16)
    qmT = sb("qmT", [64, 4, 128], BF16)
    cat = sb("cat", [128, 1024], BF16)
    pre = sb("pre", [128, 1024])
    XR = [sb("XR%d" % j, [128, 1024]) for j in range(2)]
    for i in range(NT):
        j = i % 2
        S.dma('sp', XR[j][:], xf_d[i * 128:(i + 1) * 128, :], writes=['bXR%d' % j])
        S.dma('sp', C.catT[:, 0:6, :], catT_d[:, i * 128:(i + 1) * 128].rearrange("(c p) t -> p c t", p=128), reads=['catT1_d'], writes=['catT'])
        for h in range(4):
            for c in range(8):
                S.op('pe', lambda e: e.matmul(B[6][0:64, h * 128:(h + 1) * 128], Wm[:, c, h * 64:(h + 1) * 64], XF[:, c, i * 128:(i + 1) * 128], start=(c == 0), stop=(c == 7)),
                     reads=['b_Wm', 'XF'], writes=['B6'])
        S.op('act', lambda e: e.copy(out=qmT[:].rearrange("p h t -> p (h t)"), in_=B[6][0:64, :]), reads=['B6'], writes=['b_qmT'])
        mem_attn_tile(C, qmT, 'b_qmT', kmT, vmx, '1', cat, 'b_cat', (2, 3), 4)
        outproj_tile(C, cat, 'b_cat', wout, 'wout1', XR[j], 'bXR%d' % j, 5, (7, 0), pre, chunks=(6, 7))
        ln_epilogue(C, pre, 'pre', lng, lnb, xm1_d, xm1T_d, i, 'B', 5)
    C.sb = C.sb_save
    st.close()


_PROG_CACHE = {}


def _prep_inputs(inputs, b):
    g = lambda k: np.asarray(inputs[k], dtype=np.float32)
    m = {}
    m['x'] = np.ascontiguousarray(g('x')[b])
    m['xT'] = np.ascontiguousarray(g('x')[b].T)
    m['memT'] = np.ascontiguousarray(g('mem')[b].T)
    m['a_w_in'] = np.ascontiguousarray(g('a_w_in')[0])
    m['a_w_gate2'] = np.ascontiguousarray(g('a_w_gate2')[0])
    m['a_b_gate'] = np.ascontiguousarray(g('a_b_gate')[0][None, :])
    m['a_norm_g'] = np.ascontiguousarray(g('a_norm_g')[0][None, :])
    for l in range(2):
        m['w_mem_kv%d' % l] = np.ascontiguousarray(g('w_mem_kv')[l])
        m['w_out%d' % l] = np.ascontiguousarray(g('w_out')[l])
        for nm in ('ln_mix_g', 'ln_mix_b', 'ln_ffn_g', 'ln_ffn_b'):
            m['%s%d' % (nm, l)] = np.ascontiguousarray(g(nm)[l][None, :])
    m['b_w_in'] = np.ascontiguousarray(g('b_w_in')[0])
    m['shared_w_kv'] = np.ascontiguousarray(g('shared_w_kv'))
    for l in range(2):
        m['peer_w_q%d' % l] = np.ascontiguousarray(g('peer_w_q')[l])
        m['peer_keysT%d' % l] = np.ascontiguousarray(g('peer_sub_keys')[l].transpose(0, 2, 1))
        m['peer_uT%d' % l] = np.ascontiguousarray(g('peer_u')[l].T)
        m['peer_v%d' % l] = np.ascontiguousarray(g('peer_v')[l])
    m.update(_consts_host())
    return m


def kernel(**inputs):
    if 'nc' not in _PROG_CACHE:
        _PROG_CACHE['nc'] = build()
    nc = _PROG_CACHE['nc']
    in_maps = [_prep_inputs(inputs, b) for b in range(8)]
    res = run_bass_kernel_spmd(nc, in_maps, core_ids=list(range(8)))
    out = np.stack([np.asarray(r['out'], dtype=np.float32) for r in res.results], axis=0)
    return out
```

```python
from contextlib import ExitStack
import numpy as np
import concourse.bass as bass
import concourse.mybir as mybir
from concourse.bass_utils import run_bass_kernel_spmd

F32 = mybir.dt.float32
BF16 = mybir.dt.bfloat16
ALU = mybir.AluOpType
AF = mybir.ActivationFunctionType
AX = mybir.AxisListType

D = 1024
SEQ = 4096
NT = SEQ // 128
DN_ALPHA = 4.0 ** 0.25
LN_EPS = 1e-5
HN_EPS = 1e-6
A_W = 2576
B_W = 2560


class Sched:
    def __init__(self, nc, stack):
        self.nc = nc
        self.E = {'pe': nc.tensor, 'dve': nc.vector, 'act': nc.scalar, 'pool': nc.gpsimd, 'sp': nc.sync}
        self.sem = {e: stack.enter_context(nc.semaphore('s_' + e)) for e in self.E}
        self.cnt = {e: 0 for e in self.E}
        self.seen = {e: {} for e in self.E}
        self.NDS = 32
        self.dsem = [stack.enter_context(nc.semaphore('d%d' % i)) for i in range(self.NDS)]
        self.dcnt = [0] * self.NDS
        self.dnext = 0
        self.tiles = {}
        self.nins = 0

    def _st(self, key):
        if key not in self.tiles:
            self.tiles[key] = {'w': None, 'r': {}}
        return self.tiles[key]

    def _semobj(self, sk):
        return self.sem[sk] if isinstance(sk, str) else self.dsem[sk]

    def _wait(self, eng, sk, val):
        if self.seen[eng].get(sk, 0) >= val:
            return
        self.E[eng].wait_ge(self._semobj(sk), val)
        self.seen[eng][sk] = val
        self.nins += 1

    def _deps(self, eng, reads, writes):
        for k in reads:
            st = self._st(k)
            if st['w'] is not None:
                self._wait(eng, *st['w'])
        for k in writes:
            st = self._st(k)
            if st['w'] is not None:
                self._wait(eng, *st['w'])
            for sk, v in st['r'].items():
                self._wait(eng, sk, v)

    def _mark(self, sk, val, reads, writes):
        for k in reads:
            st = self._st(k)
            st['r'][sk] = max(st['r'].get(sk, 0), val)
        for k in writes:
            st = self._st(k)
            st['w'] = (sk, val)
            st['r'] = {}

    def op(self, eng, fn, reads=(), writes=()):
        self._deps(eng, reads, writes)
        ins = fn(self.E[eng])
        self.cnt[eng] += 1
        ins.then_inc(self.sem[eng], 1)
        self._mark(eng, self.cnt[eng], reads, writes)
        self.nins += 1
        return ins

    def dma(self, eng, out, in_, reads=(), writes=(), **kw):
        s = self.dnext
        self.dnext = (self.dnext + 1) % self.NDS
        if self.dcnt[s] > 0:
            self._wait(eng, s, self.dcnt[s])
        self._deps(eng, reads, writes)
        ins = self.E[eng].dma_start(out=out, in_=in_, **kw)
        self.dcnt[s] += 16
        ins.then_inc(self.dsem[s], 16)
        self._mark(s, self.dcnt[s], reads, writes)
        self.nins += 1
        return ins

    def barrier(self):
        for e in self.E:
            for s in range(self.NDS):
                if self.dcnt[s] > 0:
                    self._wait(e, s, self.dcnt[s])
            for o in self.E:
                if o != e and self.cnt[o] > 0:
                    self._wait(e, o, self.cnt[o])

    def finish(self, eng='sp'):
        for s in range(self.NDS):
            if self.dcnt[s] > 0:
                self._wait(eng, s, self.dcnt[s])
        for e in self.E:
            if e != eng and self.cnt[e] > 0:
                self._wait(eng, e, self.cnt[e])


class Ctx:
    pass


def _consts_host():
    idx = np.arange(128)
    same = (idx[:, None] // 64) == (idx[None, :] // 64)
    M2 = (same & (idx[:, None] <= idx[None, :])).astype(np.float32)
    U2 = (same & (idx[:, None] > idx[None, :])).astype(np.float32)
    mp = (idx[:, None] >= idx[None, :]).astype(np.float32)
    mc = (idx[:, None] <= idx[None, :]).astype(np.float32)
    return {
        'c_ident': np.eye(128, dtype=np.float32),
        'c_m2': M2, 'c_u2': U2,
        'c_dmask': np.ascontiguousarray(np.stack([mp, mc], axis=1)),
        'c_ones': np.ones((128, 128), dtype=np.float32),
    }


def build(phases=('A', 'P0', 'B', 'P1'), dbg=False, n_super=SEQ // 512):
    nc = bass.Bass("TRN2", target_bir_lowering=False)
    C = Ctx()
    C.nc = nc
    st = ExitStack()
    C.st = st
    S = Sched(nc, st)
    C.S = S

    def dram(name, shape, dt=F32, kind="ExternalInput"):
        return nc.dram_tensor(name, list(shape), dt, kind=kind).ap()
    C.dram = dram

    def sb(name, shape, dt=F32):
        return st.enter_context(nc.sbuf_tensor(name, list(shape), dt))
    C.sb = sb

    C.B = [st.enter_context(nc.psum_tensor("bank%d" % i, [128, 512], F32)) for i in range(8)]

    C.ident = sb("ident", [128, 128], BF16)
    C.m2 = sb("m2", [128, 128], F32)
    C.u2 = sb("u2", [128, 128], F32)
    C.ones_f = sb("ones_f", [128, 128], F32)
    C.ones_b = sb("ones_b", [128, 128], BF16)
    C.dmask = sb("dmask", [128, 2, 128], BF16)
    S.dma('pool', C.ident[:], dram('c_ident', [128, 128]), writes=['ident'])
    S.dma('sp', C.m2[:], dram('c_m2', [128, 128]), writes=['m2'])
    S.dma('sp', C.u2[:], dram('c_u2', [128, 128]), writes=['u2'])
    c_ones = dram('c_ones', [128, 128])
    S.dma('sp', C.ones_f[:], c_ones, writes=['ones_f'])
    S.dma('pool', C.ones_b[:], c_ones, writes=['ones_b'])
    S.dma('pool', C.dmask[:], dram('c_dmask', [128, 2, 128]), writes=['dmask'])

    ext_in = "ExternalInput"
    inter = "ExternalOutput" if dbg else "Internal"
    C.out = None
    names = {'A': 'xm0', 'P0': 'xf0', 'B': 'xm1'}
    prev = {'P0': 'xm0', 'B': 'xf0', 'P1': 'xm1'}
    for l in (0, 1):
        if ('P%d' % l) in phases:
            peer_convert(C, l)
    for ph in ('A', 'P0', 'B', 'P1'):
        if ph not in phases:
            continue
        if ph in prev and not hasattr(C, prev[ph]):
            setattr(C, prev[ph], dram(prev[ph], [SEQ, D], F32, ext_in))
            setattr(C, prev[ph] + 'T', dram(prev[ph] + 'T', [D, SEQ], BF16, ext_in))
        if ph in names:
            setattr(C, names[ph], dram(names[ph], [SEQ, D], F32, inter))
            setattr(C, names[ph] + 'T', dram(names[ph] + 'T', [D, SEQ], BF16, inter))
        if ph == 'A':
            phase_A(C)
        elif ph == 'P0':
            phase_P(C, 0, C.xm0, C.xm0T, C.xf0, C.xf0T, n_super)
        elif ph == 'B':
            phase_B(C, C.xf0, C.xf0T, C.xm1, C.xm1T)
        else:
            C.out = dram('out', [SEQ, D], F32, "ExternalOutput")
            phase_P(C, 1, C.xm1, C.xm1T, C.out, None, n_super)
        S.barrier()
    S.finish('sp')
    st.close()
    return nc


def ln_epilogue(C, pre, key_pre, g_rep, b_rep, out_dram, outT_dram, i, tag, bank):
    S = C.S
    stt, mv, rs, xn, xnb, xnT = C.ln_st, C.ln_mv, C.ln_rs, C.ln_xn, C.ln_xnb, C.ln_xnT
    for hlf in range(2):
        S.op('dve', lambda e: e.bn_stats(out=stt[:, hlf, :], in_=pre[:, hlf * 512:(hlf + 1) * 512]), reads=[key_pre], writes=['ln_st'])
    S.op('dve', lambda e: e.bn_aggr(out=mv[:], in_=stt[:].rearrange("p a b -> p (a b)")), reads=['ln_st'], writes=['ln_mv'])
    S.op('act', lambda e: e.activation(out=rs[:], in_=mv[:, 1:2], func=AF.Sqrt, bias=LN_EPS, scale=1.0), reads=['ln_mv'], writes=['ln_rs'])
    S.op('dve', lambda e: e.reciprocal(out=rs[:], in_=rs[:]), reads=['ln_rs'], writes=['ln_rs'])
    S.op('dve', lambda e: e.tensor_scalar(out=xn[:], in0=pre[:], scalar1=mv[:, 0:1], scalar2=rs[:, 0:1], op0=ALU.subtract, op1=ALU.mult),
         reads=[key_pre, 'ln_mv', 'ln_rs'], writes=['ln_xn'])
    S.op('pool', lambda e: e.tensor_tensor(out=xn[:], in0=xn[:], in1=g_rep[:], op=ALU.mult), reads=['ln_xn', 'lnp'], writes=['ln_xn'])
    S.op('pool', lambda e: e.tensor_tensor(out=xn[:], in0=xn[:], in1=b_rep[:], op=ALU.add), reads=['ln_xn', 'lnp'], writes=['ln_xn'])
    S.dma('sp', out_dram[i * 128:(i + 1) * 128, :], xn[:], reads=['ln_xn'])
    if outT_dram is not None:
        S.op('act', lambda e: e.copy(out=xnb[:], in_=xn[:]), reads=['ln_xn'], writes=['ln_xnb'])
        pb = C.B[bank][:].bitcast(BF16)
        for c in range(8):
            S.op('pe', lambda e: e.transpose(pb[:, c * 128:(c + 1) * 128], xnb[:, c * 128:(c + 1) * 128], C.ident[:]),
                 reads=['ln_xnb', 'ident'], writes=['B%d' % bank])
        S.op('act', lambda e: e.copy(out=xnT[:].rearrange("p c t -> p (c t)"), in_=pb[:, :]), reads=['B%d' % bank], writes=['ln_xnT'])
        S.dma('sp', outT_dram[:, i * 128:(i + 1) * 128].rearrange("(c p) t -> p c t", p=128), xnT[:], reads=['ln_xnT'])


def alloc_ln(C):
    sb = C.sb
    C.ln_st = sb("ln_st", [128, 2, 6])
    C.ln_mv = sb("ln_mv", [128, 2])
    C.ln_rs = sb("ln_rs", [128, 1])
    C.ln_xn = sb("ln_xn", [128, 1024])
    C.ln_xnb = sb("ln_xnb", [128, 1024], BF16)
    C.ln_xnT = sb("ln_xnT", [128, 8, 128], BF16)


def load_w_bf(C, name, dram_ap, ncols, key):
    t = C.sb(name, [128, 8, ncols], BF16)
    for c in range(8):
        C.S.dma('pool', t[:, c, :], dram_ap[c * 128:(c + 1) * 128, :], writes=[key])
    return t


def load_rep(C, name, dram_ap, n, key):
    t = C.sb(name, [128, n], F32)
    C.S.dma('sp', t[:], dram_ap.partition_broadcast(128), writes=[key])
    return t


def mem_kv(C, memT_d, wmkv_d, tag):
    S, sb, B = C.S, C.sb, C.B
    memT = load_w_bf(C, "memT" + tag, memT_d, 256, 'memT' + tag)
    wm = load_w_bf(C, "wmkv" + tag, wmkv_d, 512, 'wmkv' + tag)
    kmT = sb("kmT" + tag, [64, 4, 256], BF16)
    vmx = sb("vmx" + tag, [128, 2, 4, 65], BF16)
    S.op('pool', lambda e: e.memset(vmx[:].rearrange("p a b c -> p (a b c)"), 1.0), writes=['vmx' + tag])
    for h in range(4):
        for c in range(8):
            S.op('pe', lambda e: e.matmul(B[h % 2][0:64, (h // 2) * 256:(h // 2) * 256 + 256],
                                          wm[:, c, h * 64:(h + 1) * 64], memT[:, c, :], start=(c == 0), stop=(c == 7)),
                 reads=['memT' + tag, 'wmkv' + tag], writes=['B%d' % (h % 2)])
        S.op('act', lambda e: e.copy(out=kmT[:, h, :], in_=B[h % 2][0:64, (h // 2) * 256:(h // 2) * 256 + 256]), reads=['B%d' % (h % 2)], writes=['kmT' + tag])
    for j in range(2):
        for c in range(8):
            S.op('pe', lambda e: e.matmul(B[2 + j][:, 0:256], memT[:, c, j * 128:(j + 1) * 128], wm[:, c, 256:512], start=(c == 0), stop=(c == 7)),
                 reads=['memT' + tag, 'wmkv' + tag], writes=['B%d' % (2 + j)])
        S.op('act', lambda e: e.copy(out=vmx[:, j, :, 0:64], in_=B[2 + j][:, 0:256].rearrange("p (h e) -> p h e", h=4)),
             reads=['B%d' % (2 + j)], writes=['vmx' + tag])
    return kmT, vmx


def mem_attn_tile(C, qmT, key_qmT, kmT, vmx, tag, cat, key_cat, bs, bm):
    S, B = C.S, C.B
    pT = C.ma_pT
    for h in range(4):
        bk = bs[h // 2]
        for j in range(2):
            col = ((h % 2) * 2 + j) * 128
            S.op('pe', lambda e: e.matmul(B[bk][:, col:col + 128], kmT[:, h, j * 128:(j + 1) * 128], qmT[:, h, :], start=True, stop=True),
                 reads=['kmT' + tag, key_qmT], writes=['B%d' % bk])
    for hh in range(2):
        S.op('act', lambda e: e.activation(out=pT[:, hh * 512:(hh + 1) * 512], in_=B[bs[hh]][:, :], func=AF.Exp, scale=0.125),
             reads=['B%d' % bs[hh]], writes=['ma_pT'])
    for h in range(4):
        for j in range(2):
            col = (h * 2 + j) * 128
            S.op('pe', lambda e: e.matmul(B[bm][:, h * 65:h * 65 + 65], pT[:, col:col + 128], vmx[:, j, h, :], start=(j == 0), stop=(j == 1)),
                 reads=['ma_pT', 'vmx' + tag], writes=['B%d' % bm])
    mo = B[bm][:, 0:260].rearrange("p (h e) -> p h e", h=4)
    S.op('dve', lambda e: e.reciprocal(out=C.ma_rd[:], in_=mo[:, :, 64]), reads=['B%d' % bm], writes=['ma_rd'])
    S.op('dve', lambda e: e.tensor_tensor(out=cat[:, 768:1024].rearrange("p (h e) -> p h e", h=4), in0=mo[:, :, 0:64],
                                          in1=C.ma_rd[:].unsqueeze(2).to_broadcast([128, 4, 64]), op=ALU.mult),
         reads=['B%d' % bm, 'ma_rd'], writes=[key_cat])


def outproj_tile(C, cat, key_cat, wout, key_wout, XR, key_XR, bt, by, pre, chunks=range(8)):
    S, B = C.S, C.B
    pb = B[bt][:].bitcast(BF16)
    chunks = list(chunks)
    for c in chunks:
        S.op('pe', lambda e: e.transpose(pb[:, c * 128:(c + 1) * 128], cat[:, c * 128:(c + 1) * 128], C.ident[:]),
             reads=[key_cat, 'ident'], writes=['B%d' % bt])
    c0, c1 = chunks[0], chunks[-1] + 1
    S.op('act', lambda e: e.copy(out=C.catT[:, c0:c1, :].rearrange("p c t -> p (c t)"), in_=pb[:, c0 * 128:c1 * 128]), reads=['B%d' % bt], writes=['catT'])
    for hlf in range(2):
        for c in range(8):
            S.op('pe', lambda e: e.matmul(B[by[hlf]][:, :], C.catT[:, c, :], wout[:, c, hlf * 512:(hlf + 1) * 512], start=(c == 0), stop=(c == 7)),
                 reads=['catT', key_wout], writes=['B%d' % by[hlf]])
        S.op('dve', lambda e: e.scalar_tensor_tensor(out=pre[:, hlf * 512:(hlf + 1) * 512], in0=XR[:, hlf * 512:(hlf + 1) * 512], scalar=DN_ALPHA,
                                                     in1=B[by[hlf]][:, :], op0=ALU.mult, op1=ALU.add),
             reads=[key_XR, 'B%d' % by[hlf]], writes=['pre'])


def phase_A(C):
    S, B, dram, nc = C.S, C.B, C.dram, C.nc
    st = ExitStack()
    sb = lambda name, shape, dt=F32: st.enter_context(nc.sbuf_tensor("a_" + name, list(shape), dt))
    C.sb_save = C.sb
    C.sb = sb
    x_d = dram('x', [SEQ, D]); xT_d = dram('xT', [D, SEQ])
    C.memT_d = dram('memT', [D, 256])
    W = load_w_bf(C, "awin", dram('a_w_in', [D, A_W]), A_W, 'awin')
    wout = load_w_bf(C, "wout0", dram('w_out0', [D, D]), D, 'wout0')
    wg2 = sb("wg2", [16, 384]); S.dma('sp', wg2[:], dram('a_w_gate2', [16, 384]), writes=['wg2'])
    bg = sb("bg", [1, 384]); S.dma('sp', bg[:], dram('a_b_gate', [1, 384]), writes=['bg'])
    ng = load_rep(C, "ng", dram('a_norm_g', [1, 768]), 768, 'ng')
    lng = load_rep(C, "lng", dram('ln_mix_g0', [1, D]), D, 'lnp')
    lnb = load_rep(C, "lnb", dram('ln_mix_b0', [1, D]), D, 'lnp')
    kmT, vmx = mem_kv(C, C.memT_d, dram('w_mem_kv0', [D, 512]), '0')
    alloc_ln(C)
    C.ma_pT = sb("ma_pT", [128, 1024], BF16)
    C.ma_rd = sb("ma_rd", [128, 4])
    C.catT = sb("catT", [128, 8, 128], BF16)
    XT = [sb("XT%d" % j, [128, 8, 128], BF16) for j in range(2)]
    XR = [sb("XR%d" % j, [128, 1024]) for j in range(2)]
    hgT = sb("hgT", [16, 128])
    qmT = sb("qmT", [64, 4, 128], BF16)
    t1 = sb("a_t1", [128, 384]); la = sb("a_la", [128, 384])
    expb = sb("expb", [96, 4, 128]); expnb = sb("expnb", [96, 4, 128]); expE = sb("expE", [128, 384])
    qt = sb("qt", [96, 4, 128], BF16); kt = sb("kt", [96, 4, 128], BF16)
    kend = sb("kend", [128, 384], BF16); vb = sb("vb", [128, 768], BF16)
    gate = sb("gate", [128, 768])
    attnTb = sb("attnTb", [128, 4, 128], BF16)
    St = sb("St", [96, 4, 192]); SbA = [sb("SbA%d" % q, [96, 4, 192], BF16) for q in range(2)]; SbB = sb("SbB", [96, 4, 192], BF16)
    sq = sb("sq", [128, 768]); ssq = sb("ssq", [128, 4]); to = sb("to", [128, 768])
    cat = sb("cat", [128, 1024], BF16)
    pre = sb("pre", [128, 1024])
    S.op('pool', lambda e: e.memset(St[:].rearrange("p a b -> p (a b)"), 0.0), writes=['St'])
    S.op('pool', lambda e: e.memset(SbA[0][:].rearrange("p a b -> p (a b)"), 0.0), writes=['SbA0'])
    QS = 96 ** -0.5

    def load(i):
        j = i % 2
        S.dma('pool', XT[j][:], xT_d[:, i * 128:(i + 1) * 128].rearrange("(c p) t -> p c t", p=128), writes=['XT%d' % j])
        S.dma('sp', XR[j][:], x_d[i * 128:(i + 1) * 128, :], writes=['XR%d' % j])

    load(0)
    for i in range(NT):
        j = i % 2
        if i + 1 < NT:
            load(i + 1)
        xt = XT[j]; kx = 'XT%d' % j
        for h in range(4):
            for c in range(8):
                S.op('pe', lambda e: e.matmul(B[0][0:96, h * 128:(h + 1) * 128], W[:, c, h * 96:(h + 1) * 96], xt[:, c, :], start=(c == 0), stop=(c == 7)),
                     reads=['awin', kx], writes=['B0'])
        for h in range(4):
            for c in range(8):
                S.op('pe', lambda e: e.matmul(B[1][0:96, h * 128:(h + 1) * 128], W[:, c, 384 + h * 96:384 + (h + 1) * 96], xt[:, c, :], start=(c == 0), stop=(c == 7)),
                     reads=['awin', kx], writes=['B1'])
        for (bk, c0, n) in ((2, 384, 384), (3, 768, 512), (4, 1280, 512), (5, 1792, 512)):
            for c in range(8):
                S.op('pe', lambda e: e.matmul(B[bk][:, 0:n], xt[:, c, :], W[:, c, c0:c0 + n], start=(c == 0), stop=(c == 7)),
                     reads=['awin', kx], writes=['B%d' % bk])
        for h in range(4):
            for c in range(8):
                S.op('pe', lambda e: e.matmul(B[6][0:64, h * 128:(h + 1) * 128], W[:, c, 2320 + h * 64:2320 + (h + 1) * 64], xt[:, c, :], start=(c == 0), stop=(c == 7)),
                     reads=['awin', kx], writes=['B6'])
        for c in range(8):
            S.op('pe', lambda e: e.matmul(B[7][0:16, 0:128], W[:, c, 2304:2320], xt[:, c, :], start=(c == 0), stop=(c == 7)),
                 reads=['awin', kx], writes=['B7'])
        S.op('act', lambda e: e.copy(out=qmT[:].rearrange("p h t -> p (h t)"), in_=B[6][0:64, :]), reads=['B6'], writes=['qmT'])
        S.op('act', lambda e: e.copy(out=hgT[:], in_=B[7][0:16, 0:128]), reads=['B7'], writes=['hgT'])
        S.op('pe', lambda e: e.matmul(B[7][:, 0:384], hgT[:], wg2[:], start=True, stop=False), reads=['hgT', 'wg2'], writes=['B7'])
        S.op('pe', lambda e: e.matmul(B[7][:, 0:384], C.ones_f[0:1, :], bg[:], start=False, stop=True), reads=['ones_f', 'bg'], writes=['B7'])
        S.op('act', lambda e: e.activation(out=t1[:], in_=B[7][:, 0:384], func=AF.Exp, scale=-1.0), reads=['B7'], writes=['a_t1'])
        S.op('act', lambda e: e.activation(out=t1[:], in_=t1[:], func=AF.Ln, bias=1.0, scale=1.0), reads=['a_t1'], writes=['a_t1'])
        S.op('act', lambda e: e.mul(out=la[:], in_=t1[:], mul=-1.0 / 16.0), reads=['a_t1'], writes=['a_la'])
        for h in range(4):
            S.op('pe', lambda e: e.matmul(B[6][0:96, h * 128:(h + 1) * 128], la[:, h * 96:(h + 1) * 96], C.m2[:], start=True, stop=True),
                 reads=['a_la', 'm2'], writes=['B6'])
        S.op('pe', lambda e: e.matmul(B[7][:, 0:384], C.u2[:], la[:], start=True, stop=True), reads=['a_la', 'u2'], writes=['B7'])
        S.op('act', lambda e: e.activation(out=expb[:].rearrange("p h t -> p (h t)"), in_=B[6][0:96, :], func=AF.Exp), reads=['B6'], writes=['expb'])
        S.op('act', lambda e: e.activation(out=expnb[:].rearrange("p h t -> p (h t)"), in_=B[6][0:96, :], func=AF.Exp, scale=-1.0), reads=['B6'], writes=['expnb'])
        S.op('act', lambda e: e.activation(out=expE[:], in_=B[7][:, 0:384], func=AF.Exp), reads=['B7'], writes=['expE'])
        S.op('dve', lambda e: e.scalar_tensor_tensor(out=qt[:].rearrange("p h t -> p (h t)"), in0=B[0][0:96, :], scalar=QS, in1=expb[:].rearrange("p h t -> p (h t)"),
                                                     op0=ALU.mult, op1=ALU.mult), reads=['B0', 'expb'], writes=['qt'])
        S.op('dve', lambda e: e.tensor_tensor(out=kt[:].rearrange("p h t -> p (h t)"), in0=B[1][0:96, :], in1=expnb[:].rearrange("p h t -> p (h t)"), op=ALU.mult),
             reads=['B1', 'expnb'], writes=['kt'])
        S.op('dve', lambda e: e.tensor_tensor(out=kend[:], in0=B[2][:, 0:384], in1=expE[:], op=ALU.mult), reads=['B2', 'expE'], writes=['kend'])
        S.op('act', lambda e: e.copy(out=vb[:, 0:512], in_=B[3][:, :]), reads=['B3'], writes=['vb'])
        S.op('act', lambda e: e.copy(out=vb[:, 512:768], in_=B[4][:, 0:256]), reads=['B4'], writes=['vb'])
        S.op('act', lambda e: e.activation(out=gate[:, 0:256], in_=B[4][:, 256:512], func=AF.Silu), reads=['B4'], writes=['gate'])
        S.op('act', lambda e: e.activation(out=gate[:, 256:768], in_=B[5][:, :], func=AF.Silu), reads=['B5'], writes=['gate'])
        S.op('pool', lambda e: e.tensor_tensor(out=gate[:], in0=gate[:], in1=ng[:], op=ALU.mult), reads=['gate', 'ng'], writes=['gate'])
        for h in range(4):
            S.op('pe', lambda e: e.matmul(B[0][:, h * 128:(h + 1) * 128], kt[:, h, :], qt[:, h, :], start=True, stop=True), reads=['kt', 'qt'], writes=['B0'])
        S.op('dve', lambda e: e.tensor_tensor(out=attnTb[:], in0=B[0][:, :].rearrange("p (h t) -> p h t", h=4),
                                              in1=C.m2[:].unsqueeze(1).to_broadcast([128, 4, 128]), op=ALU.mult), reads=['B0', 'm2'], writes=['attnTb'])
        for ch in range(2):
            for h in range(4):
                bk = 2 + ch * 2 + h // 2
                col = (h % 2) * 192
                S.op('pe', lambda e: e.matmul(B[bk][0:96, col:col + 192], kend[ch * 64:(ch + 1) * 64, h * 96:(h + 1) * 96], vb[ch * 64:(ch + 1) * 64, h * 192:(h + 1) * 192],
                                              start=True, stop=True), reads=['kend', 'vb'], writes=['B%d' % bk])
        def o_ap(h, lo, hi):
            bk = 1 if h < 2 else 6
            col = (h % 2) * 192
            return B[bk][lo:hi, col:col + 192], 'B%d' % bk
        SbS = SbA[i % 2]; kS = 'SbA%d' % (i % 2)
        SbN = SbA[(i + 1) % 2]; kN = 'SbA%d' % ((i + 1) % 2)
        for ch in range(2):
            for h in range(4):
                bk = 2 + ch * 2 + h // 2
                col = (h % 2) * 192
                S.op('dve', lambda e: e.scalar_tensor_tensor(out=St[:, h, :], in0=St[:, h, :], scalar=expb[:, h, ch * 64 + 63:ch * 64 + 64], in1=B[bk][0:96, col:col + 192],
                                                             op0=ALU.mult, op1=ALU.add), reads=['St', 'expb', 'B%d' % bk], writes=['St'])
            Sb_dst, kd = (SbB, 'SbB') if ch == 0 else (SbN, kN)
            S.op('act', lambda e: e.copy(out=Sb_dst[:].rearrange("p a b -> p (a b)"), in_=St[:].rearrange("p a b -> p (a b)")),
                 reads=['St'], writes=[kd])
        for h in range(4):
            oap, ok = o_ap(h, 0, 128)
            S.op('pe', lambda e: e.matmul(oap, attnTb[:, h, :], vb[:, h * 192:(h + 1) * 192], start=True, stop=False), reads=['attnTb', 'vb'], writes=[ok])
            oap0, _ = o_ap(h, 0, 64)
            S.op('pe', lambda e: e.matmul(oap0, qt[:, h, 0:64], SbS[:, h, :], start=False, stop=False), reads=['qt', kS], writes=[ok])
            oap1, _ = o_ap(h, 64, 128)
            S.op('pe', lambda e: e.matmul(oap1, qt[:, h, 64:128], SbB[:, h, :], start=False, stop=True), reads=['qt', 'SbB'], writes=[ok])
        for hh in range(2):
            bk = 1 if hh == 0 else 6
            S.op('act', lambda e: e.activation(out=sq[:, hh * 384:(hh + 1) * 384], in_=B[bk][:, 0:384], func=AF.Square), reads=['B%d' % bk], writes=['sq'])
        S.op('dve', lambda e: e.tensor_reduce(out=ssq[:], in_=sq[:].rearrange("p (h v) -> p h v", h=4), axis=AX.X, op=ALU.add), reads=['sq'], writes=['ssq'])
        S.op('act', lambda e: e.activation(out=ssq[:], in_=ssq[:], func=AF.Sqrt, bias=HN_EPS, scale=1.0 / 192.0), reads=['ssq'], writes=['ssq'])
        S.op('dve', lambda e: e.reciprocal(out=ssq[:], in_=ssq[:]), reads=['ssq'], writes=['ssq'])
        for hh in range(2):
            bk = 1 if hh == 0 else 6
            S.op('dve', lambda e: e.tensor_tensor(out=to[:, hh * 384:(hh + 1) * 384].rearrange("p (h v) -> p h v", h=2),
                                                  in0=B[bk][:, 0:384].rearrange("p (h v) -> p h v", h=2),
                                                  in1=ssq[:, hh * 2:hh * 2 + 2].unsqueeze(2).to_broadcast([128, 2, 192]), op=ALU.mult),
                 reads=['B%d' % bk, 'ssq'], writes=['to'])
        S.op('dve', lambda e: e.tensor_tensor(out=cat[:, 0:768], in0=to[:], in1=gate[:], op=ALU.mult), reads=['to', 'gate'], writes=['cat'])
        mem_attn_tile(C, qmT, 'qmT', kmT, vmx, '0', cat, 'cat', (2, 3), 4)
        outproj_tile(C, cat, 'cat', wout, 'wout0', XR[j], 'XR%d' % j, 5, (7, 0), pre)
        ln_epilogue(C, pre, 'pre', lng, lnb, C.xm0, C.xm0T, i, 'A', 5)
    C.sb = C.sb_save
    st.close()


def peer_convert(C, l):
    S, dram = C.S, C.dram
    uT_d = dram('peer_uT%d' % l, [D, 16384])
    v_d = dram('peer_v%d' % l, [16384, D])
    uTb = dram('peer_uTb%d' % l, [D, 16384], BF16, "Internal")
    vb = dram('peer_vb%d' % l, [16384, D], BF16, "Internal")
    for c in range(8):
        for q in range(4):
            S.dma('pool', uTb[c * 128:(c + 1) * 128, q * 4096:(q + 1) * 4096], uT_d[c * 128:(c + 1) * 128, q * 4096:(q + 1) * 4096], writes=['uTb%d' % l])
    for c in range(32):
        S.dma('pool', vb[c * 512:(c + 1) * 512, :].rearrange("(p a) d -> p a d", p=128), v_d[c * 512:(c + 1) * 512, :].rearrange("(p a) d -> p a d", p=128), writes=['vb%d' % l])
    C.peer_w = getattr(C, 'peer_w', {})
    C.peer_w[l] = (uTb, vb)


def phase_P(C, l, xm_d, xmT_d, xf_d, xfT_d, n_super=SEQ // 512):
    S, B, dram, nc = C.S, C.B, C.dram, C.nc
    st = ExitStack()
    sb = lambda name, shape, dt=F32: st.enter_context(nc.sbuf_tensor("p%d_%s" % (l, name), list(shape), dt))
    P = 'p%d_' % l
    wq = sb("wq", [128, 8, 2048], BF16)
    wq_d = dram('peer_w_q%d' % l, [D, 2048])
    for c in range(8):
        S.dma('pool', wq[:, c, :], wq_d[c * 128:(c + 1) * 128, :], writes=[P + 'wq'])
    keysT = sb("keysT", [128, 2, 128])
    kd = dram('peer_keysT%d' % l, [2, 128, 128])
    for p in range(2):
        S.dma('sp', keysT[:, p, :], kd[p], writes=[P + 'keysT'])
    if l not in getattr(C, 'peer_w', {}):
        peer_convert(C, l)
    uT_d, v_d = C.peer_w[l]
    lng = sb("lng", [128, D]); lnb = sb("lnb", [128, D])
    S.dma('sp', lng[:], dram('ln_ffn_g%d' % l, [1, D]).partition_broadcast(128), writes=['lnp'])
    S.dma('sp', lnb[:], dram('ln_ffn_b%d' % l, [1, D]).partition_broadcast(128), writes=['lnp'])
    xmT = sb("xmT", [128, 8, 512], BF16)
    acc = sb("acc", [128, 4, 1024])
    top = sb("top", [128, 8, 2, 16])
    c16 = sb("c16", [128, 8, 16])
    dd = sb("dd", [128, 8, 16])
    zz = sb("zz", [128, 8])
    cs = sb("cs", [128, 8, 2])
    E = sb("E", [128, 4, 16, 128])
    gsc = sb("gsc", [128, 4, 8])
    uT = [sb("uT%d" % q, [128, 8, 512], BF16) for q in range(2)]
    vv = [sb("vv%d" % q, [128, 4, 1024], BF16) for q in range(2)]
    gel = [sb("gel%d" % q, [128, 512]) for q in range(2)]
    W8 = [sb("W8_%d" % q, [128, 8, 4, 128]) for q in range(2)]
    w0 = W8[0][:].rearrange("p h a b -> p (h a b)")
    w1 = W8[1][:].rearrange("p h a b -> p (h a b)")
    qT = w0[:, 0:2048].rearrange("p (a b) -> p a b", a=16)
    s_sb = w0[:, 2048:4096].rearrange("p (a b) -> p a b", a=16)
    cand = w1[:, 0:2048].rearrange("p (h a b) -> p h a b", h=8, a=16)
    tmp = w1[:, 2048:3072].rearrange("p (a b) -> p a b", a=4)
    Sg = [sb("Sg%d" % q, [128, 8, 512], BF16) for q in range(2)]
    tmpo = [sb("tmpo%d" % q, [128, 1024]) for q in range(2)]
    ngsc = sb("ngsc", [128, 4, 8])
    xr = tmpo[1]
    G = sb("G", [128, 512], BF16)
    A = [sb("A%d" % q, [128, 512], BF16) for q in range(2)]
    AT = [sb("AT%d" % q, [128, 4, 128], BF16) for q in range(2)]
    pre = tmpo[0]
    C.ln_st = sb("ln_st", [128, 2, 6]); C.ln_mv = sb("ln_mv", [128, 2]); C.ln_rs = sb("ln_rs", [128, 1])
    C.ln_xn = sb("ln_xn", [128, 1024]); C.ln_xnb = sb("ln_xnb", [128, 1024], BF16); C.ln_xnT = sb("ln_xnT", [128, 8, 128], BF16)
    DELTA = 2e-4
    NEG = -1e30
    cnt = [0]

    def load_chunk(k):
        q = k % 2
        S.dma('sp', uT[q][:], uT_d[:, k * 512:(k + 1) * 512].rearrange("(c p) e -> p c e", p=128), reads=['uTb%d' % l], writes=[P + 'uT%d' % q])
        S.dma('sp', vv[q][:], v_d[k * 512:(k + 1) * 512, :].rearrange("(a b) d -> b a d", b=128), reads=['vb%d' % l], writes=[P + 'vv%d' % q])

    for sti in range(n_super):
        t0 = sti * 512
        S.dma('sp', xmT[:], xmT_d[:, t0:t0 + 512].rearrange("(c p) t -> p c t", p=128), writes=[P + 'xmT'])
        load_chunk(0)
        load_chunk(1)
        for tt in range(4):
            for hp in range(16):
                bk = 4 + (hp % 4)
                for c in range(8):
                    S.op('pe', lambda e: e.matmul(B[bk][:, 0:128], wq[:, c, hp * 128:(hp + 1) * 128], xmT[:, c, tt * 128:(tt + 1) * 128], start=(c == 0), stop=(c == 7)),
                         reads=[P + 'wq', P + 'xmT'], writes=['B%d' % bk])
                S.op('act', lambda e: e.copy(out=qT[:, hp, :], in_=B[bk][:, 0:128]), reads=['B%d' % bk], writes=[P + 'qT'])
            for hp in range(16):
                bk = hp // 4
                col = (hp % 4) * 128
                S.op('pe', lambda e: e.matmul(B[bk][:, col:col + 128], qT[:, hp, :], keysT[:, hp % 2, :], start=True, stop=True),
                     reads=[P + 'qT', P + 'keysT'], writes=['B%d' % bk])
            for bk in range(4):
                S.op('act', lambda e: e.copy(out=s_sb[:, bk * 4:(bk + 1) * 4, :].rearrange("p a b -> p (a b)"), in_=B[bk][:, :]), reads=['B%d' % bk], writes=[P + 's_sb'])
            for hp in range(16):
                h, p = hp // 2, hp % 2
                S.op('dve', lambda e: e.max(out=top[:, h, p, 0:8], in_=s_sb[:, hp, :]), reads=[P + 's_sb'], writes=[P + 'top%d' % hp])
            for hp in range(16):
                h, p = hp // 2, hp % 2
                S.op('dve', lambda e: e.match_replace(out=tmp[:, hp % 4, 0:128], in_to_replace=top[:, h, p, 0:8], in_values=s_sb[:, hp, :], imm_value=NEG),
                     reads=[P + 's_sb', P + 'top%d' % hp], writes=[P + 'tmp%d' % (hp % 4)])
                S.op('dve', lambda e: e.max(out=top[:, h, p, 8:16], in_=tmp[:, hp % 4, 0:128]), reads=[P + 'tmp%d' % (hp % 4)], writes=[P + 'top%d' % hp])
            allt = [P + 'top%d' % hp for hp in range(16)]
            S.op('dve', lambda e: e.tensor_tensor(out=cand, in0=top[:, :, 0, :].unsqueeze(3).to_broadcast([128, 8, 16, 16]),
                                                  in1=top[:, :, 1, :].unsqueeze(2).to_broadcast([128, 8, 16, 16]), op=ALU.add),
                 reads=allt, writes=[P + 'cand'])
            for h in range(8):
                S.op('dve', lambda e: e.max(out=c16[:, h, 0:8], in_=cand[:, h, :, :].rearrange("p a b -> p (a b)")), reads=[P + 'cand'], writes=[P + 'c16_%d' % h])
            for h in range(8):
                S.op('dve', lambda e: e.match_replace(out=tmp[:, h % 4, :], in_to_replace=c16[:, h, 0:8], in_values=cand[:, h, :, :].rearrange("p a b -> p (a b)"), imm_value=NEG),
                     reads=[P + 'cand', P + 'c16_%d' % h], writes=[P + 'tmp%d' % (h % 4)])
                S.op('dve', lambda e: e.max(out=c16[:, h, 8:16], in_=tmp[:, h % 4, :]), reads=[P + 'tmp%d' % (h % 4)], writes=[P + 'c16_%d' % h])
            allc = [P + 'c16_%d' % h for h in range(8)]
            S.op('dve', lambda e: e.tensor_tensor(out=dd[:], in0=c16[:], in1=c16[:, :, 0:1].to_broadcast([128, 8, 16]), op=ALU.subtract), reads=allc, writes=[P + 'dd'])
            S.op('act', lambda e: e.activation(out=dd[:].rearrange("p a b -> p (a b)"), in_=dd[:].rearrange("p a b -> p (a b)"), func=AF.Exp), reads=[P + 'dd'], writes=[P + 'dd'])
            S.op('dve', lambda e: e.tensor_reduce(out=zz[:], in_=dd[:], axis=AX.X, op=ALU.add), reads=[P + 'dd'], writes=[P + 'zz'])
            S.op('dve', lambda e: e.reciprocal(out=zz[:], in_=zz[:]), reads=[P + 'zz'], writes=[P + 'zz'])
            S.op('dve', lambda e: e.scalar_tensor_tensor(out=gsc[:, tt, :], in0=dd[:, :, 15], scalar=float(np.exp(-DELTA)), in1=zz[:], op0=ALU.mult, op1=ALU.mult),
                 reads=[P + 'dd', P + 'zz'], writes=[P + 'gsc'])
            S.op('dve', lambda e: e.tensor_scalar(out=ngsc[:, tt, :], in0=gsc[:, tt, :], scalar1=-1.0, scalar2=None, op0=ALU.mult), reads=[P + 'gsc'], writes=[P + 'ngsc'])
            S.op('dve', lambda e: e.tensor_copy(out=cs[:, :, 0], in_=top[:, :, 0, 0]), reads=allt, writes=[P + 'cs'])
            S.op('dve', lambda e: e.scalar_tensor_tensor(out=cs[:, :, 1], in0=c16[:, :, 15], scalar=-DELTA, in1=top[:, :, 0, 0], op0=ALU.add, op1=ALU.subtract),
                 reads=allc + allt, writes=[P + 'cs'])
            S.op('dve', lambda e: e.tensor_tensor(out=E[:, tt, :, :], in0=s_sb, in1=cs[:].rearrange("p h q -> p (h q)").unsqueeze(2).to_broadcast([128, 16, 128]), op=ALU.subtract),
                 reads=[P + 's_sb', P + 'cs'], writes=[P + 'E'])
            S.op('act', lambda e: e.activation(out=E[:, tt, :, :].rearrange("p a b -> p (a b)"), in_=E[:, tt, :, :].rearrange("p a b -> p (a b)"), func=AF.Exp),
                 reads=[P + 'E'], writes=[P + 'E'])
            Ea = E[:, tt, :, :].rearrange("p (h q) n -> p h q n", q=2)[:, :, 0, :]
            S.op('dve', lambda e: e.tensor_tensor(out=Ea, in0=Ea, in1=gsc[:, tt, :].unsqueeze(2).to_broadcast([128, 8, 128]), op=ALU.mult),
                 reads=[P + 'E', P + 'gsc'], writes=[P + 'E'])
        S.barrier()
        its = [(k, tt) for k in range(32) for tt in range(4)]
        NI = len(its)

        def st_P1(n):
            k, tt = its[n]; z = n % 2; q = k % 2
            for c in range(8):
                S.op('pe', lambda e: e.matmul(B[z][:, :], xmT[:, c, tt * 128:(tt + 1) * 128], uT[q][:, c, :], start=(c == 0), stop=(c == 7)),
                     reads=[P + 'xmT', P + 'uT%d' % q], writes=['B%d' % z])
            S.op('act', lambda e: e.activation(out=gel[z][:], in_=B[z][:, :], func=AF.Gelu), reads=['B%d' % z], writes=[P + 'gel%d' % z])

        def st_D1a(n):
            k, tt = its[n]; z = n % 2
            Ev = E[:, tt, :, :].rearrange("p (h q) n -> p h q n", q=2)
            S.op('dve', lambda e: e.tensor_tensor(out=W8[z][:], in0=Ev[:, :, 0, k * 4:(k + 1) * 4].unsqueeze(3).to_broadcast([128, 8, 4, 128]),
                                                  in1=Ev[:, :, 1, :].unsqueeze(2).to_broadcast([128, 8, 4, 128]), op=ALU.mult),
                 reads=[P + 'E'], writes=[P + 'W8_%d' % z])

        def st_SG(n):
            k, tt = its[n]; z = n % 2
            W8h = W8[z][:].rearrange("p h a b -> p h (a b)")
            for h in range(8):
                S.op('act', lambda e: e.activation(out=Sg[z][:, h, :], in_=W8h[:, h, :], func=AF.Sign, bias=ngsc[:, tt, h:h + 1], scale=1.0),
                     reads=[P + 'W8_%d' % z, P + 'ngsc'], writes=[P + 'Sg%d_%d' % (z, h)])

        def st_D1b(n):
            k, tt = its[n]; z = n % 2
            kS = [P + 'Sg%d_%d' % (z, h) for h in range(8)]
            Sf = Sg[z][:].rearrange("p h n -> p (h n)")
            S.op('dve', lambda e: e.scalar_tensor_tensor(out=Sf, in0=Sf, scalar=1.0, in1=W8[z][:].rearrange("p h a b -> p (h a b)"), op0=ALU.add, op1=ALU.mult),
                 reads=kS + [P + 'W8_%d' % z], writes=kS)
            S.op('dve', lambda e: e.tensor_tensor(out=Sg[z][:, 0:4, :], in0=Sg[z][:, 0:4, :], in1=Sg[z][:, 4:8, :], op=ALU.add), reads=kS, writes=kS)
            S.op('dve', lambda e: e.tensor_tensor(out=Sg[z][:, 0:2, :], in0=Sg[z][:, 0:2, :], in1=Sg[z][:, 2:4, :], op=ALU.add), reads=kS, writes=kS)
            S.op('dve', lambda e: e.tensor_tensor(out=G[:], in0=Sg[z][:, 0, :], in1=Sg[z][:, 1, :], op=ALU.add), reads=kS, writes=[P + 'G'])
            S.op('dve', lambda e: e.scalar_tensor_tensor(out=A[z][:], in0=G[:], scalar=0.5, in1=gel[z][:], op0=ALU.mult, op1=ALU.mult),
                 reads=[P + 'gel%d' % z, P + 'G'], writes=[P + 'A%d' % z])

        def st_P2(n):
            z = n % 2
            bt = 2 + z
            pb = B[bt][:].bitcast(BF16)
            for a in range(4):
                S.op('pe', lambda e: e.transpose(pb[:, a * 128:(a + 1) * 128], A[z][:, a * 128:(a + 1) * 128], C.ident[:]),
                     reads=[P + 'A%d' % z, 'ident'], writes=['B%d' % bt])
            S.op('act', lambda e: e.copy(out=AT[z][:].rearrange("p a t -> p (a t)"), in_=pb[:, 0:512]), reads=['B%d' % bt], writes=[P + 'AT%d' % z])

        def st_P3(n):
            k, tt = its[n]; z = n % 2; q = k % 2
            for hlf in range(2):
                bo = 4 + z * 2 + hlf
                for a in range(4):
                    S.op('pe', lambda e: e.matmul(B[bo][:, :], AT[z][:, a, :], vv[q][:, a, hlf * 512:(hlf + 1) * 512], start=(a == 0), stop=(a == 3)),
                         reads=[P + 'AT%d' % z, P + 'vv%d' % q], writes=['B%d' % bo])

        def st_D2(n):
            k, tt = its[n]; z = n % 2
            for hlf in range(2):
                bo = 4 + z * 2 + hlf
                if k == 0:
                    S.op('act', lambda e: e.copy(out=acc[:, tt, hlf * 512:(hlf + 1) * 512], in_=B[bo][:, :]), reads=['B%d' % bo], writes=[P + 'acc%d' % tt])
                else:
                    S.op('act', lambda e: e.copy(out=tmpo[z][:, hlf * 512:(hlf + 1) * 512], in_=B[bo][:, :]), reads=['B%d' % bo], writes=[P + 'tmpo%d' % z])
            if k > 0:
                S.dma('pool', acc[:, tt, :], tmpo[z][:], reads=[P + 'tmpo%d' % z, P + 'acc%d' % tt], writes=[P + 'acc%d' % tt], accum_op=ALU.add)

        st_P1(0)
        st_D1a(0)
        st_SG(0)
        for n in range(NI + 1):
            if n + 1 < NI:
                st_P1(n + 1)
                st_D1a(n + 1)
                st_SG(n + 1)
            if n < NI:
                st_D1b(n)
            if n >= 1:
                st_P3(n - 1)
                k_prev, tt_prev = its[n - 1]
                if tt_prev == 3 and k_prev + 2 < 32:
                    load_chunk(k_prev + 2)
            if n < NI:
                st_P2(n)
            if n >= 1:
                st_D2(n - 1)
        S.barrier()
        for tt in range(4):
            i = sti * 4 + tt
            S.dma('sp', xr[:], xm_d[i * 128:(i + 1) * 128, :], writes=[P + 'xr'])
            S.op('dve', lambda e: e.scalar_tensor_tensor(out=pre[:], in0=xr[:], scalar=DN_ALPHA, in1=acc[:, tt, :], op0=ALU.mult, op1=ALU.add),
                 reads=[P + 'xr', P + 'acc%d' % tt], writes=['pre'])
            ln_epilogue(C, pre, 'pre', lng, lnb, xf_d, xfT_d, i, 'P%d' % l, 3)
    st.close()


def phase_B(C, xf_d, xfT_d, xm1_d, xm1T_d):
    S, B, dram, nc = C.S, C.B, C.dram, C.nc
    st = ExitStack()
    sb = lambda name, shape, dt=F32: st.enter_context(nc.sbuf_tensor("b_" + name, list(shape), dt))
    C.sb_save = C.sb
    C.sb = sb
    bw_d = dram('b_w_in', [D, B_W])
    kv_d = dram('shared_w_kv', [D, 1536])
    catT_d = dram('catT1', [768, SEQ], BF16, "Internal")
    XF = sb("XF", [128, 8, SEQ], BF16)
    for c in range(8):
        S.dma('sp' if c % 2 == 0 else 'act', XF[:, c, :], xfT_d[c * 128:(c + 1) * 128, :], writes=['XF'])
    st1 = ExitStack()
    sb_outer = sb
    sb = lambda name, shape, dt=F32: st1.enter_context(nc.sbuf_tensor("b1_" + name, list(shape), dt))
    acc = sb("acc", [128, 2, SEQ])
    mixT = sb("mixT", [128, SEQ], BF16)
    KT = sb("KT", [128, SEQ], BF16)
    QT = sb("QT", [128, SEQ], BF16)
    V = sb("V", [128, 32, 128], BF16)
    Wq = sb("Wq", [128, 8, 3, 128], BF16)
    Wk = sb("Wk", [128, 8, 128], BF16)
    Wv = sb("Wv", [128, 8, 128], BF16)
    PT = [sb("PT%d" % q, [128, 2, 128], BF16) for q in range(2)]
    SC = 128 ** -0.5
    DIL = (1, 4, 16)
    cnt = [0]

    def proj_T(dst, key_dst, wsel, key_w):
        for tg in range(8):
            bk = 4 + tg % 4
            for c in range(8):
                S.op('pe', lambda e: e.matmul(B[bk][:, :], wsel(c), XF[:, c, tg * 512:(tg + 1) * 512], start=(c == 0), stop=(c == 7)),
                     reads=[key_w, 'XF'], writes=['B%d' % bk])
            S.op('act', lambda e: e.copy(out=dst[:, tg * 512:(tg + 1) * 512], in_=B[bk][:, :]), reads=['B%d' % bk], writes=[key_dst])

    for s_ in range(6):
        for c in range(8):
            S.dma('pool', Wq[:, c, :, :], bw_d[c * 128:(c + 1) * 128, 0:2304].rearrange("p (g s e) -> p g s e", g=3, s=6)[:, :, s_, :], writes=['b_Wq'])
            S.dma('pool', Wk[:, c, :], kv_d[c * 128:(c + 1) * 128, s_ * 128:(s_ + 1) * 128], writes=['b_Wk'])
            S.dma('pool', Wv[:, c, :], kv_d[c * 128:(c + 1) * 128, 768 + s_ * 128:768 + (s_ + 1) * 128], writes=['b_Wv'])
        proj_T(KT, 'b_KT', lambda c: Wk[:, c, :], 'b_Wk')
        for g in range(3):
            d = DIL[g]
            nb = 32 // d
            proj_T(QT, 'b_QT', lambda c: Wq[:, c, g, :], 'b_Wq')
            for b4 in range(8):
                bk = 4 + b4 % 4
                for u in range(4):
                    blk = b4 * 4 + u
                    r, n = blk // nb, blk % nb
                    t0 = d * 128 * n + r
                    for c in range(8):
                        S.op('pe', lambda e: e.matmul(B[bk][:, u * 128:(u + 1) * 128], XF[:, c, t0:t0 + 127 * d + 1:d], Wv[:, c, :], start=(c == 0), stop=(c == 7)),
                             reads=['XF', 'b_Wv'], writes=['B%d' % bk])
                S.op('act', lambda e: e.copy(out=V[:, b4 * 4:(b4 + 1) * 4, :].rearrange("p a b -> p (a b)"), in_=B[bk][:, :]), reads=['B%d' % bk], writes=['b_V'])
            for blk in range(32):
                r, n = blk // nb, blk % nb
                z = cnt[0] % 2
                cnt[0] += 1
                tq = d * 128 * n + r
                qs = slice(tq, tq + 127 * d + 1, d)
                kbs = [1] if n == 0 else [0, 1]
                bs, bo = z, 2 + z
                for kb in kbs:
                    tk = d * 128 * (n - 1 + kb) + r
                    S.op('pe', lambda e: e.matmul(B[bs][:, kb * 128:(kb + 1) * 128], KT[:, tk:tk + 127 * d + 1:d], QT[:, qs], start=True, stop=True),
                         reads=['b_KT', 'b_QT'], writes=['B%d' % bs])
                lo = kbs[0]
                S.op('act', lambda e: e.activation(out=PT[z][:, lo:2, :].rearrange("p a b -> p (a b)"), in_=B[bs][:, lo * 128:256], func=AF.Exp, scale=SC),
                     reads=['B%d' % bs], writes=['b_PT%d' % z])
                S.op('dve', lambda e: e.tensor_tensor(out=PT[z][:, lo:2, :], in0=PT[z][:, lo:2, :], in1=C.dmask[:, lo:2, :], op=ALU.mult),
                     reads=['b_PT%d' % z, 'dmask'], writes=['b_PT%d' % z])
                for which in range(2):
                    for kb in kbs:
                        lhs = V[:, blk - 1 + kb, :] if which == 0 else C.ones_b[:]
                        S.op('pe', lambda e: e.matmul(B[bo][:, which * 128:(which + 1) * 128], lhs, PT[z][:, kb, :], start=(kb == kbs[0]), stop=(kb == 1)),
                             reads=['b_V', 'ones_b', 'b_PT%d' % z], writes=['B%d' % bo])
                pv = B[bo][:, 0:256].rearrange("p (a b) -> p a b", a=2)
                if g == 0:
                    S.op('act', lambda e: e.copy(out=acc[:, :, qs], in_=pv), reads=['B%d' % bo], writes=['b_acc'])
                else:
                    S.op('dve', lambda e: e.tensor_tensor(out=acc[:, :, qs], in0=acc[:, :, qs], in1=pv, op=ALU.add), reads=['B%d' % bo, 'b_acc'], writes=['b_acc'])
        S.op('dve', lambda e: e.reciprocal(out=acc[:, 1, :], in_=acc[:, 1, :]), reads=['b_acc'], writes=['b_acc'])
        S.op('dve', lambda e: e.tensor_tensor(out=mixT[:], in0=acc[:, 0, :], in1=acc[:, 1, :], op=ALU.mult), reads=['b_acc'], writes=['b_mixT'])
        S.dma('sp', catT_d[s_ * 128:(s_ + 1) * 128, :], mixT[:], reads=['b_mixT'], writes=['catT1_d'])
    S.barrier()
    st1.close()
    sb = sb_outer
    Wm = sb("Wm", [128, 8, 256], BF16)
    for c in range(8):
        S.dma('pool', Wm[:, c, :], bw_d[c * 128:(c + 1) * 128, 2304:2560], writes=['b_Wm'])
    wout = load_w_bf(C, "wout1", dram('w_out1', [D, D]), D, 'wout1')
    lng = load_rep(C, "lng", dram('ln_mix_g1', [1, D]), D, 'lnp')
    lnb = load_rep(C, "lnb", dram('ln_mix_b1', [1, D]), D, 'lnp')
    kmT, vmx = mem_kv(C, dram('memT', [D, 256]) if not hasattr(C, 'memT_d') else C.memT_d, dram('w_mem_kv1', [D, 512]), '1')
    alloc_ln(C)
    C.ma_pT = sb("ma_pT", [128, 1024], BF16)
    C.ma_rd = sb("ma_rd", [128, 4])
    C.catT = sb("catT", [128, 8, 128], BF16)
    qmT = sb("qmT", [64, 4, 128], BF16)
    cat = sb("cat", [128, 1024], BF16)
    pre = sb("pre", [128, 1024])
    XR = [sb("XR%d" % j, [128, 1024]) for j in range(2)]
    for i in range(NT):
        j = i % 2
        S.dma('sp', XR[j][:], xf_d[i * 128:(i + 1) * 128, :], writes=['bXR%d' % j])
        S.dma('sp', C.catT[:, 0:6, :], catT_d[:, i * 128:(i + 1) * 128].rearrange("(c p) t -> p c t", p=128), reads=['catT1_d'], writes=['catT'])
        for h in range(4):
            for c in range(8):
                S.op('pe', lambda e: e.matmul(B[6][0:64, h * 128:(h + 1) * 128], Wm[:, c, h * 64:(h + 1) * 64], XF[:, c, i * 128:(i + 1) * 128], start=(c == 0), stop=(c == 7)),
                     reads=['b_Wm', 'XF'], writes=['B6'])
        S.op('act', lambda e: e.copy(out=qmT[:].rearrange("p h t -> p (h t)"), in_=B[6][0:64, :]), reads=['B6'], writes=['b_qmT'])
        mem_attn_tile(C, qmT, 'b_qmT', kmT, vmx, '1', cat, 'b_cat', (2, 3), 4)
        outproj_tile(C, cat, 'b_cat', wout, 'wout1', XR[j], 'bXR%d' % j, 5, (7, 0), pre, chunks=(6, 7))
        ln_epilogue(C, pre, 'pre', lng, lnb, xm1_d, xm1T_d, i, 'B', 5)
    C.sb = C.sb_save
    st.close()


_PROG_CACHE = {}


def _prep_inputs(inputs, b):
    g = lambda k: np.asarray(inputs[k], dtype=np.float32)
    m = {}
    m['x'] = np.ascontiguousarray(g('x')[b])
    m['xT'] = np.ascontiguousarray(g('x')[b].T)
    m['memT'] = np.ascontiguousarray(g('mem')[b].T)
    m['a_w_in'] = np.ascontiguousarray(g('a_w_in')[0])
    m['a_w_gate2'] = np.ascontiguousarray(g('a_w_gate2')[0])
    m['a_b_gate'] = np.ascontiguousarray(g('a_b_gate')[0][None, :])
    m['a_norm_g'] = np.ascontiguousarray(g('a_norm_g')[0][None, :])
    for l in range(2):
        m['w_mem_kv%d' % l] = np.ascontiguousarray(g('w_mem_kv')[l])
        m['w_out%d' % l] = np.ascontiguousarray(g('w_out')[l])
        for nm in ('ln_mix_g', 'ln_mix_b', 'ln_ffn_g', 'ln_ffn_b'):
            m['%s%d' % (nm, l)] = np.ascontiguousarray(g(nm)[l][None, :])
    m['b_w_in'] = np.ascontiguousarray(g('b_w_in')[0])
    m['shared_w_kv'] = np.ascontiguousarray(g('shared_w_kv'))
    for l in range(2):
        m['peer_w_q%d' % l] = np.ascontiguousarray(g('peer_w_q')[l])
        m['peer_keysT%d' % l] = np.ascontiguousarray(g('peer_sub_keys')[l].transpose(0, 2, 1))
        m['peer_uT%d' % l] = np.ascontiguousarray(g('peer_u')[l].T)
        m['peer_v%d' % l] = np.ascontiguousarray(g('peer_v')[l])
    m.update(_consts_host())
    return m


def kernel(**inputs):
    if 'nc' not in _PROG_CACHE:
        _PROG_CACHE['nc'] = build()
    nc = _PROG_CACHE['nc']
    in_maps = [_prep_inputs(inputs, b) for b in range(8)]
    res = run_bass_kernel_spmd(nc, in_maps, core_ids=list(range(8)))
    out = np.stack([np.asarray(r['out'], dtype=np.float32) for r in res.results], axis=0)
    return out
```

```python
from contextlib import ExitStack
import numpy as np
import concourse.bass as bass
import concourse.mybir as mybir
from concourse.bass_utils import run_bass_kernel_spmd

F32 = mybir.dt.float32
BF16 = mybir.dt.bfloat16
ALU = mybir.AluOpType
AF = mybir.ActivationFunctionType
AX = mybir.AxisListType

D = 1024
SEQ = 4096
NT = SEQ // 128
DN_ALPHA = 4.0 ** 0.25
LN_EPS = 1e-5
HN_EPS = 1e-6
A_W = 2576
B_W = 2560


class Sched:
    def __init__(self, nc, stack):
        self.nc = nc
        self.E = {'pe': nc.tensor, 'dve': nc.vector, 'act': nc.scalar, 'pool': nc.gpsimd, 'sp': nc.sync}
        self.sem = {e: stack.enter_context(nc.semaphore('s_' + e)) for e in self.E}
        self.cnt = {e: 0 for e in self.E}
        self.seen = {e: {} for e in self.E}
        self.NDS = 32
        self.dsem = [stack.enter_context(nc.semaphore('d%d' % i)) for i in range(self.NDS)]
        self.dcnt = [0] * self.NDS
        self.dnext = 0
        self.tiles = {}
        self.nins = 0

    def _st(self, key):
        if key not in self.tiles:
            self.tiles[key] = {'w': None, 'r': {}}
        return self.tiles[key]

    def _semobj(self, sk):
        return self.sem[sk] if isinstance(sk, str) else self.dsem[sk]

    def _wait(self, eng, sk, val):
        if self.seen[eng].get(sk, 0) >= val:
            return
        self.E[eng].wait_ge(self._semobj(sk), val)
        self.seen[eng][sk] = val
        self.nins += 1

    def _deps(self, eng, reads, writes):
        for k in reads:
            st = self._st(k)
            if st['w'] is not None:
                self._wait(eng, *st['w'])
        for k in writes:
            st = self._st(k)
            if st['w'] is not None:
                self._wait(eng, *st['w'])
            for sk, v in st['r'].items():
                self._wait(eng, sk, v)

    def _mark(self, sk, val, reads, writes):
        for k in reads:
            st = self._st(k)
            st['r'][sk] = max(st['r'].get(sk, 0), val)
        for k in writes:
            st = self._st(k)
            st['w'] = (sk, val)
            st['r'] = {}

    def op(self, eng, fn, reads=(), writes=()):
        self._deps(eng, reads, writes)
        ins = fn(self.E[eng])
        self.cnt[eng] += 1
        ins.then_inc(self.sem[eng], 1)
        self._mark(eng, self.cnt[eng], reads, writes)
        self.nins += 1
        return ins

    def dma(self, eng, out, in_, reads=(), writes=(), **kw):
        s = self.dnext
        self.dnext = (self.dnext + 1) % self.NDS
        if self.dcnt[s] > 0:
            self._wait(eng, s, self.dcnt[s])
        self._deps(eng, reads, writes)
        ins = self.E[eng].dma_start(out=out, in_=in_, **kw)
        self.dcnt[s] += 16
        ins.then_inc(self.dsem[s], 16)
        self._mark(s, self.dcnt[s], reads, writes)
        self.nins += 1
        return ins

    def barrier(self):
        for e in self.E:
            for s in range(self.NDS):
                if self.dcnt[s] > 0:
                    self._wait(e, s, self.dcnt[s])
            for o in self.E:
                if o != e and self.cnt[o] > 0:
                    self._wait(e, o, self.cnt[o])

    def finish(self, eng='sp'):
        for s in range(self.NDS):
            if self.dcnt[s] > 0:
                self._wait(eng, s, self.dcnt[s])
        for e in self.E:
            if e != eng and self.cnt[e] > 0:
                self._wait(eng, e, self.cnt[e])


class Ctx:
    pass


def _consts_host():
    idx = np.arange(128)
    same = (idx[:, None] // 64) == (idx[None, :] // 64)
    M2 = (same & (idx[:, None] <= idx[None, :])).astype(np.float32)
    U2 = (same & (idx[:, None] > idx[None, :])).astype(np.float32)
    mp = (idx[:, None] >= idx[None, :]).astype(np.float32)
    mc = (idx[:, None] <= idx[None, :]).astype(np.float32)
    return {
        'c_ident': np.eye(128, dtype=np.float32),
        'c_m2': M2, 'c_u2': U2,
        'c_dmask': np.ascontiguousarray(np.stack([mp, mc], axis=1)),
        'c_ones': np.ones((128, 128), dtype=np.float32),
    }


def build(phases=('A', 'P0', 'B', 'P1'), dbg=False, n_super=SEQ // 512):
    nc = bass.Bass("TRN2", target_bir_lowering=False)
    C = Ctx()
    C.nc = nc
    st = ExitStack()
    C.st = st
    S = Sched(nc, st)
    C.S = S

    def dram(name, shape, dt=F32, kind="ExternalInput"):
        return nc.dram_tensor(name, list(shape), dt, kind=kind).ap()
    C.dram = dram

    def sb(name, shape, dt=F32):
        return st.enter_context(nc.sbuf_tensor(name, list(shape), dt))
    C.sb = sb

    C.B = [st.enter_context(nc.psum_tensor("bank%d" % i, [128, 512], F32)) for i in range(8)]

    C.ident = sb("ident", [128, 128], BF16)
    C.m2 = sb("m2", [128, 128], F32)
    C.u2 = sb("u2", [128, 128], F32)
    C.ones_f = sb("ones_f", [128, 128], F32)
    C.ones_b = sb("ones_b", [128, 128], BF16)
    C.dmask = sb("dmask", [128, 2, 128], BF16)
    S.dma('pool', C.ident[:], dram('c_ident', [128, 128]), writes=['ident'])
    S.dma('sp', C.m2[:], dram('c_m2', [128, 128]), writes=['m2'])
    S.dma('sp', C.u2[:], dram('c_u2', [128, 128]), writes=['u2'])
    c_ones = dram('c_ones', [128, 128])
    S.dma('sp', C.ones_f[:], c_ones, writes=['ones_f'])
    S.dma('pool', C.ones_b[:], c_ones, writes=['ones_b'])
    S.dma('pool', C.dmask[:], dram('c_dmask', [128, 2, 128]), writes=['dmask'])

    ext_in = "ExternalInput"
    inter = "ExternalOutput" if dbg else "Internal"
    C.out = None
    names = {'A': 'xm0', 'P0': 'xf0', 'B': 'xm1'}
    prev = {'P0': 'xm0', 'B': 'xf0', 'P1': 'xm1'}
    C.deferred_convert = [l for l in (0, 1) if ('P%d' % l) in phases]
    if 'A' not in phases:
        for l in C.deferred_convert:
            peer_convert(C, l)
    for ph in ('A', 'P0', 'B', 'P1'):
        if ph not in phases:
            continue
        if ph in prev and not hasattr(C, prev[ph]):
            setattr(C, prev[ph], dram(prev[ph], [SEQ, D], F32, ext_in))
            setattr(C, prev[ph] + 'T', dram(prev[ph] + 'T', [D, SEQ], BF16, ext_in))
        if ph in names:
            setattr(C, names[ph], dram(names[ph], [SEQ, D], F32, inter))
            setattr(C, names[ph] + 'T', dram(names[ph] + 'T', [D, SEQ], BF16, inter))
        if ph == 'A':
            phase_A(C)
        elif ph == 'P0':
            phase_P(C, 0, C.xm0, C.xm0T, C.xf0, C.xf0T, n_super)
        elif ph == 'B':
            phase_B(C, C.xf0, C.xf0T, C.xm1, C.xm1T)
        else:
            C.out = dram('out', [SEQ, D], F32, "ExternalOutput")
            phase_P(C, 1, C.xm1, C.xm1T, C.out, None, n_super)
        S.barrier()
    S.finish('sp')
    st.close()
    return nc


def ln_epilogue(C, pre, key_pre, g_rep, b_rep, out_dram, outT_dram, i, tag, bank):
    S = C.S
    stt, mv, rs, xn, xnb, xnT = C.ln_st, C.ln_mv, C.ln_rs, C.ln_xn, C.ln_xnb, C.ln_xnT
    for hlf in range(2):
        S.op('dve', lambda e: e.bn_stats(out=stt[:, hlf, :], in_=pre[:, hlf * 512:(hlf + 1) * 512]), reads=[key_pre], writes=['ln_st'])
    S.op('dve', lambda e: e.bn_aggr(out=mv[:], in_=stt[:].rearrange("p a b -> p (a b)")), reads=['ln_st'], writes=['ln_mv'])
    S.op('act', lambda e: e.activation(out=rs[:], in_=mv[:, 1:2], func=AF.Sqrt, bias=LN_EPS, scale=1.0), reads=['ln_mv'], writes=['ln_rs'])
    S.op('dve', lambda e: e.reciprocal(out=rs[:], in_=rs[:]), reads=['ln_rs'], writes=['ln_rs'])
    S.op('dve', lambda e: e.tensor_scalar(out=xn[:], in0=pre[:], scalar1=mv[:, 0:1], scalar2=rs[:, 0:1], op0=ALU.subtract, op1=ALU.mult),
         reads=[key_pre, 'ln_mv', 'ln_rs'], writes=['ln_xn'])
    S.op('pool', lambda e: e.tensor_tensor(out=xn[:], in0=xn[:], in1=g_rep[:], op=ALU.mult), reads=['ln_xn', 'lnp'], writes=['ln_xn'])
    S.op('pool', lambda e: e.tensor_tensor(out=xn[:], in0=xn[:], in1=b_rep[:], op=ALU.add), reads=['ln_xn', 'lnp'], writes=['ln_xn'])
    S.dma('sp', out_dram[i * 128:(i + 1) * 128, :], xn[:], reads=['ln_xn'])
    if outT_dram is not None:
        S.op('act', lambda e: e.copy(out=xnb[:], in_=xn[:]), reads=['ln_xn'], writes=['ln_xnb'])
        pb = C.B[bank][:].bitcast(BF16)
        for c in range(8):
            S.op('pe', lambda e: e.transpose(pb[:, c * 128:(c + 1) * 128], xnb[:, c * 128:(c + 1) * 128], C.ident[:]),
                 reads=['ln_xnb', 'ident'], writes=['B%d' % bank])
        S.op('act', lambda e: e.copy(out=xnT[:].rearrange("p c t -> p (c t)"), in_=pb[:, :]), reads=['B%d' % bank], writes=['ln_xnT'])
        S.dma('sp', outT_dram[:, i * 128:(i + 1) * 128].rearrange("(c p) t -> p c t", p=128), xnT[:], reads=['ln_xnT'])


def alloc_ln(C):
    sb = C.sb
    C.ln_st = sb("ln_st", [128, 2, 6])
    C.ln_mv = sb("ln_mv", [128, 2])
    C.ln_rs = sb("ln_rs", [128, 1])
    C.ln_xn = sb("ln_xn", [128, 1024])
    C.ln_xnb = sb("ln_xnb", [128, 1024], BF16)
    C.ln_xnT = sb("ln_xnT", [128, 8, 128], BF16)


def load_w_bf(C, name, dram_ap, ncols, key):
    t = C.sb(name, [128, 8, ncols], BF16)
    for c in range(8):
        C.S.dma('pool', t[:, c, :], dram_ap[c * 128:(c + 1) * 128, :], writes=[key])
    return t


def load_rep(C, name, dram_ap, n, key):
    t = C.sb(name, [128, n], F32)
    C.S.dma('sp', t[:], dram_ap.partition_broadcast(128), writes=[key])
    return t


def mem_kv(C, memT_d, wmkv_d, tag):
    S, sb, B = C.S, C.sb, C.B
    memT = load_w_bf(C, "memT" + tag, memT_d, 256, 'memT' + tag)
    wm = load_w_bf(C, "wmkv" + tag, wmkv_d, 512, 'wmkv' + tag)
    kmT = sb("kmT" + tag, [64, 4, 256], BF16)
    vmx = sb("vmx" + tag, [128, 2, 4, 65], BF16)
    S.op('pool', lambda e: e.memset(vmx[:].rearrange("p a b c -> p (a b c)"), 1.0), writes=['vmx' + tag])
    for h in range(4):
        for c in range(8):
            S.op('pe', lambda e: e.matmul(B[h % 2][0:64, (h // 2) * 256:(h // 2) * 256 + 256],
                                          wm[:, c, h * 64:(h + 1) * 64], memT[:, c, :], start=(c == 0), stop=(c == 7)),
                 reads=['memT' + tag, 'wmkv' + tag], writes=['B%d' % (h % 2)])
        S.op('act', lambda e: e.copy(out=kmT[:, h, :], in_=B[h % 2][0:64, (h // 2) * 256:(h // 2) * 256 + 256]), reads=['B%d' % (h % 2)], writes=['kmT' + tag])
    for j in range(2):
        for c in range(8):
            S.op('pe', lambda e: e.matmul(B[2 + j][:, 0:256], memT[:, c, j * 128:(j + 1) * 128], wm[:, c, 256:512], start=(c == 0), stop=(c == 7)),
                 reads=['memT' + tag, 'wmkv' + tag], writes=['B%d' % (2 + j)])
        S.op('act', lambda e: e.copy(out=vmx[:, j, :, 0:64], in_=B[2 + j][:, 0:256].rearrange("p (h e) -> p h e", h=4)),
             reads=['B%d' % (2 + j)], writes=['vmx' + tag])
    return kmT, vmx


def mem_attn_tile(C, qmT, key_qmT, kmT, vmx, tag, cat, key_cat, bs, bm):
    S, B = C.S, C.B
    pT = C.ma_pT
    for h in range(4):
        bk = bs[h // 2]
        for j in range(2):
            col = ((h % 2) * 2 + j) * 128
            S.op('pe', lambda e: e.matmul(B[bk][:, col:col + 128], kmT[:, h, j * 128:(j + 1) * 128], qmT[:, h, :], start=True, stop=True),
                 reads=['kmT' + tag, key_qmT], writes=['B%d' % bk])
    for hh in range(2):
        S.op('act', lambda e: e.activation(out=pT[:, hh * 512:(hh + 1) * 512], in_=B[bs[hh]][:, :], func=AF.Exp, scale=0.125),
             reads=['B%d' % bs[hh]], writes=['ma_pT'])
    for h in range(4):
        for j in range(2):
            col = (h * 2 + j) * 128
            S.op('pe', lambda e: e.matmul(B[bm][:, h * 65:h * 65 + 65], pT[:, col:col + 128], vmx[:, j, h, :], start=(j == 0), stop=(j == 1)),
                 reads=['ma_pT', 'vmx' + tag], writes=['B%d' % bm])
    mo = B[bm][:, 0:260].rearrange("p (h e) -> p h e", h=4)
    S.op('dve', lambda e: e.reciprocal(out=C.ma_rd[:], in_=mo[:, :, 64]), reads=['B%d' % bm], writes=['ma_rd'])
    S.op('dve', lambda e: e.tensor_tensor(out=cat[:, 768:1024].rearrange("p (h e) -> p h e", h=4), in0=mo[:, :, 0:64],
                                          in1=C.ma_rd[:].unsqueeze(2).to_broadcast([128, 4, 64]), op=ALU.mult),
         reads=['B%d' % bm, 'ma_rd'], writes=[key_cat])


def outproj_tile(C, cat, key_cat, wout, key_wout, XR, key_XR, bt, by, pre, chunks=range(8)):
    S, B = C.S, C.B
    pb = B[bt][:].bitcast(BF16)
    chunks = list(chunks)
    for c in chunks:
        S.op('pe', lambda e: e.transpose(pb[:, c * 128:(c + 1) * 128], cat[:, c * 128:(c + 1) * 128], C.ident[:]),
             reads=[key_cat, 'ident'], writes=['B%d' % bt])
    c0, c1 = chunks[0], chunks[-1] + 1
    S.op('act', lambda e: e.copy(out=C.catT[:, c0:c1, :].rearrange("p c t -> p (c t)"), in_=pb[:, c0 * 128:c1 * 128]), reads=['B%d' % bt], writes=['catT'])
    for hlf in range(2):
        for c in range(8):
            S.op('pe', lambda e: e.matmul(B[by[hlf]][:, :], C.catT[:, c, :], wout[:, c, hlf * 512:(hlf + 1) * 512], start=(c == 0), stop=(c == 7)),
                 reads=['catT', key_wout], writes=['B%d' % by[hlf]])
        S.op('dve', lambda e: e.scalar_tensor_tensor(out=pre[:, hlf * 512:(hlf + 1) * 512], in0=XR[:, hlf * 512:(hlf + 1) * 512], scalar=DN_ALPHA,
                                                     in1=B[by[hlf]][:, :], op0=ALU.mult, op1=ALU.add),
             reads=[key_XR, 'B%d' % by[hlf]], writes=['pre'])


def phase_A(C):
    S, B, dram, nc = C.S, C.B, C.dram, C.nc
    st = ExitStack()
    sb = lambda name, shape, dt=F32: st.enter_context(nc.sbuf_tensor("a_" + name, list(shape), dt))
    C.sb_save = C.sb
    C.sb = sb
    x_d = dram('x', [SEQ, D]); xT_d = dram('xT', [D, SEQ])
    C.memT_d = dram('memT', [D, 256])
    W = load_w_bf(C, "awin", dram('a_w_in', [D, A_W]), A_W, 'awin')
    wout = load_w_bf(C, "wout0", dram('w_out0', [D, D]), D, 'wout0')
    wg2 = sb("wg2", [16, 384]); S.dma('sp', wg2[:], dram('a_w_gate2', [16, 384]), writes=['wg2'])
    bg = sb("bg", [1, 384]); S.dma('sp', bg[:], dram('a_b_gate', [1, 384]), writes=['bg'])
    ng = load_rep(C, "ng", dram('a_norm_g', [1, 768]), 768, 'ng')
    lng = load_rep(C, "lng", dram('ln_mix_g0', [1, D]), D, 'lnp')
    lnb = load_rep(C, "lnb", dram('ln_mix_b0', [1, D]), D, 'lnp')
    kmT, vmx = mem_kv(C, C.memT_d, dram('w_mem_kv0', [D, 512]), '0')
    alloc_ln(C)
    C.ma_pT = sb("ma_pT", [128, 1024], BF16)
    C.ma_rd = sb("ma_rd", [128, 4])
    C.catT = sb("catT", [128, 8, 128], BF16)
    XT = [sb("XT%d" % j, [128, 8, 128], BF16) for j in range(2)]
    XR = [sb("XR%d" % j, [128, 1024]) for j in range(2)]
    hgT = sb("hgT", [16, 128])
    qmT = sb("qmT", [64, 4, 128], BF16)
    t1 = sb("a_t1", [128, 384]); la = sb("a_la", [128, 384])
    expb = sb("expb", [96, 4, 128]); expnb = sb("expnb", [96, 4, 128]); expE = sb("expE", [128, 384])
    qt = sb("qt", [96, 4, 128], BF16); kt = sb("kt", [96, 4, 128], BF16)
    kend = sb("kend", [128, 384], BF16); vb = sb("vb", [128, 768], BF16)
    gate = sb("gate", [128, 768])
    attnTb = sb("attnTb", [128, 4, 128], BF16)
    St = sb("St", [96, 4, 192]); SbA = [sb("SbA%d" % q, [96, 4, 192], BF16) for q in range(2)]; SbB = sb("SbB", [96, 4, 192], BF16)
    sq = sb("sq", [128, 768]); ssq = sb("ssq", [128, 4]); to = sb("to", [128, 768])
    cat = sb("cat", [128, 1024], BF16)
    pre = sb("pre", [128, 1024])
    S.op('pool', lambda e: e.memset(St[:].rearrange("p a b -> p (a b)"), 0.0), writes=['St'])
    S.op('pool', lambda e: e.memset(SbA[0][:].rearrange("p a b -> p (a b)"), 0.0), writes=['SbA0'])
    QS = 96 ** -0.5

    def load(i):
        j = i % 2
        S.dma('pool', XT[j][:], xT_d[:, i * 128:(i + 1) * 128].rearrange("(c p) t -> p c t", p=128), writes=['XT%d' % j])
        S.dma('sp', XR[j][:], x_d[i * 128:(i + 1) * 128, :], writes=['XR%d' % j])

    load(0)
    for i in range(NT):
        j = i % 2
        if i + 1 < NT:
            load(i + 1)
        if i == 0:
            for l in getattr(C, 'deferred_convert', []):
                peer_convert(C, l)
        xt = XT[j]; kx = 'XT%d' % j
        for h in range(4):
            for c in range(8):
                S.op('pe', lambda e: e.matmul(B[0][0:96, h * 128:(h + 1) * 128], W[:, c, h * 96:(h + 1) * 96], xt[:, c, :], start=(c == 0), stop=(c == 7)),
                     reads=['awin', kx], writes=['B0'])
        for h in range(4):
            for c in range(8):
                S.op('pe', lambda e: e.matmul(B[1][0:96, h * 128:(h + 1) * 128], W[:, c, 384 + h * 96:384 + (h + 1) * 96], xt[:, c, :], start=(c == 0), stop=(c == 7)),
                     reads=['awin', kx], writes=['B1'])
        for (bk, c0, n) in ((2, 384, 384), (3, 768, 512), (4, 1280, 512), (5, 1792, 512)):
            for c in range(8):
                S.op('pe', lambda e: e.matmul(B[bk][:, 0:n], xt[:, c, :], W[:, c, c0:c0 + n], start=(c == 0), stop=(c == 7)),
                     reads=['awin', kx], writes=['B%d' % bk])
        for h in range(4):
            for c in range(8):
                S.op('pe', lambda e: e.matmul(B[6][0:64, h * 128:(h + 1) * 128], W[:, c, 2320 + h * 64:2320 + (h + 1) * 64], xt[:, c, :], start=(c == 0), stop=(c == 7)),
                     reads=['awin', kx], writes=['B6'])
        for c in range(8):
            S.op('pe', lambda e: e.matmul(B[7][0:16, 0:128], W[:, c, 2304:2320], xt[:, c, :], start=(c == 0), stop=(c == 7)),
                 reads=['awin', kx], writes=['B7'])
        S.op('act', lambda e: e.copy(out=qmT[:].rearrange("p h t -> p (h t)"), in_=B[6][0:64, :]), reads=['B6'], writes=['qmT'])
        S.op('act', lambda e: e.copy(out=hgT[:], in_=B[7][0:16, 0:128]), reads=['B7'], writes=['hgT'])
        S.op('pe', lambda e: e.matmul(B[7][:, 0:384], hgT[:], wg2[:], start=True, stop=False), reads=['hgT', 'wg2'], writes=['B7'])
        S.op('pe', lambda e: e.matmul(B[7][:, 0:384], C.ones_f[0:1, :], bg[:], start=False, stop=True), reads=['ones_f', 'bg'], writes=['B7'])
        S.op('act', lambda e: e.activation(out=t1[:], in_=B[7][:, 0:384], func=AF.Exp, scale=-1.0), reads=['B7'], writes=['a_t1'])
        S.op('act', lambda e: e.activation(out=t1[:], in_=t1[:], func=AF.Ln, bias=1.0, scale=1.0), reads=['a_t1'], writes=['a_t1'])
        S.op('act', lambda e: e.mul(out=la[:], in_=t1[:], mul=-1.0 / 16.0), reads=['a_t1'], writes=['a_la'])
        for h in range(4):
            S.op('pe', lambda e: e.matmul(B[6][0:96, h * 128:(h + 1) * 128], la[:, h * 96:(h + 1) * 96], C.m2[:], start=True, stop=True),
                 reads=['a_la', 'm2'], writes=['B6'])
        S.op('pe', lambda e: e.matmul(B[7][:, 0:384], C.u2[:], la[:], start=True, stop=True), reads=['a_la', 'u2'], writes=['B7'])
        S.op('act', lambda e: e.activation(out=expb[:].rearrange("p h t -> p (h t)"), in_=B[6][0:96, :], func=AF.Exp), reads=['B6'], writes=['expb'])
        S.op('act', lambda e: e.activation(out=expnb[:].rearrange("p h t -> p (h t)"), in_=B[6][0:96, :], func=AF.Exp, scale=-1.0), reads=['B6'], writes=['expnb'])
        S.op('act', lambda e: e.activation(out=expE[:], in_=B[7][:, 0:384], func=AF.Exp), reads=['B7'], writes=['expE'])
        S.op('dve', lambda e: e.scalar_tensor_tensor(out=qt[:].rearrange("p h t -> p (h t)"), in0=B[0][0:96, :], scalar=QS, in1=expb[:].rearrange("p h t -> p (h t)"),
                                                     op0=ALU.mult, op1=ALU.mult), reads=['B0', 'expb'], writes=['qt'])
        S.op('dve', lambda e: e.tensor_tensor(out=kt[:].rearrange("p h t -> p (h t)"), in0=B[1][0:96, :], in1=expnb[:].rearrange("p h t -> p (h t)"), op=ALU.mult),
             reads=['B1', 'expnb'], writes=['kt'])
        S.op('dve', lambda e: e.tensor_tensor(out=kend[:], in0=B[2][:, 0:384], in1=expE[:], op=ALU.mult), reads=['B2', 'expE'], writes=['kend'])
        S.op('act', lambda e: e.copy(out=vb[:, 0:512], in_=B[3][:, :]), reads=['B3'], writes=['vb'])
        S.op('act', lambda e: e.copy(out=vb[:, 512:768], in_=B[4][:, 0:256]), reads=['B4'], writes=['vb'])
        S.op('act', lambda e: e.activation(out=gate[:, 0:256], in_=B[4][:, 256:512], func=AF.Silu), reads=['B4'], writes=['gate'])
        S.op('act', lambda e: e.activation(out=gate[:, 256:768], in_=B[5][:, :], func=AF.Silu), reads=['B5'], writes=['gate'])
        S.op('pool', lambda e: e.tensor_tensor(out=gate[:], in0=gate[:], in1=ng[:], op=ALU.mult), reads=['gate', 'ng'], writes=['gate'])
        for h in range(4):
            S.op('pe', lambda e: e.matmul(B[0][:, h * 128:(h + 1) * 128], kt[:, h, :], qt[:, h, :], start=True, stop=True), reads=['kt', 'qt'], writes=['B0'])
        S.op('dve', lambda e: e.tensor_tensor(out=attnTb[:], in0=B[0][:, :].rearrange("p (h t) -> p h t", h=4),
                                              in1=C.m2[:].unsqueeze(1).to_broadcast([128, 4, 128]), op=ALU.mult), reads=['B0', 'm2'], writes=['attnTb'])
        for ch in range(2):
            for h in range(4):
                bk = 2 + ch * 2 + h // 2
                col = (h % 2) * 192
                S.op('pe', lambda e: e.matmul(B[bk][0:96, col:col + 192], kend[ch * 64:(ch + 1) * 64, h * 96:(h + 1) * 96], vb[ch * 64:(ch + 1) * 64, h * 192:(h + 1) * 192],
                                              start=True, stop=True), reads=['kend', 'vb'], writes=['B%d' % bk])
        def o_ap(h, lo, hi):
            bk = 1 if h < 2 else 6
            col = (h % 2) * 192
            return B[bk][lo:hi, col:col + 192], 'B%d' % bk
        SbS = SbA[i % 2]; kS = 'SbA%d' % (i % 2)
        SbN = SbA[(i + 1) % 2]; kN = 'SbA%d' % ((i + 1) % 2)
        for ch in range(2):
            for h in range(4):
                bk = 2 + ch * 2 + h // 2
                col = (h % 2) * 192
                S.op('dve', lambda e: e.scalar_tensor_tensor(out=St[:, h, :], in0=St[:, h, :], scalar=expb[:, h, ch * 64 + 63:ch * 64 + 64], in1=B[bk][0:96, col:col + 192],
                                                             op0=ALU.mult, op1=ALU.add), reads=['St', 'expb', 'B%d' % bk], writes=['St'])
            Sb_dst, kd = (SbB, 'SbB') if ch == 0 else (SbN, kN)
            S.op('act', lambda e: e.copy(out=Sb_dst[:].rearrange("p a b -> p (a b)"), in_=St[:].rearrange("p a b -> p (a b)")),
                 reads=['St'], writes=[kd])
        for h in range(4):
            oap, ok = o_ap(h, 0, 128)
            S.op('pe', lambda e: e.matmul(oap, attnTb[:, h, :], vb[:, h * 192:(h + 1) * 192], start=True, stop=False), reads=['attnTb', 'vb'], writes=[ok])
            oap0, _ = o_ap(h, 0, 64)
            S.op('pe', lambda e: e.matmul(oap0, qt[:, h, 0:64], SbS[:, h, :], start=False, stop=False), reads=['qt', kS], writes=[ok])
            oap1, _ = o_ap(h, 64, 128)
            S.op('pe', lambda e: e.matmul(oap1, qt[:, h, 64:128], SbB[:, h, :], start=False, stop=True), reads=['qt', 'SbB'], writes=[ok])
        for hh in range(2):
            bk = 1 if hh == 0 else 6
            S.op('act', lambda e: e.activation(out=sq[:, hh * 384:(hh + 1) * 384], in_=B[bk][:, 0:384], func=AF.Square), reads=['B%d' % bk], writes=['sq'])
        S.op('dve', lambda e: e.tensor_reduce(out=ssq[:], in_=sq[:].rearrange("p (h v) -> p h v", h=4), axis=AX.X, op=ALU.add), reads=['sq'], writes=['ssq'])
        S.op('act', lambda e: e.activation(out=ssq[:], in_=ssq[:], func=AF.Sqrt, bias=HN_EPS, scale=1.0 / 192.0), reads=['ssq'], writes=['ssq'])
        S.op('dve', lambda e: e.reciprocal(out=ssq[:], in_=ssq[:]), reads=['ssq'], writes=['ssq'])
        for hh in range(2):
            bk = 1 if hh == 0 else 6
            S.op('dve', lambda e: e.tensor_tensor(out=to[:, hh * 384:(hh + 1) * 384].rearrange("p (h v) -> p h v", h=2),
                                                  in0=B[bk][:, 0:384].rearrange("p (h v) -> p h v", h=2),
                                                  in1=ssq[:, hh * 2:hh * 2 + 2].unsqueeze(2).to_broadcast([128, 2, 192]), op=ALU.mult),
                 reads=['B%d' % bk, 'ssq'], writes=['to'])
        S.op('dve', lambda e: e.tensor_tensor(out=cat[:, 0:768], in0=to[:], in1=gate[:], op=ALU.mult), reads=['to', 'gate'], writes=['cat'])
        mem_attn_tile(C, qmT, 'qmT', kmT, vmx, '0', cat, 'cat', (2, 3), 4)
        outproj_tile(C, cat, 'cat', wout, 'wout0', XR[j], 'XR%d' % j, 5, (7, 0), pre)
        ln_epilogue(C, pre, 'pre', lng, lnb, C.xm0, C.xm0T, i, 'A', 5)
    C.sb = C.sb_save
    st.close()


def peer_convert(C, l):
    S, dram = C.S, C.dram
    uT_d = dram('peer_uT%d' % l, [D, 16384])
    v_d = dram('peer_v%d' % l, [16384, D])
    uTb = dram('peer_uTb%d' % l, [D, 16384], BF16, "Internal")
    vb = dram('peer_vb%d' % l, [16384, D], BF16, "Internal")
    for c in range(8):
        for q in range(4):
            S.dma('pool', uTb[c * 128:(c + 1) * 128, q * 4096:(q + 1) * 4096], uT_d[c * 128:(c + 1) * 128, q * 4096:(q + 1) * 4096], writes=['uTb%d' % l])
    for c in range(32):
        S.dma('pool', vb[c * 512:(c + 1) * 512, :].rearrange("(p a) d -> p a d", p=128), v_d[c * 512:(c + 1) * 512, :].rearrange("(p a) d -> p a d", p=128), writes=['vb%d' % l])
    C.peer_w = getattr(C, 'peer_w', {})
    C.peer_w[l] = (uTb, vb)


def phase_P(C, l, xm_d, xmT_d, xf_d, xfT_d, n_super=SEQ // 512):
    S, B, dram, nc = C.S, C.B, C.dram, C.nc
    st = ExitStack()
    sb = lambda name, shape, dt=F32: st.enter_context(nc.sbuf_tensor("p%d_%s" % (l, name), list(shape), dt))
    P = 'p%d_' % l
    wq = sb("wq", [128, 8, 2048], BF16)
    wq_d = dram('peer_w_q%d' % l, [D, 2048])
    for c in range(8):
        S.dma('pool', wq[:, c, :], wq_d[c * 128:(c + 1) * 128, :], writes=[P + 'wq'])
    keysT = sb("keysT", [128, 2, 128])
    kd = dram('peer_keysT%d' % l, [2, 128, 128])
    for p in range(2):
        S.dma('sp', keysT[:, p, :], kd[p], writes=[P + 'keysT'])
    if l not in getattr(C, 'peer_w', {}):
        peer_convert(C, l)
    uT_d, v_d = C.peer_w[l]
    lng = sb("lng", [128, D]); lnb = sb("lnb", [128, D])
    S.dma('sp', lng[:], dram('ln_ffn_g%d' % l, [1, D]).partition_broadcast(128), writes=['lnp'])
    S.dma('sp', lnb[:], dram('ln_ffn_b%d' % l, [1, D]).partition_broadcast(128), writes=['lnp'])
    xmT = sb("xmT", [128, 8, 512], BF16)
    acc = sb("acc", [128, 4, 1024])
    top = sb("top", [128, 8, 2, 16])
    c16 = sb("c16", [128, 8, 16])
    dd = sb("dd", [128, 8, 16])
    zz = sb("zz", [128, 8])
    cs = sb("cs", [128, 8, 2])
    E = sb("E", [128, 4, 16, 128])
    gsc = sb("gsc", [128, 4, 8])
    uT = [sb("uT%d" % q, [128, 8, 512], BF16) for q in range(2)]
    vv = [sb("vv%d" % q, [128, 4, 1024], BF16) for q in range(2)]
    gel = [sb("gel%d" % q, [128, 512]) for q in range(2)]
    W8 = [sb("W8_%d" % q, [128, 8, 4, 128]) for q in range(2)]
    w0 = W8[0][:].rearrange("p h a b -> p (h a b)")
    w1 = W8[1][:].rearrange("p h a b -> p (h a b)")
    qT2 = w0[:, 0:4096].rearrange("p (a b) -> p a b", a=16)
    s_sb = w1[:, 0:2048].rearrange("p (a b) -> p a b", a=16)
    cand = w1[:, 2048:4096].rearrange("p (h a b) -> p h a b", h=8, a=16)
    Sg = [sb("Sg%d" % q, [128, 8, 512], BF16) for q in range(2)]
    tmpo = [sb("tmpo%d" % q, [128, 1024]) for q in range(2)]
    ngsc = sb("ngsc", [128, 4, 8])
    xr = tmpo[1]
    tmp = tmpo[0][:, 0:1024].rearrange("p (a b) -> p a b", a=4)
    G = sb("G", [128, 512], BF16)
    A = [sb("A%d" % q, [128, 512], BF16) for q in range(2)]
    AT = [sb("AT%d" % q, [128, 4, 128], BF16) for q in range(2)]
    pre = tmpo[0]
    C.ln_st = sb("ln_st", [128, 2, 6]); C.ln_mv = sb("ln_mv", [128, 2]); C.ln_rs = sb("ln_rs", [128, 1])
    C.ln_xn = sb("ln_xn", [128, 1024]); C.ln_xnb = sb("ln_xnb", [128, 1024], BF16); C.ln_xnT = sb("ln_xnT", [128, 8, 128], BF16)
    DELTA = 2e-4
    NEG = -1e30
    cnt = [0]

    def load_chunk(k):
        q = k % 2
        S.dma('sp', uT[q][:], uT_d[:, k * 512:(k + 1) * 512].rearrange("(c p) e -> p c e", p=128), reads=['uTb%d' % l], writes=[P + 'uT%d' % q])
        S.dma('sp', vv[q][:], v_d[k * 512:(k + 1) * 512, :].rearrange("(a b) d -> b a d", b=128), reads=['vb%d' % l], writes=[P + 'vv%d' % q])

    for sti in range(n_super):
        t0 = sti * 512
        S.dma('sp', xmT[:], xmT_d[:, t0:t0 + 512].rearrange("(c p) t -> p c t", p=128), writes=[P + 'xmT'])
        load_chunk(0)
        load_chunk(1)
        for tt in range(4):
            if tt % 2 == 0:
                for hp in range(16):
                    bk = 4 + (hp % 4)
                    for c in range(8):
                        S.op('pe', lambda e: e.matmul(B[bk][:, 0:256], wq[:, c, hp * 128:(hp + 1) * 128], xmT[:, c, tt * 128:(tt + 2) * 128], start=(c == 0), stop=(c == 7)),
                             reads=[P + 'wq', P + 'xmT'], writes=['B%d' % bk])
                    S.op('act', lambda e: e.copy(out=qT2[:, hp, :], in_=B[bk][:, 0:256]), reads=['B%d' % bk], writes=[P + 'qT'])
            qT = qT2[:, :, (tt % 2) * 128:(tt % 2 + 1) * 128]
            for hp in range(16):
                bk = hp // 4
                col = (hp % 4) * 128
                S.op('pe', lambda e: e.matmul(B[bk][:, col:col + 128], qT[:, hp, :], keysT[:, hp % 2, :], start=True, stop=True),
                     reads=[P + 'qT', P + 'keysT'], writes=['B%d' % bk])
            for bk in range(4):
                S.op('act', lambda e: e.copy(out=s_sb[:, bk * 4:(bk + 1) * 4, :].rearrange("p a b -> p (a b)"), in_=B[bk][:, :]), reads=['B%d' % bk], writes=[P + 's_sb'])
            for hp in range(16):
                h, p = hp // 2, hp % 2
                S.op('dve', lambda e: e.max(out=top[:, h, p, 0:8], in_=s_sb[:, hp, :]), reads=[P + 's_sb'], writes=[P + 'top%d' % hp])
            for hp in range(16):
                h, p = hp // 2, hp % 2
                S.op('dve', lambda e: e.match_replace(out=tmp[:, hp % 4, 0:128], in_to_replace=top[:, h, p, 0:8], in_values=s_sb[:, hp, :], imm_value=NEG),
                     reads=[P + 's_sb', P + 'top%d' % hp], writes=[P + 'tmp%d' % (hp % 4)])
                S.op('dve', lambda e: e.max(out=top[:, h, p, 8:16], in_=tmp[:, hp % 4, 0:128]), reads=[P + 'tmp%d' % (hp % 4)], writes=[P + 'top%d' % hp])
            allt = [P + 'top%d' % hp for hp in range(16)]
            S.op('dve', lambda e: e.tensor_tensor(out=cand, in0=top[:, :, 0, :].unsqueeze(3).to_broadcast([128, 8, 16, 16]),
                                                  in1=top[:, :, 1, :].unsqueeze(2).to_broadcast([128, 8, 16, 16]), op=ALU.add),
                 reads=allt, writes=[P + 'cand'])
            for h in range(8):
                S.op('dve', lambda e: e.max(out=c16[:, h, 0:8], in_=cand[:, h, :, :].rearrange("p a b -> p (a b)")), reads=[P + 'cand'], writes=[P + 'c16_%d' % h])
            for h in range(8):
                S.op('dve', lambda e: e.match_replace(out=tmp[:, h % 4, :], in_to_replace=c16[:, h, 0:8], in_values=cand[:, h, :, :].rearrange("p a b -> p (a b)"), imm_value=NEG),
                     reads=[P + 'cand', P + 'c16_%d' % h], writes=[P + 'tmp%d' % (h % 4)])
                S.op('dve', lambda e: e.max(out=c16[:, h, 8:16], in_=tmp[:, h % 4, :]), reads=[P + 'tmp%d' % (h % 4)], writes=[P + 'c16_%d' % h])
            allc = [P + 'c16_%d' % h for h in range(8)]
            S.op('dve', lambda e: e.tensor_tensor(out=dd[:], in0=c16[:], in1=c16[:, :, 0:1].to_broadcast([128, 8, 16]), op=ALU.subtract), reads=allc, writes=[P + 'dd'])
            S.op('act', lambda e: e.activation(out=dd[:].rearrange("p a b -> p (a b)"), in_=dd[:].rearrange("p a b -> p (a b)"), func=AF.Exp), reads=[P + 'dd'], writes=[P + 'dd'])
            S.op('dve', lambda e: e.tensor_reduce(out=zz[:], in_=dd[:], axis=AX.X, op=ALU.add), reads=[P + 'dd'], writes=[P + 'zz'])
            S.op('dve', lambda e: e.reciprocal(out=zz[:], in_=zz[:]), reads=[P + 'zz'], writes=[P + 'zz'])
            S.op('dve', lambda e: e.scalar_tensor_tensor(out=gsc[:, tt, :], in0=dd[:, :, 15], scalar=float(np.exp(-DELTA)), in1=zz[:], op0=ALU.mult, op1=ALU.mult),
                 reads=[P + 'dd', P + 'zz'], writes=[P + 'gsc'])
            S.op('dve', lambda e: e.tensor_scalar(out=ngsc[:, tt, :], in0=gsc[:, tt, :], scalar1=-1.0, scalar2=None, op0=ALU.mult), reads=[P + 'gsc'], writes=[P + 'ngsc'])
            S.op('dve', lambda e: e.tensor_copy(out=cs[:, :, 0], in_=top[:, :, 0, 0]), reads=allt, writes=[P + 'cs'])
            S.op('dve', lambda e: e.scalar_tensor_tensor(out=cs[:, :, 1], in0=c16[:, :, 15], scalar=-DELTA, in1=top[:, :, 0, 0], op0=ALU.add, op1=ALU.subtract),
                 reads=allc + allt, writes=[P + 'cs'])
            S.op('dve', lambda e: e.tensor_tensor(out=E[:, tt, :, :], in0=s_sb, in1=cs[:].rearrange("p h q -> p (h q)").unsqueeze(2).to_broadcast([128, 16, 128]), op=ALU.subtract),
                 reads=[P + 's_sb', P + 'cs'], writes=[P + 'E'])
            S.op('act', lambda e: e.activation(out=E[:, tt, :, :].rearrange("p a b -> p (a b)"), in_=E[:, tt, :, :].rearrange("p a b -> p (a b)"), func=AF.Exp),
                 reads=[P + 'E'], writes=[P + 'E'])
            Ea = E[:, tt, :, :].rearrange("p (h q) n -> p h q n", q=2)[:, :, 0, :]
            S.op('dve', lambda e: e.tensor_tensor(out=Ea, in0=Ea, in1=gsc[:, tt, :].unsqueeze(2).to_broadcast([128, 8, 128]), op=ALU.mult),
                 reads=[P + 'E', P + 'gsc'], writes=[P + 'E'])
        S.barrier()
        its = [(k, tt) for k in range(32) for tt in range(4)]
        NI = len(its)

        def st_P1(n):
            k, tt = its[n]; z = n % 2; q = k % 2
            for c in range(8):
                S.op('pe', lambda e: e.matmul(B[z][:, :], xmT[:, c, tt * 128:(tt + 1) * 128], uT[q][:, c, :], start=(c == 0), stop=(c == 7)),
                     reads=[P + 'xmT', P + 'uT%d' % q], writes=['B%d' % z])
            S.op('act', lambda e: e.activation(out=gel[z][:], in_=B[z][:, :], func=AF.Gelu), reads=['B%d' % z], writes=[P + 'gel%d' % z])

        def st_D1a(n):
            k, tt = its[n]; z = n % 2
            Ev = E[:, tt, :, :].rearrange("p (h q) n -> p h q n", q=2)
            S.op('dve', lambda e: e.tensor_tensor(out=W8[z][:], in0=Ev[:, :, 0, k * 4:(k + 1) * 4].unsqueeze(3).to_broadcast([128, 8, 4, 128]),
                                                  in1=Ev[:, :, 1, :].unsqueeze(2).to_broadcast([128, 8, 4, 128]), op=ALU.mult),
                 reads=[P + 'E'], writes=[P + 'W8_%d' % z])

        def st_SG(n):
            k, tt = its[n]; z = n % 2
            W8h = W8[z][:].rearrange("p h a b -> p h (a b)")
            for h in range(8):
                S.op('act', lambda e: e.activation(out=Sg[z][:, h, :], in_=W8h[:, h, :], func=AF.Sign, bias=ngsc[:, tt, h:h + 1], scale=1.0),
                     reads=[P + 'W8_%d' % z, P + 'ngsc'], writes=[P + 'Sg%d_%d' % (z, h)])

        def st_D1b(n):
            k, tt = its[n]; z = n % 2
            kS = [P + 'Sg%d_%d' % (z, h) for h in range(8)]
            Sf = Sg[z][:].rearrange("p h n -> p (h n)")
            S.op('dve', lambda e: e.scalar_tensor_tensor(out=Sf, in0=Sf, scalar=1.0, in1=W8[z][:].rearrange("p h a b -> p (h a b)"), op0=ALU.add, op1=ALU.mult),
                 reads=kS + [P + 'W8_%d' % z], writes=kS)
            S.op('dve', lambda e: e.tensor_tensor(out=Sg[z][:, 0:4, :], in0=Sg[z][:, 0:4, :], in1=Sg[z][:, 4:8, :], op=ALU.add), reads=kS, writes=kS)
            S.op('dve', lambda e: e.tensor_tensor(out=Sg[z][:, 0:2, :], in0=Sg[z][:, 0:2, :], in1=Sg[z][:, 2:4, :], op=ALU.add), reads=kS, writes=kS)
            S.op('dve', lambda e: e.tensor_tensor(out=G[:], in0=Sg[z][:, 0, :], in1=Sg[z][:, 1, :], op=ALU.add), reads=kS, writes=[P + 'G'])
            S.op('dve', lambda e: e.scalar_tensor_tensor(out=A[z][:], in0=G[:], scalar=0.5, in1=gel[z][:], op0=ALU.mult, op1=ALU.mult),
                 reads=[P + 'gel%d' % z, P + 'G'], writes=[P + 'A%d' % z])

        def st_P2(n):
            z = n % 2
            bt = 2 + z
            pb = B[bt][:].bitcast(BF16)
            for a in range(4):
                S.op('pe', lambda e: e.transpose(pb[:, a * 128:(a + 1) * 128], A[z][:, a * 128:(a + 1) * 128], C.ident[:]),
                     reads=[P + 'A%d' % z, 'ident'], writes=['B%d' % bt])
            S.op('act', lambda e: e.copy(out=AT[z][:].rearrange("p a t -> p (a t)"), in_=pb[:, 0:512]), reads=['B%d' % bt], writes=[P + 'AT%d' % z])

        def st_P3(n):
            k, tt = its[n]; z = n % 2; q = k % 2
            for hlf in range(2):
                bo = 4 + z * 2 + hlf
                for a in range(4):
                    S.op('pe', lambda e: e.matmul(B[bo][:, :], AT[z][:, a, :], vv[q][:, a, hlf * 512:(hlf + 1) * 512], start=(a == 0), stop=(a == 3)),
                         reads=[P + 'AT%d' % z, P + 'vv%d' % q], writes=['B%d' % bo])

        def st_D2(n):
            k, tt = its[n]; z = n % 2
            for hlf in range(2):
                bo = 4 + z * 2 + hlf
                if k == 0:
                    S.op('act', lambda e: e.copy(out=acc[:, tt, hlf * 512:(hlf + 1) * 512], in_=B[bo][:, :]), reads=['B%d' % bo], writes=[P + 'acc%d' % tt])
                else:
                    S.op('act', lambda e: e.copy(out=tmpo[z][:, hlf * 512:(hlf + 1) * 512], in_=B[bo][:, :]), reads=['B%d' % bo], writes=[P + 'tmpo%d' % z])
            if k > 0:
                S.dma('pool', acc[:, tt, :], tmpo[z][:], reads=[P + 'tmpo%d' % z, P + 'acc%d' % tt], writes=[P + 'acc%d' % tt], accum_op=ALU.add)

        st_P1(0)
        st_D1a(0)
        st_SG(0)
        for n in range(NI + 1):
            if n + 1 < NI:
                st_P1(n + 1)
                st_D1a(n + 1)
                st_SG(n + 1)
            if n < NI:
                st_D1b(n)
            if n >= 1:
                st_P3(n - 1)
                k_prev, tt_prev = its[n - 1]
                if tt_prev == 3 and k_prev + 2 < 32:
                    load_chunk(k_prev + 2)
            if n < NI:
                st_P2(n)
            if n >= 1:
                st_D2(n - 1)
        S.barrier()
        for tt in range(4):
            i = sti * 4 + tt
            S.dma('sp', xr[:], xm_d[i * 128:(i + 1) * 128, :], writes=[P + 'xr'])
            S.op('dve', lambda e: e.scalar_tensor_tensor(out=pre[:], in0=xr[:], scalar=DN_ALPHA, in1=acc[:, tt, :], op0=ALU.mult, op1=ALU.add),
                 reads=[P + 'xr', P + 'acc%d' % tt], writes=['pre'])
            ln_epilogue(C, pre, 'pre', lng, lnb, xf_d, xfT_d, i, 'P%d' % l, 3)
    st.close()


def phase_B(C, xf_d, xfT_d, xm1_d, xm1T_d):
    S, B, dram, nc = C.S, C.B, C.dram, C.nc
    st = ExitStack()
    sb = lambda name, shape, dt=F32: st.enter_context(nc.sbuf_tensor("b_" + name, list(shape), dt))
    C.sb_save = C.sb
    C.sb = sb
    bw_d = dram('b_w_in', [D, B_W])
    kv_d = dram('shared_w_kv', [D, 1536])
    catT_d = dram('catT1', [768, SEQ], BF16, "Internal")
    XF = sb("XF", [128, 8, SEQ], BF16)
    for c in range(8):
        S.dma('sp' if c % 2 == 0 else 'act', XF[:, c, :], xfT_d[c * 128:(c + 1) * 128, :], writes=['XF'])
    st1 = ExitStack()
    sb_outer = sb
    sb = lambda name, shape, dt=F32: st1.enter_context(nc.sbuf_tensor("b1_" + name, list(shape), dt))
    acc = sb("acc", [128, 2, SEQ])
    mixT = sb("mixT", [128, SEQ], BF16)
    KT = sb("KT", [128, SEQ], BF16)
    QT = [sb("QT%d" % q, [128, SEQ], BF16) for q in range(2)]
    V = [sb("V%d" % q, [128, 32, 128], BF16) for q in range(2)]
    Wq = sb("Wq", [128, 8, 3, 128], BF16)
    Wk = sb("Wk", [128, 8, 128], BF16)
    Wva = sb("Wva", [128, 8, 768], BF16)
    vsb = [sb("vsb%d" % q, [128, 768], BF16) for q in range(2)]
    PT = [sb("PT%d" % q, [128, 2, 128], BF16) for q in range(2)]
    SC = 128 ** -0.5
    DIL = (1, 4, 16)
    Vd = dram('b_Vd', [SEQ, 768], BF16, "Internal")

    for c in range(8):
        S.dma('pool', Wva[:, c, :], kv_d[c * 128:(c + 1) * 128, 768:1536], writes=['b_Wva'])
    for i in range(NT):
        z = i % 2
        for (bk, c0, n) in ((4 + 2 * z, 0, 512), (5 + 2 * z, 512, 256)):
            for c in range(8):
                S.op('pe', lambda e: e.matmul(B[bk][:, 0:n], XF[:, c, i * 128:(i + 1) * 128], Wva[:, c, c0:c0 + n], start=(c == 0), stop=(c == 7)),
                     reads=['XF', 'b_Wva'], writes=['B%d' % bk])
            S.op('act', lambda e: e.copy(out=vsb[z][:, c0:c0 + n], in_=B[bk][:, 0:n]), reads=['B%d' % bk], writes=['b_vsb%d' % z])
        S.dma('sp', Vd[i * 128:(i + 1) * 128, :], vsb[z][:], reads=['b_vsb%d' % z], writes=['b_Vd'])

    def proj_T(dst, key_dst, wsel, key_w):
        for tg in range(8):
            bk = 4 + tg % 4
            for c in range(8):
                S.op('pe', lambda e: e.matmul(B[bk][:, :], wsel(c), XF[:, c, tg * 512:(tg + 1) * 512], start=(c == 0), stop=(c == 7)),
                     reads=[key_w, 'XF'], writes=['B%d' % bk])
            S.op('act', lambda e: e.copy(out=dst[:, tg * 512:(tg + 1) * 512], in_=B[bk][:, :]), reads=['B%d' % bk], writes=[key_dst])

    def load_V(s_, g, q):
        d = DIL[g]
        nb = 32 // d
        src = Vd.rearrange("(n p r) f -> p r n f", p=128, r=d)[:, :, :, s_ * 128:(s_ + 1) * 128]
        S.dma('act', V[q][:].rearrange("p (r n) e -> p r n e", r=d), src, reads=['b_Vd'], writes=['b_V%d' % q])

    sg = [(s_, g) for s_ in range(6) for g in range(3)]
    load_V(0, 0, 0)
    for idx, (s_, g) in enumerate(sg):
        vq = idx % 2
        if idx + 1 < len(sg):
            load_V(sg[idx + 1][0], sg[idx + 1][1], (idx + 1) % 2)
        if g == 0:
            for c in range(8):
                S.dma('pool', Wq[:, c, :, :], bw_d[c * 128:(c + 1) * 128, 0:2304].rearrange("p (g s e) -> p g s e", g=3, s=6)[:, :, s_, :], writes=['b_Wq'])
                S.dma('pool', Wk[:, c, :], kv_d[c * 128:(c + 1) * 128, s_ * 128:(s_ + 1) * 128], writes=['b_Wk'])
            proj_T(KT, 'b_KT', lambda c: Wk[:, c, :], 'b_Wk')
        d = DIL[g]
        nb = 32 // d
        QTg = QT[idx % 2]
        kQ = 'b_QT%d' % (idx % 2)
        proj_T(QTg, kQ, lambda c: Wq[:, c, g, :], 'b_Wq')
        Vg = V[vq]
        kV = 'b_V%d' % vq

        def blk_info(blk):
            r, n = blk // nb, blk % nb
            tq = d * 128 * n + r
            return r, n, slice(tq, tq + 127 * d + 1, d), ([1] if n == 0 else [0, 1])

        def stg1(blk):
            r, n, qs, kbs = blk_info(blk)
            z = blk % 2
            for kb in kbs:
                tk = d * 128 * (n - 1 + kb) + r
                S.op('pe', lambda e: e.matmul(B[z][:, kb * 128:(kb + 1) * 128], KT[:, tk:tk + 127 * d + 1:d], QTg[:, qs], start=True, stop=True),
                     reads=['b_KT', kQ], writes=['B%d' % z])
            lo = kbs[0]
            S.op('act', lambda e: e.activation(out=PT[z][:, lo:2, :].rearrange("p a b -> p (a b)"), in_=B[z][:, lo * 128:256], func=AF.Exp, scale=SC),
                 reads=['B%d' % z], writes=['b_PT%d' % z])
            S.op('dve', lambda e: e.tensor_tensor(out=PT[z][:, lo:2, :], in0=PT[z][:, lo:2, :], in1=C.dmask[:, lo:2, :], op=ALU.mult),
                 reads=['b_PT%d' % z, 'dmask'], writes=['b_PT%d' % z])

        def stg2(blk):
            r, n, qs, kbs = blk_info(blk)
            z = blk % 2
            bo = 2 + z
            for which in range(2):
                for kb in kbs:
                    lhs = Vg[:, blk - 1 + kb, :] if which == 0 else C.ones_b[:]
                    S.op('pe', lambda e: e.matmul(B[bo][:, which * 128:(which + 1) * 128], lhs, PT[z][:, kb, :], start=(kb == kbs[0]), stop=(kb == 1)),
                         reads=[kV, 'ones_b', 'b_PT%d' % z], writes=['B%d' % bo])
            pv = B[bo][:, 0:256].rearrange("p (a b) -> p a b", a=2)
            if g == 0:
                S.op('act', lambda e: e.copy(out=acc[:, :, qs], in_=pv), reads=['B%d' % bo], writes=['b_acc'])
            else:
                S.op('dve', lambda e: e.tensor_tensor(out=acc[:, :, qs], in0=acc[:, :, qs], in1=pv, op=ALU.add), reads=['B%d' % bo, 'b_acc'], writes=['b_acc'])

        stg1(0)
        for blk in range(32):
            if blk + 1 < 32:
                stg1(blk + 1)
            stg2(blk)
        if g == 2:
            S.op('dve', lambda e: e.reciprocal(out=acc[:, 1, :], in_=acc[:, 1, :]), reads=['b_acc'], writes=['b_acc'])
            S.op('dve', lambda e: e.tensor_tensor(out=mixT[:], in0=acc[:, 0, :], in1=acc[:, 1, :], op=ALU.mult), reads=['b_acc'], writes=['b_mixT'])
            S.dma('sp', catT_d[s_ * 128:(s_ + 1) * 128, :], mixT[:], reads=['b_mixT'], writes=['catT1_d'])
    S.barrier()
    st1.close()
    sb = sb_outer
    Wm = sb("Wm", [128, 8, 256], BF16)
    for c in range(8):
        S.dma('pool', Wm[:, c, :], bw_d[c * 128:(c + 1) * 128, 2304:2560], writes=['b_Wm'])
    wout = load_w_bf(C, "wout1", dram('w_out1', [D, D]), D, 'wout1')
    lng = load_rep(C, "lng", dram('ln_mix_g1', [1, D]), D, 'lnp')
    lnb = load_rep(C, "lnb", dram('ln_mix_b1', [1, D]), D, 'lnp')
    kmT, vmx = mem_kv(C, dram('memT', [D, 256]) if not hasattr(C, 'memT_d') else C.memT_d, dram('w_mem_kv1', [D, 512]), '1')
    alloc_ln(C)
    C.ma_pT = sb("ma_pT", [128, 1024], BF16)
    C.ma_rd = sb("ma_rd", [128, 4])
    C.catT = sb("catT", [128, 8, 128], BF16)
    qmT = sb("qmT", [64, 4, 128], BF16)
    cat = sb("cat", [128, 1024], BF16)
    pre = sb("pre", [128, 1024])
    XR = [sb("XR%d" % j, [128, 1024]) for j in range(2)]
    for i in range(NT):
        j = i % 2
        S.dma('sp', XR[j][:], xf_d[i * 128:(i + 1) * 128, :], writes=['bXR%d' % j])
        S.dma('sp', C.catT[:, 0:6, :], catT_d[:, i * 128:(i + 1) * 128].rearrange("(c p) t -> p c t", p=128), reads=['catT1_d'], writes=['catT'])
        for h in range(4):
            for c in range(8):
                S.op('pe', lambda e: e.matmul(B[6][0:64, h * 128:(h + 1) * 128], Wm[:, c, h * 64:(h + 1) * 64], XF[:, c, i * 128:(i + 1) * 128], start=(c == 0), stop=(c == 7)),
                     reads=['b_Wm', 'XF'], writes=['B6'])
        S.op('act', lambda e: e.copy(out=qmT[:].rearrange("p h t -> p (h t)"), in_=B[6][0:64, :]), reads=['B6'], writes=['b_qmT'])
        mem_attn_tile(C, qmT, 'b_qmT', kmT, vmx, '1', cat, 'b_cat', (2, 3), 4)
        outproj_tile(C, cat, 'b_cat', wout, 'wout1', XR[j], 'bXR%d' % j, 5, (7, 0), pre, chunks=(6, 7))
        ln_epilogue(C, pre, 'pre', lng, lnb, xm1_d, xm1T_d, i, 'B', 5)
    C.sb = C.sb_save
    st.close()


_PROG_CACHE = {}


def _prep_inputs(inputs, b):
    g = lambda k: np.asarray(inputs[k], dtype=np.float32)
    m = {}
    m['x'] = np.ascontiguousarray(g('x')[b])
    m['xT'] = np.ascontiguousarray(g('x')[b].T)
    m['memT'] = np.ascontiguousarray(g('mem')[b].T)
    m['a_w_in'] = np.ascontiguousarray(g('a_w_in')[0])
    m['a_w_gate2'] = np.ascontiguousarray(g('a_w_gate2')[0])
    m['a_b_gate'] = np.ascontiguousarray(g('a_b_gate')[0][None, :])
    m['a_norm_g'] = np.ascontiguousarray(g('a_norm_g')[0][None, :])
    for l in range(2):
        m['w_mem_kv%d' % l] = np.ascontiguousarray(g('w_mem_kv')[l])
        m['w_out%d' % l] = np.ascontiguousarray(g('w_out')[l])
        for nm in ('ln_mix_g', 'ln_mix_b', 'ln_ffn_g', 'ln_ffn_b'):
            m['%s%d' % (nm, l)] = np.ascontiguousarray(g(nm)[l][None, :])
    m['b_w_in'] = np.ascontiguousarray(g('b_w_in')[0])
    m['shared_w_kv'] = np.ascontiguousarray(g('shared_w_kv'))
    for l in range(2):
        m['peer_w_q%d' % l] = np.ascontiguousarray(g('peer_w_q')[l])
        m['peer_keysT%d' % l] = np.ascontiguousarray(g('peer_sub_keys')[l].transpose(0, 2, 1))
        m['peer_uT%d' % l] = np.ascontiguousarray(g('peer_u')[l].T)
        m['peer_v%d' % l] = np.ascontiguousarray(g('peer_v')[l])
    m.update(_consts_host())
    return m


def kernel(**inputs):
    if 'nc' not in _PROG_CACHE:
        _PROG_CACHE['nc'] = build()
    nc = _PROG_CACHE['nc']
    in_maps = [_prep_inputs(inputs, b) for b in range(8)]
    res = run_bass_kernel_spmd(nc, in_maps, core_ids=list(range(8)))
    out = np.stack([np.asarray(r['out'], dtype=np.float32) for r in res.results], axis=0)
    return out
```

```python
from contextlib import ExitStack
import numpy as np
import concourse.bass as bass
import concourse.mybir as mybir
from concourse.bass_utils import run_bass_kernel_spmd

F32 = mybir.dt.float32
BF16 = mybir.dt.bfloat16
ALU = mybir.AluOpType
AF = mybir.ActivationFunctionType
AX = mybir.AxisListType

D = 1024
SEQ = 4096
NT = SEQ // 128
DN_ALPHA = 4.0 ** 0.25
LN_EPS = 1e-5
HN_EPS = 1e-6
A_W = 2576
B_W = 2560


class Sched:
    def __init__(self, nc, stack):
        self.nc = nc
        self.E = {'pe': nc.tensor, 'dve': nc.vector, 'act': nc.scalar, 'pool': nc.gpsimd, 'sp': nc.sync}
        self.sem = {e: stack.enter_context(nc.semaphore('s_' + e)) for e in self.E}
        self.cnt = {e: 0 for e in self.E}
        self.seen = {e: {} for e in self.E}
        self.NDS = 32
        self.dsem = [stack.enter_context(nc.semaphore('d%d' % i)) for i in range(self.NDS)]
        self.dcnt = [0] * self.NDS
        self.dnext = 0
        self.tiles = {}
        self.nins = 0

    def _st(self, key):
        if key not in self.tiles:
            self.tiles[key] = {'w': None, 'r': {}}
        return self.tiles[key]

    def _semobj(self, sk):
        return self.sem[sk] if isinstance(sk, str) else self.dsem[sk]

    def _wait(self, eng, sk, val):
        if self.seen[eng].get(sk, 0) >= val:
            return
        self.E[eng].wait_ge(self._semobj(sk), val)
        self.seen[eng][sk] = val
        self.nins += 1

    def _deps(self, eng, reads, writes):
        for k in reads:
            st = self._st(k)
            if st['w'] is not None:
                self._wait(eng, *st['w'])
        for k in writes:
            st = self._st(k)
            if st['w'] is not None:
                self._wait(eng, *st['w'])
            for sk, v in st['r'].items():
                self._wait(eng, sk, v)

    def _mark(self, sk, val, reads, writes):
        for k in reads:
            st = self._st(k)
            st['r'][sk] = max(st['r'].get(sk, 0), val)
        for k in writes:
            st = self._st(k)
            st['w'] = (sk, val)
            st['r'] = {}

    def op(self, eng, fn, reads=(), writes=()):
        self._deps(eng, reads, writes)
        ins = fn(self.E[eng])
        self.cnt[eng] += 1
        ins.then_inc(self.sem[eng], 1)
        self._mark(eng, self.cnt[eng], reads, writes)
        self.nins += 1
        return ins

    def dma(self, eng, out, in_, reads=(), writes=(), **kw):
        s = self.dnext
        self.dnext = (self.dnext + 1) % self.NDS
        if self.dcnt[s] > 0:
            self._wait(eng, s, self.dcnt[s])
        self._deps(eng, reads, writes)
        ins = self.E[eng].dma_start(out=out, in_=in_, **kw)
        self.dcnt[s] += 16
        ins.then_inc(self.dsem[s], 16)
        self._mark(s, self.dcnt[s], reads, writes)
        self.nins += 1
        return ins

    def barrier(self):
        for e in self.E:
            for s in range(self.NDS):
                if self.dcnt[s] > 0:
                    self._wait(e, s, self.dcnt[s])
            for o in self.E:
                if o != e and self.cnt[o] > 0:
                    self._wait(e, o, self.cnt[o])

    def finish(self, eng='sp'):
        for s in range(self.NDS):
            if self.dcnt[s] > 0:
                self._wait(eng, s, self.dcnt[s])
        for e in self.E:
            if e != eng and self.cnt[e] > 0:
                self._wait(eng, e, self.cnt[e])


class Ctx:
    pass


def _consts_host():
    idx = np.arange(128)
    same = (idx[:, None] // 64) == (idx[None, :] // 64)
    M2 = (same & (idx[:, None] <= idx[None, :])).astype(np.float32)
    U2 = (same & (idx[:, None] > idx[None, :])).astype(np.float32)
    mp = (idx[:, None] >= idx[None, :]).astype(np.float32)
    mc = (idx[:, None] <= idx[None, :]).astype(np.float32)
    return {
        'c_ident': np.eye(128, dtype=np.float32),
        'c_m2': M2, 'c_u2': U2,
        'c_dmask': np.ascontiguousarray(np.stack([mp, mc], axis=1)),
        'c_ones': np.ones((128, 128), dtype=np.float32),
    }


def build(phases=('A', 'P0', 'B', 'P1'), dbg=False, n_super=SEQ // 512):
    nc = bass.Bass("TRN2", target_bir_lowering=False)
    C = Ctx()
    C.nc = nc
    st = ExitStack()
    C.st = st
    S = Sched(nc, st)
    C.S = S

    def dram(name, shape, dt=F32, kind="ExternalInput"):
        return nc.dram_tensor(name, list(shape), dt, kind=kind).ap()
    C.dram = dram

    def sb(name, shape, dt=F32):
        return st.enter_context(nc.sbuf_tensor(name, list(shape), dt))
    C.sb = sb

    C.B = [st.enter_context(nc.psum_tensor("bank%d" % i, [128, 512], F32)) for i in range(8)]

    C.ident = sb("ident", [128, 128], BF16)
    C.m2 = sb("m2", [128, 128], F32)
    C.u2 = sb("u2", [128, 128], F32)
    C.ones_f = sb("ones_f", [128, 128], F32)
    C.ones_b = sb("ones_b", [128, 128], BF16)
    C.dmask = sb("dmask", [128, 2, 128], BF16)
    S.dma('pool', C.ident[:], dram('c_ident', [128, 128]), writes=['ident'])
    S.dma('sp', C.m2[:], dram('c_m2', [128, 128]), writes=['m2'])
    S.dma('sp', C.u2[:], dram('c_u2', [128, 128]), writes=['u2'])
    c_ones = dram('c_ones', [128, 128])
    S.dma('sp', C.ones_f[:], c_ones, writes=['ones_f'])
    S.dma('pool', C.ones_b[:], c_ones, writes=['ones_b'])
    S.dma('pool', C.dmask[:], dram('c_dmask', [128, 2, 128]), writes=['dmask'])

    ext_in = "ExternalInput"
    inter = "ExternalOutput" if dbg else "Internal"
    C.out = None
    names = {'A': 'xm0', 'P0': 'xf0', 'B': 'xm1'}
    prev = {'P0': 'xm0', 'B': 'xf0', 'P1': 'xm1'}
    for l in (0, 1):
        if ('P%d' % l) in phases:
            peer_convert(C, l)
    for ph in ('A', 'P0', 'B', 'P1'):
        if ph not in phases:
            continue
        if ph in prev and not hasattr(C, prev[ph]):
            setattr(C, prev[ph], dram(prev[ph], [SEQ, D], F32, ext_in))
            setattr(C, prev[ph] + 'T', dram(prev[ph] + 'T', [D, SEQ], BF16, ext_in))
        if ph in names:
            setattr(C, names[ph], dram(names[ph], [SEQ, D], F32, inter))
            setattr(C, names[ph] + 'T', dram(names[ph] + 'T', [D, SEQ], BF16, inter))
        if ph == 'A':
            phase_A(C)
        elif ph == 'P0':
            phase_P(C, 0, C.xm0, C.xm0T, C.xf0, C.xf0T, n_super)
        elif ph == 'B':
            phase_B(C, C.xf0, C.xf0T, C.xm1, C.xm1T)
        else:
            C.out = dram('out', [SEQ, D], F32, "ExternalOutput")
            phase_P(C, 1, C.xm1, C.xm1T, C.out, None, n_super)
        S.barrier()
    S.finish('sp')
    st.close()
    return nc


def ln_epilogue(C, pre, key_pre, g_rep, b_rep, out_dram, outT_dram, i, tag, bank):
    S = C.S
    stt, mv, rs, xn, xnb, xnT = C.ln_st, C.ln_mv, C.ln_rs, C.ln_xn, C.ln_xnb, C.ln_xnT
    for hlf in range(2):
        S.op('dve', lambda e: e.bn_stats(out=stt[:, hlf, :], in_=pre[:, hlf * 512:(hlf + 1) * 512]), reads=[key_pre], writes=['ln_st'])
    S.op('dve', lambda e: e.bn_aggr(out=mv[:], in_=stt[:].rearrange("p a b -> p (a b)")), reads=['ln_st'], writes=['ln_mv'])
    S.op('act', lambda e: e.activation(out=rs[:], in_=mv[:, 1:2], func=AF.Sqrt, bias=LN_EPS, scale=1.0), reads=['ln_mv'], writes=['ln_rs'])
    S.op('dve', lambda e: e.reciprocal(out=rs[:], in_=rs[:]), reads=['ln_rs'], writes=['ln_rs'])
    S.op('dve', lambda e: e.tensor_scalar(out=xn[:], in0=pre[:], scalar1=mv[:, 0:1], scalar2=rs[:, 0:1], op0=ALU.subtract, op1=ALU.mult),
         reads=[key_pre, 'ln_mv', 'ln_rs'], writes=['ln_xn'])
    S.op('pool', lambda e: e.tensor_tensor(out=xn[:], in0=xn[:], in1=g_rep[:], op=ALU.mult), reads=['ln_xn', 'lnp'], writes=['ln_xn'])
    S.op('pool', lambda e: e.tensor_tensor(out=xn[:], in0=xn[:], in1=b_rep[:], op=ALU.add), reads=['ln_xn', 'lnp'], writes=['ln_xn'])
    S.dma('sp', out_dram[i * 128:(i + 1) * 128, :], xn[:], reads=['ln_xn'])
    if outT_dram is not None:
        S.op('act', lambda e: e.copy(out=xnb[:], in_=xn[:]), reads=['ln_xn'], writes=['ln_xnb'])
        pb = C.B[bank][:].bitcast(BF16)
        for c in range(8):
            S.op('pe', lambda e: e.transpose(pb[:, c * 128:(c + 1) * 128], xnb[:, c * 128:(c + 1) * 128], C.ident[:]),
                 reads=['ln_xnb', 'ident'], writes=['B%d' % bank])
        S.op('act', lambda e: e.copy(out=xnT[:].rearrange("p c t -> p (c t)"), in_=pb[:, :]), reads=['B%d' % bank], writes=['ln_xnT'])
        S.dma('sp', outT_dram[:, i * 128:(i + 1) * 128].rearrange("(c p) t -> p c t", p=128), xnT[:], reads=['ln_xnT'])


def alloc_ln(C):
    sb = C.sb
    C.ln_st = sb("ln_st", [128, 2, 6])
    C.ln_mv = sb("ln_mv", [128, 2])
    C.ln_rs = sb("ln_rs", [128, 1])
    C.ln_xn = sb("ln_xn", [128, 1024])
    C.ln_xnb = sb("ln_xnb", [128, 1024], BF16)
    C.ln_xnT = sb("ln_xnT", [128, 8, 128], BF16)


def load_w_bf(C, name, dram_ap, ncols, key):
    t = C.sb(name, [128, 8, ncols], BF16)
    for c in range(8):
        C.S.dma('pool', t[:, c, :], dram_ap[c * 128:(c + 1) * 128, :], writes=[key])
    return t


def load_rep(C, name, dram_ap, n, key):
    t = C.sb(name, [128, n], F32)
    C.S.dma('sp', t[:], dram_ap.partition_broadcast(128), writes=[key])
    return t


def mem_kv(C, memT_d, wmkv_d, tag):
    S, sb, B = C.S, C.sb, C.B
    memT = load_w_bf(C, "memT" + tag, memT_d, 256, 'memT' + tag)
    wm = load_w_bf(C, "wmkv" + tag, wmkv_d, 512, 'wmkv' + tag)
    kmT = sb("kmT" + tag, [64, 4, 256], BF16)
    vmx = sb("vmx" + tag, [128, 2, 4, 65], BF16)
    S.op('pool', lambda e: e.memset(vmx[:].rearrange("p a b c -> p (a b c)"), 1.0), writes=['vmx' + tag])
    for h in range(4):
        for c in range(8):
            S.op('pe', lambda e: e.matmul(B[h % 2][0:64, (h // 2) * 256:(h // 2) * 256 + 256],
                                          wm[:, c, h * 64:(h + 1) * 64], memT[:, c, :], start=(c == 0), stop=(c == 7)),
                 reads=['memT' + tag, 'wmkv' + tag], writes=['B%d' % (h % 2)])
        S.op('act', lambda e: e.copy(out=kmT[:, h, :], in_=B[h % 2][0:64, (h // 2) * 256:(h // 2) * 256 + 256]), reads=['B%d' % (h % 2)], writes=['kmT' + tag])
    for j in range(2):
        for c in range(8):
            S.op('pe', lambda e: e.matmul(B[2 + j][:, 0:256], memT[:, c, j * 128:(j + 1) * 128], wm[:, c, 256:512], start=(c == 0), stop=(c == 7)),
                 reads=['memT' + tag, 'wmkv' + tag], writes=['B%d' % (2 + j)])
        S.op('act', lambda e: e.copy(out=vmx[:, j, :, 0:64], in_=B[2 + j][:, 0:256].rearrange("p (h e) -> p h e", h=4)),
             reads=['B%d' % (2 + j)], writes=['vmx' + tag])
    return kmT, vmx


def mem_attn_tile(C, qmT, key_qmT, kmT, vmx, tag, cat, key_cat, bs, bm):
    S, B = C.S, C.B
    pT = C.ma_pT
    for h in range(4):
        bk = bs[h // 2]
        for j in range(2):
            col = ((h % 2) * 2 + j) * 128
            S.op('pe', lambda e: e.matmul(B[bk][:, col:col + 128], kmT[:, h, j * 128:(j + 1) * 128], qmT[:, h, :], start=True, stop=True),
                 reads=['kmT' + tag, key_qmT], writes=['B%d' % bk])
    for hh in range(2):
        S.op('act', lambda e: e.activation(out=pT[:, hh * 512:(hh + 1) * 512], in_=B[bs[hh]][:, :], func=AF.Exp, scale=0.125),
             reads=['B%d' % bs[hh]], writes=['ma_pT'])
    for h in range(4):
        for j in range(2):
            col = (h * 2 + j) * 128
            S.op('pe', lambda e: e.matmul(B[bm][:, h * 65:h * 65 + 65], pT[:, col:col + 128], vmx[:, j, h, :], start=(j == 0), stop=(j == 1)),
                 reads=['ma_pT', 'vmx' + tag], writes=['B%d' % bm])
    mo = B[bm][:, 0:260].rearrange("p (h e) -> p h e", h=4)
    S.op('dve', lambda e: e.reciprocal(out=C.ma_rd[:], in_=mo[:, :, 64]), reads=['B%d' % bm], writes=['ma_rd'])
    S.op('dve', lambda e: e.tensor_tensor(out=cat[:, 768:1024].rearrange("p (h e) -> p h e", h=4), in0=mo[:, :, 0:64],
                                          in1=C.ma_rd[:].unsqueeze(2).to_broadcast([128, 4, 64]), op=ALU.mult),
         reads=['B%d' % bm, 'ma_rd'], writes=[key_cat])


def outproj_tile(C, cat, key_cat, wout, key_wout, XR, key_XR, bt, by, pre, chunks=range(8), kpre='pre'):
    S, B = C.S, C.B
    pb = B[bt][:].bitcast(BF16)
    chunks = list(chunks)
    for c in chunks:
        S.op('pe', lambda e: e.transpose(pb[:, c * 128:(c + 1) * 128], cat[:, c * 128:(c + 1) * 128], C.ident[:]),
             reads=[key_cat, 'ident'], writes=['B%d' % bt])
    c0, c1 = chunks[0], chunks[-1] + 1
    S.op('act', lambda e: e.copy(out=C.catT[:, c0:c1, :].rearrange("p c t -> p (c t)"), in_=pb[:, c0 * 128:c1 * 128]), reads=['B%d' % bt], writes=['catT'])
    for hlf in range(2):
        for c in range(8):
            S.op('pe', lambda e: e.matmul(B[by[hlf]][:, :], C.catT[:, c, :], wout[:, c, hlf * 512:(hlf + 1) * 512], start=(c == 0), stop=(c == 7)),
                 reads=['catT', key_wout], writes=['B%d' % by[hlf]])
        S.op('dve', lambda e: e.scalar_tensor_tensor(out=pre[:, hlf * 512:(hlf + 1) * 512], in0=XR[:, hlf * 512:(hlf + 1) * 512], scalar=DN_ALPHA,
                                                     in1=B[by[hlf]][:, :], op0=ALU.mult, op1=ALU.add),
             reads=[key_XR, 'B%d' % by[hlf]], writes=[kpre])


def phase_A(C):
    S, B, dram, nc = C.S, C.B, C.dram, C.nc
    st = ExitStack()
    sb = lambda name, shape, dt=F32: st.enter_context(nc.sbuf_tensor("a_" + name, list(shape), dt))
    C.sb_save = C.sb
    C.sb = sb
    x_d = dram('x', [SEQ, D]); xT_d = dram('xT', [D, SEQ])
    C.memT_d = dram('memT', [D, 256])
    W = load_w_bf(C, "awin", dram('a_w_in', [D, A_W]), A_W, 'awin')
    wout = load_w_bf(C, "wout0", dram('w_out0', [D, D]), D, 'wout0')
    wg2 = sb("wg2", [16, 384]); S.dma('sp', wg2[:], dram('a_w_gate2', [16, 384]), writes=['wg2'])
    bg = sb("bg", [1, 384]); S.dma('sp', bg[:], dram('a_b_gate', [1, 384]), writes=['bg'])
    ng = load_rep(C, "ng", dram('a_norm_g', [1, 768]), 768, 'ng')
    lng = load_rep(C, "lng", dram('ln_mix_g0', [1, D]), D, 'lnp')
    lnb = load_rep(C, "lnb", dram('ln_mix_b0', [1, D]), D, 'lnp')
    kmT, vmx = mem_kv(C, C.memT_d, dram('w_mem_kv0', [D, 512]), '0')
    alloc_ln(C)
    C.ma_pT = sb("ma_pT", [128, 1024], BF16)
    C.ma_rd = sb("ma_rd", [128, 4])
    C.catT = sb("catT", [128, 8, 128], BF16)
    XT = [sb("XT%d" % j, [128, 8, 128], BF16) for j in range(2)]
    XR = [sb("XR%d" % j, [128, 1024]) for j in range(2)]
    hgT = sb("hgT", [16, 128])
    qmT = sb("qmT", [64, 4, 128], BF16)
    t1 = sb("a_t1", [128, 384]); la = sb("a_la", [128, 384])
    expb = sb("expb", [96, 4, 128]); expnb = sb("expnb", [96, 4, 128]); expE = sb("expE", [128, 384])
    qt = sb("qt", [96, 4, 128], BF16); kt = sb("kt", [96, 4, 128], BF16)
    kend = sb("kend", [128, 384], BF16); vb = sb("vb", [128, 768], BF16)
    gate = sb("gate", [128, 768])
    attnTb = sb("attnTb", [128, 4, 128], BF16)
    St = sb("St", [96, 4, 192]); SbA = [sb("SbA%d" % q, [96, 4, 192], BF16) for q in range(2)]; SbB = sb("SbB", [96, 4, 192], BF16)
    sq = sb("sq", [128, 768]); ssq = sb("ssq", [128, 4]); to = sb("to", [128, 768])
    cat = sb("cat", [128, 1024], BF16)
    pre = sb("pre", [128, 1024])
    S.op('pool', lambda e: e.memset(St[:].rearrange("p a b -> p (a b)"), 0.0), writes=['St'])
    S.op('pool', lambda e: e.memset(SbA[0][:].rearrange("p a b -> p (a b)"), 0.0), writes=['SbA0'])
    QS = 96 ** -0.5

    def load(i):
        j = i % 2
        S.dma('pool', XT[j][:], xT_d[:, i * 128:(i + 1) * 128].rearrange("(c p) t -> p c t", p=128), writes=['XT%d' % j])
        S.dma('sp', XR[j][:], x_d[i * 128:(i + 1) * 128, :], writes=['XR%d' % j])

    pres = [pre, sb("pre2", [128, 1024])]
    load(0)

    def tile_X(i):
        j = i % 2
        pre = pres[j]
        kpre = 'a_pre%d' % j
        if i + 1 < NT:
            load(i + 1)
        conv_pop(C, 0, 2)
        xt = XT[j]; kx = 'XT%d' % j
        for h in range(4):
            for c in range(8):
                S.op('pe', lambda e: e.matmul(B[0][0:96, h * 128:(h + 1) * 128], W[:, c, h * 96:(h + 1) * 96], xt[:, c, :], start=(c == 0), stop=(c == 7)),
                     reads=['awin', kx], writes=['B0'])
        for h in range(4):
            for c in range(8):
                S.op('pe', lambda e: e.matmul(B[1][0:96, h * 128:(h + 1) * 128], W[:, c, 384 + h * 96:384 + (h + 1) * 96], xt[:, c, :], start=(c == 0), stop=(c == 7)),
                     reads=['awin', kx], writes=['B1'])
        for (bk, c0, n) in ((2, 384, 384), (3, 768, 512), (4, 1280, 512), (5, 1792, 512)):
            for c in range(8):
                S.op('pe', lambda e: e.matmul(B[bk][:, 0:n], xt[:, c, :], W[:, c, c0:c0 + n], start=(c == 0), stop=(c == 7)),
                     reads=['awin', kx], writes=['B%d' % bk])
        for h in range(4):
            for c in range(8):
                S.op('pe', lambda e: e.matmul(B[6][0:64, h * 128:(h + 1) * 128], W[:, c, 2320 + h * 64:2320 + (h + 1) * 64], xt[:, c, :], start=(c == 0), stop=(c == 7)),
                     reads=['awin', kx], writes=['B6'])
        for c in range(8):
            S.op('pe', lambda e: e.matmul(B[7][0:16, 0:128], W[:, c, 2304:2320], xt[:, c, :], start=(c == 0), stop=(c == 7)),
                 reads=['awin', kx], writes=['B7'])
        S.op('act', lambda e: e.copy(out=qmT[:].rearrange("p h t -> p (h t)"), in_=B[6][0:64, :]), reads=['B6'], writes=['qmT'])
        S.op('act', lambda e: e.copy(out=hgT[:], in_=B[7][0:16, 0:128]), reads=['B7'], writes=['hgT'])
        S.op('pe', lambda e: e.matmul(B[7][:, 0:384], hgT[:], wg2[:], start=True, stop=False), reads=['hgT', 'wg2'], writes=['B7'])
        S.op('pe', lambda e: e.matmul(B[7][:, 0:384], C.ones_f[0:1, :], bg[:], start=False, stop=True), reads=['ones_f', 'bg'], writes=['B7'])
        S.op('act', lambda e: e.activation(out=t1[:], in_=B[7][:, 0:384], func=AF.Exp, scale=-1.0), reads=['B7'], writes=['a_t1'])
        S.op('act', lambda e: e.activation(out=t1[:], in_=t1[:], func=AF.Ln, bias=1.0, scale=1.0), reads=['a_t1'], writes=['a_t1'])
        S.op('act', lambda e: e.mul(out=la[:], in_=t1[:], mul=-1.0 / 16.0), reads=['a_t1'], writes=['a_la'])
        for h in range(4):
            S.op('pe', lambda e: e.matmul(B[6][0:96, h * 128:(h + 1) * 128], la[:, h * 96:(h + 1) * 96], C.m2[:], start=True, stop=True),
                 reads=['a_la', 'm2'], writes=['B6'])
        S.op('pe', lambda e: e.matmul(B[7][:, 0:384], C.u2[:], la[:], start=True, stop=True), reads=['a_la', 'u2'], writes=['B7'])
        S.op('act', lambda e: e.activation(out=expb[:].rearrange("p h t -> p (h t)"), in_=B[6][0:96, :], func=AF.Exp), reads=['B6'], writes=['expb'])
        S.op('act', lambda e: e.activation(out=expnb[:].rearrange("p h t -> p (h t)"), in_=B[6][0:96, :], func=AF.Exp, scale=-1.0), reads=['B6'], writes=['expnb'])
        S.op('act', lambda e: e.activation(out=expE[:], in_=B[7][:, 0:384], func=AF.Exp), reads=['B7'], writes=['expE'])
        S.op('dve', lambda e: e.scalar_tensor_tensor(out=qt[:].rearrange("p h t -> p (h t)"), in0=B[0][0:96, :], scalar=QS, in1=expb[:].rearrange("p h t -> p (h t)"),
                                                     op0=ALU.mult, op1=ALU.mult), reads=['B0', 'expb'], writes=['qt'])
        S.op('dve', lambda e: e.tensor_tensor(out=kt[:].rearrange("p h t -> p (h t)"), in0=B[1][0:96, :], in1=expnb[:].rearrange("p h t -> p (h t)"), op=ALU.mult),
             reads=['B1', 'expnb'], writes=['kt'])
        S.op('dve', lambda e: e.tensor_tensor(out=kend[:], in0=B[2][:, 0:384], in1=expE[:], op=ALU.mult), reads=['B2', 'expE'], writes=['kend'])
        S.op('act', lambda e: e.copy(out=vb[:, 0:512], in_=B[3][:, :]), reads=['B3'], writes=['vb'])
        S.op('act', lambda e: e.copy(out=vb[:, 512:768], in_=B[4][:, 0:256]), reads=['B4'], writes=['vb'])
        S.op('act', lambda e: e.activation(out=gate[:, 0:256], in_=B[4][:, 256:512], func=AF.Silu), reads=['B4'], writes=['gate'])
        S.op('act', lambda e: e.activation(out=gate[:, 256:768], in_=B[5][:, :], func=AF.Silu), reads=['B5'], writes=['gate'])
        S.op('pool', lambda e: e.tensor_tensor(out=gate[:], in0=gate[:], in1=ng[:], op=ALU.mult), reads=['gate', 'ng'], writes=['gate'])
        for h in range(4):
            S.op('pe', lambda e: e.matmul(B[0][:, h * 128:(h + 1) * 128], kt[:, h, :], qt[:, h, :], start=True, stop=True), reads=['kt', 'qt'], writes=['B0'])
        S.op('dve', lambda e: e.tensor_tensor(out=attnTb[:], in0=B[0][:, :].rearrange("p (h t) -> p h t", h=4),
                                              in1=C.m2[:].unsqueeze(1).to_broadcast([128, 4, 128]), op=ALU.mult), reads=['B0', 'm2'], writes=['attnTb'])
        for ch in range(2):
            for h in range(4):
                bk = 2 + ch * 2 + h // 2
                col = (h % 2) * 192
                S.op('pe', lambda e: e.matmul(B[bk][0:96, col:col + 192], kend[ch * 64:(ch + 1) * 64, h * 96:(h + 1) * 96], vb[ch * 64:(ch + 1) * 64, h * 192:(h + 1) * 192],
                                              start=True, stop=True), reads=['kend', 'vb'], writes=['B%d' % bk])
        def o_ap(h, lo, hi):
            bk = 1 if h < 2 else 6
            col = (h % 2) * 192
            return B[bk][lo:hi, col:col + 192], 'B%d' % bk
        SbS = SbA[i % 2]; kS = 'SbA%d' % (i % 2)
        SbN = SbA[(i + 1) % 2]; kN = 'SbA%d' % ((i + 1) % 2)
        for ch in range(2):
            for h in range(4):
                bk = 2 + ch * 2 + h // 2
                col = (h % 2) * 192
                S.op('dve', lambda e: e.scalar_tensor_tensor(out=St[:, h, :], in0=St[:, h, :], scalar=expb[:, h, ch * 64 + 63:ch * 64 + 64], in1=B[bk][0:96, col:col + 192],
                                                             op0=ALU.mult, op1=ALU.add), reads=['St', 'expb', 'B%d' % bk], writes=['St'])
            Sb_dst, kd = (SbB, 'SbB') if ch == 0 else (SbN, kN)
            S.op('act', lambda e: e.copy(out=Sb_dst[:].rearrange("p a b -> p (a b)"), in_=St[:].rearrange("p a b -> p (a b)")),
                 reads=['St'], writes=[kd])
        for h in range(4):
            oap, ok = o_ap(h, 0, 128)
            S.op('pe', lambda e: e.matmul(oap, attnTb[:, h, :], vb[:, h * 192:(h + 1) * 192], start=True, stop=False), reads=['attnTb', 'vb'], writes=[ok])
            oap0, _ = o_ap(h, 0, 64)
            S.op('pe', lambda e: e.matmul(oap0, qt[:, h, 0:64], SbS[:, h, :], start=False, stop=False), reads=['qt', kS], writes=[ok])
            oap1, _ = o_ap(h, 64, 128)
            S.op('pe', lambda e: e.matmul(oap1, qt[:, h, 64:128], SbB[:, h, :], start=False, stop=True), reads=['qt', 'SbB'], writes=[ok])
        for hh in range(2):
            bk = 1 if hh == 0 else 6
            S.op('act', lambda e: e.activation(out=sq[:, hh * 384:(hh + 1) * 384], in_=B[bk][:, 0:384], func=AF.Square), reads=['B%d' % bk], writes=['sq'])
        S.op('dve', lambda e: e.tensor_reduce(out=ssq[:], in_=sq[:].rearrange("p (h v) -> p h v", h=4), axis=AX.X, op=ALU.add), reads=['sq'], writes=['ssq'])
        S.op('act', lambda e: e.activation(out=ssq[:], in_=ssq[:], func=AF.Sqrt, bias=HN_EPS, scale=1.0 / 192.0), reads=['ssq'], writes=['ssq'])
        S.op('dve', lambda e: e.reciprocal(out=ssq[:], in_=ssq[:]), reads=['ssq'], writes=['ssq'])
        for hh in range(2):
            bk = 1 if hh == 0 else 6
            S.op('dve', lambda e: e.tensor_tensor(out=to[:, hh * 384:(hh + 1) * 384].rearrange("p (h v) -> p h v", h=2),
                                                  in0=B[bk][:, 0:384].rearrange("p (h v) -> p h v", h=2),
                                                  in1=ssq[:, hh * 2:hh * 2 + 2].unsqueeze(2).to_broadcast([128, 2, 192]), op=ALU.mult),
                 reads=['B%d' % bk, 'ssq'], writes=['to'])
        S.op('dve', lambda e: e.tensor_tensor(out=cat[:, 0:768], in0=to[:], in1=gate[:], op=ALU.mult), reads=['to', 'gate'], writes=['cat'])
        mem_attn_tile(C, qmT, 'qmT', kmT, vmx, '0', cat, 'cat', (2, 3), 4)
        outproj_tile(C, cat, 'cat', wout, 'wout0', XR[j], 'XR%d' % j, 5, (7, 0), pre, kpre=kpre)

    tile_X(0)
    for i in range(NT):
        if i + 1 < NT:
            tile_X(i + 1)
        ln_epilogue(C, pres[i % 2], 'a_pre%d' % (i % 2), lng, lnb, C.xm0, C.xm0T, i, 'A', 5)
    C.sb = C.sb_save
    st.close()


def peer_convert(C, l):
    S, dram = C.S, C.dram
    uT_d = dram('peer_uT%d' % l, [D, 16384])
    v_d = dram('peer_v%d' % l, [16384, D])
    uTb = dram('peer_uTb%d' % l, [D, 16384], BF16, "Internal")
    vb = dram('peer_vb%d' % l, [16384, D], BF16, "Internal")
    C.peer_w = getattr(C, 'peer_w', {})
    C.peer_w[l] = (uTb, vb)
    q = []
    for qq in range(4):
        for c in range(8):
            q.append(lambda c=c, qq=qq: S.dma('pool', uTb[c * 128:(c + 1) * 128, qq * 4096:(qq + 1) * 4096], uT_d[c * 128:(c + 1) * 128, qq * 4096:(qq + 1) * 4096], writes=['uTb%d' % l]))
        for c in range(qq * 8, qq * 8 + 8):
            q.append(lambda c=c: S.dma('pool', vb[c * 512:(c + 1) * 512, :].rearrange("(p a) d -> p a d", p=128), v_d[c * 512:(c + 1) * 512, :].rearrange("(p a) d -> p a d", p=128), writes=['vb%d' % l]))
    C.conv_q = getattr(C, 'conv_q', {})
    C.conv_q[l] = q


def conv_pop(C, l, n):
    q = getattr(C, 'conv_q', {}).get(l, [])
    for _ in range(min(n, len(q))):
        q.pop(0)()


def phase_P(C, l, xm_d, xmT_d, xf_d, xfT_d, n_super=SEQ // 512):
    S, B, dram, nc = C.S, C.B, C.dram, C.nc
    st = ExitStack()
    sb = lambda name, shape, dt=F32: st.enter_context(nc.sbuf_tensor("p%d_%s" % (l, name), list(shape), dt))
    P = 'p%d_' % l
    wq = sb("wq", [128, 8, 2048], BF16)
    wq_d = dram('peer_w_q%d' % l, [D, 2048])
    for c in range(8):
        S.dma('pool', wq[:, c, :], wq_d[c * 128:(c + 1) * 128, :], writes=[P + 'wq'])
    keysT = sb("keysT", [128, 2, 128])
    kd = dram('peer_keysT%d' % l, [2, 128, 128])
    for p in range(2):
        S.dma('sp', keysT[:, p, :], kd[p], writes=[P + 'keysT'])
    if l not in getattr(C, 'peer_w', {}):
        peer_convert(C, l)
    conv_pop(C, l, 1000)
    uT_d, v_d = C.peer_w[l]
    lng = sb("lng", [128, D]); lnb = sb("lnb", [128, D])
    S.dma('sp', lng[:], dram('ln_ffn_g%d' % l, [1, D]).partition_broadcast(128), writes=['lnp'])
    S.dma('sp', lnb[:], dram('ln_ffn_b%d' % l, [1, D]).partition_broadcast(128), writes=['lnp'])
    xmT = sb("xmT", [128, 8, 512], BF16)
    acc = sb("acc", [128, 4, 1024])
    top = sb("top", [128, 8, 2, 16])
    c16 = sb("c16", [128, 8, 16])
    dd = sb("dd", [128, 8, 16])
    zz = sb("zz", [128, 8])
    cs = sb("cs", [128, 8, 2])
    E = sb("E", [128, 4, 16, 128])
    gsc = sb("gsc", [128, 4, 8])
    uT = [sb("uT%d" % q, [128, 8, 512], BF16) for q in range(2)]
    vv = [sb("vv%d" % q, [128, 4, 1024], BF16) for q in range(2)]
    gel = [sb("gel%d" % q, [128, 512]) for q in range(2)]
    W8 = [sb("W8_%d" % q, [128, 8, 4, 128]) for q in range(2)]
    w0 = W8[0][:].rearrange("p h a b -> p (h a b)")
    w1 = W8[1][:].rearrange("p h a b -> p (h a b)")
    qT2 = w0[:, 0:4096].rearrange("p (a b) -> p a b", a=16)
    s_sb = w1[:, 0:2048].rearrange("p (a b) -> p a b", a=16)
    cand = w1[:, 2048:4096].rearrange("p (h a b) -> p h a b", h=8, a=16)
    Sg = [sb("Sg%d" % q, [128, 8, 512], BF16) for q in range(2)]
    tmpo = [sb("tmpo%d" % q, [128, 1024]) for q in range(2)]
    ngsc = sb("ngsc", [128, 4, 8])
    xr = tmpo[1]
    tmp = tmpo[0][:, 0:1024].rearrange("p (a b) -> p a b", a=4)
    G = sb("G", [128, 512], BF16)
    A = [sb("A%d" % q, [128, 512], BF16) for q in range(2)]
    AT = [sb("AT%d" % q, [128, 4, 128], BF16) for q in range(2)]
    pre = tmpo[0]
    C.ln_st = sb("ln_st", [128, 2, 6]); C.ln_mv = sb("ln_mv", [128, 2]); C.ln_rs = sb("ln_rs", [128, 1])
    C.ln_xn = sb("ln_xn", [128, 1024]); C.ln_xnb = sb("ln_xnb", [128, 1024], BF16); C.ln_xnT = sb("ln_xnT", [128, 8, 128], BF16)
    DELTA = 2e-4
    NEG = -1e30
    cnt = [0]

    def load_chunk(k):
        q = k % 2
        S.dma('sp', uT[q][:], uT_d[:, k * 512:(k + 1) * 512].rearrange("(c p) e -> p c e", p=128), reads=['uTb%d' % l], writes=[P + 'uT%d' % q])
        S.dma('sp', vv[q][:], v_d[k * 512:(k + 1) * 512, :].rearrange("(a b) d -> b a d", b=128), reads=['vb%d' % l], writes=[P + 'vv%d' % q])

    for sti in range(n_super):
        t0 = sti * 512
        S.dma('sp', xmT[:], xmT_d[:, t0:t0 + 512].rearrange("(c p) t -> p c t", p=128), writes=[P + 'xmT'])
        load_chunk(0)
        load_chunk(1)
        if l == 0:
            conv_pop(C, 1, 8)
        for tt in range(4):
            if tt % 2 == 0:
                for hp in range(16):
                    bk = 4 + (hp % 4)
                    for c in range(8):
                        S.op('pe', lambda e: e.matmul(B[bk][:, 0:256], wq[:, c, hp * 128:(hp + 1) * 128], xmT[:, c, tt * 128:(tt + 2) * 128], start=(c == 0), stop=(c == 7)),
                             reads=[P + 'wq', P + 'xmT'], writes=['B%d' % bk])
                    S.op('act', lambda e: e.copy(out=qT2[:, hp, :], in_=B[bk][:, 0:256]), reads=['B%d' % bk], writes=[P + 'qT'])
            qT = qT2[:, :, (tt % 2) * 128:(tt % 2 + 1) * 128]
            for hp in range(16):
                bk = hp // 4
                col = (hp % 4) * 128
                S.op('pe', lambda e: e.matmul(B[bk][:, col:col + 128], qT[:, hp, :], keysT[:, hp % 2, :], start=True, stop=True),
                     reads=[P + 'qT', P + 'keysT'], writes=['B%d' % bk])
            for bk in range(4):
                S.op('act', lambda e: e.copy(out=s_sb[:, bk * 4:(bk + 1) * 4, :].rearrange("p a b -> p (a b)"), in_=B[bk][:, :]), reads=['B%d' % bk], writes=[P + 's_sb'])
            for hp in range(16):
                h, p = hp // 2, hp % 2
                S.op('dve', lambda e: e.max(out=top[:, h, p, 0:8], in_=s_sb[:, hp, :]), reads=[P + 's_sb'], writes=[P + 'top%d' % hp])
            for hp in range(16):
                h, p = hp // 2, hp % 2
                S.op('dve', lambda e: e.match_replace(out=tmp[:, hp % 4, 0:128], in_to_replace=top[:, h, p, 0:8], in_values=s_sb[:, hp, :], imm_value=NEG),
                     reads=[P + 's_sb', P + 'top%d' % hp], writes=[P + 'tmp%d' % (hp % 4)])
                S.op('dve', lambda e: e.max(out=top[:, h, p, 8:16], in_=tmp[:, hp % 4, 0:128]), reads=[P + 'tmp%d' % (hp % 4)], writes=[P + 'top%d' % hp])
            allt = [P + 'top%d' % hp for hp in range(16)]
            S.op('dve', lambda e: e.tensor_tensor(out=cand, in0=top[:, :, 0, :].unsqueeze(3).to_broadcast([128, 8, 16, 16]),
                                                  in1=top[:, :, 1, :].unsqueeze(2).to_broadcast([128, 8, 16, 16]), op=ALU.add),
                 reads=allt, writes=[P + 'cand'])
            for h in range(8):
                S.op('dve', lambda e: e.max(out=c16[:, h, 0:8], in_=cand[:, h, :, :].rearrange("p a b -> p (a b)")), reads=[P + 'cand'], writes=[P + 'c16_%d' % h])
            for h in range(8):
                S.op('dve', lambda e: e.match_replace(out=tmp[:, h % 4, :], in_to_replace=c16[:, h, 0:8], in_values=cand[:, h, :, :].rearrange("p a b -> p (a b)"), imm_value=NEG),
                     reads=[P + 'cand', P + 'c16_%d' % h], writes=[P + 'tmp%d' % (h % 4)])
                S.op('dve', lambda e: e.max(out=c16[:, h, 8:16], in_=tmp[:, h % 4, :]), reads=[P + 'tmp%d' % (h % 4)], writes=[P + 'c16_%d' % h])
            allc = [P + 'c16_%d' % h for h in range(8)]
            S.op('dve', lambda e: e.tensor_tensor(out=dd[:], in0=c16[:], in1=c16[:, :, 0:1].to_broadcast([128, 8, 16]), op=ALU.subtract), reads=allc, writes=[P + 'dd'])
            S.op('act', lambda e: e.activation(out=dd[:].rearrange("p a b -> p (a b)"), in_=dd[:].rearrange("p a b -> p (a b)"), func=AF.Exp), reads=[P + 'dd'], writes=[P + 'dd'])
            S.op('dve', lambda e: e.tensor_reduce(out=zz[:], in_=dd[:], axis=AX.X, op=ALU.add), reads=[P + 'dd'], writes=[P + 'zz'])
            S.op('dve', lambda e: e.reciprocal(out=zz[:], in_=zz[:]), reads=[P + 'zz'], writes=[P + 'zz'])
            S.op('dve', lambda e: e.scalar_tensor_tensor(out=gsc[:, tt, :], in0=dd[:, :, 15], scalar=float(np.exp(-DELTA)), in1=zz[:], op0=ALU.mult, op1=ALU.mult),
                 reads=[P + 'dd', P + 'zz'], writes=[P + 'gsc'])
            S.op('dve', lambda e: e.tensor_scalar(out=ngsc[:, tt, :], in0=gsc[:, tt, :], scalar1=-1.0, scalar2=None, op0=ALU.mult), reads=[P + 'gsc'], writes=[P + 'ngsc'])
            S.op('dve', lambda e: e.tensor_copy(out=cs[:, :, 0], in_=top[:, :, 0, 0]), reads=allt, writes=[P + 'cs'])
            S.op('dve', lambda e: e.scalar_tensor_tensor(out=cs[:, :, 1], in0=c16[:, :, 15], scalar=-DELTA, in1=top[:, :, 0, 0], op0=ALU.add, op1=ALU.subtract),
                 reads=allc + allt, writes=[P + 'cs'])
            S.op('dve', lambda e: e.tensor_tensor(out=E[:, tt, :, :], in0=s_sb, in1=cs[:].rearrange("p h q -> p (h q)").unsqueeze(2).to_broadcast([128, 16, 128]), op=ALU.subtract),
                 reads=[P + 's_sb', P + 'cs'], writes=[P + 'E'])
            S.op('act', lambda e: e.activation(out=E[:, tt, :, :].rearrange("p a b -> p (a b)"), in_=E[:, tt, :, :].rearrange("p a b -> p (a b)"), func=AF.Exp),
                 reads=[P + 'E'], writes=[P + 'E'])
            Ea = E[:, tt, :, :].rearrange("p (h q) n -> p h q n", q=2)[:, :, 0, :]
            S.op('dve', lambda e: e.tensor_tensor(out=Ea, in0=Ea, in1=gsc[:, tt, :].unsqueeze(2).to_broadcast([128, 8, 128]), op=ALU.mult),
                 reads=[P + 'E', P + 'gsc'], writes=[P + 'E'])
        S.barrier()
        its = [(k, tt) for k in range(32) for tt in range(4)]
        NI = len(its)

        def st_P1(n):
            k, tt = its[n]; z = n % 2; q = k % 2
            for c in range(8):
                S.op('pe', lambda e: e.matmul(B[z][:, :], xmT[:, c, tt * 128:(tt + 1) * 128], uT[q][:, c, :], start=(c == 0), stop=(c == 7)),
                     reads=[P + 'xmT', P + 'uT%d' % q], writes=['B%d' % z])
            S.op('act', lambda e: e.activation(out=gel[z][:], in_=B[z][:, :], func=AF.Gelu), reads=['B%d' % z], writes=[P + 'gel%d' % z])

        def st_D1a(n):
            k, tt = its[n]; z = n % 2
            Ev = E[:, tt, :, :].rearrange("p (h q) n -> p h q n", q=2)
            S.op('dve', lambda e: e.tensor_tensor(out=W8[z][:], in0=Ev[:, :, 0, k * 4:(k + 1) * 4].unsqueeze(3).to_broadcast([128, 8, 4, 128]),
                                                  in1=Ev[:, :, 1, :].unsqueeze(2).to_broadcast([128, 8, 4, 128]), op=ALU.mult),
                 reads=[P + 'E'], writes=[P + 'W8_%d' % z])

        def st_SG(n):
            k, tt = its[n]; z = n % 2
            W8h = W8[z][:].rearrange("p h a b -> p h (a b)")
            for h in range(8):
                S.op('act', lambda e: e.activation(out=Sg[z][:, h, :], in_=W8h[:, h, :], func=AF.Sign, bias=ngsc[:, tt, h:h + 1], scale=1.0),
                     reads=[P + 'W8_%d' % z, P + 'ngsc'], writes=[P + 'Sg%d_%d' % (z, h)])

        def st_D1b(n):
            k, tt = its[n]; z = n % 2
            kS = [P + 'Sg%d_%d' % (z, h) for h in range(8)]
            Sf = Sg[z][:].rearrange("p h n -> p (h n)")
            S.op('dve', lambda e: e.scalar_tensor_tensor(out=Sf, in0=Sf, scalar=1.0, in1=W8[z][:].rearrange("p h a b -> p (h a b)"), op0=ALU.add, op1=ALU.mult),
                 reads=kS + [P + 'W8_%d' % z], writes=kS)
            S.op('dve', lambda e: e.tensor_tensor(out=Sg[z][:, 0:4, :], in0=Sg[z][:, 0:4, :], in1=Sg[z][:, 4:8, :], op=ALU.add), reads=kS, writes=kS)
            S.op('dve', lambda e: e.tensor_tensor(out=Sg[z][:, 0:2, :], in0=Sg[z][:, 0:2, :], in1=Sg[z][:, 2:4, :], op=ALU.add), reads=kS, writes=kS)
            S.op('dve', lambda e: e.tensor_tensor(out=G[:], in0=Sg[z][:, 0, :], in1=Sg[z][:, 1, :], op=ALU.add), reads=kS, writes=[P + 'G'])
            S.op('dve', lambda e: e.scalar_tensor_tensor(out=A[z][:], in0=G[:], scalar=0.5, in1=gel[z][:], op0=ALU.mult, op1=ALU.mult),
                 reads=[P + 'gel%d' % z, P + 'G'], writes=[P + 'A%d' % z])

        def st_P2(n):
            z = n % 2
            bt = 2 + z
            pb = B[bt][:].bitcast(BF16)
            for a in range(4):
                S.op('pe', lambda e: e.transpose(pb[:, a * 128:(a + 1) * 128], A[z][:, a * 128:(a + 1) * 128], C.ident[:]),
                     reads=[P + 'A%d' % z, 'ident'], writes=['B%d' % bt])
            S.op('act', lambda e: e.copy(out=AT[z][:].rearrange("p a t -> p (a t)"), in_=pb[:, 0:512]), reads=['B%d' % bt], writes=[P + 'AT%d' % z])

        def st_P3(n):
            k, tt = its[n]; z = n % 2; q = k % 2
            for hlf in range(2):
                bo = 4 + z * 2 + hlf
                for a in range(4):
                    S.op('pe', lambda e: e.matmul(B[bo][:, :], AT[z][:, a, :], vv[q][:, a, hlf * 512:(hlf + 1) * 512], start=(a == 0), stop=(a == 3)),
                         reads=[P + 'AT%d' % z, P + 'vv%d' % q], writes=['B%d' % bo])

        def st_D2(n):
            k, tt = its[n]; z = n % 2
            for hlf in range(2):
                bo = 4 + z * 2 + hlf
                if k == 0:
                    S.op('act', lambda e: e.copy(out=acc[:, tt, hlf * 512:(hlf + 1) * 512], in_=B[bo][:, :]), reads=['B%d' % bo], writes=[P + 'acc%d' % tt])
                else:
                    S.op('act', lambda e: e.copy(out=tmpo[z][:, hlf * 512:(hlf + 1) * 512], in_=B[bo][:, :]), reads=['B%d' % bo], writes=[P + 'tmpo%d' % z])
            if k > 0:
                S.dma('pool', acc[:, tt, :], tmpo[z][:], reads=[P + 'tmpo%d' % z, P + 'acc%d' % tt], writes=[P + 'acc%d' % tt], accum_op=ALU.add)

        st_P1(0)
        st_D1a(0)
        st_SG(0)
        for n in range(NI + 1):
            if n + 1 < NI:
                st_P1(n + 1)
                st_D1a(n + 1)
                st_SG(n + 1)
            if n < NI:
                st_D1b(n)
            if n >= 1:
                st_P3(n - 1)
                k_prev, tt_prev = its[n - 1]
                if tt_prev == 3 and k_prev + 2 < 32:
                    load_chunk(k_prev + 2)
            if n < NI:
                st_P2(n)
            if n >= 1:
                st_D2(n - 1)
        S.barrier()
        for tt in range(4):
            i = sti * 4 + tt
            S.dma('sp', xr[:], xm_d[i * 128:(i + 1) * 128, :], writes=[P + 'xr'])
            S.op('dve', lambda e: e.scalar_tensor_tensor(out=pre[:], in0=xr[:], scalar=DN_ALPHA, in1=acc[:, tt, :], op0=ALU.mult, op1=ALU.add),
                 reads=[P + 'xr', P + 'acc%d' % tt], writes=['pre'])
            ln_epilogue(C, pre, 'pre', lng, lnb, xf_d, xfT_d, i, 'P%d' % l, 3)
    st.close()


def phase_B(C, xf_d, xfT_d, xm1_d, xm1T_d):
    S, B, dram, nc = C.S, C.B, C.dram, C.nc
    st = ExitStack()
    sb = lambda name, shape, dt=F32: st.enter_context(nc.sbuf_tensor("b_" + name, list(shape), dt))
    C.sb_save = C.sb
    C.sb = sb
    bw_d = dram('b_w_in', [D, B_W])
    kv_d = dram('shared_w_kv', [D, 1536])
    catT_d = dram('catT1', [768, SEQ], BF16, "Internal")
    XF = sb("XF", [128, 8, SEQ], BF16)
    for c in range(8):
        S.dma('sp' if c % 2 == 0 else 'act', XF[:, c, :], xfT_d[c * 128:(c + 1) * 128, :], writes=['XF'])
    st1 = ExitStack()
    sb_outer = sb
    sb = lambda name, shape, dt=F32: st1.enter_context(nc.sbuf_tensor("b1_" + name, list(shape), dt))
    acc = sb("acc", [128, 2, SEQ])
    mixT = sb("mixT", [128, SEQ], BF16)
    KT = sb("KT", [128, SEQ], BF16)
    QT = [sb("QT%d" % q, [128, SEQ], BF16) for q in range(2)]
    V = [sb("V%d" % q, [128, 32, 128], BF16) for q in range(2)]
    Wq = sb("Wq", [128, 8, 3, 128], BF16)
    Wk = sb("Wk", [128, 8, 128], BF16)
    Wva = sb("Wva", [128, 8, 768], BF16)
    vsb = [sb("vsb%d" % q, [128, 768], BF16) for q in range(2)]
    PT = [sb("PT%d" % q, [128, 2, 128], BF16) for q in range(2)]
    SC = 128 ** -0.5
    DIL = (1, 4, 16)
    Vd = dram('b_Vd', [SEQ, 768], BF16, "Internal")

    for c in range(8):
        S.dma('pool', Wva[:, c, :], kv_d[c * 128:(c + 1) * 128, 768:1536], writes=['b_Wva'])
    for i in range(NT):
        z = i % 2
        for (bk, c0, n) in ((4 + 2 * z, 0, 512), (5 + 2 * z, 512, 256)):
            for c in range(8):
                S.op('pe', lambda e: e.matmul(B[bk][:, 0:n], XF[:, c, i * 128:(i + 1) * 128], Wva[:, c, c0:c0 + n], start=(c == 0), stop=(c == 7)),
                     reads=['XF', 'b_Wva'], writes=['B%d' % bk])
            S.op('act', lambda e: e.copy(out=vsb[z][:, c0:c0 + n], in_=B[bk][:, 0:n]), reads=['B%d' % bk], writes=['b_vsb%d' % z])
        S.dma('sp', Vd[i * 128:(i + 1) * 128, :], vsb[z][:], reads=['b_vsb%d' % z], writes=['b_Vd'])

    def proj_T(dst, key_dst, wsel, key_w):
        for tg in range(8):
            bk = 4 + tg % 4
            for c in range(8):
                S.op('pe', lambda e: e.matmul(B[bk][:, :], wsel(c), XF[:, c, tg * 512:(tg + 1) * 512], start=(c == 0), stop=(c == 7)),
                     reads=[key_w, 'XF'], writes=['B%d' % bk])
            S.op('act', lambda e: e.copy(out=dst[:, tg * 512:(tg + 1) * 512], in_=B[bk][:, :]), reads=['B%d' % bk], writes=[key_dst])

    def load_V(s_, g, q):
        d = DIL[g]
        nb = 32 // d
        src = Vd.rearrange("(n p r) f -> p r n f", p=128, r=d)[:, :, :, s_ * 128:(s_ + 1) * 128]
        S.dma('act', V[q][:].rearrange("p (r n) e -> p r n e", r=d), src, reads=['b_Vd'], writes=['b_V%d' % q])

    sg = [(s_, g) for s_ in range(6) for g in range(3)]
    load_V(0, 0, 0)
    for idx, (s_, g) in enumerate(sg):
        vq = idx % 2
        if idx + 1 < len(sg):
            load_V(sg[idx + 1][0], sg[idx + 1][1], (idx + 1) % 2)
        if g == 0:
            for c in range(8):
                S.dma('pool', Wq[:, c, :, :], bw_d[c * 128:(c + 1) * 128, 0:2304].rearrange("p (g s e) -> p g s e", g=3, s=6)[:, :, s_, :], writes=['b_Wq'])
                S.dma('pool', Wk[:, c, :], kv_d[c * 128:(c + 1) * 128, s_ * 128:(s_ + 1) * 128], writes=['b_Wk'])
            proj_T(KT, 'b_KT', lambda c: Wk[:, c, :], 'b_Wk')
        d = DIL[g]
        nb = 32 // d
        QTg = QT[idx % 2]
        kQ = 'b_QT%d' % (idx % 2)
        proj_T(QTg, kQ, lambda c: Wq[:, c, g, :], 'b_Wq')
        Vg = V[vq]
        kV = 'b_V%d' % vq

        def blk_info(blk):
            r, n = blk // nb, blk % nb
            tq = d * 128 * n + r
            return r, n, slice(tq, tq + 127 * d + 1, d), ([1] if n == 0 else [0, 1])

        def stg1(blk):
            r, n, qs, kbs = blk_info(blk)
            z = blk % 2
            for kb in kbs:
                tk = d * 128 * (n - 1 + kb) + r
                S.op('pe', lambda e: e.matmul(B[z][:, kb * 128:(kb + 1) * 128], KT[:, tk:tk + 127 * d + 1:d], QTg[:, qs], start=True, stop=True),
                     reads=['b_KT', kQ], writes=['B%d' % z])
            lo = kbs[0]
            S.op('act', lambda e: e.activation(out=PT[z][:, lo:2, :].rearrange("p a b -> p (a b)"), in_=B[z][:, lo * 128:256], func=AF.Exp, scale=SC),
                 reads=['B%d' % z], writes=['b_PT%d' % z])
            S.op('dve', lambda e: e.tensor_tensor(out=PT[z][:, lo:2, :], in0=PT[z][:, lo:2, :], in1=C.dmask[:, lo:2, :], op=ALU.mult),
                 reads=['b_PT%d' % z, 'dmask'], writes=['b_PT%d' % z])

        def stg2(blk):
            r, n, qs, kbs = blk_info(blk)
            z = blk % 2
            bo = 2 + z
            for which in range(2):
                for kb in kbs:
                    lhs = Vg[:, blk - 1 + kb, :] if which == 0 else C.ones_b[:]
                    S.op('pe', lambda e: e.matmul(B[bo][:, which * 128:(which + 1) * 128], lhs, PT[z][:, kb, :], start=(kb == kbs[0]), stop=(kb == 1)),
                         reads=[kV, 'ones_b', 'b_PT%d' % z], writes=['B%d' % bo])
            pv = B[bo][:, 0:256].rearrange("p (a b) -> p a b", a=2)
            if g == 0:
                S.op('act', lambda e: e.copy(out=acc[:, :, qs], in_=pv), reads=['B%d' % bo], writes=['b_acc'])
            else:
                S.op('dve', lambda e: e.tensor_tensor(out=acc[:, :, qs], in0=acc[:, :, qs], in1=pv, op=ALU.add), reads=['B%d' % bo, 'b_acc'], writes=['b_acc'])

        stg1(0)
        for blk in range(32):
            if blk + 1 < 32:
                stg1(blk + 1)
            stg2(blk)
        if g == 2:
            S.op('dve', lambda e: e.reciprocal(out=acc[:, 1, :], in_=acc[:, 1, :]), reads=['b_acc'], writes=['b_acc'])
            S.op('dve', lambda e: e.tensor_tensor(out=mixT[:], in0=acc[:, 0, :], in1=acc[:, 1, :], op=ALU.mult), reads=['b_acc'], writes=['b_mixT'])
            S.dma('sp', catT_d[s_ * 128:(s_ + 1) * 128, :], mixT[:], reads=['b_mixT'], writes=['catT1_d'])
    S.barrier()
    st1.close()
    sb = sb_outer
    Wm = sb("Wm", [128, 8, 256], BF16)
    for c in range(8):
        S.dma('pool', Wm[:, c, :], bw_d[c * 128:(c + 1) * 128, 2304:2560], writes=['b_Wm'])
    wout = load_w_bf(C, "wout1", dram('w_out1', [D, D]), D, 'wout1')
    lng = load_rep(C, "lng", dram('ln_mix_g1', [1, D]), D, 'lnp')
    lnb = load_rep(C, "lnb", dram('ln_mix_b1', [1, D]), D, 'lnp')
    kmT, vmx = mem_kv(C, dram('memT', [D, 256]) if not hasattr(C, 'memT_d') else C.memT_d, dram('w_mem_kv1', [D, 512]), '1')
    alloc_ln(C)
    C.ma_pT = sb("ma_pT", [128, 1024], BF16)
    C.ma_rd = sb("ma_rd", [128, 4])
    C.catT = sb("catT", [128, 8, 128], BF16)
    qmT = sb("qmT", [64, 4, 128], BF16)
    cat = sb("cat", [128, 1024], BF16)
    pre = sb("pre", [128, 1024])
    XR = [sb("XR%d" % j, [128, 1024]) for j in range(2)]
    pres = [pre, sb("pre2", [128, 1024])]

    def tile_X(i):
        j = i % 2
        S.dma('sp', XR[j][:], xf_d[i * 128:(i + 1) * 128, :], writes=['bXR%d' % j])
        S.dma('sp', C.catT[:, 0:6, :], catT_d[:, i * 128:(i + 1) * 128].rearrange("(c p) t -> p c t", p=128), reads=['catT1_d'], writes=['catT'])
        for h in range(4):
            for c in range(8):
                S.op('pe', lambda e: e.matmul(B[6][0:64, h * 128:(h + 1) * 128], Wm[:, c, h * 64:(h + 1) * 64], XF[:, c, i * 128:(i + 1) * 128], start=(c == 0), stop=(c == 7)),
                     reads=['b_Wm', 'XF'], writes=['B6'])
        S.op('act', lambda e: e.copy(out=qmT[:].rearrange("p h t -> p (h t)"), in_=B[6][0:64, :]), reads=['B6'], writes=['b_qmT'])
        mem_attn_tile(C, qmT, 'b_qmT', kmT, vmx, '1', cat, 'b_cat', (2, 3), 4)
        outproj_tile(C, cat, 'b_cat', wout, 'wout1', XR[j], 'bXR%d' % j, 5, (7, 0), pres[j], chunks=(6, 7), kpre='b_pre%d' % j)

    tile_X(0)
    for i in range(NT):
        if i + 1 < NT:
            tile_X(i + 1)
        ln_epilogue(C, pres[i % 2], 'b_pre%d' % (i % 2), lng, lnb, xm1_d, xm1T_d, i, 'B', 1)
    C.sb = C.sb_save
    st.close()


_PROG_CACHE = {}


def _prep_inputs(inputs, b):
    g = lambda k: np.asarray(inputs[k], dtype=np.float32)
    m = {}
    m['x'] = np.ascontiguousarray(g('x')[b])
    m['xT'] = np.ascontiguousarray(g('x')[b].T)
    m['memT'] = np.ascontiguousarray(g('mem')[b].T)
    m['a_w_in'] = np.ascontiguousarray(g('a_w_in')[0])
    m['a_w_gate2'] = np.ascontiguousarray(g('a_w_gate2')[0])
    m['a_b_gate'] = np.ascontiguousarray(g('a_b_gate')[0][None, :])
    m['a_norm_g'] = np.ascontiguousarray(g('a_norm_g')[0][None, :])
    for l in range(2):
        m['w_mem_kv%d' % l] = np.ascontiguousarray(g('w_mem_kv')[l])
        m['w_out%d' % l] = np.ascontiguousarray(g('w_out')[l])
        for nm in ('ln_mix_g', 'ln_mix_b', 'ln_ffn_g', 'ln_ffn_b'):
            m['%s%d' % (nm, l)] = np.ascontiguousarray(g(nm)[l][None, :])
    m['b_w_in'] = np.ascontiguousarray(g('b_w_in')[0])
    m['shared_w_kv'] = np.ascontiguousarray(g('shared_w_kv'))
    for l in range(2):
        m['peer_w_q%d' % l] = np.ascontiguousarray(g('peer_w_q')[l])
        m['peer_keysT%d' % l] = np.ascontiguousarray(g('peer_sub_keys')[l].transpose(0, 2, 1))
        m['peer_uT%d' % l] = np.ascontiguousarray(g('peer_u')[l].T)
        m['peer_v%d' % l] = np.ascontiguousarray(g('peer_v')[l])
    m.update(_consts_host())
    return m


def kernel(**inputs):
    if 'nc' not in _PROG_CACHE:
        _PROG_CACHE['nc'] = build()
    nc = _PROG_CACHE['nc']
    in_maps = [_prep_inputs(inputs, b) for b in range(8)]
    res = run_bass_kernel_spmd(nc, in_maps, core_ids=list(range(8)))
    out = np.stack([np.asarray(r['out'], dtype=np.float32) for r in res.results], axis=0)
    return out
```

```python
from contextlib import ExitStack
import numpy as np
import concourse.bass as bass
import concourse.mybir as mybir
from concourse.bass_utils import run_bass_kernel_spmd

F32 = mybir.dt.float32
BF16 = mybir.dt.bfloat16
ALU = mybir.AluOpType
AF = mybir.ActivationFunctionType
AX = mybir.AxisListType

D = 1024
SEQ = 4096
NT = SEQ // 128
DN_ALPHA = 4.0 ** 0.25
LN_EPS = 1e-5
HN_EPS = 1e-6
A_W = 2576
B_W = 2560


class Sched:
    def __init__(self, nc, stack):
        self.nc = nc
        self.E = {'pe': nc.tensor, 'dve': nc.vector, 'act': nc.scalar, 'pool': nc.gpsimd, 'sp': nc.sync}
        self.sem = {e: stack.enter_context(nc.semaphore('s_' + e)) for e in self.E}
        self.cnt = {e: 0 for e in self.E}
        self.seen = {e: {} for e in self.E}
        self.NDS = 32
        self.dsem = [stack.enter_context(nc.semaphore('d%d' % i)) for i in range(self.NDS)]
        self.dcnt = [0] * self.NDS
        self.dnext = 0
        self.tiles = {}
        self.nins = 0

    def _st(self, key):
        if key not in self.tiles:
            self.tiles[key] = {'w': None, 'r': {}}
        return self.tiles[key]

    def _semobj(self, sk):
        return self.sem[sk] if isinstance(sk, str) else self.dsem[sk]

    def _wait(self, eng, sk, val):
        if self.seen[eng].get(sk, 0) >= val:
            return
        self.E[eng].wait_ge(self._semobj(sk), val)
        self.seen[eng][sk] = val
        self.nins += 1

    def _deps(self, eng, reads, writes):
        for k in reads:
            st = self._st(k)
            if st['w'] is not None:
                self._wait(eng, *st['w'])
        for k in writes:
            st = self._st(k)
            if st['w'] is not None:
                self._wait(eng, *st['w'])
            for sk, v in st['r'].items():
                self._wait(eng, sk, v)

    def _mark(self, sk, val, reads, writes):
        for k in reads:
            st = self._st(k)
            st['r'][sk] = max(st['r'].get(sk, 0), val)
        for k in writes:
            st = self._st(k)
            st['w'] = (sk, val)
            st['r'] = {}

    def op(self, eng, fn, reads=(), writes=()):
        self._deps(eng, reads, writes)
        ins = fn(self.E[eng])
        self.cnt[eng] += 1
        ins.then_inc(self.sem[eng], 1)
        self._mark(eng, self.cnt[eng], reads, writes)
        self.nins += 1
        return ins

    def dma(self, eng, out, in_, reads=(), writes=(), **kw):
        s = self.dnext
        self.dnext = (self.dnext + 1) % self.NDS
        if self.dcnt[s] > 0:
            self._wait(eng, s, self.dcnt[s])
        self._deps(eng, reads, writes)
        ins = self.E[eng].dma_start(out=out, in_=in_, **kw)
        self.dcnt[s] += 16
        ins.then_inc(self.dsem[s], 16)
        self._mark(s, self.dcnt[s], reads, writes)
        self.nins += 1
        return ins

    def barrier(self):
        for e in self.E:
            for s in range(self.NDS):
                if self.dcnt[s] > 0:
                    self._wait(e, s, self.dcnt[s])
            for o in self.E:
                if o != e and self.cnt[o] > 0:
                    self._wait(e, o, self.cnt[o])

    def finish(self, eng='sp'):
        for s in range(self.NDS):
            if self.dcnt[s] > 0:
                self._wait(eng, s, self.dcnt[s])
        for e in self.E:
            if e != eng and self.cnt[e] > 0:
                self._wait(eng, e, self.cnt[e])


class Ctx:
    pass


def _consts_host():
    idx = np.arange(128)
    same = (idx[:, None] // 64) == (idx[None, :] // 64)
    M2 = (same & (idx[:, None] <= idx[None, :])).astype(np.float32)
    U2 = (same & (idx[:, None] > idx[None, :])).astype(np.float32)
    mp = (idx[:, None] >= idx[None, :]).astype(np.float32)
    mc = (idx[:, None] <= idx[None, :]).astype(np.float32)
    return {
        'c_ident': np.eye(128, dtype=np.float32),
        'c_m2': M2, 'c_u2': U2,
        'c_dmask': np.ascontiguousarray(np.stack([mp, mc], axis=1)),
        'c_ones': np.ones((128, 128), dtype=np.float32),
    }


def build(phases=('A', 'P0', 'B', 'P1'), dbg=False, n_super=SEQ // 512):
    nc = bass.Bass("TRN2", target_bir_lowering=False)
    C = Ctx()
    C.nc = nc
    st = ExitStack()
    C.st = st
    S = Sched(nc, st)
    C.S = S

    def dram(name, shape, dt=F32, kind="ExternalInput"):
        return nc.dram_tensor(name, list(shape), dt, kind=kind).ap()
    C.dram = dram

    def sb(name, shape, dt=F32):
        return st.enter_context(nc.sbuf_tensor(name, list(shape), dt))
    C.sb = sb

    C.B = [st.enter_context(nc.psum_tensor("bank%d" % i, [128, 512], F32)) for i in range(8)]

    C.ident = sb("ident", [128, 128], BF16)
    C.m2 = sb("m2", [128, 128], F32)
    C.u2 = sb("u2", [128, 128], F32)
    C.ones_f = sb("ones_f", [128, 128], F32)
    C.ones_b = sb("ones_b", [128, 128], BF16)
    C.dmask = sb("dmask", [128, 2, 128], BF16)
    S.dma('pool', C.ident[:], dram('c_ident', [128, 128]), writes=['ident'])
    S.dma('sp', C.m2[:], dram('c_m2', [128, 128]), writes=['m2'])
    S.dma('sp', C.u2[:], dram('c_u2', [128, 128]), writes=['u2'])
    c_ones = dram('c_ones', [128, 128])
    S.dma('sp', C.ones_f[:], c_ones, writes=['ones_f'])
    S.dma('pool', C.ones_b[:], c_ones, writes=['ones_b'])
    S.dma('pool', C.dmask[:], dram('c_dmask', [128, 2, 128]), writes=['dmask'])

    ext_in = "ExternalInput"
    inter = "ExternalOutput" if dbg else "Internal"
    C.out = None
    names = {'A': 'xm0', 'P0': 'xf0', 'B': 'xm1'}
    prev = {'P0': 'xm0', 'B': 'xf0', 'P1': 'xm1'}
    for l in (0, 1):
        if ('P%d' % l) in phases:
            peer_convert(C, l)
    for ph in ('A', 'P0', 'B', 'P1'):
        if ph not in phases:
            continue
        if ph in prev and not hasattr(C, prev[ph]):
            setattr(C, prev[ph], dram(prev[ph], [SEQ, D], F32, ext_in))
            setattr(C, prev[ph] + 'T', dram(prev[ph] + 'T', [D, SEQ], BF16, ext_in))
        if ph in names:
            setattr(C, names[ph], dram(names[ph], [SEQ, D], F32, inter))
            setattr(C, names[ph] + 'T', dram(names[ph] + 'T', [D, SEQ], BF16, inter))
        if ph == 'A':
            phase_A(C)
        elif ph == 'P0':
            phase_P(C, 0, C.xm0, C.xm0T, C.xf0, C.xf0T, n_super)
        elif ph == 'B':
            phase_B(C, C.xf0, C.xf0T, C.xm1, C.xm1T)
        else:
            C.out = dram('out', [SEQ, D], F32, "ExternalOutput")
            phase_P(C, 1, C.xm1, C.xm1T, C.out, None, n_super)
        S.barrier()
    S.finish('sp')
    st.close()
    return nc


def ln_epilogue(C, pre, key_pre, g_rep, b_rep, out_dram, outT_dram, i, tag, bank):
    S = C.S
    stt, mv, rs, xn, xnb, xnT = C.ln_st, C.ln_mv, C.ln_rs, C.ln_xn, C.ln_xnb, C.ln_xnT
    for hlf in range(2):
        S.op('dve', lambda e: e.bn_stats(out=stt[:, hlf, :], in_=pre[:, hlf * 512:(hlf + 1) * 512]), reads=[key_pre], writes=['ln_st'])
    S.op('dve', lambda e: e.bn_aggr(out=mv[:], in_=stt[:].rearrange("p a b -> p (a b)")), reads=['ln_st'], writes=['ln_mv'])
    S.op('act', lambda e: e.activation(out=rs[:], in_=mv[:, 1:2], func=AF.Sqrt, bias=LN_EPS, scale=1.0), reads=['ln_mv'], writes=['ln_rs'])
    S.op('dve', lambda e: e.reciprocal(out=rs[:], in_=rs[:]), reads=['ln_rs'], writes=['ln_rs'])
    S.op('dve', lambda e: e.tensor_scalar(out=xn[:], in0=pre[:], scalar1=mv[:, 0:1], scalar2=rs[:, 0:1], op0=ALU.subtract, op1=ALU.mult),
         reads=[key_pre, 'ln_mv', 'ln_rs'], writes=['ln_xn'])
    S.op('pool', lambda e: e.tensor_tensor(out=xn[:], in0=xn[:], in1=g_rep[:], op=ALU.mult), reads=['ln_xn', 'lnp'], writes=['ln_xn'])
    S.op('pool', lambda e: e.tensor_tensor(out=xn[:], in0=xn[:], in1=b_rep[:], op=ALU.add), reads=['ln_xn', 'lnp'], writes=['ln_xn'])
    S.dma('sp', out_dram[i * 128:(i + 1) * 128, :], xn[:], reads=['ln_xn'])
    if outT_dram is not None:
        S.op('act', lambda e: e.copy(out=xnb[:], in_=xn[:]), reads=['ln_xn'], writes=['ln_xnb'])
        pb = C.B[bank][:].bitcast(BF16)
        for c in range(8):
            S.op('pe', lambda e: e.transpose(pb[:, c * 128:(c + 1) * 128], xnb[:, c * 128:(c + 1) * 128], C.ident[:]),
                 reads=['ln_xnb', 'ident'], writes=['B%d' % bank])
        S.op('act', lambda e: e.copy(out=xnT[:].rearrange("p c t -> p (c t)"), in_=pb[:, :]), reads=['B%d' % bank], writes=['ln_xnT'])
        S.dma('sp', outT_dram[:, i * 128:(i + 1) * 128].rearrange("(c p) t -> p c t", p=128), xnT[:], reads=['ln_xnT'])


def alloc_ln(C):
    sb = C.sb
    C.ln_st = sb("ln_st", [128, 2, 6])
    C.ln_mv = sb("ln_mv", [128, 2])
    C.ln_rs = sb("ln_rs", [128, 1])
    C.ln_xn = sb("ln_xn", [128, 1024])
    C.ln_xnb = sb("ln_xnb", [128, 1024], BF16)
    C.ln_xnT = sb("ln_xnT", [128, 8, 128], BF16)


def load_w_bf(C, name, dram_ap, ncols, key):
    t = C.sb(name, [128, 8, ncols], BF16)
    for c in range(8):
        C.S.dma('pool', t[:, c, :], dram_ap[c * 128:(c + 1) * 128, :], writes=[key])
    return t


def load_rep(C, name, dram_ap, n, key):
    t = C.sb(name, [128, n], F32)
    C.S.dma('sp', t[:], dram_ap.partition_broadcast(128), writes=[key])
    return t


def mem_kv(C, memT_d, wmkv_d, tag):
    S, sb, B = C.S, C.sb, C.B
    memT = load_w_bf(C, "memT" + tag, memT_d, 256, 'memT' + tag)
    wm = load_w_bf(C, "wmkv" + tag, wmkv_d, 512, 'wmkv' + tag)
    kmT = sb("kmT" + tag, [64, 4, 256], BF16)
    vmx = sb("vmx" + tag, [128, 2, 4, 65], BF16)
    S.op('pool', lambda e: e.memset(vmx[:].rearrange("p a b c -> p (a b c)"), 1.0), writes=['vmx' + tag])
    for h in range(4):
        for c in range(8):
            S.op('pe', lambda e: e.matmul(B[h % 2][0:64, (h // 2) * 256:(h // 2) * 256 + 256],
                                          wm[:, c, h * 64:(h + 1) * 64], memT[:, c, :], start=(c == 0), stop=(c == 7)),
                 reads=['memT' + tag, 'wmkv' + tag], writes=['B%d' % (h % 2)])
        S.op('act', lambda e: e.copy(out=kmT[:, h, :], in_=B[h % 2][0:64, (h // 2) * 256:(h // 2) * 256 + 256]), reads=['B%d' % (h % 2)], writes=['kmT' + tag])
    for j in range(2):
        for c in range(8):
            S.op('pe', lambda e: e.matmul(B[2 + j][:, 0:256], memT[:, c, j * 128:(j + 1) * 128], wm[:, c, 256:512], start=(c == 0), stop=(c == 7)),
                 reads=['memT' + tag, 'wmkv' + tag], writes=['B%d' % (2 + j)])
        S.op('act', lambda e: e.copy(out=vmx[:, j, :, 0:64], in_=B[2 + j][:, 0:256].rearrange("p (h e) -> p h e", h=4)),
             reads=['B%d' % (2 + j)], writes=['vmx' + tag])
    return kmT, vmx


def mem_attn_tile(C, qmT, key_qmT, kmT, vmx, tag, cat, key_cat, bs, bm):
    S, B = C.S, C.B
    pT = C.ma_pT
    for h in range(4):
        bk = bs[h // 2]
        for j in range(2):
            col = ((h % 2) * 2 + j) * 128
            S.op('pe', lambda e: e.matmul(B[bk][:, col:col + 128], kmT[:, h, j * 128:(j + 1) * 128], qmT[:, h, :], start=True, stop=True),
                 reads=['kmT' + tag, key_qmT], writes=['B%d' % bk])
    for hh in range(2):
        S.op('act', lambda e: e.activation(out=pT[:, hh * 512:(hh + 1) * 512], in_=B[bs[hh]][:, :], func=AF.Exp, scale=0.125),
             reads=['B%d' % bs[hh]], writes=['ma_pT'])
    for h in range(4):
        for j in range(2):
            col = (h * 2 + j) * 128
            S.op('pe', lambda e: e.matmul(B[bm][:, h * 65:h * 65 + 65], pT[:, col:col + 128], vmx[:, j, h, :], start=(j == 0), stop=(j == 1)),
                 reads=['ma_pT', 'vmx' + tag], writes=['B%d' % bm])
    mo = B[bm][:, 0:260].rearrange("p (h e) -> p h e", h=4)
    S.op('dve', lambda e: e.reciprocal(out=C.ma_rd[:], in_=mo[:, :, 64]), reads=['B%d' % bm], writes=['ma_rd'])
    S.op('dve', lambda e: e.tensor_tensor(out=cat[:, 768:1024].rearrange("p (h e) -> p h e", h=4), in0=mo[:, :, 0:64],
                                          in1=C.ma_rd[:].unsqueeze(2).to_broadcast([128, 4, 64]), op=ALU.mult),
         reads=['B%d' % bm, 'ma_rd'], writes=[key_cat])


def outproj_tile(C, cat, key_cat, wout, key_wout, XR, key_XR, bt, by, pre, chunks=range(8), kpre='pre'):
    S, B = C.S, C.B
    pb = B[bt][:].bitcast(BF16)
    chunks = list(chunks)
    for c in chunks:
        S.op('pe', lambda e: e.transpose(pb[:, c * 128:(c + 1) * 128], cat[:, c * 128:(c + 1) * 128], C.ident[:]),
             reads=[key_cat, 'ident'], writes=['B%d' % bt])
    c0, c1 = chunks[0], chunks[-1] + 1
    S.op('act', lambda e: e.copy(out=C.catT[:, c0:c1, :].rearrange("p c t -> p (c t)"), in_=pb[:, c0 * 128:c1 * 128]), reads=['B%d' % bt], writes=['catT'])
    for hlf in range(2):
        for c in range(8):
            S.op('pe', lambda e: e.matmul(B[by[hlf]][:, :], C.catT[:, c, :], wout[:, c, hlf * 512:(hlf + 1) * 512], start=(c == 0), stop=(c == 7)),
                 reads=['catT', key_wout], writes=['B%d' % by[hlf]])
        S.op('dve', lambda e: e.scalar_tensor_tensor(out=pre[:, hlf * 512:(hlf + 1) * 512], in0=XR[:, hlf * 512:(hlf + 1) * 512], scalar=DN_ALPHA,
                                                     in1=B[by[hlf]][:, :], op0=ALU.mult, op1=ALU.add),
             reads=[key_XR, 'B%d' % by[hlf]], writes=[kpre])


def phase_A(C):
    S, B, dram, nc = C.S, C.B, C.dram, C.nc
    st = ExitStack()
    sb = lambda name, shape, dt=F32: st.enter_context(nc.sbuf_tensor("a_" + name, list(shape), dt))
    C.sb_save = C.sb
    C.sb = sb
    x_d = dram('x', [SEQ, D]); xT_d = dram('xT', [D, SEQ])
    C.memT_d = dram('memT', [D, 256])
    W = load_w_bf(C, "awin", dram('a_w_in', [D, A_W]), A_W, 'awin')
    wout = load_w_bf(C, "wout0", dram('w_out0', [D, D]), D, 'wout0')
    wg2 = sb("wg2", [16, 384]); S.dma('sp', wg2[:], dram('a_w_gate2', [16, 384]), writes=['wg2'])
    bg = sb("bg", [1, 384]); S.dma('sp', bg[:], dram('a_b_gate', [1, 384]), writes=['bg'])
    ng = load_rep(C, "ng", dram('a_norm_g', [1, 768]), 768, 'ng')
    lng = load_rep(C, "lng", dram('ln_mix_g0', [1, D]), D, 'lnp')
    lnb = load_rep(C, "lnb", dram('ln_mix_b0', [1, D]), D, 'lnp')
    kmT, vmx = mem_kv(C, C.memT_d, dram('w_mem_kv0', [D, 512]), '0')
    alloc_ln(C)
    C.ma_pT = sb("ma_pT", [128, 1024], BF16)
    C.ma_rd = sb("ma_rd", [128, 4])
    C.catT = sb("catT", [128, 8, 128], BF16)
    XT = [sb("XT%d" % j, [128, 8, 128], BF16) for j in range(2)]
    XR = [sb("XR%d" % j, [128, 1024]) for j in range(2)]
    hgT = sb("hgT", [16, 128])
    qmT = sb("qmT", [64, 4, 128], BF16)
    t1 = sb("a_t1", [128, 384]); la = sb("a_la", [128, 384])
    expb = sb("expb", [96, 4, 128]); expnb = sb("expnb", [96, 4, 128]); expE = sb("expE", [128, 384])
    qt = sb("qt", [96, 4, 128], BF16); kt = sb("kt", [96, 4, 128], BF16)
    kend = sb("kend", [128, 384], BF16); vb = sb("vb", [128, 768], BF16)
    gate = sb("gate", [128, 768])
    attnTb = sb("attnTb", [128, 4, 128], BF16)
    St = sb("St", [96, 4, 192]); SbA = [sb("SbA%d" % q, [96, 4, 192], BF16) for q in range(2)]; SbB = sb("SbB", [96, 4, 192], BF16)
    sq = sb("sq", [128, 768]); ssq = sb("ssq", [128, 4]); to = sb("to", [128, 768])
    cat = sb("cat", [128, 1024], BF16)
    pre = sb("pre", [128, 1024])
    S.op('pool', lambda e: e.memset(St[:].rearrange("p a b -> p (a b)"), 0.0), writes=['St'])
    S.op('pool', lambda e: e.memset(SbA[0][:].rearrange("p a b -> p (a b)"), 0.0), writes=['SbA0'])
    QS = 96 ** -0.5

    def load(i):
        j = i % 2
        S.dma('pool', XT[j][:], xT_d[:, i * 128:(i + 1) * 128].rearrange("(c p) t -> p c t", p=128), writes=['XT%d' % j])
        S.dma('sp', XR[j][:], x_d[i * 128:(i + 1) * 128, :], writes=['XR%d' % j])

    pres = [pre, sb("pre2", [128, 1024])]
    load(0)

    def tile_X(i):
        j = i % 2
        pre = pres[j]
        kpre = 'a_pre%d' % j
        if i + 1 < NT:
            load(i + 1)
        conv_pop(C, 0, 2)
        xt = XT[j]; kx = 'XT%d' % j
        for h in range(4):
            for c in range(8):
                S.op('pe', lambda e: e.matmul(B[0][0:96, h * 128:(h + 1) * 128], W[:, c, h * 96:(h + 1) * 96], xt[:, c, :], start=(c == 0), stop=(c == 7)),
                     reads=['awin', kx], writes=['B0'])
        for h in range(4):
            for c in range(8):
                S.op('pe', lambda e: e.matmul(B[1][0:96, h * 128:(h + 1) * 128], W[:, c, 384 + h * 96:384 + (h + 1) * 96], xt[:, c, :], start=(c == 0), stop=(c == 7)),
                     reads=['awin', kx], writes=['B1'])
        for (bk, c0, n) in ((2, 384, 384), (3, 768, 512), (4, 1280, 512), (5, 1792, 512)):
            for c in range(8):
                S.op('pe', lambda e: e.matmul(B[bk][:, 0:n], xt[:, c, :], W[:, c, c0:c0 + n], start=(c == 0), stop=(c == 7)),
                     reads=['awin', kx], writes=['B%d' % bk])
        for h in range(4):
            for c in range(8):
                S.op('pe', lambda e: e.matmul(B[6][0:64, h * 128:(h + 1) * 128], W[:, c, 2320 + h * 64:2320 + (h + 1) * 64], xt[:, c, :], start=(c == 0), stop=(c == 7)),
                     reads=['awin', kx], writes=['B6'])
        for c in range(8):
            S.op('pe', lambda e: e.matmul(B[7][0:16, 0:128], W[:, c, 2304:2320], xt[:, c, :], start=(c == 0), stop=(c == 7)),
                 reads=['awin', kx], writes=['B7'])
        S.op('act', lambda e: e.copy(out=qmT[:].rearrange("p h t -> p (h t)"), in_=B[6][0:64, :]), reads=['B6'], writes=['qmT'])
        S.op('act', lambda e: e.copy(out=hgT[:], in_=B[7][0:16, 0:128]), reads=['B7'], writes=['hgT'])
        S.op('pe', lambda e: e.matmul(B[7][:, 0:384], hgT[:], wg2[:], start=True, stop=False), reads=['hgT', 'wg2'], writes=['B7'])
        S.op('pe', lambda e: e.matmul(B[7][:, 0:384], C.ones_f[0:1, :], bg[:], start=False, stop=True), reads=['ones_f', 'bg'], writes=['B7'])
        S.op('act', lambda e: e.activation(out=t1[:], in_=B[7][:, 0:384], func=AF.Exp, scale=-1.0), reads=['B7'], writes=['a_t1'])
        S.op('act', lambda e: e.activation(out=t1[:], in_=t1[:], func=AF.Ln, bias=1.0, scale=1.0), reads=['a_t1'], writes=['a_t1'])
        S.op('act', lambda e: e.mul(out=la[:], in_=t1[:], mul=-1.0 / 16.0), reads=['a_t1'], writes=['a_la'])
        for h in range(4):
            S.op('pe', lambda e: e.matmul(B[6][0:96, h * 128:(h + 1) * 128], la[:, h * 96:(h + 1) * 96], C.m2[:], start=True, stop=True),
                 reads=['a_la', 'm2'], writes=['B6'])
        S.op('pe', lambda e: e.matmul(B[7][:, 0:384], C.u2[:], la[:], start=True, stop=True), reads=['a_la', 'u2'], writes=['B7'])
        S.op('act', lambda e: e.activation(out=expb[:].rearrange("p h t -> p (h t)"), in_=B[6][0:96, :], func=AF.Exp), reads=['B6'], writes=['expb'])
        S.op('act', lambda e: e.activation(out=expnb[:].rearrange("p h t -> p (h t)"), in_=B[6][0:96, :], func=AF.Exp, scale=-1.0), reads=['B6'], writes=['expnb'])
        S.op('act', lambda e: e.activation(out=expE[:], in_=B[7][:, 0:384], func=AF.Exp), reads=['B7'], writes=['expE'])
        S.op('dve', lambda e: e.scalar_tensor_tensor(out=qt[:].rearrange("p h t -> p (h t)"), in0=B[0][0:96, :], scalar=QS, in1=expb[:].rearrange("p h t -> p (h t)"),
                                                     op0=ALU.mult, op1=ALU.mult), reads=['B0', 'expb'], writes=['qt'])
        S.op('dve', lambda e: e.tensor_tensor(out=kt[:].rearrange("p h t -> p (h t)"), in0=B[1][0:96, :], in1=expnb[:].rearrange("p h t -> p (h t)"), op=ALU.mult),
             reads=['B1', 'expnb'], writes=['kt'])
        S.op('dve', lambda e: e.tensor_tensor(out=kend[:], in0=B[2][:, 0:384], in1=expE[:], op=ALU.mult), reads=['B2', 'expE'], writes=['kend'])
        S.op('act', lambda e: e.copy(out=vb[:, 0:512], in_=B[3][:, :]), reads=['B3'], writes=['vb'])
        S.op('act', lambda e: e.copy(out=vb[:, 512:768], in_=B[4][:, 0:256]), reads=['B4'], writes=['vb'])
        S.op('act', lambda e: e.activation(out=gate[:, 0:256], in_=B[4][:, 256:512], func=AF.Silu), reads=['B4'], writes=['gate'])
        S.op('act', lambda e: e.activation(out=gate[:, 256:768], in_=B[5][:, :], func=AF.Silu), reads=['B5'], writes=['gate'])
        S.op('pool', lambda e: e.tensor_tensor(out=gate[:], in0=gate[:], in1=ng[:], op=ALU.mult), reads=['gate', 'ng'], writes=['gate'])
        for h in range(4):
            S.op('pe', lambda e: e.matmul(B[0][:, h * 128:(h + 1) * 128], kt[:, h, :], qt[:, h, :], start=True, stop=True), reads=['kt', 'qt'], writes=['B0'])
        S.op('dve', lambda e: e.tensor_tensor(out=attnTb[:], in0=B[0][:, :].rearrange("p (h t) -> p h t", h=4),
                                              in1=C.m2[:].unsqueeze(1).to_broadcast([128, 4, 128]), op=ALU.mult), reads=['B0', 'm2'], writes=['attnTb'])
        for ch in range(2):
            for h in range(4):
                bk = 2 + ch * 2 + h // 2
                col = (h % 2) * 192
                S.op('pe', lambda e: e.matmul(B[bk][0:96, col:col + 192], kend[ch * 64:(ch + 1) * 64, h * 96:(h + 1) * 96], vb[ch * 64:(ch + 1) * 64, h * 192:(h + 1) * 192],
                                              start=True, stop=True), reads=['kend', 'vb'], writes=['B%d' % bk])
        def o_ap(h, lo, hi):
            bk = 1 if h < 2 else 6
            col = (h % 2) * 192
            return B[bk][lo:hi, col:col + 192], 'B%d' % bk
        SbS = SbA[i % 2]; kS = 'SbA%d' % (i % 2)
        SbN = SbA[(i + 1) % 2]; kN = 'SbA%d' % ((i + 1) % 2)
        for ch in range(2):
            for h in range(4):
                bk = 2 + ch * 2 + h // 2
                col = (h % 2) * 192
                S.op('dve', lambda e: e.scalar_tensor_tensor(out=St[:, h, :], in0=St[:, h, :], scalar=expb[:, h, ch * 64 + 63:ch * 64 + 64], in1=B[bk][0:96, col:col + 192],
                                                             op0=ALU.mult, op1=ALU.add), reads=['St', 'expb', 'B%d' % bk], writes=['St'])
            Sb_dst, kd = (SbB, 'SbB') if ch == 0 else (SbN, kN)
            S.op('act', lambda e: e.copy(out=Sb_dst[:].rearrange("p a b -> p (a b)"), in_=St[:].rearrange("p a b -> p (a b)")),
                 reads=['St'], writes=[kd])
        for h in range(4):
            oap, ok = o_ap(h, 0, 128)
            S.op('pe', lambda e: e.matmul(oap, attnTb[:, h, :], vb[:, h * 192:(h + 1) * 192], start=True, stop=False), reads=['attnTb', 'vb'], writes=[ok])
            oap0, _ = o_ap(h, 0, 64)
            S.op('pe', lambda e: e.matmul(oap0, qt[:, h, 0:64], SbS[:, h, :], start=False, stop=False), reads=['qt', kS], writes=[ok])
            oap1, _ = o_ap(h, 64, 128)
            S.op('pe', lambda e: e.matmul(oap1, qt[:, h, 64:128], SbB[:, h, :], start=False, stop=True), reads=['qt', 'SbB'], writes=[ok])
        for hh in range(2):
            bk = 1 if hh == 0 else 6
            S.op('act', lambda e: e.activation(out=sq[:, hh * 384:(hh + 1) * 384], in_=B[bk][:, 0:384], func=AF.Square), reads=['B%d' % bk], writes=['sq'])
        S.op('dve', lambda e: e.tensor_reduce(out=ssq[:], in_=sq[:].rearrange("p (h v) -> p h v", h=4), axis=AX.X, op=ALU.add), reads=['sq'], writes=['ssq'])
        S.op('act', lambda e: e.activation(out=ssq[:], in_=ssq[:], func=AF.Sqrt, bias=HN_EPS, scale=1.0 / 192.0), reads=['ssq'], writes=['ssq'])
        S.op('dve', lambda e: e.reciprocal(out=ssq[:], in_=ssq[:]), reads=['ssq'], writes=['ssq'])
        for hh in range(2):
            bk = 1 if hh == 0 else 6
            S.op('dve', lambda e: e.tensor_tensor(out=to[:, hh * 384:(hh + 1) * 384].rearrange("p (h v) -> p h v", h=2),
                                                  in0=B[bk][:, 0:384].rearrange("p (h v) -> p h v", h=2),
                                                  in1=ssq[:, hh * 2:hh * 2 + 2].unsqueeze(2).to_broadcast([128, 2, 192]), op=ALU.mult),
                 reads=['B%d' % bk, 'ssq'], writes=['to'])
        S.op('dve', lambda e: e.tensor_tensor(out=cat[:, 0:768], in0=to[:], in1=gate[:], op=ALU.mult), reads=['to', 'gate'], writes=['cat'])
        mem_attn_tile(C, qmT, 'qmT', kmT, vmx, '0', cat, 'cat', (2, 3), 4)
        outproj_tile(C, cat, 'cat', wout, 'wout0', XR[j], 'XR%d' % j, 5, (7, 0), pre, kpre=kpre)

    tile_X(0)
    for i in range(NT):
        if i + 1 < NT:
            tile_X(i + 1)
        ln_epilogue(C, pres[i % 2], 'a_pre%d' % (i % 2), lng, lnb, C.xm0, C.xm0T, i, 'A', 5)
    C.sb = C.sb_save
    st.close()


def peer_convert(C, l):
    S, dram = C.S, C.dram
    uT_d = dram('peer_uT%d' % l, [D, 16384])
    v_d = dram('peer_v%d' % l, [16384, D])
    uTb = dram('peer_uTb%d' % l, [D, 16384], BF16, "Internal")
    vb = dram('peer_vb%d' % l, [16384, D], BF16, "Internal")
    C.peer_w = getattr(C, 'peer_w', {})
    C.peer_w[l] = (uTb, vb)
    q = []
    for qq in range(4):
        for c in range(8):
            q.append(lambda c=c, qq=qq: S.dma('pool', uTb[c * 128:(c + 1) * 128, qq * 4096:(qq + 1) * 4096], uT_d[c * 128:(c + 1) * 128, qq * 4096:(qq + 1) * 4096], writes=['uTb%d' % l]))
        for c in range(qq * 8, qq * 8 + 8):
            q.append(lambda c=c: S.dma('pool', vb[c * 512:(c + 1) * 512, :].rearrange("(p a) d -> p a d", p=128), v_d[c * 512:(c + 1) * 512, :].rearrange("(p a) d -> p a d", p=128), writes=['vb%d' % l]))
    C.conv_q = getattr(C, 'conv_q', {})
    C.conv_q[l] = q


def conv_pop(C, l, n):
    q = getattr(C, 'conv_q', {}).get(l, [])
    for _ in range(min(n, len(q))):
        q.pop(0)()


def phase_P(C, l, xm_d, xmT_d, xf_d, xfT_d, n_super=SEQ // 512):
    S, B, dram, nc = C.S, C.B, C.dram, C.nc
    st = ExitStack()
    sb = lambda name, shape, dt=F32: st.enter_context(nc.sbuf_tensor("p%d_%s" % (l, name), list(shape), dt))
    P = 'p%d_' % l
    wq = sb("wq", [128, 8, 2048], BF16)
    wq_d = dram('peer_w_q%d' % l, [D, 2048])
    for c in range(8):
        S.dma('pool', wq[:, c, :], wq_d[c * 128:(c + 1) * 128, :], writes=[P + 'wq'])
    keysT = sb("keysT", [128, 2, 128])
    kd = dram('peer_keysT%d' % l, [2, 128, 128])
    for p in range(2):
        S.dma('sp', keysT[:, p, :], kd[p], writes=[P + 'keysT'])
    if l not in getattr(C, 'peer_w', {}):
        peer_convert(C, l)
    conv_pop(C, l, 1000)
    uT_d, v_d = C.peer_w[l]
    lng = sb("lng", [128, D]); lnb = sb("lnb", [128, D])
    S.dma('sp', lng[:], dram('ln_ffn_g%d' % l, [1, D]).partition_broadcast(128), writes=['lnp'])
    S.dma('sp', lnb[:], dram('ln_ffn_b%d' % l, [1, D]).partition_broadcast(128), writes=['lnp'])
    xmT = sb("xmT", [128, 8, 512], BF16)
    acc = sb("acc", [128, 4, 1024])
    top = sb("top", [128, 8, 2, 16])
    c16 = sb("c16", [128, 8, 16])
    dd = sb("dd", [128, 8, 16])
    zz = sb("zz", [128, 8])
    cs = sb("cs", [128, 8, 2])
    E = sb("E", [128, 4, 16, 128])
    gsc = sb("gsc", [128, 4, 8])
    uT = [sb("uT%d" % q, [128, 8, 512], BF16) for q in range(2)]
    vv = [sb("vv%d" % q, [128, 4, 1024], BF16) for q in range(2)]
    gel = [sb("gel%d" % q, [128, 512]) for q in range(2)]
    W8 = [sb("W8_%d" % q, [128, 8, 4, 128]) for q in range(2)]
    w0 = W8[0][:].rearrange("p h a b -> p (h a b)")
    w1 = W8[1][:].rearrange("p h a b -> p (h a b)")
    qT4a = w0[:, 0:4096].rearrange("p (a b) -> p a b", a=8)
    qT4b = w1[:, 0:4096].rearrange("p (a b) -> p a b", a=8)
    Sg = [sb("Sg%d" % q, [128, 8, 512], BF16) for q in range(2)]
    tmpo = [sb("tmpo%d" % q, [128, 1024]) for q in range(2)]
    ngsc = sb("ngsc", [128, 4, 8])
    xr = tmpo[1]
    s_sb = Sg[0][:].rearrange("p h n -> p (h n)").bitcast(F32).rearrange("p (a b) -> p a b", a=16)
    cand = Sg[1][:].rearrange("p h n -> p (h n)").bitcast(F32).rearrange("p (h a b) -> p h a b", h=8, a=16)
    tmpA = tmpo[0][:, :].rearrange("p (a b) -> p a b", a=8)
    tmpB = tmpo[1][:, :].rearrange("p (a b) -> p a b", a=8)
    tmpC = tmpo[0][:, :].rearrange("p (a b) -> p a b", a=4)
    tmpD = tmpo[1][:, :].rearrange("p (a b) -> p a b", a=4)
    G = sb("G", [128, 512], BF16)
    A = [sb("A%d" % q, [128, 512], BF16) for q in range(2)]
    AT = [sb("AT%d" % q, [128, 4, 128], BF16) for q in range(2)]
    pre = tmpo[0]
    C.ln_st = sb("ln_st", [128, 2, 6]); C.ln_mv = sb("ln_mv", [128, 2]); C.ln_rs = sb("ln_rs", [128, 1])
    C.ln_xn = sb("ln_xn", [128, 1024]); C.ln_xnb = sb("ln_xnb", [128, 1024], BF16); C.ln_xnT = sb("ln_xnT", [128, 8, 128], BF16)
    DELTA = 2e-4
    NEG = -1e30
    cnt = [0]

    def load_chunk(k):
        q = k % 2
        S.dma('sp', uT[q][:], uT_d[:, k * 512:(k + 1) * 512].rearrange("(c p) e -> p c e", p=128), reads=['uTb%d' % l], writes=[P + 'uT%d' % q])
        S.dma('sp', vv[q][:], v_d[k * 512:(k + 1) * 512, :].rearrange("(a b) d -> b a d", b=128), reads=['vb%d' % l], writes=[P + 'vv%d' % q])

    for sti in range(n_super):
        t0 = sti * 512
        S.dma('sp', xmT[:], xmT_d[:, t0:t0 + 512].rearrange("(c p) t -> p c t", p=128), writes=[P + 'xmT'])
        load_chunk(0)
        load_chunk(1)
        if l == 0:
            conv_pop(C, 1, 8)
        for hp in range(16):
            bk = 4 + (hp % 4)
            for c in range(8):
                S.op('pe', lambda e: e.matmul(B[bk][:, :], wq[:, c, hp * 128:(hp + 1) * 128], xmT[:, c, :], start=(c == 0), stop=(c == 7)),
                     reads=[P + 'wq', P + 'xmT'], writes=['B%d' % bk])
            qdst = (qT4a if hp < 8 else qT4b)[:, hp % 8, :]
            S.op('act', lambda e: e.copy(out=qdst, in_=B[bk][:, :]), reads=['B%d' % bk], writes=[P + 'qT'])
        for tt in range(4):
            qTf = lambda hp: (qT4a if hp < 8 else qT4b)[:, hp % 8, tt * 128:(tt + 1) * 128]
            for hp in range(16):
                bk = hp // 4
                col = (hp % 4) * 128
                S.op('pe', lambda e: e.matmul(B[bk][:, col:col + 128], qTf(hp), keysT[:, hp % 2, :], start=True, stop=True),
                     reads=[P + 'qT', P + 'keysT'], writes=['B%d' % bk])
            for bk in range(4):
                S.op('act', lambda e: e.copy(out=s_sb[:, bk * 4:(bk + 1) * 4, :].rearrange("p a b -> p (a b)"), in_=B[bk][:, :]), reads=['B%d' % bk], writes=[P + 's_sb'])
            for hp in range(16):
                h, p = hp // 2, hp % 2
                S.op('dve', lambda e: e.max(out=top[:, h, p, 0:8], in_=s_sb[:, hp, :]), reads=[P + 's_sb'], writes=[P + 'top%d' % hp])
            tslot = lambda hp: (tmpA if hp < 8 else tmpB)[:, hp % 8, :]
            for hp in range(16):
                h, p = hp // 2, hp % 2
                S.op('dve', lambda e: e.match_replace(out=tslot(hp), in_to_replace=top[:, h, p, 0:8], in_values=s_sb[:, hp, :], imm_value=NEG),
                     reads=[P + 's_sb', P + 'top%d' % hp], writes=[P + 'tmp%d' % hp])
            for hp in range(16):
                h, p = hp // 2, hp % 2
                S.op('dve', lambda e: e.max(out=top[:, h, p, 8:16], in_=tslot(hp)), reads=[P + 'tmp%d' % hp], writes=[P + 'top%d' % hp])
            allt = [P + 'top%d' % hp for hp in range(16)]
            S.op('dve', lambda e: e.tensor_tensor(out=cand, in0=top[:, :, 0, :].unsqueeze(3).to_broadcast([128, 8, 16, 16]),
                                                  in1=top[:, :, 1, :].unsqueeze(2).to_broadcast([128, 8, 16, 16]), op=ALU.add),
                 reads=allt, writes=[P + 'cand'])
            for h in range(8):
                S.op('dve', lambda e: e.max(out=c16[:, h, 0:8], in_=cand[:, h, :, :].rearrange("p a b -> p (a b)")), reads=[P + 'cand'], writes=[P + 'c16_%d' % h])
            cslot = lambda h: (tmpC if h < 4 else tmpD)[:, h % 4, :]
            for h in range(8):
                S.op('dve', lambda e: e.match_replace(out=cslot(h), in_to_replace=c16[:, h, 0:8], in_values=cand[:, h, :, :].rearrange("p a b -> p (a b)"), imm_value=NEG),
                     reads=[P + 'cand', P + 'c16_%d' % h], writes=[P + 'tmp%d' % (2 * (h % 4) + (0 if h < 4 else 8)), P + 'tmp%d' % (2 * (h % 4) + 1 + (0 if h < 4 else 8))])
            for h in range(8):
                S.op('dve', lambda e: e.max(out=c16[:, h, 8:16], in_=cslot(h)),
                     reads=[P + 'tmp%d' % (2 * (h % 4) + (0 if h < 4 else 8)), P + 'tmp%d' % (2 * (h % 4) + 1 + (0 if h < 4 else 8))], writes=[P + 'c16_%d' % h])
            allc = [P + 'c16_%d' % h for h in range(8)]
            S.op('dve', lambda e: e.tensor_tensor(out=dd[:], in0=c16[:], in1=c16[:, :, 0:1].to_broadcast([128, 8, 16]), op=ALU.subtract), reads=allc, writes=[P + 'dd'])
            S.op('act', lambda e: e.activation(out=dd[:].rearrange("p a b -> p (a b)"), in_=dd[:].rearrange("p a b -> p (a b)"), func=AF.Exp), reads=[P + 'dd'], writes=[P + 'dd'])
            S.op('dve', lambda e: e.tensor_reduce(out=zz[:], in_=dd[:], axis=AX.X, op=ALU.add), reads=[P + 'dd'], writes=[P + 'zz'])
            S.op('dve', lambda e: e.reciprocal(out=zz[:], in_=zz[:]), reads=[P + 'zz'], writes=[P + 'zz'])
            S.op('dve', lambda e: e.scalar_tensor_tensor(out=gsc[:, tt, :], in0=dd[:, :, 15], scalar=float(np.exp(-DELTA)), in1=zz[:], op0=ALU.mult, op1=ALU.mult),
                 reads=[P + 'dd', P + 'zz'], writes=[P + 'gsc'])
            S.op('dve', lambda e: e.tensor_scalar(out=ngsc[:, tt, :], in0=gsc[:, tt, :], scalar1=-1.0, scalar2=None, op0=ALU.mult), reads=[P + 'gsc'], writes=[P + 'ngsc'])
            S.op('dve', lambda e: e.tensor_copy(out=cs[:, :, 0], in_=top[:, :, 0, 0]), reads=allt, writes=[P + 'cs'])
            S.op('dve', lambda e: e.scalar_tensor_tensor(out=cs[:, :, 1], in0=c16[:, :, 15], scalar=-DELTA, in1=top[:, :, 0, 0], op0=ALU.add, op1=ALU.subtract),
                 reads=allc + allt, writes=[P + 'cs'])
            S.op('dve', lambda e: e.tensor_tensor(out=E[:, tt, :, :], in0=s_sb, in1=cs[:].rearrange("p h q -> p (h q)").unsqueeze(2).to_broadcast([128, 16, 128]), op=ALU.subtract),
                 reads=[P + 's_sb', P + 'cs'], writes=[P + 'E'])
            S.op('act', lambda e: e.activation(out=E[:, tt, :, :].rearrange("p a b -> p (a b)"), in_=E[:, tt, :, :].rearrange("p a b -> p (a b)"), func=AF.Exp),
                 reads=[P + 'E'], writes=[P + 'E'])
            Ea = E[:, tt, :, :].rearrange("p (h q) n -> p h q n", q=2)[:, :, 0, :]
            S.op('dve', lambda e: e.tensor_tensor(out=Ea, in0=Ea, in1=gsc[:, tt, :].unsqueeze(2).to_broadcast([128, 8, 128]), op=ALU.mult),
                 reads=[P + 'E', P + 'gsc'], writes=[P + 'E'])
        S.barrier()
        its = [(k, tt) for k in range(32) for tt in range(4)]
        NI = len(its)

        def st_P1(n):
            k, tt = its[n]; z = n % 2; q = k % 2
            for c in range(8):
                S.op('pe', lambda e: e.matmul(B[z][:, :], xmT[:, c, tt * 128:(tt + 1) * 128], uT[q][:, c, :], start=(c == 0), stop=(c == 7)),
                     reads=[P + 'xmT', P + 'uT%d' % q], writes=['B%d' % z])
            S.op('act', lambda e: e.activation(out=gel[z][:], in_=B[z][:, :], func=AF.Gelu), reads=['B%d' % z], writes=[P + 'gel%d' % z])

        def st_D1a(n):
            k, tt = its[n]; z = n % 2
            Ev = E[:, tt, :, :].rearrange("p (h q) n -> p h q n", q=2)
            S.op('dve', lambda e: e.tensor_tensor(out=W8[z][:], in0=Ev[:, :, 0, k * 4:(k + 1) * 4].unsqueeze(3).to_broadcast([128, 8, 4, 128]),
                                                  in1=Ev[:, :, 1, :].unsqueeze(2).to_broadcast([128, 8, 4, 128]), op=ALU.mult),
                 reads=[P + 'E'], writes=[P + 'W8_%d' % z])

        def st_SG(n):
            k, tt = its[n]; z = n % 2
            W8h = W8[z][:].rearrange("p h a b -> p h (a b)")
            for h in range(8):
                S.op('act', lambda e: e.activation(out=Sg[z][:, h, :], in_=W8h[:, h, :], func=AF.Sign, bias=ngsc[:, tt, h:h + 1], scale=1.0),
                     reads=[P + 'W8_%d' % z, P + 'ngsc'], writes=[P + 'Sg%d_%d' % (z, h)])

        def st_D1b(n):
            k, tt = its[n]; z = n % 2
            kS = [P + 'Sg%d_%d' % (z, h) for h in range(8)]
            Sf = Sg[z][:].rearrange("p h n -> p (h n)")
            S.op('dve', lambda e: e.scalar_tensor_tensor(out=Sf, in0=Sf, scalar=1.0, in1=W8[z][:].rearrange("p h a b -> p (h a b)"), op0=ALU.add, op1=ALU.mult),
                 reads=kS + [P + 'W8_%d' % z], writes=kS)
            S.op('dve', lambda e: e.tensor_tensor(out=Sg[z][:, 0:4, :], in0=Sg[z][:, 0:4, :], in1=Sg[z][:, 4:8, :], op=ALU.add), reads=kS, writes=kS)
            S.op('dve', lambda e: e.tensor_tensor(out=Sg[z][:, 0:2, :], in0=Sg[z][:, 0:2, :], in1=Sg[z][:, 2:4, :], op=ALU.add), reads=kS, writes=kS)
            S.op('dve', lambda e: e.tensor_tensor(out=G[:], in0=Sg[z][:, 0, :], in1=Sg[z][:, 1, :], op=ALU.add), reads=kS, writes=[P + 'G'])
            S.op('dve', lambda e: e.scalar_tensor_tensor(out=A[z][:], in0=G[:], scalar=0.5, in1=gel[z][:], op0=ALU.mult, op1=ALU.mult),
                 reads=[P + 'gel%d' % z, P + 'G'], writes=[P + 'A%d' % z])

        def st_P2(n):
            z = n % 2
            bt = 2 + z
            pb = B[bt][:].bitcast(BF16)
            for a in range(4):
                S.op('pe', lambda e: e.transpose(pb[:, a * 128:(a + 1) * 128], A[z][:, a * 128:(a + 1) * 128], C.ident[:]),
                     reads=[P + 'A%d' % z, 'ident'], writes=['B%d' % bt])
            S.op('act', lambda e: e.copy(out=AT[z][:].rearrange("p a t -> p (a t)"), in_=pb[:, 0:512]), reads=['B%d' % bt], writes=[P + 'AT%d' % z])

        def st_P3(n):
            k, tt = its[n]; z = n % 2; q = k % 2
            for hlf in range(2):
                bo = 4 + z * 2 + hlf
                for a in range(4):
                    S.op('pe', lambda e: e.matmul(B[bo][:, :], AT[z][:, a, :], vv[q][:, a, hlf * 512:(hlf + 1) * 512], start=(a == 0), stop=(a == 3)),
                         reads=[P + 'AT%d' % z, P + 'vv%d' % q], writes=['B%d' % bo])

        def st_D2(n):
            k, tt = its[n]; z = n % 2
            for hlf in range(2):
                bo = 4 + z * 2 + hlf
                if k == 0:
                    S.op('act', lambda e: e.copy(out=acc[:, tt, hlf * 512:(hlf + 1) * 512], in_=B[bo][:, :]), reads=['B%d' % bo], writes=[P + 'acc%d' % tt])
                else:
                    S.op('act', lambda e: e.copy(out=tmpo[z][:, hlf * 512:(hlf + 1) * 512], in_=B[bo][:, :]), reads=['B%d' % bo], writes=[P + 'tmpo%d' % z])
            if k > 0:
                S.dma('pool', acc[:, tt, :], tmpo[z][:], reads=[P + 'tmpo%d' % z, P + 'acc%d' % tt], writes=[P + 'acc%d' % tt], accum_op=ALU.add)

        st_P1(0)
        st_D1a(0)
        st_SG(0)
        for n in range(NI + 1):
            if n + 1 < NI:
                st_P1(n + 1)
                st_D1a(n + 1)
                st_SG(n + 1)
            if n < NI:
                st_D1b(n)
            if n >= 1:
                st_P3(n - 1)
                k_prev, tt_prev = its[n - 1]
                if tt_prev == 3 and k_prev + 2 < 32:
                    load_chunk(k_prev + 2)
            if n < NI:
                st_P2(n)
            if n >= 1:
                st_D2(n - 1)
        S.barrier()
        for tt in range(4):
            i = sti * 4 + tt
            S.dma('sp', xr[:], xm_d[i * 128:(i + 1) * 128, :], writes=[P + 'xr'])
            S.op('dve', lambda e: e.scalar_tensor_tensor(out=pre[:], in0=xr[:], scalar=DN_ALPHA, in1=acc[:, tt, :], op0=ALU.mult, op1=ALU.add),
                 reads=[P + 'xr', P + 'acc%d' % tt], writes=['pre'])
            ln_epilogue(C, pre, 'pre', lng, lnb, xf_d, xfT_d, i, 'P%d' % l, 3)
    st.close()


def phase_B(C, xf_d, xfT_d, xm1_d, xm1T_d):
    S, B, dram, nc = C.S, C.B, C.dram, C.nc
    st = ExitStack()
    sb = lambda name, shape, dt=F32: st.enter_context(nc.sbuf_tensor("b_" + name, list(shape), dt))
    C.sb_save = C.sb
    C.sb = sb
    bw_d = dram('b_w_in', [D, B_W])
    kv_d = dram('shared_w_kv', [D, 1536])
    catT_d = dram('catT1', [768, SEQ], BF16, "Internal")
    XF = sb("XF", [128, 8, SEQ], BF16)
    for c in range(8):
        S.dma('sp' if c % 2 == 0 else 'act', XF[:, c, :], xfT_d[c * 128:(c + 1) * 128, :], writes=['XF'])
    st1 = ExitStack()
    sb_outer = sb
    sb = lambda name, shape, dt=F32: st1.enter_context(nc.sbuf_tensor("b1_" + name, list(shape), dt))
    acc = sb("acc", [128, 2, SEQ])
    mixT = sb("mixT", [128, SEQ], BF16)
    KT = sb("KT", [128, SEQ], BF16)
    QT = [sb("QT%d" % q, [128, SEQ], BF16) for q in range(2)]
    V = [sb("V%d" % q, [128, 32, 128], BF16) for q in range(2)]
    Wq = sb("Wq", [128, 8, 3, 128], BF16)
    Wk = sb("Wk", [128, 8, 128], BF16)
    Wva = sb("Wva", [128, 8, 768], BF16)
    vsb = [sb("vsb%d" % q, [128, 768], BF16) for q in range(2)]
    PT = [sb("PT%d" % q, [128, 2, 128], BF16) for q in range(2)]
    SC = 128 ** -0.5
    DIL = (1, 4, 16)
    Vd = dram('b_Vd', [SEQ, 768], BF16, "Internal")

    for c in range(8):
        S.dma('pool', Wva[:, c, :], kv_d[c * 128:(c + 1) * 128, 768:1536], writes=['b_Wva'])
    for i in range(NT):
        z = i % 2
        for (bk, c0, n) in ((4 + 2 * z, 0, 512), (5 + 2 * z, 512, 256)):
            for c in range(8):
                S.op('pe', lambda e: e.matmul(B[bk][:, 0:n], XF[:, c, i * 128:(i + 1) * 128], Wva[:, c, c0:c0 + n], start=(c == 0), stop=(c == 7)),
                     reads=['XF', 'b_Wva'], writes=['B%d' % bk])
            S.op('act', lambda e: e.copy(out=vsb[z][:, c0:c0 + n], in_=B[bk][:, 0:n]), reads=['B%d' % bk], writes=['b_vsb%d' % z])
        S.dma('sp', Vd[i * 128:(i + 1) * 128, :], vsb[z][:], reads=['b_vsb%d' % z], writes=['b_Vd'])

    def proj_T(dst, key_dst, wsel, key_w):
        for tg in range(8):
            bk = 4 + tg % 4
            for c in range(8):
                S.op('pe', lambda e: e.matmul(B[bk][:, :], wsel(c), XF[:, c, tg * 512:(tg + 1) * 512], start=(c == 0), stop=(c == 7)),
                     reads=[key_w, 'XF'], writes=['B%d' % bk])
            S.op('act', lambda e: e.copy(out=dst[:, tg * 512:(tg + 1) * 512], in_=B[bk][:, :]), reads=['B%d' % bk], writes=[key_dst])

    def load_V(s_, g, q):
        d = DIL[g]
        nb = 32 // d
        src = Vd.rearrange("(n p r) f -> p r n f", p=128, r=d)[:, :, :, s_ * 128:(s_ + 1) * 128]
        S.dma('act', V[q][:].rearrange("p (r n) e -> p r n e", r=d), src, reads=['b_Vd'], writes=['b_V%d' % q])

    sg = [(s_, g) for s_ in range(6) for g in range(3)]
    load_V(0, 0, 0)
    for idx, (s_, g) in enumerate(sg):
        vq = idx % 2
        if idx + 1 < len(sg):
            load_V(sg[idx + 1][0], sg[idx + 1][1], (idx + 1) % 2)
        if g == 0:
            for c in range(8):
                S.dma('pool', Wq[:, c, :, :], bw_d[c * 128:(c + 1) * 128, 0:2304].rearrange("p (g s e) -> p g s e", g=3, s=6)[:, :, s_, :], writes=['b_Wq'])
                S.dma('pool', Wk[:, c, :], kv_d[c * 128:(c + 1) * 128, s_ * 128:(s_ + 1) * 128], writes=['b_Wk'])
            proj_T(KT, 'b_KT', lambda c: Wk[:, c, :], 'b_Wk')
        d = DIL[g]
        nb = 32 // d
        QTg = QT[idx % 2]
        kQ = 'b_QT%d' % (idx % 2)
        proj_T(QTg, kQ, lambda c: Wq[:, c, g, :], 'b_Wq')
        Vg = V[vq]
        kV = 'b_V%d' % vq

        def blk_info(blk):
            r, n = blk // nb, blk % nb
            tq = d * 128 * n + r
            return r, n, slice(tq, tq + 127 * d + 1, d), ([1] if n == 0 else [0, 1])

        def stg1(blk):
            r, n, qs, kbs = blk_info(blk)
            z = blk % 2
            for kb in kbs:
                tk = d * 128 * (n - 1 + kb) + r
                S.op('pe', lambda e: e.matmul(B[z][:, kb * 128:(kb + 1) * 128], KT[:, tk:tk + 127 * d + 1:d], QTg[:, qs], start=True, stop=True),
                     reads=['b_KT', kQ], writes=['B%d' % z])
            lo = kbs[0]
            S.op('act', lambda e: e.activation(out=PT[z][:, lo:2, :].rearrange("p a b -> p (a b)"), in_=B[z][:, lo * 128:256], func=AF.Exp, scale=SC),
                 reads=['B%d' % z], writes=['b_PT%d' % z])
            S.op('dve', lambda e: e.tensor_tensor(out=PT[z][:, lo:2, :], in0=PT[z][:, lo:2, :], in1=C.dmask[:, lo:2, :], op=ALU.mult),
                 reads=['b_PT%d' % z, 'dmask'], writes=['b_PT%d' % z])

        def stg2(blk):
            r, n, qs, kbs = blk_info(blk)
            z = blk % 2
            bo = 2 + z
            for which in range(2):
                for kb in kbs:
                    lhs = Vg[:, blk - 1 + kb, :] if which == 0 else C.ones_b[:]
                    S.op('pe', lambda e: e.matmul(B[bo][:, which * 128:(which + 1) * 128], lhs, PT[z][:, kb, :], start=(kb == kbs[0]), stop=(kb == 1)),
                         reads=[kV, 'ones_b', 'b_PT%d' % z], writes=['B%d' % bo])
            pv = B[bo][:, 0:256].rearrange("p (a b) -> p a b", a=2)
            if g == 0:
                S.op('act', lambda e: e.copy(out=acc[:, :, qs], in_=pv), reads=['B%d' % bo], writes=['b_acc'])
            else:
                S.op('dve', lambda e: e.tensor_tensor(out=acc[:, :, qs], in0=acc[:, :, qs], in1=pv, op=ALU.add), reads=['B%d' % bo, 'b_acc'], writes=['b_acc'])

        stg1(0)
        for blk in range(32):
            if blk + 1 < 32:
                stg1(blk + 1)
            stg2(blk)
        if g == 2:
            S.op('dve', lambda e: e.reciprocal(out=acc[:, 1, :], in_=acc[:, 1, :]), reads=['b_acc'], writes=['b_acc'])
            S.op('dve', lambda e: e.tensor_tensor(out=mixT[:], in0=acc[:, 0, :], in1=acc[:, 1, :], op=ALU.mult), reads=['b_acc'], writes=['b_mixT'])
            S.dma('sp', catT_d[s_ * 128:(s_ + 1) * 128, :], mixT[:], reads=['b_mixT'], writes=['catT1_d'])
    S.barrier()
    st1.close()
    sb = sb_outer
    Wm = sb("Wm", [128, 8, 256], BF16)
    for c in range(8):
        S.dma('pool', Wm[:, c, :], bw_d[c * 128:(c + 1) * 128, 2304:2560], writes=['b_Wm'])
    wout = load_w_bf(C, "wout1", dram('w_out1', [D, D]), D, 'wout1')
    lng = load_rep(C, "lng", dram('ln_mix_g1', [1, D]), D, 'lnp')
    lnb = load_rep(C, "lnb", dram('ln_mix_b1', [1, D]), D, 'lnp')
    kmT, vmx = mem_kv(C, dram('memT', [D, 256]) if not hasattr(C, 'memT_d') else C.memT_d, dram('w_mem_kv1', [D, 512]), '1')
    alloc_ln(C)
    C.ma_pT = sb("ma_pT", [128, 1024], BF16)
    C.ma_rd = sb("ma_rd", [128, 4])
    C.catT = sb("catT", [128, 8, 128], BF16)
    qmT = sb("qmT", [64, 4, 128], BF16)
    cat = sb("cat", [128, 1024], BF16)
    pre = sb("pre", [128, 1024])
    XR = [sb("XR%d" % j, [128, 1024]) for j in range(2)]
    pres = [pre, sb("pre2", [128, 1024])]

    def tile_X(i):
        j = i % 2
        S.dma('sp', XR[j][:], xf_d[i * 128:(i + 1) * 128, :], writes=['bXR%d' % j])
        S.dma('sp', C.catT[:, 0:6, :], catT_d[:, i * 128:(i + 1) * 128].rearrange("(c p) t -> p c t", p=128), reads=['catT1_d'], writes=['catT'])
        for h in range(4):
            for c in range(8):
                S.op('pe', lambda e: e.matmul(B[6][0:64, h * 128:(h + 1) * 128], Wm[:, c, h * 64:(h + 1) * 64], XF[:, c, i * 128:(i + 1) * 128], start=(c == 0), stop=(c == 7)),
                     reads=['b_Wm', 'XF'], writes=['B6'])
        S.op('act', lambda e: e.copy(out=qmT[:].rearrange("p h t -> p (h t)"), in_=B[6][0:64, :]), reads=['B6'], writes=['b_qmT'])
        mem_attn_tile(C, qmT, 'b_qmT', kmT, vmx, '1', cat, 'b_cat', (2, 3), 4)
        outproj_tile(C, cat, 'b_cat', wout, 'wout1', XR[j], 'bXR%d' % j, 5, (7, 0), pres[j], chunks=(6, 7), kpre='b_pre%d' % j)

    tile_X(0)
    for i in range(NT):
        if i + 1 < NT:
            tile_X(i + 1)
        ln_epilogue(C, pres[i % 2], 'b_pre%d' % (i % 2), lng, lnb, xm1_d, xm1T_d, i, 'B', 1)
    C.sb = C.sb_save
    st.close()


_PROG_CACHE = {}


def _prep_inputs(inputs, b):
    g = lambda k: np.asarray(inputs[k], dtype=np.float32)
    m = {}
    m['x'] = np.ascontiguousarray(g('x')[b])
    m['xT'] = np.ascontiguousarray(g('x')[b].T)
    m['memT'] = np.ascontiguousarray(g('mem')[b].T)
    m['a_w_in'] = np.ascontiguousarray(g('a_w_in')[0])
    m['a_w_gate2'] = np.ascontiguousarray(g('a_w_gate2')[0])
    m['a_b_gate'] = np.ascontiguousarray(g('a_b_gate')[0][None, :])
    m['a_norm_g'] = np.ascontiguousarray(g('a_norm_g')[0][None, :])
    for l in range(2):
        m['w_mem_kv%d' % l] = np.ascontiguousarray(g('w_mem_kv')[l])
        m['w_out%d' % l] = np.ascontiguousarray(g('w_out')[l])
        for nm in ('ln_mix_g', 'ln_mix_b', 'ln_ffn_g', 'ln_ffn_b'):
            m['%s%d' % (nm, l)] = np.ascontiguousarray(g(nm)[l][None, :])
    m['b_w_in'] = np.ascontiguousarray(g('b_w_in')[0])
    m['shared_w_kv'] = np.ascontiguousarray(g('shared_w_kv'))
    for l in range(2):
        m['peer_w_q%d' % l] = np.ascontiguousarray(g('peer_w_q')[l])
        m['peer_keysT%d' % l] = np.ascontiguousarray(g('peer_sub_keys')[l].transpose(0, 2, 1))
        m['peer_uT%d' % l] = np.ascontiguousarray(g('peer_u')[l].T)
        m['peer_v%d' % l] = np.ascontiguousarray(g('peer_v')[l])
    m.update(_consts_host())
    return m


def kernel(**inputs):
    if 'nc' not in _PROG_CACHE:
        _PROG_CACHE['nc'] = build()
    nc = _PROG_CACHE['nc']
    in_maps = [_prep_inputs(inputs, b) for b in range(8)]
    res = run_bass_kernel_spmd(nc, in_maps, core_ids=list(range(8)))
    out = np.stack([np.asarray(r['out'], dtype=np.float32) for r in res.results], axis=0)
    return out
```

```python
from contextlib import ExitStack
import numpy as np
import concourse.bass as bass
import concourse.mybir as mybir
from concourse.bass_utils import run_bass_kernel_spmd

F32 = mybir.dt.float32
BF16 = mybir.dt.bfloat16
ALU = mybir.AluOpType
AF = mybir.ActivationFunctionType
AX = mybir.AxisListType

D = 1024
SEQ = 4096
NT = SEQ // 128
DN_ALPHA = 4.0 ** 0.25
N_ACT_HEADS = 1
LN_EPS = 1e-5
HN_EPS = 1e-6
A_W = 2576
B_W = 2560


class Sched:
    def __init__(self, nc, stack):
        self.nc = nc
        self.E = {'pe': nc.tensor, 'dve': nc.vector, 'act': nc.scalar, 'pool': nc.gpsimd, 'sp': nc.sync}
        self.sem = {e: stack.enter_context(nc.semaphore('s_' + e)) for e in self.E}
        self.cnt = {e: 0 for e in self.E}
        self.seen = {e: {} for e in self.E}
        self.NDS = 32
        self.dsem = [stack.enter_context(nc.semaphore('d%d' % i)) for i in range(self.NDS)]
        self.dcnt = [0] * self.NDS
        self.dnext = 0
        self.tiles = {}
        self.nins = 0

    def _st(self, key):
        if key not in self.tiles:
            self.tiles[key] = {'w': None, 'r': {}}
        return self.tiles[key]

    def _semobj(self, sk):
        return self.sem[sk] if isinstance(sk, str) else self.dsem[sk]

    def _wait(self, eng, sk, val):
        if self.seen[eng].get(sk, 0) >= val:
            return
        self.E[eng].wait_ge(self._semobj(sk), val)
        self.seen[eng][sk] = val
        self.nins += 1

    def _deps(self, eng, reads, writes):
        for k in reads:
            st = self._st(k)
            if st['w'] is not None:
                self._wait(eng, *st['w'])
        for k in writes:
            st = self._st(k)
            if st['w'] is not None:
                self._wait(eng, *st['w'])
            for sk, v in st['r'].items():
                self._wait(eng, sk, v)

    def _mark(self, sk, val, reads, writes):
        for k in reads:
            st = self._st(k)
            st['r'][sk] = max(st['r'].get(sk, 0), val)
        for k in writes:
            st = self._st(k)
            st['w'] = (sk, val)
            st['r'] = {}

    def op(self, eng, fn, reads=(), writes=()):
        self._deps(eng, reads, writes)
        ins = fn(self.E[eng])
        self.cnt[eng] += 1
        ins.then_inc(self.sem[eng], 1)
        self._mark(eng, self.cnt[eng], reads, writes)
        self.nins += 1
        return ins

    def dma(self, eng, out, in_, reads=(), writes=(), **kw):
        s = self.dnext
        self.dnext = (self.dnext + 1) % self.NDS
        if self.dcnt[s] > 0:
            self._wait(eng, s, self.dcnt[s])
        self._deps(eng, reads, writes)
        ins = self.E[eng].dma_start(out=out, in_=in_, **kw)
        self.dcnt[s] += 16
        ins.then_inc(self.dsem[s], 16)
        self._mark(s, self.dcnt[s], reads, writes)
        self.nins += 1
        return ins

    def barrier(self):
        for e in self.E:
            for s in range(self.NDS):
                if self.dcnt[s] > 0:
                    self._wait(e, s, self.dcnt[s])
            for o in self.E:
                if o != e and self.cnt[o] > 0:
                    self._wait(e, o, self.cnt[o])

    def finish(self, eng='sp'):
        for s in range(self.NDS):
            if self.dcnt[s] > 0:
                self._wait(eng, s, self.dcnt[s])
        for e in self.E:
            if e != eng and self.cnt[e] > 0:
                self._wait(eng, e, self.cnt[e])


class Ctx:
    pass


def _consts_host():
    idx = np.arange(128)
    same = (idx[:, None] // 64) == (idx[None, :] // 64)
    M2 = (same & (idx[:, None] <= idx[None, :])).astype(np.float32)
    U2 = (same & (idx[:, None] > idx[None, :])).astype(np.float32)
    mp = (idx[:, None] >= idx[None, :]).astype(np.float32)
    mc = (idx[:, None] <= idx[None, :]).astype(np.float32)
    return {
        'c_ident': np.eye(128, dtype=np.float32),
        'c_m2': M2, 'c_u2': U2,
        'c_dmask': np.ascontiguousarray(np.stack([mp, mc], axis=1)),
        'c_ones': np.ones((128, 128), dtype=np.float32),
    }


def build(phases=('A', 'P0', 'B', 'P1'), dbg=False, n_super=SEQ // 512):
    nc = bass.Bass("TRN2", target_bir_lowering=False)
    C = Ctx()
    C.nc = nc
    st = ExitStack()
    C.st = st
    S = Sched(nc, st)
    C.S = S

    def dram(name, shape, dt=F32, kind="ExternalInput"):
        return nc.dram_tensor(name, list(shape), dt, kind=kind).ap()
    C.dram = dram

    def sb(name, shape, dt=F32):
        return st.enter_context(nc.sbuf_tensor(name, list(shape), dt))
    C.sb = sb

    C.B = [st.enter_context(nc.psum_tensor("bank%d" % i, [128, 512], F32)) for i in range(8)]

    C.ident = sb("ident", [128, 128], BF16)
    C.m2 = sb("m2", [128, 128], F32)
    C.u2 = sb("u2", [128, 128], F32)
    C.ones_f = sb("ones_f", [128, 128], F32)
    C.ones_b = sb("ones_b", [128, 128], BF16)
    C.dmask = sb("dmask", [128, 2, 128], BF16)
    S.dma('pool', C.ident[:], dram('c_ident', [128, 128]), writes=['ident'])
    S.dma('sp', C.m2[:], dram('c_m2', [128, 128]), writes=['m2'])
    S.dma('sp', C.u2[:], dram('c_u2', [128, 128]), writes=['u2'])
    c_ones = dram('c_ones', [128, 128])
    S.dma('sp', C.ones_f[:], c_ones, writes=['ones_f'])
    S.dma('pool', C.ones_b[:], c_ones, writes=['ones_b'])
    S.dma('pool', C.dmask[:], dram('c_dmask', [128, 2, 128]), writes=['dmask'])

    ext_in = "ExternalInput"
    inter = "ExternalOutput" if dbg else "Internal"
    C.out = None
    names = {'A': 'xm0', 'P0': 'xf0', 'B': 'xm1'}
    prev = {'P0': 'xm0', 'B': 'xf0', 'P1': 'xm1'}
    for l in (0, 1):
        if ('P%d' % l) in phases:
            peer_convert(C, l)
    for ph in ('A', 'P0', 'B', 'P1'):
        if ph not in phases:
            continue
        if ph in prev and not hasattr(C, prev[ph]):
            setattr(C, prev[ph], dram(prev[ph], [SEQ, D], F32, ext_in))
            setattr(C, prev[ph] + 'T', dram(prev[ph] + 'T', [D, SEQ], BF16, ext_in))
        if ph in names:
            setattr(C, names[ph], dram(names[ph], [SEQ, D], F32, inter))
            setattr(C, names[ph] + 'T', dram(names[ph] + 'T', [D, SEQ], BF16, inter))
        if ph == 'A':
            phase_A(C)
        elif ph == 'P0':
            phase_P(C, 0, C.xm0, C.xm0T, C.xf0, C.xf0T, n_super)
        elif ph == 'B':
            phase_B(C, C.xf0, C.xf0T, C.xm1, C.xm1T)
        else:
            C.out = dram('out', [SEQ, D], F32, "ExternalOutput")
            phase_P(C, 1, C.xm1, C.xm1T, C.out, None, n_super)
        S.barrier()
    S.finish('sp')
    st.close()
    return nc


def ln_epilogue(C, pre, key_pre, g_rep, b_rep, out_dram, outT_dram, i, tag, bank):
    S = C.S
    stt, mv, rs, xn, xnb, xnT = C.ln_st, C.ln_mv, C.ln_rs, C.ln_xn, C.ln_xnb, C.ln_xnT
    for hlf in range(2):
        S.op('dve', lambda e: e.bn_stats(out=stt[:, hlf, :], in_=pre[:, hlf * 512:(hlf + 1) * 512]), reads=[key_pre], writes=['ln_st'])
    S.op('dve', lambda e: e.bn_aggr(out=mv[:], in_=stt[:].rearrange("p a b -> p (a b)")), reads=['ln_st'], writes=['ln_mv'])
    S.op('act', lambda e: e.activation(out=rs[:], in_=mv[:, 1:2], func=AF.Sqrt, bias=LN_EPS, scale=1.0), reads=['ln_mv'], writes=['ln_rs'])
    S.op('dve', lambda e: e.reciprocal(out=rs[:], in_=rs[:]), reads=['ln_rs'], writes=['ln_rs'])
    S.op('dve', lambda e: e.tensor_scalar(out=xn[:], in0=pre[:], scalar1=mv[:, 0:1], scalar2=rs[:, 0:1], op0=ALU.subtract, op1=ALU.mult),
         reads=[key_pre, 'ln_mv', 'ln_rs'], writes=['ln_xn'])
    S.op('pool', lambda e: e.tensor_tensor(out=xn[:], in0=xn[:], in1=g_rep[:], op=ALU.mult), reads=['ln_xn', 'lnp'], writes=['ln_xn'])
    S.op('pool', lambda e: e.tensor_tensor(out=xn[:], in0=xn[:], in1=b_rep[:], op=ALU.add), reads=['ln_xn', 'lnp'], writes=['ln_xn'])
    S.dma('sp', out_dram[i * 128:(i + 1) * 128, :], xn[:], reads=['ln_xn'])
    if outT_dram is not None:
        S.op('act', lambda e: e.copy(out=xnb[:], in_=xn[:]), reads=['ln_xn'], writes=['ln_xnb'])
        pb = C.B[bank][:].bitcast(BF16)
        for c in range(8):
            S.op('pe', lambda e: e.transpose(pb[:, c * 128:(c + 1) * 128], xnb[:, c * 128:(c + 1) * 128], C.ident[:]),
                 reads=['ln_xnb', 'ident'], writes=['B%d' % bank])
        S.op('act', lambda e: e.copy(out=xnT[:].rearrange("p c t -> p (c t)"), in_=pb[:, :]), reads=['B%d' % bank], writes=['ln_xnT'])
        S.dma('sp', outT_dram[:, i * 128:(i + 1) * 128].rearrange("(c p) t -> p c t", p=128), xnT[:], reads=['ln_xnT'])


def alloc_ln(C):
    sb = C.sb
    C.ln_st = sb("ln_st", [128, 2, 6])
    C.ln_mv = sb("ln_mv", [128, 2])
    C.ln_rs = sb("ln_rs", [128, 1])
    C.ln_xn = sb("ln_xn", [128, 1024])
    C.ln_xnb = sb("ln_xnb", [128, 1024], BF16)
    C.ln_xnT = sb("ln_xnT", [128, 8, 128], BF16)


def load_w_bf(C, name, dram_ap, ncols, key):
    t = C.sb(name, [128, 8, ncols], BF16)
    for c in range(8):
        C.S.dma('pool', t[:, c, :], dram_ap[c * 128:(c + 1) * 128, :], writes=[key])
    return t


def load_rep(C, name, dram_ap, n, key):
    t = C.sb(name, [128, n], F32)
    C.S.dma('sp', t[:], dram_ap.partition_broadcast(128), writes=[key])
    return t


def mem_kv(C, memT_d, wmkv_d, tag):
    S, sb, B = C.S, C.sb, C.B
    memT = load_w_bf(C, "memT" + tag, memT_d, 256, 'memT' + tag)
    wm = load_w_bf(C, "wmkv" + tag, wmkv_d, 512, 'wmkv' + tag)
    kmT = sb("kmT" + tag, [64, 4, 256], BF16)
    vmx = sb("vmx" + tag, [128, 2, 4, 65], BF16)
    S.op('pool', lambda e: e.memset(vmx[:].rearrange("p a b c -> p (a b c)"), 1.0), writes=['vmx' + tag])
    for h in range(4):
        for c in range(8):
            S.op('pe', lambda e: e.matmul(B[h % 2][0:64, (h // 2) * 256:(h // 2) * 256 + 256],
                                          wm[:, c, h * 64:(h + 1) * 64], memT[:, c, :], start=(c == 0), stop=(c == 7)),
                 reads=['memT' + tag, 'wmkv' + tag], writes=['B%d' % (h % 2)])
        S.op('act', lambda e: e.copy(out=kmT[:, h, :], in_=B[h % 2][0:64, (h // 2) * 256:(h // 2) * 256 + 256]), reads=['B%d' % (h % 2)], writes=['kmT' + tag])
    for j in range(2):
        for c in range(8):
            S.op('pe', lambda e: e.matmul(B[2 + j][:, 0:256], memT[:, c, j * 128:(j + 1) * 128], wm[:, c, 256:512], start=(c == 0), stop=(c == 7)),
                 reads=['memT' + tag, 'wmkv' + tag], writes=['B%d' % (2 + j)])
        S.op('act', lambda e: e.copy(out=vmx[:, j, :, 0:64], in_=B[2 + j][:, 0:256].rearrange("p (h e) -> p h e", h=4)),
             reads=['B%d' % (2 + j)], writes=['vmx' + tag])
    return kmT, vmx


def mem_attn_tile(C, qmT, key_qmT, kmT, vmx, tag, cat, key_cat, bs, bm):
    S, B = C.S, C.B
    pT = C.ma_pT
    for h in range(4):
        bk = bs[h // 2]
        for j in range(2):
            col = ((h % 2) * 2 + j) * 128
            S.op('pe', lambda e: e.matmul(B[bk][:, col:col + 128], kmT[:, h, j * 128:(j + 1) * 128], qmT[:, h, :], start=True, stop=True),
                 reads=['kmT' + tag, key_qmT], writes=['B%d' % bk])
    for hh in range(2):
        S.op('act', lambda e: e.activation(out=pT[:, hh * 512:(hh + 1) * 512], in_=B[bs[hh]][:, :], func=AF.Exp, scale=0.125),
             reads=['B%d' % bs[hh]], writes=['ma_pT'])
    for h in range(4):
        for j in range(2):
            col = (h * 2 + j) * 128
            S.op('pe', lambda e: e.matmul(B[bm][:, h * 65:h * 65 + 65], pT[:, col:col + 128], vmx[:, j, h, :], start=(j == 0), stop=(j == 1)),
                 reads=['ma_pT', 'vmx' + tag], writes=['B%d' % bm])
    mo = B[bm][:, 0:260].rearrange("p (h e) -> p h e", h=4)
    S.op('dve', lambda e: e.reciprocal(out=C.ma_rd[:], in_=mo[:, :, 64]), reads=['B%d' % bm], writes=['ma_rd'])
    S.op('dve', lambda e: e.tensor_tensor(out=cat[:, 768:1024].rearrange("p (h e) -> p h e", h=4), in0=mo[:, :, 0:64],
                                          in1=C.ma_rd[:].unsqueeze(2).to_broadcast([128, 4, 64]), op=ALU.mult),
         reads=['B%d' % bm, 'ma_rd'], writes=[key_cat])


def outproj_tile(C, cat, key_cat, wout, key_wout, XR, key_XR, bt, by, pre, chunks=range(8), kpre='pre'):
    S, B = C.S, C.B
    pb = B[bt][:].bitcast(BF16)
    chunks = list(chunks)
    for c in chunks:
        S.op('pe', lambda e: e.transpose(pb[:, c * 128:(c + 1) * 128], cat[:, c * 128:(c + 1) * 128], C.ident[:]),
             reads=[key_cat, 'ident'], writes=['B%d' % bt])
    c0, c1 = chunks[0], chunks[-1] + 1
    S.op('act', lambda e: e.copy(out=C.catT[:, c0:c1, :].rearrange("p c t -> p (c t)"), in_=pb[:, c0 * 128:c1 * 128]), reads=['B%d' % bt], writes=['catT'])
    for hlf in range(2):
        for c in range(8):
            S.op('pe', lambda e: e.matmul(B[by[hlf]][:, :], C.catT[:, c, :], wout[:, c, hlf * 512:(hlf + 1) * 512], start=(c == 0), stop=(c == 7)),
                 reads=['catT', key_wout], writes=['B%d' % by[hlf]])
        S.op('dve', lambda e: e.scalar_tensor_tensor(out=pre[:, hlf * 512:(hlf + 1) * 512], in0=XR[:, hlf * 512:(hlf + 1) * 512], scalar=DN_ALPHA,
                                                     in1=B[by[hlf]][:, :], op0=ALU.mult, op1=ALU.add),
             reads=[key_XR, 'B%d' % by[hlf]], writes=[kpre])


def phase_A(C):
    S, B, dram, nc = C.S, C.B, C.dram, C.nc
    st = ExitStack()
    sb = lambda name, shape, dt=F32: st.enter_context(nc.sbuf_tensor("a_" + name, list(shape), dt))
    C.sb_save = C.sb
    C.sb = sb
    x_d = dram('x', [SEQ, D]); xT_d = dram('xT', [D, SEQ])
    C.memT_d = dram('memT', [D, 256])
    W = load_w_bf(C, "awin", dram('a_w_in', [D, A_W]), A_W, 'awin')
    wout = load_w_bf(C, "wout0", dram('w_out0', [D, D]), D, 'wout0')
    wg2 = sb("wg2", [16, 384]); S.dma('sp', wg2[:], dram('a_w_gate2', [16, 384]), writes=['wg2'])
    bg = sb("bg", [1, 384]); S.dma('sp', bg[:], dram('a_b_gate', [1, 384]), writes=['bg'])
    ng = load_rep(C, "ng", dram('a_norm_g', [1, 768]), 768, 'ng')
    lng = load_rep(C, "lng", dram('ln_mix_g0', [1, D]), D, 'lnp')
    lnb = load_rep(C, "lnb", dram('ln_mix_b0', [1, D]), D, 'lnp')
    kmT, vmx = mem_kv(C, C.memT_d, dram('w_mem_kv0', [D, 512]), '0')
    alloc_ln(C)
    C.ma_pT = sb("ma_pT", [128, 1024], BF16)
    C.ma_rd = sb("ma_rd", [128, 4])
    C.catT = sb("catT", [128, 8, 128], BF16)
    XT = [sb("XT%d" % j, [128, 8, 128], BF16) for j in range(2)]
    XR = [sb("XR%d" % j, [128, 1024]) for j in range(2)]
    hgT = sb("hgT", [16, 128])
    qmT = sb("qmT", [64, 4, 128], BF16)
    t1 = sb("a_t1", [128, 384]); la = sb("a_la", [128, 384])
    expb = sb("expb", [96, 4, 128]); expnb = sb("expnb", [96, 4, 128]); expE = sb("expE", [128, 384])
    qt = sb("qt", [96, 4, 128], BF16); kt = sb("kt", [96, 4, 128], BF16)
    kend = sb("kend", [128, 384], BF16); vb = sb("vb", [128, 768], BF16)
    gate = sb("gate", [128, 768])
    attnTb = sb("attnTb", [128, 4, 128], BF16)
    St = sb("St", [96, 4, 192]); SbA = [sb("SbA%d" % q, [96, 4, 192], BF16) for q in range(2)]; SbB = sb("SbB", [96, 4, 192], BF16)
    sq = sb("sq", [128, 768]); ssq = sb("ssq", [128, 4]); to = sb("to", [128, 768])
    cat = sb("cat", [128, 1024], BF16)
    pre = sb("pre", [128, 1024])
    S.op('pool', lambda e: e.memset(St[:].rearrange("p a b -> p (a b)"), 0.0), writes=['St'])
    S.op('pool', lambda e: e.memset(SbA[0][:].rearrange("p a b -> p (a b)"), 0.0), writes=['SbA0'])
    QS = 96 ** -0.5

    def load(i):
        j = i % 2
        S.dma('pool', XT[j][:], xT_d[:, i * 128:(i + 1) * 128].rearrange("(c p) t -> p c t", p=128), writes=['XT%d' % j])
        S.dma('sp', XR[j][:], x_d[i * 128:(i + 1) * 128, :], writes=['XR%d' % j])

    pres = [pre, sb("pre2", [128, 1024])]
    load(0)

    def tile_X(i):
        j = i % 2
        pre = pres[j]
        kpre = 'a_pre%d' % j
        if i + 1 < NT:
            load(i + 1)
        conv_pop(C, 0, 2)
        xt = XT[j]; kx = 'XT%d' % j
        for h in range(4):
            for c in range(8):
                S.op('pe', lambda e: e.matmul(B[0][0:96, h * 128:(h + 1) * 128], W[:, c, h * 96:(h + 1) * 96], xt[:, c, :], start=(c == 0), stop=(c == 7)),
                     reads=['awin', kx], writes=['B0'])
        for h in range(4):
            for c in range(8):
                S.op('pe', lambda e: e.matmul(B[1][0:96, h * 128:(h + 1) * 128], W[:, c, 384 + h * 96:384 + (h + 1) * 96], xt[:, c, :], start=(c == 0), stop=(c == 7)),
                     reads=['awin', kx], writes=['B1'])
        for (bk, c0, n) in ((2, 384, 384), (3, 768, 512), (4, 1280, 512), (5, 1792, 512)):
            for c in range(8):
                S.op('pe', lambda e: e.matmul(B[bk][:, 0:n], xt[:, c, :], W[:, c, c0:c0 + n], start=(c == 0), stop=(c == 7)),
                     reads=['awin', kx], writes=['B%d' % bk])
        for h in range(4):
            for c in range(8):
                S.op('pe', lambda e: e.matmul(B[6][0:64, h * 128:(h + 1) * 128], W[:, c, 2320 + h * 64:2320 + (h + 1) * 64], xt[:, c, :], start=(c == 0), stop=(c == 7)),
                     reads=['awin', kx], writes=['B6'])
        for c in range(8):
            S.op('pe', lambda e: e.matmul(B[7][0:16, 0:128], W[:, c, 2304:2320], xt[:, c, :], start=(c == 0), stop=(c == 7)),
                 reads=['awin', kx], writes=['B7'])
        S.op('act', lambda e: e.copy(out=qmT[:].rearrange("p h t -> p (h t)"), in_=B[6][0:64, :]), reads=['B6'], writes=['qmT'])
        S.op('act', lambda e: e.copy(out=hgT[:], in_=B[7][0:16, 0:128]), reads=['B7'], writes=['hgT'])
        S.op('pe', lambda e: e.matmul(B[7][:, 0:384], hgT[:], wg2[:], start=True, stop=False), reads=['hgT', 'wg2'], writes=['B7'])
        S.op('pe', lambda e: e.matmul(B[7][:, 0:384], C.ones_f[0:1, :], bg[:], start=False, stop=True), reads=['ones_f', 'bg'], writes=['B7'])
        S.op('act', lambda e: e.activation(out=t1[:], in_=B[7][:, 0:384], func=AF.Exp, scale=-1.0), reads=['B7'], writes=['a_t1'])
        S.op('act', lambda e: e.activation(out=t1[:], in_=t1[:], func=AF.Ln, bias=1.0, scale=1.0), reads=['a_t1'], writes=['a_t1'])
        S.op('act', lambda e: e.mul(out=la[:], in_=t1[:], mul=-1.0 / 16.0), reads=['a_t1'], writes=['a_la'])
        for h in range(4):
            S.op('pe', lambda e: e.matmul(B[6][0:96, h * 128:(h + 1) * 128], la[:, h * 96:(h + 1) * 96], C.m2[:], start=True, stop=True),
                 reads=['a_la', 'm2'], writes=['B6'])
        S.op('pe', lambda e: e.matmul(B[7][:, 0:384], C.u2[:], la[:], start=True, stop=True), reads=['a_la', 'u2'], writes=['B7'])
        S.op('act', lambda e: e.activation(out=expb[:].rearrange("p h t -> p (h t)"), in_=B[6][0:96, :], func=AF.Exp), reads=['B6'], writes=['expb'])
        S.op('act', lambda e: e.activation(out=expnb[:].rearrange("p h t -> p (h t)"), in_=B[6][0:96, :], func=AF.Exp, scale=-1.0), reads=['B6'], writes=['expnb'])
        S.op('act', lambda e: e.activation(out=expE[:], in_=B[7][:, 0:384], func=AF.Exp), reads=['B7'], writes=['expE'])
        S.op('dve', lambda e: e.scalar_tensor_tensor(out=qt[:].rearrange("p h t -> p (h t)"), in0=B[0][0:96, :], scalar=QS, in1=expb[:].rearrange("p h t -> p (h t)"),
                                                     op0=ALU.mult, op1=ALU.mult), reads=['B0', 'expb'], writes=['qt'])
        S.op('dve', lambda e: e.tensor_tensor(out=kt[:].rearrange("p h t -> p (h t)"), in0=B[1][0:96, :], in1=expnb[:].rearrange("p h t -> p (h t)"), op=ALU.mult),
             reads=['B1', 'expnb'], writes=['kt'])
        S.op('dve', lambda e: e.tensor_tensor(out=kend[:], in0=B[2][:, 0:384], in1=expE[:], op=ALU.mult), reads=['B2', 'expE'], writes=['kend'])
        S.op('act', lambda e: e.copy(out=vb[:, 0:512], in_=B[3][:, :]), reads=['B3'], writes=['vb'])
        S.op('act', lambda e: e.copy(out=vb[:, 512:768], in_=B[4][:, 0:256]), reads=['B4'], writes=['vb'])
        S.op('act', lambda e: e.activation(out=gate[:, 0:256], in_=B[4][:, 256:512], func=AF.Silu), reads=['B4'], writes=['gate'])
        S.op('act', lambda e: e.activation(out=gate[:, 256:768], in_=B[5][:, :], func=AF.Silu), reads=['B5'], writes=['gate'])
        S.op('pool', lambda e: e.tensor_tensor(out=gate[:], in0=gate[:], in1=ng[:], op=ALU.mult), reads=['gate', 'ng'], writes=['gate'])
        for h in range(4):
            S.op('pe', lambda e: e.matmul(B[0][:, h * 128:(h + 1) * 128], kt[:, h, :], qt[:, h, :], start=True, stop=True), reads=['kt', 'qt'], writes=['B0'])
        S.op('dve', lambda e: e.tensor_tensor(out=attnTb[:], in0=B[0][:, :].rearrange("p (h t) -> p h t", h=4),
                                              in1=C.m2[:].unsqueeze(1).to_broadcast([128, 4, 128]), op=ALU.mult), reads=['B0', 'm2'], writes=['attnTb'])
        for ch in range(2):
            for h in range(4):
                bk = 2 + ch * 2 + h // 2
                col = (h % 2) * 192
                S.op('pe', lambda e: e.matmul(B[bk][0:96, col:col + 192], kend[ch * 64:(ch + 1) * 64, h * 96:(h + 1) * 96], vb[ch * 64:(ch + 1) * 64, h * 192:(h + 1) * 192],
                                              start=True, stop=True), reads=['kend', 'vb'], writes=['B%d' % bk])
        def o_ap(h, lo, hi):
            bk = 1 if h < 2 else 6
            col = (h % 2) * 192
            return B[bk][lo:hi, col:col + 192], 'B%d' % bk
        SbS = SbA[i % 2]; kS = 'SbA%d' % (i % 2)
        SbN = SbA[(i + 1) % 2]; kN = 'SbA%d' % ((i + 1) % 2)
        for ch in range(2):
            for h in range(4):
                bk = 2 + ch * 2 + h // 2
                col = (h % 2) * 192
                S.op('dve', lambda e: e.scalar_tensor_tensor(out=St[:, h, :], in0=St[:, h, :], scalar=expb[:, h, ch * 64 + 63:ch * 64 + 64], in1=B[bk][0:96, col:col + 192],
                                                             op0=ALU.mult, op1=ALU.add), reads=['St', 'expb', 'B%d' % bk], writes=['St'])
            Sb_dst, kd = (SbB, 'SbB') if ch == 0 else (SbN, kN)
            S.op('act', lambda e: e.copy(out=Sb_dst[:].rearrange("p a b -> p (a b)"), in_=St[:].rearrange("p a b -> p (a b)")),
                 reads=['St'], writes=[kd])
        for h in range(4):
            oap, ok = o_ap(h, 0, 128)
            S.op('pe', lambda e: e.matmul(oap, attnTb[:, h, :], vb[:, h * 192:(h + 1) * 192], start=True, stop=False), reads=['attnTb', 'vb'], writes=[ok])
            oap0, _ = o_ap(h, 0, 64)
            S.op('pe', lambda e: e.matmul(oap0, qt[:, h, 0:64], SbS[:, h, :], start=False, stop=False), reads=['qt', kS], writes=[ok])
            oap1, _ = o_ap(h, 64, 128)
            S.op('pe', lambda e: e.matmul(oap1, qt[:, h, 64:128], SbB[:, h, :], start=False, stop=True), reads=['qt', 'SbB'], writes=[ok])
        for hh in range(2):
            bk = 1 if hh == 0 else 6
            S.op('act', lambda e: e.activation(out=sq[:, hh * 384:(hh + 1) * 384], in_=B[bk][:, 0:384], func=AF.Square), reads=['B%d' % bk], writes=['sq'])
        S.op('dve', lambda e: e.tensor_reduce(out=ssq[:], in_=sq[:].rearrange("p (h v) -> p h v", h=4), axis=AX.X, op=ALU.add), reads=['sq'], writes=['ssq'])
        S.op('act', lambda e: e.activation(out=ssq[:], in_=ssq[:], func=AF.Sqrt, bias=HN_EPS, scale=1.0 / 192.0), reads=['ssq'], writes=['ssq'])
        S.op('dve', lambda e: e.reciprocal(out=ssq[:], in_=ssq[:]), reads=['ssq'], writes=['ssq'])
        for hh in range(2):
            bk = 1 if hh == 0 else 6
            S.op('dve', lambda e: e.tensor_tensor(out=to[:, hh * 384:(hh + 1) * 384].rearrange("p (h v) -> p h v", h=2),
                                                  in0=B[bk][:, 0:384].rearrange("p (h v) -> p h v", h=2),
                                                  in1=ssq[:, hh * 2:hh * 2 + 2].unsqueeze(2).to_broadcast([128, 2, 192]), op=ALU.mult),
                 reads=['B%d' % bk, 'ssq'], writes=['to'])
        S.op('dve', lambda e: e.tensor_tensor(out=cat[:, 0:768], in0=to[:], in1=gate[:], op=ALU.mult), reads=['to', 'gate'], writes=['cat'])
        mem_attn_tile(C, qmT, 'qmT', kmT, vmx, '0', cat, 'cat', (2, 3), 4)
        outproj_tile(C, cat, 'cat', wout, 'wout0', XR[j], 'XR%d' % j, 5, (7, 0), pre, kpre=kpre)

    tile_X(0)
    for i in range(NT):
        if i + 1 < NT:
            tile_X(i + 1)
        ln_epilogue(C, pres[i % 2], 'a_pre%d' % (i % 2), lng, lnb, C.xm0, C.xm0T, i, 'A', 5)
    C.sb = C.sb_save
    st.close()


def peer_convert(C, l):
    S, dram = C.S, C.dram
    uT_d = dram('peer_uT%d' % l, [D, 16384])
    v_d = dram('peer_v%d' % l, [16384, D])
    uTb = dram('peer_uTb%d' % l, [D, 16384], BF16, "Internal")
    vb = dram('peer_vb%d' % l, [16384, D], BF16, "Internal")
    C.peer_w = getattr(C, 'peer_w', {})
    C.peer_w[l] = (uTb, vb)
    q = []
    for qq in range(4):
        for c in range(8):
            q.append(lambda c=c, qq=qq: S.dma('pool', uTb[c * 128:(c + 1) * 128, qq * 4096:(qq + 1) * 4096], uT_d[c * 128:(c + 1) * 128, qq * 4096:(qq + 1) * 4096], writes=['uTb%d' % l]))
        for c in range(qq * 8, qq * 8 + 8):
            q.append(lambda c=c: S.dma('pool', vb[c * 512:(c + 1) * 512, :].rearrange("(p a) d -> p a d", p=128), v_d[c * 512:(c + 1) * 512, :].rearrange("(p a) d -> p a d", p=128), writes=['vb%d' % l]))
    C.conv_q = getattr(C, 'conv_q', {})
    C.conv_q[l] = q


def conv_pop(C, l, n):
    q = getattr(C, 'conv_q', {}).get(l, [])
    for _ in range(min(n, len(q))):
        q.pop(0)()


def phase_P(C, l, xm_d, xmT_d, xf_d, xfT_d, n_super=SEQ // 512):
    S, B, dram, nc = C.S, C.B, C.dram, C.nc
    st = ExitStack()
    sb = lambda name, shape, dt=F32: st.enter_context(nc.sbuf_tensor("p%d_%s" % (l, name), list(shape), dt))
    P = 'p%d_' % l
    wq = sb("wq", [128, 8, 2048], BF16)
    wq_d = dram('peer_w_q%d' % l, [D, 2048])
    for c in range(8):
        S.dma('pool', wq[:, c, :], wq_d[c * 128:(c + 1) * 128, :], writes=[P + 'wq'])
    keysT = sb("keysT", [128, 2, 128])
    kd = dram('peer_keysT%d' % l, [2, 128, 128])
    for p in range(2):
        S.dma('sp', keysT[:, p, :], kd[p], writes=[P + 'keysT'])
    if l not in getattr(C, 'peer_w', {}):
        peer_convert(C, l)
    conv_pop(C, l, 1000)
    uT_d, v_d = C.peer_w[l]
    lng = sb("lng", [128, D]); lnb = sb("lnb", [128, D])
    S.dma('sp', lng[:], dram('ln_ffn_g%d' % l, [1, D]).partition_broadcast(128), writes=['lnp'])
    S.dma('sp', lnb[:], dram('ln_ffn_b%d' % l, [1, D]).partition_broadcast(128), writes=['lnp'])
    xmT = sb("xmT", [128, 8, 512], BF16)
    acc = sb("acc", [128, 4, 1024])
    top = sb("top", [128, 8, 2, 16])
    c16 = sb("c16", [128, 8, 16])
    dd = sb("dd", [128, 8, 16])
    zz = sb("zz", [128, 8])
    cs = sb("cs", [128, 8, 2])
    E = sb("E", [128, 4, 16, 128])
    gsc = sb("gsc", [128, 4, 8])
    uT = [sb("uT%d" % q, [128, 8, 512], BF16) for q in range(2)]
    vv = [sb("vv%d" % q, [128, 4, 1024], BF16) for q in range(2)]
    gel = [sb("gel%d" % q, [128, 512]) for q in range(2)]
    W8 = [sb("W8_%d" % q, [128, 8, 4, 128]) for q in range(2)]
    w0 = W8[0][:].rearrange("p h a b -> p (h a b)")
    w1 = W8[1][:].rearrange("p h a b -> p (h a b)")
    qT4a = w0[:, 0:4096].rearrange("p (a b) -> p a b", a=8)
    qT4b = w1[:, 0:4096].rearrange("p (a b) -> p a b", a=8)
    Sg = [sb("Sg%d" % q, [128, 8, 512], BF16) for q in range(2)]
    tmpo = [sb("tmpo%d" % q, [128, 1024]) for q in range(2)]
    ngsc = sb("ngsc", [128, 4, 8])
    xr = tmpo[1]
    s_sb = Sg[0][:].rearrange("p h n -> p (h n)").bitcast(F32).rearrange("p (a b) -> p a b", a=16)
    cand = Sg[1][:].rearrange("p h n -> p (h n)").bitcast(F32).rearrange("p (h a b) -> p h a b", h=8, a=16)
    tmpA = tmpo[0][:, :].rearrange("p (a b) -> p a b", a=8)
    tmpB = tmpo[1][:, :].rearrange("p (a b) -> p a b", a=8)
    tmpC = tmpo[0][:, :].rearrange("p (a b) -> p a b", a=4)
    tmpD = tmpo[1][:, :].rearrange("p (a b) -> p a b", a=4)
    G = sb("G", [128, 512], BF16)
    A = [sb("A%d" % q, [128, 512], BF16) for q in range(2)]
    AT = [sb("AT%d" % q, [128, 4, 128], BF16) for q in range(2)]
    pre = tmpo[0]
    C.ln_st = sb("ln_st", [128, 2, 6]); C.ln_mv = sb("ln_mv", [128, 2]); C.ln_rs = sb("ln_rs", [128, 1])
    C.ln_xn = sb("ln_xn", [128, 1024]); C.ln_xnb = sb("ln_xnb", [128, 1024], BF16); C.ln_xnT = sb("ln_xnT", [128, 8, 128], BF16)
    DELTA = 2e-4
    NEG = -1e30
    cnt = [0]

    def load_chunk(k):
        q = k % 2
        S.dma('sp', uT[q][:], uT_d[:, k * 512:(k + 1) * 512].rearrange("(c p) e -> p c e", p=128), reads=['uTb%d' % l], writes=[P + 'uT%d' % q])
        S.dma('sp', vv[q][:], v_d[k * 512:(k + 1) * 512, :].rearrange("(a b) d -> b a d", b=128), reads=['vb%d' % l], writes=[P + 'vv%d' % q])

    for sti in range(n_super):
        t0 = sti * 512
        S.dma('sp', xmT[:], xmT_d[:, t0:t0 + 512].rearrange("(c p) t -> p c t", p=128), writes=[P + 'xmT'])
        load_chunk(0)
        load_chunk(1)
        if l == 0:
            conv_pop(C, 1, 8)
        for hp in range(16):
            bk = 4 + (hp % 4)
            for c in range(8):
                S.op('pe', lambda e: e.matmul(B[bk][:, :], wq[:, c, hp * 128:(hp + 1) * 128], xmT[:, c, :], start=(c == 0), stop=(c == 7)),
                     reads=[P + 'wq', P + 'xmT'], writes=['B%d' % bk])
            qdst = (qT4a if hp < 8 else qT4b)[:, hp % 8, :]
            S.op('act', lambda e: e.copy(out=qdst, in_=B[bk][:, :]), reads=['B%d' % bk], writes=[P + 'qT'])
        for tt in range(4):
            qTf = lambda hp: (qT4a if hp < 8 else qT4b)[:, hp % 8, tt * 128:(tt + 1) * 128]
            for hp in range(16):
                bk = hp // 4
                col = (hp % 4) * 128
                S.op('pe', lambda e: e.matmul(B[bk][:, col:col + 128], qTf(hp), keysT[:, hp % 2, :], start=True, stop=True),
                     reads=[P + 'qT', P + 'keysT'], writes=['B%d' % bk])
            for bk in range(4):
                S.op('act', lambda e: e.copy(out=s_sb[:, bk * 4:(bk + 1) * 4, :].rearrange("p a b -> p (a b)"), in_=B[bk][:, :]), reads=['B%d' % bk], writes=[P + 's_sb'])
            for hp in range(16):
                h, p = hp // 2, hp % 2
                S.op('dve', lambda e: e.max(out=top[:, h, p, 0:8], in_=s_sb[:, hp, :]), reads=[P + 's_sb'], writes=[P + 'top%d' % hp])
            tslot = lambda hp: (tmpA if hp < 8 else tmpB)[:, hp % 8, :]
            for hp in range(16):
                h, p = hp // 2, hp % 2
                S.op('dve', lambda e: e.match_replace(out=tslot(hp), in_to_replace=top[:, h, p, 0:8], in_values=s_sb[:, hp, :], imm_value=NEG),
                     reads=[P + 's_sb', P + 'top%d' % hp], writes=[P + 'tmp%d' % hp])
            for hp in range(16):
                h, p = hp // 2, hp % 2
                S.op('dve', lambda e: e.max(out=top[:, h, p, 8:16], in_=tslot(hp)), reads=[P + 'tmp%d' % hp], writes=[P + 'top%d' % hp])
            allt = [P + 'top%d' % hp for hp in range(16)]
            S.op('dve', lambda e: e.tensor_tensor(out=cand, in0=top[:, :, 0, :].unsqueeze(3).to_broadcast([128, 8, 16, 16]),
                                                  in1=top[:, :, 1, :].unsqueeze(2).to_broadcast([128, 8, 16, 16]), op=ALU.add),
                 reads=allt, writes=[P + 'cand'])
            for h in range(8):
                S.op('dve', lambda e: e.max(out=c16[:, h, 0:8], in_=cand[:, h, :, :].rearrange("p a b -> p (a b)")), reads=[P + 'cand'], writes=[P + 'c16_%d' % h])
            cslot = lambda h: (tmpC if h < 4 else tmpD)[:, h % 4, :]
            for h in range(8):
                S.op('dve', lambda e: e.match_replace(out=cslot(h), in_to_replace=c16[:, h, 0:8], in_values=cand[:, h, :, :].rearrange("p a b -> p (a b)"), imm_value=NEG),
                     reads=[P + 'cand', P + 'c16_%d' % h], writes=[P + 'tmp%d' % (2 * (h % 4) + (0 if h < 4 else 8)), P + 'tmp%d' % (2 * (h % 4) + 1 + (0 if h < 4 else 8))])
            for h in range(8):
                S.op('dve', lambda e: e.max(out=c16[:, h, 8:16], in_=cslot(h)),
                     reads=[P + 'tmp%d' % (2 * (h % 4) + (0 if h < 4 else 8)), P + 'tmp%d' % (2 * (h % 4) + 1 + (0 if h < 4 else 8))], writes=[P + 'c16_%d' % h])
            allc = [P + 'c16_%d' % h for h in range(8)]
            S.op('dve', lambda e: e.tensor_tensor(out=dd[:], in0=c16[:], in1=c16[:, :, 0:1].to_broadcast([128, 8, 16]), op=ALU.subtract), reads=allc, writes=[P + 'dd'])
            S.op('act', lambda e: e.activation(out=dd[:].rearrange("p a b -> p (a b)"), in_=dd[:].rearrange("p a b -> p (a b)"), func=AF.Exp), reads=[P + 'dd'], writes=[P + 'dd'])
            S.op('dve', lambda e: e.tensor_reduce(out=zz[:], in_=dd[:], axis=AX.X, op=ALU.add), reads=[P + 'dd'], writes=[P + 'zz'])
            S.op('dve', lambda e: e.reciprocal(out=zz[:], in_=zz[:]), reads=[P + 'zz'], writes=[P + 'zz'])
            S.op('dve', lambda e: e.scalar_tensor_tensor(out=gsc[:, tt, :], in0=dd[:, :, 15], scalar=float(np.exp(-DELTA)), in1=zz[:], op0=ALU.mult, op1=ALU.mult),
                 reads=[P + 'dd', P + 'zz'], writes=[P + 'gsc'])
            S.op('dve', lambda e: e.tensor_scalar(out=ngsc[:, tt, :], in0=gsc[:, tt, :], scalar1=-1.0, scalar2=None, op0=ALU.mult), reads=[P + 'gsc'], writes=[P + 'ngsc'])
            S.op('dve', lambda e: e.tensor_copy(out=cs[:, :, 0], in_=top[:, :, 0, 0]), reads=allt, writes=[P + 'cs'])
            S.op('dve', lambda e: e.scalar_tensor_tensor(out=cs[:, :, 1], in0=c16[:, :, 15], scalar=-DELTA, in1=top[:, :, 0, 0], op0=ALU.add, op1=ALU.subtract),
                 reads=allc + allt, writes=[P + 'cs'])
            S.op('dve', lambda e: e.tensor_tensor(out=E[:, tt, :, :], in0=s_sb, in1=cs[:].rearrange("p h q -> p (h q)").unsqueeze(2).to_broadcast([128, 16, 128]), op=ALU.subtract),
                 reads=[P + 's_sb', P + 'cs'], writes=[P + 'E'])
            S.op('act', lambda e: e.activation(out=E[:, tt, :, :].rearrange("p a b -> p (a b)"), in_=E[:, tt, :, :].rearrange("p a b -> p (a b)"), func=AF.Exp),
                 reads=[P + 'E'], writes=[P + 'E'])
            Ea = E[:, tt, :, :].rearrange("p (h q) n -> p h q n", q=2)[:, :, 0, :]
            S.op('dve', lambda e: e.tensor_tensor(out=Ea, in0=Ea, in1=gsc[:, tt, :].unsqueeze(2).to_broadcast([128, 8, 128]), op=ALU.mult),
                 reads=[P + 'E', P + 'gsc'], writes=[P + 'E'])
        S.barrier()
        its = [(k, tt) for k in range(32) for tt in range(4)]
        NI = len(its)

        def st_P1(n):
            k, tt = its[n]; z = n % 2; q = k % 2
            for c in range(8):
                S.op('pe', lambda e: e.matmul(B[z][:, :], xmT[:, c, tt * 128:(tt + 1) * 128], uT[q][:, c, :], start=(c == 0), stop=(c == 7)),
                     reads=[P + 'xmT', P + 'uT%d' % q], writes=['B%d' % z])
            S.op('act', lambda e: e.activation(out=gel[z][:], in_=B[z][:, :], func=AF.Gelu), reads=['B%d' % z], writes=[P + 'gel%d' % z])

        def st_D1a(n):
            k, tt = its[n]; z = n % 2
            Ev = E[:, tt, :, :].rearrange("p (h q) n -> p h q n", q=2)
            HD = 8 - N_ACT_HEADS
            S.op('dve', lambda e: e.tensor_tensor(out=W8[z][:, 0:HD, :, :], in0=Ev[:, 0:HD, 0, k * 4:(k + 1) * 4].unsqueeze(3).to_broadcast([128, HD, 4, 128]),
                                                  in1=Ev[:, 0:HD, 1, :].unsqueeze(2).to_broadcast([128, HD, 4, 128]), op=ALU.mult),
                 reads=[P + 'E'], writes=[P + 'W8_%d' % z])
            for h in range(HD, 8):
                for a in range(4):
                    S.op('act', lambda e: e.activation(out=W8[z][:, h, a, :], in_=Ev[:, h, 1, :], func=AF.Copy, scale=Ev[:, h, 0, k * 4 + a:k * 4 + a + 1]),
                         reads=[P + 'E'], writes=[P + 'W8a_%d' % z])

        def st_SG(n):
            k, tt = its[n]; z = n % 2
            W8h = W8[z][:].rearrange("p h a b -> p h (a b)")
            for h in range(8):
                S.op('act', lambda e: e.activation(out=Sg[z][:, h, :], in_=W8h[:, h, :], func=AF.Sign, bias=ngsc[:, tt, h:h + 1], scale=1.0),
                     reads=[P + 'W8_%d' % z, P + 'W8a_%d' % z, P + 'ngsc'], writes=[P + 'Sg%d_%d' % (z, h)])

        def st_D1b(n):
            k, tt = its[n]; z = n % 2
            kS = [P + 'Sg%d_%d' % (z, h) for h in range(8)]
            Sf = Sg[z][:].rearrange("p h n -> p (h n)")
            S.op('dve', lambda e: e.scalar_tensor_tensor(out=Sf, in0=Sf, scalar=1.0, in1=W8[z][:].rearrange("p h a b -> p (h a b)"), op0=ALU.add, op1=ALU.mult),
                 reads=kS + [P + 'W8_%d' % z, P + 'W8a_%d' % z], writes=kS)
            S.op('dve', lambda e: e.tensor_tensor(out=Sg[z][:, 0:4, :], in0=Sg[z][:, 0:4, :], in1=Sg[z][:, 4:8, :], op=ALU.add), reads=kS, writes=kS)
            S.op('dve', lambda e: e.tensor_tensor(out=Sg[z][:, 0:2, :], in0=Sg[z][:, 0:2, :], in1=Sg[z][:, 2:4, :], op=ALU.add), reads=kS, writes=kS)
            S.op('dve', lambda e: e.tensor_tensor(out=G[:], in0=Sg[z][:, 0, :], in1=Sg[z][:, 1, :], op=ALU.add), reads=kS, writes=[P + 'G'])
            S.op('dve', lambda e: e.scalar_tensor_tensor(out=A[z][:], in0=G[:], scalar=0.5, in1=gel[z][:], op0=ALU.mult, op1=ALU.mult),
                 reads=[P + 'gel%d' % z, P + 'G'], writes=[P + 'A%d' % z])

        def st_P2(n):
            z = n % 2
            bt = 2 + z
            pb = B[bt][:].bitcast(BF16)
            for a in range(4):
                S.op('pe', lambda e: e.transpose(pb[:, a * 128:(a + 1) * 128], A[z][:, a * 128:(a + 1) * 128], C.ident[:]),
                     reads=[P + 'A%d' % z, 'ident'], writes=['B%d' % bt])
            S.op('act', lambda e: e.copy(out=AT[z][:].rearrange("p a t -> p (a t)"), in_=pb[:, 0:512]), reads=['B%d' % bt], writes=[P + 'AT%d' % z])

        def st_P3(n):
            k, tt = its[n]; z = n % 2; q = k % 2
            for hlf in range(2):
                bo = 4 + z * 2 + hlf
                for a in range(4):
                    S.op('pe', lambda e: e.matmul(B[bo][:, :], AT[z][:, a, :], vv[q][:, a, hlf * 512:(hlf + 1) * 512], start=(a == 0), stop=(a == 3)),
                         reads=[P + 'AT%d' % z, P + 'vv%d' % q], writes=['B%d' % bo])

        def st_D2(n):
            k, tt = its[n]; z = n % 2
            for hlf in range(2):
                bo = 4 + z * 2 + hlf
                if k == 0:
                    S.op('act', lambda e: e.copy(out=acc[:, tt, hlf * 512:(hlf + 1) * 512], in_=B[bo][:, :]), reads=['B%d' % bo], writes=[P + 'acc%d' % tt])
                else:
                    S.op('act', lambda e: e.copy(out=tmpo[z][:, hlf * 512:(hlf + 1) * 512], in_=B[bo][:, :]), reads=['B%d' % bo], writes=[P + 'tmpo%d' % z])
            if k > 0:
                S.dma('pool', acc[:, tt, :], tmpo[z][:], reads=[P + 'tmpo%d' % z, P + 'acc%d' % tt], writes=[P + 'acc%d' % tt], accum_op=ALU.add)

        st_P1(0)
        st_D1a(0)
        st_SG(0)
        for n in range(NI + 1):
            if n + 1 < NI:
                st_P1(n + 1)
                st_D1a(n + 1)
                st_SG(n + 1)
            if n < NI:
                st_D1b(n)
            if n >= 1:
                st_P3(n - 1)
                k_prev, tt_prev = its[n - 1]
                if tt_prev == 3 and k_prev + 2 < 32:
                    load_chunk(k_prev + 2)
            if n < NI:
                st_P2(n)
            if n >= 1:
                st_D2(n - 1)
        S.barrier()
        for tt in range(4):
            i = sti * 4 + tt
            S.dma('sp', xr[:], xm_d[i * 128:(i + 1) * 128, :], writes=[P + 'xr'])
            S.op('dve', lambda e: e.scalar_tensor_tensor(out=pre[:], in0=xr[:], scalar=DN_ALPHA, in1=acc[:, tt, :], op0=ALU.mult, op1=ALU.add),
                 reads=[P + 'xr', P + 'acc%d' % tt], writes=['pre'])
            ln_epilogue(C, pre, 'pre', lng, lnb, xf_d, xfT_d, i, 'P%d' % l, 3)
    st.close()


def phase_B(C, xf_d, xfT_d, xm1_d, xm1T_d):
    S, B, dram, nc = C.S, C.B, C.dram, C.nc
    st = ExitStack()
    sb = lambda name, shape, dt=F32: st.enter_context(nc.sbuf_tensor("b_" + name, list(shape), dt))
    C.sb_save = C.sb
    C.sb = sb
    bw_d = dram('b_w_in', [D, B_W])
    kv_d = dram('shared_w_kv', [D, 1536])
    catT_d = dram('catT1', [768, SEQ], BF16, "Internal")
    XF = sb("XF", [128, 8, SEQ], BF16)
    for c in range(8):
        S.dma('sp' if c % 2 == 0 else 'act', XF[:, c, :], xfT_d[c * 128:(c + 1) * 128, :], writes=['XF'])
    st1 = ExitStack()
    sb_outer = sb
    sb = lambda name, shape, dt=F32: st1.enter_context(nc.sbuf_tensor("b1_" + name, list(shape), dt))
    acc = sb("acc", [128, 2, SEQ])
    mixT = sb("mixT", [128, SEQ], BF16)
    KT = sb("KT", [128, SEQ], BF16)
    QT = [sb("QT%d" % q, [128, SEQ], BF16) for q in range(2)]
    V = [sb("V%d" % q, [128, 32, 128], BF16) for q in range(2)]
    Wq = sb("Wq", [128, 8, 3, 128], BF16)
    Wk = sb("Wk", [128, 8, 128], BF16)
    Wva = sb("Wva", [128, 8, 768], BF16)
    vsb = [sb("vsb%d" % q, [128, 768], BF16) for q in range(2)]
    PT = [sb("PT%d" % q, [128, 2, 128], BF16) for q in range(2)]
    SC = 128 ** -0.5
    DIL = (1, 4, 16)
    Vd = dram('b_Vd', [SEQ, 768], BF16, "Internal")

    for c in range(8):
        S.dma('pool', Wva[:, c, :], kv_d[c * 128:(c + 1) * 128, 768:1536], writes=['b_Wva'])
    for i in range(NT):
        z = i % 2
        for (bk, c0, n) in ((4 + 2 * z, 0, 512), (5 + 2 * z, 512, 256)):
            for c in range(8):
                S.op('pe', lambda e: e.matmul(B[bk][:, 0:n], XF[:, c, i * 128:(i + 1) * 128], Wva[:, c, c0:c0 + n], start=(c == 0), stop=(c == 7)),
                     reads=['XF', 'b_Wva'], writes=['B%d' % bk])
            S.op('act', lambda e: e.copy(out=vsb[z][:, c0:c0 + n], in_=B[bk][:, 0:n]), reads=['B%d' % bk], writes=['b_vsb%d' % z])
        S.dma('sp', Vd[i * 128:(i + 1) * 128, :], vsb[z][:], reads=['b_vsb%d' % z], writes=['b_Vd'])

    def proj_T(dst, key_dst, wsel, key_w):
        for tg in range(8):
            bk = 4 + tg % 4
            for c in range(8):
                S.op('pe', lambda e: e.matmul(B[bk][:, :], wsel(c), XF[:, c, tg * 512:(tg + 1) * 512], start=(c == 0), stop=(c == 7)),
                     reads=[key_w, 'XF'], writes=['B%d' % bk])
            S.op('act', lambda e: e.copy(out=dst[:, tg * 512:(tg + 1) * 512], in_=B[bk][:, :]), reads=['B%d' % bk], writes=[key_dst])

    def load_V(s_, g, q):
        d = DIL[g]
        nb = 32 // d
        src = Vd.rearrange("(n p r) f -> p r n f", p=128, r=d)[:, :, :, s_ * 128:(s_ + 1) * 128]
        S.dma('act', V[q][:].rearrange("p (r n) e -> p r n e", r=d), src, reads=['b_Vd'], writes=['b_V%d' % q])

    sg = [(s_, g) for s_ in range(6) for g in range(3)]
    load_V(0, 0, 0)
    for idx, (s_, g) in enumerate(sg):
        vq = idx % 2
        if idx + 1 < len(sg):
            load_V(sg[idx + 1][0], sg[idx + 1][1], (idx + 1) % 2)
        if g == 0:
            for c in range(8):
                S.dma('pool', Wq[:, c, :, :], bw_d[c * 128:(c + 1) * 128, 0:2304].rearrange("p (g s e) -> p g s e", g=3, s=6)[:, :, s_, :], writes=['b_Wq'])
                S.dma('pool', Wk[:, c, :], kv_d[c * 128:(c + 1) * 128, s_ * 128:(s_ + 1) * 128], writes=['b_Wk'])
            proj_T(KT, 'b_KT', lambda c: Wk[:, c, :], 'b_Wk')
        d = DIL[g]
        nb = 32 // d
        QTg = QT[idx % 2]
        kQ = 'b_QT%d' % (idx % 2)
        proj_T(QTg, kQ, lambda c: Wq[:, c, g, :], 'b_Wq')
        Vg = V[vq]
        kV = 'b_V%d' % vq

        def blk_info(blk):
            r, n = blk // nb, blk % nb
            tq = d * 128 * n + r
            return r, n, slice(tq, tq + 127 * d + 1, d), ([1] if n == 0 else [0, 1])

        def stg1(blk):
            r, n, qs, kbs = blk_info(blk)
            z = blk % 2
            for kb in kbs:
                tk = d * 128 * (n - 1 + kb) + r
                S.op('pe', lambda e: e.matmul(B[z][:, kb * 128:(kb + 1) * 128], KT[:, tk:tk + 127 * d + 1:d], QTg[:, qs], start=True, stop=True),
                     reads=['b_KT', kQ], writes=['B%d' % z])
            lo = kbs[0]
            S.op('act', lambda e: e.activation(out=PT[z][:, lo:2, :].rearrange("p a b -> p (a b)"), in_=B[z][:, lo * 128:256], func=AF.Exp, scale=SC),
                 reads=['B%d' % z], writes=['b_PT%d' % z])
            S.op('dve', lambda e: e.tensor_tensor(out=PT[z][:, lo:2, :], in0=PT[z][:, lo:2, :], in1=C.dmask[:, lo:2, :], op=ALU.mult),
                 reads=['b_PT%d' % z, 'dmask'], writes=['b_PT%d' % z])

        def stg2(blk):
            r, n, qs, kbs = blk_info(blk)
            z = blk % 2
            bo = 2 + z
            for which in range(2):
                for kb in kbs:
                    lhs = Vg[:, blk - 1 + kb, :] if which == 0 else C.ones_b[:]
                    S.op('pe', lambda e: e.matmul(B[bo][:, which * 128:(which + 1) * 128], lhs, PT[z][:, kb, :], start=(kb == kbs[0]), stop=(kb == 1)),
                         reads=[kV, 'ones_b', 'b_PT%d' % z], writes=['B%d' % bo])
            pv = B[bo][:, 0:256].rearrange("p (a b) -> p a b", a=2)
            if g == 0:
                S.op('act', lambda e: e.copy(out=acc[:, :, qs], in_=pv), reads=['B%d' % bo], writes=['b_acc'])
            else:
                S.op('dve', lambda e: e.tensor_tensor(out=acc[:, :, qs], in0=acc[:, :, qs], in1=pv, op=ALU.add), reads=['B%d' % bo, 'b_acc'], writes=['b_acc'])

        stg1(0)
        for blk in range(32):
            if blk + 1 < 32:
                stg1(blk + 1)
            stg2(blk)
        if g == 2:
            S.op('dve', lambda e: e.reciprocal(out=acc[:, 1, :], in_=acc[:, 1, :]), reads=['b_acc'], writes=['b_acc'])
            S.op('dve', lambda e: e.tensor_tensor(out=mixT[:], in0=acc[:, 0, :], in1=acc[:, 1, :], op=ALU.mult), reads=['b_acc'], writes=['b_mixT'])
            S.dma('sp', catT_d[s_ * 128:(s_ + 1) * 128, :], mixT[:], reads=['b_mixT'], writes=['catT1_d'])
    S.barrier()
    st1.close()
    sb = sb_outer
    Wm = sb("Wm", [128, 8, 256], BF16)
    for c in range(8):
        S.dma('pool', Wm[:, c, :], bw_d[c * 128:(c + 1) * 128, 2304:2560], writes=['b_Wm'])
    wout = load_w_bf(C, "wout1", dram('w_out1', [D, D]), D, 'wout1')
    lng = load_rep(C, "lng", dram('ln_mix_g1', [1, D]), D, 'lnp')
    lnb = load_rep(C, "lnb", dram('ln_mix_b1', [1, D]), D, 'lnp')
    kmT, vmx = mem_kv(C, dram('memT', [D, 256]) if not hasattr(C, 'memT_d') else C.memT_d, dram('w_mem_kv1', [D, 512]), '1')
    alloc_ln(C)
    C.ma_pT = sb("ma_pT", [128, 1024], BF16)
    C.ma_rd = sb("ma_rd", [128, 4])
    C.catT = sb("catT", [128, 8, 128], BF16)
    qmT = sb("qmT", [64, 4, 128], BF16)
    cat = sb("cat", [128, 1024], BF16)
    pre = sb("pre", [128, 1024])
    XR = [sb("XR%d" % j, [128, 1024]) for j in range(2)]
    pres = [pre, sb("pre2", [128, 1024])]

    def tile_X(i):
        j = i % 2
        S.dma('sp', XR[j][:], xf_d[i * 128:(i + 1) * 128, :], writes=['bXR%d' % j])
        S.dma('sp', C.catT[:, 0:6, :], catT_d[:, i * 128:(i + 1) * 128].rearrange("(c p) t -> p c t", p=128), reads=['catT1_d'], writes=['catT'])
        for h in range(4):
            for c in range(8):
                S.op('pe', lambda e: e.matmul(B[6][0:64, h * 128:(h + 1) * 128], Wm[:, c, h * 64:(h + 1) * 64], XF[:, c, i * 128:(i + 1) * 128], start=(c == 0), stop=(c == 7)),
                     reads=['b_Wm', 'XF'], writes=['B6'])
        S.op('act', lambda e: e.copy(out=qmT[:].rearrange("p h t -> p (h t)"), in_=B[6][0:64, :]), reads=['B6'], writes=['b_qmT'])
        mem_attn_tile(C, qmT, 'b_qmT', kmT, vmx, '1', cat, 'b_cat', (2, 3), 4)
        outproj_tile(C, cat, 'b_cat', wout, 'wout1', XR[j], 'bXR%d' % j, 5, (7, 0), pres[j], chunks=(6, 7), kpre='b_pre%d' % j)

    tile_X(0)
    for i in range(NT):
        if i + 1 < NT:
            tile_X(i + 1)
        ln_epilogue(C, pres[i % 2], 'b_pre%d' % (i % 2), lng, lnb, xm1_d, xm1T_d, i, 'B', 1)
    C.sb = C.sb_save
    st.close()


_PROG_CACHE = {}


def _prep_inputs(inputs, b):
    g = lambda k: np.asarray(inputs[k], dtype=np.float32)
    m = {}
    m['x'] = np.ascontiguousarray(g('x')[b])
    m['xT'] = np.ascontiguousarray(g('x')[b].T)
    m['memT'] = np.ascontiguousarray(g('mem')[b].T)
    m['a_w_in'] = np.ascontiguousarray(g('a_w_in')[0])
    m['a_w_gate2'] = np.ascontiguousarray(g('a_w_gate2')[0])
    m['a_b_gate'] = np.ascontiguousarray(g('a_b_gate')[0][None, :])
    m['a_norm_g'] = np.ascontiguousarray(g('a_norm_g')[0][None, :])
    for l in range(2):
        m['w_mem_kv%d' % l] = np.ascontiguousarray(g('w_mem_kv')[l])
        m['w_out%d' % l] = np.ascontiguousarray(g('w_out')[l])
        for nm in ('ln_mix_g', 'ln_mix_b', 'ln_ffn_g', 'ln_ffn_b'):
            m['%s%d' % (nm, l)] = np.ascontiguousarray(g(nm)[l][None, :])
    m['b_w_in'] = np.ascontiguousarray(g('b_w_in')[0])
    m['shared_w_kv'] = np.ascontiguousarray(g('shared_w_kv'))
    for l in range(2):
        m['peer_w_q%d' % l] = np.ascontiguousarray(g('peer_w_q')[l])
        m['peer_keysT%d' % l] = np.ascontiguousarray(g('peer_sub_keys')[l].transpose(0, 2, 1))
        m['peer_uT%d' % l] = np.ascontiguousarray(g('peer_u')[l].T)
        m['peer_v%d' % l] = np.ascontiguousarray(g('peer_v')[l])
    m.update(_consts_host())
    return m


def kernel(**inputs):
    if 'nc' not in _PROG_CACHE:
        _PROG_CACHE['nc'] = build()
    nc = _PROG_CACHE['nc']
    in_maps = [_prep_inputs(inputs, b) for b in range(8)]
    res = run_bass_kernel_spmd(nc, in_maps, core_ids=list(range(8)))
    out = np.stack([np.asarray(r['out'], dtype=np.float32) for r in res.results], axis=0)
    return out
```

```python
from contextlib import ExitStack
import numpy as np
import concourse.bass as bass
import concourse.mybir as mybir
from concourse.bass_utils import run_bass_kernel_spmd

F32 = mybir.dt.float32
BF16 = mybir.dt.bfloat16
ALU = mybir.AluOpType
AF = mybir.ActivationFunctionType
AX = mybir.AxisListType

D = 1024
SEQ = 4096
NT = SEQ // 128
DN_ALPHA = 4.0 ** 0.25
N_ACT_HEADS = 1
LN_EPS = 1e-5
HN_EPS = 1e-6
A_W = 2576
B_W = 2560


class Sched:
    def __init__(self, nc, stack):
        self.nc = nc
        self.E = {'pe': nc.tensor, 'dve': nc.vector, 'act': nc.scalar, 'pool': nc.gpsimd, 'sp': nc.sync}
        self.sem = {e: stack.enter_context(nc.semaphore('s_' + e)) for e in self.E}
        self.cnt = {e: 0 for e in self.E}
        self.seen = {e: {} for e in self.E}
        self.NDS = 32
        self.dsem = [stack.enter_context(nc.semaphore('d%d' % i)) for i in range(self.NDS)]
        self.dcnt = [0] * self.NDS
        self.dnext = 0
        self.dnext_sw = 0
        self.tiles = {}
        self.nins = 0

    def _st(self, key):
        if key not in self.tiles:
            self.tiles[key] = {'w': None, 'r': {}}
        return self.tiles[key]

    def _semobj(self, sk):
        return self.sem[sk] if isinstance(sk, str) else self.dsem[sk]

    def _wait(self, eng, sk, val):
        if self.seen[eng].get(sk, 0) >= val:
            return
        self.E[eng].wait_ge(self._semobj(sk), val)
        self.seen[eng][sk] = val
        self.nins += 1

    def _deps(self, eng, reads, writes):
        for k in reads:
            st = self._st(k)
            if st['w'] is not None:
                self._wait(eng, *st['w'])
        for k in writes:
            st = self._st(k)
            if st['w'] is not None:
                self._wait(eng, *st['w'])
            for sk, v in st['r'].items():
                self._wait(eng, sk, v)

    def _mark(self, sk, val, reads, writes):
        for k in reads:
            st = self._st(k)
            st['r'][sk] = max(st['r'].get(sk, 0), val)
        for k in writes:
            st = self._st(k)
            st['w'] = (sk, val)
            st['r'] = {}

    def op(self, eng, fn, reads=(), writes=()):
        self._deps(eng, reads, writes)
        ins = fn(self.E[eng])
        self.cnt[eng] += 1
        ins.then_inc(self.sem[eng], 1)
        self._mark(eng, self.cnt[eng], reads, writes)
        self.nins += 1
        return ins

    def dma(self, eng, out, in_, reads=(), writes=(), **kw):
        half = self.NDS // 2
        if eng == 'pool':
            s = self.dnext_sw
            self.dnext_sw = (self.dnext_sw + 1) % half
        else:
            s = half + self.dnext
            self.dnext = (self.dnext + 1) % half
        if self.dcnt[s] > 0:
            self._wait(eng, s, self.dcnt[s])
        self._deps(eng, reads, writes)
        ins = self.E[eng].dma_start(out=out, in_=in_, **kw)
        self.dcnt[s] += 16
        ins.then_inc(self.dsem[s], 16)
        self._mark(s, self.dcnt[s], reads, writes)
        self.nins += 1
        return ins

    def barrier(self):
        for e in self.E:
            for s in range(self.NDS):
                if self.dcnt[s] > 0:
                    self._wait(e, s, self.dcnt[s])
            for o in self.E:
                if o != e and self.cnt[o] > 0:
                    self._wait(e, o, self.cnt[o])

    def finish(self, eng='sp'):
        for s in range(self.NDS):
            if self.dcnt[s] > 0:
                self._wait(eng, s, self.dcnt[s])
        for e in self.E:
            if e != eng and self.cnt[e] > 0:
                self._wait(eng, e, self.cnt[e])


class Ctx:
    pass


def _consts_host():
    idx = np.arange(128)
    same = (idx[:, None] // 64) == (idx[None, :] // 64)
    M2 = (same & (idx[:, None] <= idx[None, :])).astype(np.float32)
    U2 = (same & (idx[:, None] > idx[None, :])).astype(np.float32)
    mp = (idx[:, None] >= idx[None, :]).astype(np.float32)
    mc = (idx[:, None] <= idx[None, :]).astype(np.float32)
    return {
        'c_ident': np.eye(128, dtype=np.float32),
        'c_m2': M2, 'c_u2': U2,
        'c_dmask': np.ascontiguousarray(np.stack([mp, mc], axis=1)),
        'c_ones': np.ones((128, 128), dtype=np.float32),
    }


def build(phases=('A', 'P0', 'B', 'P1'), dbg=False, n_super=SEQ // 512):
    nc = bass.Bass("TRN2", target_bir_lowering=False)
    C = Ctx()
    C.nc = nc
    st = ExitStack()
    C.st = st
    S = Sched(nc, st)
    C.S = S

    def dram(name, shape, dt=F32, kind="ExternalInput"):
        return nc.dram_tensor(name, list(shape), dt, kind=kind).ap()
    C.dram = dram

    def sb(name, shape, dt=F32):
        return st.enter_context(nc.sbuf_tensor(name, list(shape), dt))
    C.sb = sb

    C.B = [st.enter_context(nc.psum_tensor("bank%d" % i, [128, 512], F32)) for i in range(8)]

    C.ident = sb("ident", [128, 128], BF16)
    C.m2 = sb("m2", [128, 128], F32)
    C.u2 = sb("u2", [128, 128], F32)
    C.ones_f = sb("ones_f", [128, 128], F32)
    C.ones_b = sb("ones_b", [128, 128], BF16)
    C.dmask = sb("dmask", [128, 2, 128], BF16)
    S.dma('pool', C.ident[:], dram('c_ident', [128, 128]), writes=['ident'])
    S.dma('sp', C.m2[:], dram('c_m2', [128, 128]), writes=['m2'])
    S.dma('sp', C.u2[:], dram('c_u2', [128, 128]), writes=['u2'])
    c_ones = dram('c_ones', [128, 128])
    S.dma('sp', C.ones_f[:], c_ones, writes=['ones_f'])
    S.dma('pool', C.ones_b[:], c_ones, writes=['ones_b'])
    S.dma('pool', C.dmask[:], dram('c_dmask', [128, 2, 128]), writes=['dmask'])

    ext_in = "ExternalInput"
    inter = "ExternalOutput" if dbg else "Internal"
    C.out = None
    names = {'A': 'xm0', 'P0': 'xf0', 'B': 'xm1'}
    prev = {'P0': 'xm0', 'B': 'xf0', 'P1': 'xm1'}
    for l in (0, 1):
        if ('P%d' % l) in phases:
            peer_convert(C, l)
    for ph in ('A', 'P0', 'B', 'P1'):
        if ph not in phases:
            continue
        if ph in prev and not hasattr(C, prev[ph]):
            setattr(C, prev[ph], dram(prev[ph], [SEQ, D], F32, ext_in))
            setattr(C, prev[ph] + 'T', dram(prev[ph] + 'T', [D, SEQ], BF16, ext_in))
        if ph in names:
            setattr(C, names[ph], dram(names[ph], [SEQ, D], F32, inter))
            setattr(C, names[ph] + 'T', dram(names[ph] + 'T', [D, SEQ], BF16, inter))
        if ph == 'A':
            phase_A(C)
        elif ph == 'P0':
            phase_P(C, 0, C.xm0, C.xm0T, C.xf0, C.xf0T, n_super)
        elif ph == 'B':
            phase_B(C, C.xf0, C.xf0T, C.xm1, C.xm1T)
        else:
            C.out = dram('out', [SEQ, D], F32, "ExternalOutput")
            phase_P(C, 1, C.xm1, C.xm1T, C.out, None, n_super)
        S.barrier()
    S.finish('sp')
    st.close()
    return nc


def ln_epilogue(C, pre, key_pre, g_rep, b_rep, out_dram, outT_dram, i, tag, bank):
    S = C.S
    stt, mv, rs, xn, xnb, xnT = C.ln_st, C.ln_mv, C.ln_rs, C.ln_xn, C.ln_xnb, C.ln_xnT
    for hlf in range(2):
        S.op('dve', lambda e: e.bn_stats(out=stt[:, hlf, :], in_=pre[:, hlf * 512:(hlf + 1) * 512]), reads=[key_pre], writes=['ln_st'])
    S.op('dve', lambda e: e.bn_aggr(out=mv[:], in_=stt[:].rearrange("p a b -> p (a b)")), reads=['ln_st'], writes=['ln_mv'])
    S.op('act', lambda e: e.activation(out=rs[:], in_=mv[:, 1:2], func=AF.Sqrt, bias=LN_EPS, scale=1.0), reads=['ln_mv'], writes=['ln_rs'])
    S.op('dve', lambda e: e.reciprocal(out=rs[:], in_=rs[:]), reads=['ln_rs'], writes=['ln_rs'])
    S.op('dve', lambda e: e.tensor_scalar(out=xn[:], in0=pre[:], scalar1=mv[:, 0:1], scalar2=rs[:, 0:1], op0=ALU.subtract, op1=ALU.mult),
         reads=[key_pre, 'ln_mv', 'ln_rs'], writes=['ln_xn'])
    S.op('pool', lambda e: e.tensor_tensor(out=xn[:], in0=xn[:], in1=g_rep[:], op=ALU.mult), reads=['ln_xn', 'lnp'], writes=['ln_xn'])
    S.op('pool', lambda e: e.tensor_tensor(out=xn[:], in0=xn[:], in1=b_rep[:], op=ALU.add), reads=['ln_xn', 'lnp'], writes=['ln_xn'])
    S.dma('sp', out_dram[i * 128:(i + 1) * 128, :], xn[:], reads=['ln_xn'])
    if outT_dram is not None:
        S.op('act', lambda e: e.copy(out=xnb[:], in_=xn[:]), reads=['ln_xn'], writes=['ln_xnb'])
        pb = C.B[bank][:].bitcast(BF16)
        for c in range(8):
            S.op('pe', lambda e: e.transpose(pb[:, c * 128:(c + 1) * 128], xnb[:, c * 128:(c + 1) * 128], C.ident[:]),
                 reads=['ln_xnb', 'ident'], writes=['B%d' % bank])
        S.op('act', lambda e: e.copy(out=xnT[:].rearrange("p c t -> p (c t)"), in_=pb[:, :]), reads=['B%d' % bank], writes=['ln_xnT'])
        S.dma('sp', outT_dram[:, i * 128:(i + 1) * 128].rearrange("(c p) t -> p c t", p=128), xnT[:], reads=['ln_xnT'])


def alloc_ln(C):
    sb = C.sb
    C.ln_st = sb("ln_st", [128, 2, 6])
    C.ln_mv = sb("ln_mv", [128, 2])
    C.ln_rs = sb("ln_rs", [128, 1])
    C.ln_xn = sb("ln_xn", [128, 1024])
    C.ln_xnb = sb("ln_xnb", [128, 1024], BF16)
    C.ln_xnT = sb("ln_xnT", [128, 8, 128], BF16)


def load_w_bf(C, name, dram_ap, ncols, key):
    t = C.sb(name, [128, 8, ncols], BF16)
    for c in range(8):
        C.S.dma('pool', t[:, c, :], dram_ap[c * 128:(c + 1) * 128, :], writes=[key])
    return t


def load_rep(C, name, dram_ap, n, key):
    t = C.sb(name, [128, n], F32)
    C.S.dma('sp', t[:], dram_ap.partition_broadcast(128), writes=[key])
    return t


def mem_kv(C, memT_d, wmkv_d, tag):
    S, sb, B = C.S, C.sb, C.B
    memT = load_w_bf(C, "memT" + tag, memT_d, 256, 'memT' + tag)
    wm = load_w_bf(C, "wmkv" + tag, wmkv_d, 512, 'wmkv' + tag)
    kmT = sb("kmT" + tag, [64, 4, 256], BF16)
    vmx = sb("vmx" + tag, [128, 2, 4, 65], BF16)
    S.op('pool', lambda e: e.memset(vmx[:].rearrange("p a b c -> p (a b c)"), 1.0), writes=['vmx' + tag])
    for h in range(4):
        for c in range(8):
            S.op('pe', lambda e: e.matmul(B[h % 2][0:64, (h // 2) * 256:(h // 2) * 256 + 256],
                                          wm[:, c, h * 64:(h + 1) * 64], memT[:, c, :], start=(c == 0), stop=(c == 7)),
                 reads=['memT' + tag, 'wmkv' + tag], writes=['B%d' % (h % 2)])
        S.op('act', lambda e: e.copy(out=kmT[:, h, :], in_=B[h % 2][0:64, (h // 2) * 256:(h // 2) * 256 + 256]), reads=['B%d' % (h % 2)], writes=['kmT' + tag])
    for j in range(2):
        for c in range(8):
            S.op('pe', lambda e: e.matmul(B[2 + j][:, 0:256], memT[:, c, j * 128:(j + 1) * 128], wm[:, c, 256:512], start=(c == 0), stop=(c == 7)),
                 reads=['memT' + tag, 'wmkv' + tag], writes=['B%d' % (2 + j)])
        S.op('act', lambda e: e.copy(out=vmx[:, j, :, 0:64], in_=B[2 + j][:, 0:256].rearrange("p (h e) -> p h e", h=4)),
             reads=['B%d' % (2 + j)], writes=['vmx' + tag])
    return kmT, vmx


def mem_attn_tile(C, qmT, key_qmT, kmT, vmx, tag, cat, key_cat, bs, bm):
    S, B = C.S, C.B
    pT = C.ma_pT
    for h in range(4):
        bk = bs[h // 2]
        for j in range(2):
            col = ((h % 2) * 2 + j) * 128
            S.op('pe', lambda e: e.matmul(B[bk][:, col:col + 128], kmT[:, h, j * 128:(j + 1) * 128], qmT[:, h, :], start=True, stop=True),
                 reads=['kmT' + tag, key_qmT], writes=['B%d' % bk])
    for hh in range(2):
        S.op('act', lambda e: e.activation(out=pT[:, hh * 512:(hh + 1) * 512], in_=B[bs[hh]][:, :], func=AF.Exp, scale=0.125),
             reads=['B%d' % bs[hh]], writes=['ma_pT'])
    for h in range(4):
        for j in range(2):
            col = (h * 2 + j) * 128
            S.op('pe', lambda e: e.matmul(B[bm][:, h * 65:h * 65 + 65], pT[:, col:col + 128], vmx[:, j, h, :], start=(j == 0), stop=(j == 1)),
                 reads=['ma_pT', 'vmx' + tag], writes=['B%d' % bm])
    mo = B[bm][:, 0:260].rearrange("p (h e) -> p h e", h=4)
    S.op('dve', lambda e: e.reciprocal(out=C.ma_rd[:], in_=mo[:, :, 64]), reads=['B%d' % bm], writes=['ma_rd'])
    S.op('dve', lambda e: e.tensor_tensor(out=cat[:, 768:1024].rearrange("p (h e) -> p h e", h=4), in0=mo[:, :, 0:64],
                                          in1=C.ma_rd[:].unsqueeze(2).to_broadcast([128, 4, 64]), op=ALU.mult),
         reads=['B%d' % bm, 'ma_rd'], writes=[key_cat])


def outproj_tile(C, cat, key_cat, wout, key_wout, XR, key_XR, bt, by, pre, chunks=range(8), kpre='pre'):
    S, B = C.S, C.B
    pb = B[bt][:].bitcast(BF16)
    chunks = list(chunks)
    for c in chunks:
        S.op('pe', lambda e: e.transpose(pb[:, c * 128:(c + 1) * 128], cat[:, c * 128:(c + 1) * 128], C.ident[:]),
             reads=[key_cat, 'ident'], writes=['B%d' % bt])
    c0, c1 = chunks[0], chunks[-1] + 1
    S.op('act', lambda e: e.copy(out=C.catT[:, c0:c1, :].rearrange("p c t -> p (c t)"), in_=pb[:, c0 * 128:c1 * 128]), reads=['B%d' % bt], writes=['catT'])
    for hlf in range(2):
        for c in range(8):
            S.op('pe', lambda e: e.matmul(B[by[hlf]][:, :], C.catT[:, c, :], wout[:, c, hlf * 512:(hlf + 1) * 512], start=(c == 0), stop=(c == 7)),
                 reads=['catT', key_wout], writes=['B%d' % by[hlf]])
        S.op('dve', lambda e: e.scalar_tensor_tensor(out=pre[:, hlf * 512:(hlf + 1) * 512], in0=XR[:, hlf * 512:(hlf + 1) * 512], scalar=DN_ALPHA,
                                                     in1=B[by[hlf]][:, :], op0=ALU.mult, op1=ALU.add),
             reads=[key_XR, 'B%d' % by[hlf]], writes=[kpre])


def phase_A(C):
    S, B, dram, nc = C.S, C.B, C.dram, C.nc
    st = ExitStack()
    sb = lambda name, shape, dt=F32: st.enter_context(nc.sbuf_tensor("a_" + name, list(shape), dt))
    C.sb_save = C.sb
    C.sb = sb
    x_d = dram('x', [SEQ, D]); xT_d = dram('xT', [D, SEQ])
    C.memT_d = dram('memT', [D, 256])
    W = load_w_bf(C, "awin", dram('a_w_in', [D, A_W]), A_W, 'awin')
    wout = load_w_bf(C, "wout0", dram('w_out0', [D, D]), D, 'wout0')
    wg2 = sb("wg2", [16, 384]); S.dma('sp', wg2[:], dram('a_w_gate2', [16, 384]), writes=['wg2'])
    bg = sb("bg", [1, 384]); S.dma('sp', bg[:], dram('a_b_gate', [1, 384]), writes=['bg'])
    ng = load_rep(C, "ng", dram('a_norm_g', [1, 768]), 768, 'ng')
    lng = load_rep(C, "lng", dram('ln_mix_g0', [1, D]), D, 'lnp')
    lnb = load_rep(C, "lnb", dram('ln_mix_b0', [1, D]), D, 'lnp')
    kmT, vmx = mem_kv(C, C.memT_d, dram('w_mem_kv0', [D, 512]), '0')
    alloc_ln(C)
    C.ma_pT = sb("ma_pT", [128, 1024], BF16)
    C.ma_rd = sb("ma_rd", [128, 4])
    C.catT = sb("catT", [128, 8, 128], BF16)
    XT = [sb("XT%d" % j, [128, 8, 128], BF16) for j in range(2)]
    XR = [sb("XR%d" % j, [128, 1024]) for j in range(2)]
    hgT = sb("hgT", [16, 128])
    qmT = sb("qmT", [64, 4, 128], BF16)
    t1 = sb("a_t1", [128, 384]); la = sb("a_la", [128, 384])
    expb = sb("expb", [96, 4, 128]); expnb = sb("expnb", [96, 4, 128]); expE = sb("expE", [128, 384])
    qt = sb("qt", [96, 4, 128], BF16); kt = sb("kt", [96, 4, 128], BF16)
    kend = sb("kend", [128, 384], BF16); vb = sb("vb", [128, 768], BF16)
    gate = sb("gate", [128, 768])
    attnTb = sb("attnTb", [128, 4, 128], BF16)
    St = sb("St", [96, 4, 192]); SbA = [sb("SbA%d" % q, [96, 4, 192], BF16) for q in range(2)]; SbB = sb("SbB", [96, 4, 192], BF16)
    sq = sb("sq", [128, 768]); ssq = sb("ssq", [128, 4]); to = sb("to", [128, 768])
    cat = sb("cat", [128, 1024], BF16)
    pre = sb("pre", [128, 1024])
    S.op('pool', lambda e: e.memset(St[:].rearrange("p a b -> p (a b)"), 0.0), writes=['St'])
    S.op('pool', lambda e: e.memset(SbA[0][:].rearrange("p a b -> p (a b)"), 0.0), writes=['SbA0'])
    QS = 96 ** -0.5

    def load(i):
        j = i % 2
        S.dma('pool', XT[j][:], xT_d[:, i * 128:(i + 1) * 128].rearrange("(c p) t -> p c t", p=128), writes=['XT%d' % j])
        S.dma('sp', XR[j][:], x_d[i * 128:(i + 1) * 128, :], writes=['XR%d' % j])

    pres = [pre, sb("pre2", [128, 1024])]
    load(0)

    def tile_X(i):
        j = i % 2
        pre = pres[j]
        kpre = 'a_pre%d' % j
        if i + 1 < NT:
            load(i + 1)
        conv_pop(C, 0, 2)
        xt = XT[j]; kx = 'XT%d' % j
        for h in range(4):
            for c in range(8):
                S.op('pe', lambda e: e.matmul(B[0][0:96, h * 128:(h + 1) * 128], W[:, c, h * 96:(h + 1) * 96], xt[:, c, :], start=(c == 0), stop=(c == 7)),
                     reads=['awin', kx], writes=['B0'])
        for h in range(4):
            for c in range(8):
                S.op('pe', lambda e: e.matmul(B[1][0:96, h * 128:(h + 1) * 128], W[:, c, 384 + h * 96:384 + (h + 1) * 96], xt[:, c, :], start=(c == 0), stop=(c == 7)),
                     reads=['awin', kx], writes=['B1'])
        for (bk, c0, n) in ((2, 384, 384), (3, 768, 512), (4, 1280, 512), (5, 1792, 512)):
            for c in range(8):
                S.op('pe', lambda e: e.matmul(B[bk][:, 0:n], xt[:, c, :], W[:, c, c0:c0 + n], start=(c == 0), stop=(c == 7)),
                     reads=['awin', kx], writes=['B%d' % bk])
        for h in range(4):
            for c in range(8):
                S.op('pe', lambda e: e.matmul(B[6][0:64, h * 128:(h + 1) * 128], W[:, c, 2320 + h * 64:2320 + (h + 1) * 64], xt[:, c, :], start=(c == 0), stop=(c == 7)),
                     reads=['awin', kx], writes=['B6'])
        for c in range(8):
            S.op('pe', lambda e: e.matmul(B[7][0:16, 0:128], W[:, c, 2304:2320], xt[:, c, :], start=(c == 0), stop=(c == 7)),
                 reads=['awin', kx], writes=['B7'])
        S.op('act', lambda e: e.copy(out=qmT[:].rearrange("p h t -> p (h t)"), in_=B[6][0:64, :]), reads=['B6'], writes=['qmT'])
        S.op('act', lambda e: e.copy(out=hgT[:], in_=B[7][0:16, 0:128]), reads=['B7'], writes=['hgT'])
        S.op('pe', lambda e: e.matmul(B[7][:, 0:384], hgT[:], wg2[:], start=True, stop=False), reads=['hgT', 'wg2'], writes=['B7'])
        S.op('pe', lambda e: e.matmul(B[7][:, 0:384], C.ones_f[0:1, :], bg[:], start=False, stop=True), reads=['ones_f', 'bg'], writes=['B7'])
        S.op('act', lambda e: e.activation(out=t1[:], in_=B[7][:, 0:384], func=AF.Exp, scale=-1.0), reads=['B7'], writes=['a_t1'])
        S.op('act', lambda e: e.activation(out=t1[:], in_=t1[:], func=AF.Ln, bias=1.0, scale=1.0), reads=['a_t1'], writes=['a_t1'])
        S.op('act', lambda e: e.mul(out=la[:], in_=t1[:], mul=-1.0 / 16.0), reads=['a_t1'], writes=['a_la'])
        for h in range(4):
            S.op('pe', lambda e: e.matmul(B[6][0:96, h * 128:(h + 1) * 128], la[:, h * 96:(h + 1) * 96], C.m2[:], start=True, stop=True),
                 reads=['a_la', 'm2'], writes=['B6'])
        S.op('pe', lambda e: e.matmul(B[7][:, 0:384], C.u2[:], la[:], start=True, stop=True), reads=['a_la', 'u2'], writes=['B7'])
        S.op('act', lambda e: e.activation(out=expb[:].rearrange("p h t -> p (h t)"), in_=B[6][0:96, :], func=AF.Exp), reads=['B6'], writes=['expb'])
        S.op('act', lambda e: e.activation(out=expnb[:].rearrange("p h t -> p (h t)"), in_=B[6][0:96, :], func=AF.Exp, scale=-1.0), reads=['B6'], writes=['expnb'])
        S.op('act', lambda e: e.activation(out=expE[:], in_=B[7][:, 0:384], func=AF.Exp), reads=['B7'], writes=['expE'])
        S.op('dve', lambda e: e.scalar_tensor_tensor(out=qt[:].rearrange("p h t -> p (h t)"), in0=B[0][0:96, :], scalar=QS, in1=expb[:].rearrange("p h t -> p (h t)"),
                                                     op0=ALU.mult, op1=ALU.mult), reads=['B0', 'expb'], writes=['qt'])
        S.op('dve', lambda e: e.tensor_tensor(out=kt[:].rearrange("p h t -> p (h t)"), in0=B[1][0:96, :], in1=expnb[:].rearrange("p h t -> p (h t)"), op=ALU.mult),
             reads=['B1', 'expnb'], writes=['kt'])
        S.op('dve', lambda e: e.tensor_tensor(out=kend[:], in0=B[2][:, 0:384], in1=expE[:], op=ALU.mult), reads=['B2', 'expE'], writes=['kend'])
        S.op('act', lambda e: e.copy(out=vb[:, 0:512], in_=B[3][:, :]), reads=['B3'], writes=['vb'])
        S.op('act', lambda e: e.copy(out=vb[:, 512:768], in_=B[4][:, 0:256]), reads=['B4'], writes=['vb'])
        S.op('act', lambda e: e.activation(out=gate[:, 0:256], in_=B[4][:, 256:512], func=AF.Silu), reads=['B4'], writes=['gate'])
        S.op('act', lambda e: e.activation(out=gate[:, 256:768], in_=B[5][:, :], func=AF.Silu), reads=['B5'], writes=['gate'])
        S.op('pool', lambda e: e.tensor_tensor(out=gate[:], in0=gate[:], in1=ng[:], op=ALU.mult), reads=['gate', 'ng'], writes=['gate'])
        for h in range(4):
            S.op('pe', lambda e: e.matmul(B[0][:, h * 128:(h + 1) * 128], kt[:, h, :], qt[:, h, :], start=True, stop=True), reads=['kt', 'qt'], writes=['B0'])
        S.op('dve', lambda e: e.tensor_tensor(out=attnTb[:], in0=B[0][:, :].rearrange("p (h t) -> p h t", h=4),
                                              in1=C.m2[:].unsqueeze(1).to_broadcast([128, 4, 128]), op=ALU.mult), reads=['B0', 'm2'], writes=['attnTb'])
        for ch in range(2):
            for h in range(4):
                bk = 2 + ch * 2 + h // 2
                col = (h % 2) * 192
                S.op('pe', lambda e: e.matmul(B[bk][0:96, col:col + 192], kend[ch * 64:(ch + 1) * 64, h * 96:(h + 1) * 96], vb[ch * 64:(ch + 1) * 64, h * 192:(h + 1) * 192],
                                              start=True, stop=True), reads=['kend', 'vb'], writes=['B%d' % bk])
        def o_ap(h, lo, hi):
            bk = 1 if h < 2 else 6
            col = (h % 2) * 192
            return B[bk][lo:hi, col:col + 192], 'B%d' % bk
        SbS = SbA[i % 2]; kS = 'SbA%d' % (i % 2)
        SbN = SbA[(i + 1) % 2]; kN = 'SbA%d' % ((i + 1) % 2)
        for ch in range(2):
            for h in range(4):
                bk = 2 + ch * 2 + h // 2
                col = (h % 2) * 192
                S.op('dve', lambda e: e.scalar_tensor_tensor(out=St[:, h, :], in0=St[:, h, :], scalar=expb[:, h, ch * 64 + 63:ch * 64 + 64], in1=B[bk][0:96, col:col + 192],
                                                             op0=ALU.mult, op1=ALU.add), reads=['St', 'expb', 'B%d' % bk], writes=['St'])
            Sb_dst, kd = (SbB, 'SbB') if ch == 0 else (SbN, kN)
            S.op('act', lambda e: e.copy(out=Sb_dst[:].rearrange("p a b -> p (a b)"), in_=St[:].rearrange("p a b -> p (a b)")),
                 reads=['St'], writes=[kd])
        for h in range(4):
            oap, ok = o_ap(h, 0, 128)
            S.op('pe', lambda e: e.matmul(oap, attnTb[:, h, :], vb[:, h * 192:(h + 1) * 192], start=True, stop=False), reads=['attnTb', 'vb'], writes=[ok])
            oap0, _ = o_ap(h, 0, 64)
            S.op('pe', lambda e: e.matmul(oap0, qt[:, h, 0:64], SbS[:, h, :], start=False, stop=False), reads=['qt', kS], writes=[ok])
            oap1, _ = o_ap(h, 64, 128)
            S.op('pe', lambda e: e.matmul(oap1, qt[:, h, 64:128], SbB[:, h, :], start=False, stop=True), reads=['qt', 'SbB'], writes=[ok])
        for hh in range(2):
            bk = 1 if hh == 0 else 6
            S.op('act', lambda e: e.activation(out=sq[:, hh * 384:(hh + 1) * 384], in_=B[bk][:, 0:384], func=AF.Square), reads=['B%d' % bk], writes=['sq'])
        S.op('dve', lambda e: e.tensor_reduce(out=ssq[:], in_=sq[:].rearrange("p (h v) -> p h v", h=4), axis=AX.X, op=ALU.add), reads=['sq'], writes=['ssq'])
        S.op('act', lambda e: e.activation(out=ssq[:], in_=ssq[:], func=AF.Sqrt, bias=HN_EPS, scale=1.0 / 192.0), reads=['ssq'], writes=['ssq'])
        S.op('dve', lambda e: e.reciprocal(out=ssq[:], in_=ssq[:]), reads=['ssq'], writes=['ssq'])
        for hh in range(2):
            bk = 1 if hh == 0 else 6
            S.op('dve', lambda e: e.tensor_tensor(out=to[:, hh * 384:(hh + 1) * 384].rearrange("p (h v) -> p h v", h=2),
                                                  in0=B[bk][:, 0:384].rearrange("p (h v) -> p h v", h=2),
                                                  in1=ssq[:, hh * 2:hh * 2 + 2].unsqueeze(2).to_broadcast([128, 2, 192]), op=ALU.mult),
                 reads=['B%d' % bk, 'ssq'], writes=['to'])
        S.op('dve', lambda e: e.tensor_tensor(out=cat[:, 0:768], in0=to[:], in1=gate[:], op=ALU.mult), reads=['to', 'gate'], writes=['cat'])
        mem_attn_tile(C, qmT, 'qmT', kmT, vmx, '0', cat, 'cat', (2, 3), 4)
        outproj_tile(C, cat, 'cat', wout, 'wout0', XR[j], 'XR%d' % j, 5, (7, 0), pre, kpre=kpre)

    tile_X(0)
    for i in range(NT):
        if i + 1 < NT:
            tile_X(i + 1)
        ln_epilogue(C, pres[i % 2], 'a_pre%d' % (i % 2), lng, lnb, C.xm0, C.xm0T, i, 'A', 5)
    C.sb = C.sb_save
    st.close()


def peer_convert(C, l):
    S, dram = C.S, C.dram
    uT_d = dram('peer_uT%d' % l, [D, 16384])
    v_d = dram('peer_v%d' % l, [16384, D])
    uTb = dram('peer_uTb%d' % l, [D, 16384], BF16, "Internal")
    vb = dram('peer_vb%d' % l, [16384, D], BF16, "Internal")
    C.peer_w = getattr(C, 'peer_w', {})
    C.peer_w[l] = (uTb, vb)
    q = []
    for qq in range(4):
        for c in range(8):
            q.append(lambda c=c, qq=qq: S.dma('pool', uTb[c * 128:(c + 1) * 128, qq * 4096:(qq + 1) * 4096], uT_d[c * 128:(c + 1) * 128, qq * 4096:(qq + 1) * 4096], writes=['uTb%d' % l]))
        for c in range(qq * 8, qq * 8 + 8):
            q.append(lambda c=c: S.dma('pool', vb[c * 512:(c + 1) * 512, :].rearrange("(p a) d -> p a d", p=128), v_d[c * 512:(c + 1) * 512, :].rearrange("(p a) d -> p a d", p=128), writes=['vb%d' % l]))
    C.conv_q = getattr(C, 'conv_q', {})
    C.conv_q[l] = q


def conv_pop(C, l, n):
    q = getattr(C, 'conv_q', {}).get(l, [])
    for _ in range(min(n, len(q))):
        q.pop(0)()


def phase_P(C, l, xm_d, xmT_d, xf_d, xfT_d, n_super=SEQ // 512):
    S, B, dram, nc = C.S, C.B, C.dram, C.nc
    st = ExitStack()
    sb = lambda name, shape, dt=F32: st.enter_context(nc.sbuf_tensor("p%d_%s" % (l, name), list(shape), dt))
    P = 'p%d_' % l
    wq = sb("wq", [128, 8, 2048], BF16)
    wq_d = dram('peer_w_q%d' % l, [D, 2048])
    for c in range(8):
        S.dma('pool', wq[:, c, :], wq_d[c * 128:(c + 1) * 128, :], writes=[P + 'wq'])
    keysT = sb("keysT", [128, 2, 128])
    kd = dram('peer_keysT%d' % l, [2, 128, 128])
    for p in range(2):
        S.dma('sp', keysT[:, p, :], kd[p], writes=[P + 'keysT'])
    if l not in getattr(C, 'peer_w', {}):
        peer_convert(C, l)
    conv_pop(C, l, 1000)
    uT_d, v_d = C.peer_w[l]
    lng = sb("lng", [128, D]); lnb = sb("lnb", [128, D])
    S.dma('sp', lng[:], dram('ln_ffn_g%d' % l, [1, D]).partition_broadcast(128), writes=['lnp'])
    S.dma('sp', lnb[:], dram('ln_ffn_b%d' % l, [1, D]).partition_broadcast(128), writes=['lnp'])
    xmT = sb("xmT", [128, 8, 512], BF16)
    acc = sb("acc", [128, 4, 1024])
    top = sb("top", [128, 8, 2, 16])
    c16 = sb("c16", [128, 8, 16])
    dd = sb("dd", [128, 8, 16])
    zz = sb("zz", [128, 8])
    cs = sb("cs", [128, 8, 2])
    E = sb("E", [128, 4, 16, 128])
    gsc = sb("gsc", [128, 4, 8])
    uT = [sb("uT%d" % q, [128, 8, 512], BF16) for q in range(2)]
    vv = [sb("vv%d" % q, [128, 4, 1024], BF16) for q in range(2)]
    gel = [sb("gel%d" % q, [128, 512]) for q in range(2)]
    W8 = [sb("W8_%d" % q, [128, 8, 4, 128]) for q in range(2)]
    w0 = W8[0][:].rearrange("p h a b -> p (h a b)")
    w1 = W8[1][:].rearrange("p h a b -> p (h a b)")
    qT4a = w0[:, 0:4096].rearrange("p (a b) -> p a b", a=8)
    qT4b = w1[:, 0:4096].rearrange("p (a b) -> p a b", a=8)
    Sg = [sb("Sg%d" % q, [128, 8, 512], BF16) for q in range(2)]
    tmpo = [sb("tmpo%d" % q, [128, 1024]) for q in range(2)]
    ngsc = sb("ngsc", [128, 4, 8])
    xr = tmpo[1]
    s_sb = Sg[0][:].rearrange("p h n -> p (h n)").bitcast(F32).rearrange("p (a b) -> p a b", a=16)
    cand = Sg[1][:].rearrange("p h n -> p (h n)").bitcast(F32).rearrange("p (h a b) -> p h a b", h=8, a=16)
    tmpA = tmpo[0][:, :].rearrange("p (a b) -> p a b", a=8)
    tmpB = tmpo[1][:, :].rearrange("p (a b) -> p a b", a=8)
    tmpC = tmpo[0][:, :].rearrange("p (a b) -> p a b", a=4)
    tmpD = tmpo[1][:, :].rearrange("p (a b) -> p a b", a=4)
    G = sb("G", [128, 512], BF16)
    A = [sb("A%d" % q, [128, 512], BF16) for q in range(2)]
    AT = [sb("AT%d" % q, [128, 4, 128], BF16) for q in range(2)]
    pre = tmpo[0]
    C.ln_st = sb("ln_st", [128, 2, 6]); C.ln_mv = sb("ln_mv", [128, 2]); C.ln_rs = sb("ln_rs", [128, 1])
    C.ln_xn = sb("ln_xn", [128, 1024]); C.ln_xnb = sb("ln_xnb", [128, 1024], BF16); C.ln_xnT = sb("ln_xnT", [128, 8, 128], BF16)
    DELTA = 2e-4
    NEG = -1e30
    cnt = [0]

    def load_chunk(k):
        q = k % 2
        S.dma('sp', uT[q][:], uT_d[:, k * 512:(k + 1) * 512].rearrange("(c p) e -> p c e", p=128), reads=['uTb%d' % l], writes=[P + 'uT%d' % q])
        S.dma('sp', vv[q][:], v_d[k * 512:(k + 1) * 512, :].rearrange("(a b) d -> b a d", b=128), reads=['vb%d' % l], writes=[P + 'vv%d' % q])

    for sti in range(n_super):
        t0 = sti * 512
        S.dma('sp', xmT[:], xmT_d[:, t0:t0 + 512].rearrange("(c p) t -> p c t", p=128), writes=[P + 'xmT'])
        load_chunk(0)
        load_chunk(1)
        if l == 0:
            conv_pop(C, 1, 8)
        for hp in range(16):
            bk = 4 + (hp % 4)
            for c in range(8):
                S.op('pe', lambda e: e.matmul(B[bk][:, :], wq[:, c, hp * 128:(hp + 1) * 128], xmT[:, c, :], start=(c == 0), stop=(c == 7)),
                     reads=[P + 'wq', P + 'xmT'], writes=['B%d' % bk])
            qdst = (qT4a if hp < 8 else qT4b)[:, hp % 8, :]
            S.op('act', lambda e: e.copy(out=qdst, in_=B[bk][:, :]), reads=['B%d' % bk], writes=[P + 'qT'])
        for tt in range(4):
            qTf = lambda hp: (qT4a if hp < 8 else qT4b)[:, hp % 8, tt * 128:(tt + 1) * 128]
            for hp in range(16):
                bk = hp // 4
                col = (hp % 4) * 128
                S.op('pe', lambda e: e.matmul(B[bk][:, col:col + 128], qTf(hp), keysT[:, hp % 2, :], start=True, stop=True),
                     reads=[P + 'qT', P + 'keysT'], writes=['B%d' % bk])
            for bk in range(4):
                S.op('act', lambda e: e.copy(out=s_sb[:, bk * 4:(bk + 1) * 4, :].rearrange("p a b -> p (a b)"), in_=B[bk][:, :]), reads=['B%d' % bk], writes=[P + 's_sb'])
            for hp in range(16):
                h, p = hp // 2, hp % 2
                S.op('dve', lambda e: e.max(out=top[:, h, p, 0:8], in_=s_sb[:, hp, :]), reads=[P + 's_sb'], writes=[P + 'top%d' % hp])
            tslot = lambda hp: (tmpA if hp < 8 else tmpB)[:, hp % 8, :]
            for hp in range(16):
                h, p = hp // 2, hp % 2
                S.op('dve', lambda e: e.match_replace(out=tslot(hp), in_to_replace=top[:, h, p, 0:8], in_values=s_sb[:, hp, :], imm_value=NEG),
                     reads=[P + 's_sb', P + 'top%d' % hp], writes=[P + 'tmp%d' % hp])
            for hp in range(16):
                h, p = hp // 2, hp % 2
                S.op('dve', lambda e: e.max(out=top[:, h, p, 8:16], in_=tslot(hp)), reads=[P + 'tmp%d' % hp], writes=[P + 'top%d' % hp])
            allt = [P + 'top%d' % hp for hp in range(16)]
            S.op('dve', lambda e: e.tensor_tensor(out=cand, in0=top[:, :, 0, :].unsqueeze(3).to_broadcast([128, 8, 16, 16]),
                                                  in1=top[:, :, 1, :].unsqueeze(2).to_broadcast([128, 8, 16, 16]), op=ALU.add),
                 reads=allt, writes=[P + 'cand'])
            for h in range(8):
                S.op('dve', lambda e: e.max(out=c16[:, h, 0:8], in_=cand[:, h, :, :].rearrange("p a b -> p (a b)")), reads=[P + 'cand'], writes=[P + 'c16_%d' % h])
            cslot = lambda h: (tmpC if h < 4 else tmpD)[:, h % 4, :]
            for h in range(8):
                S.op('dve', lambda e: e.match_replace(out=cslot(h), in_to_replace=c16[:, h, 0:8], in_values=cand[:, h, :, :].rearrange("p a b -> p (a b)"), imm_value=NEG),
                     reads=[P + 'cand', P + 'c16_%d' % h], writes=[P + 'tmp%d' % (2 * (h % 4) + (0 if h < 4 else 8)), P + 'tmp%d' % (2 * (h % 4) + 1 + (0 if h < 4 else 8))])
            for h in range(8):
                S.op('dve', lambda e: e.max(out=c16[:, h, 8:16], in_=cslot(h)),
                     reads=[P + 'tmp%d' % (2 * (h % 4) + (0 if h < 4 else 8)), P + 'tmp%d' % (2 * (h % 4) + 1 + (0 if h < 4 else 8))], writes=[P + 'c16_%d' % h])
            allc = [P + 'c16_%d' % h for h in range(8)]
            S.op('dve', lambda e: e.tensor_tensor(out=dd[:], in0=c16[:], in1=c16[:, :, 0:1].to_broadcast([128, 8, 16]), op=ALU.subtract), reads=allc, writes=[P + 'dd'])
            S.op('act', lambda e: e.activation(out=dd[:].rearrange("p a b -> p (a b)"), in_=dd[:].rearrange("p a b -> p (a b)"), func=AF.Exp), reads=[P + 'dd'], writes=[P + 'dd'])
            S.op('dve', lambda e: e.tensor_reduce(out=zz[:], in_=dd[:], axis=AX.X, op=ALU.add), reads=[P + 'dd'], writes=[P + 'zz'])
            S.op('dve', lambda e: e.reciprocal(out=zz[:], in_=zz[:]), reads=[P + 'zz'], writes=[P + 'zz'])
            S.op('dve', lambda e: e.scalar_tensor_tensor(out=gsc[:, tt, :], in0=dd[:, :, 15], scalar=float(np.exp(-DELTA)), in1=zz[:], op0=ALU.mult, op1=ALU.mult),
                 reads=[P + 'dd', P + 'zz'], writes=[P + 'gsc'])
            S.op('dve', lambda e: e.tensor_scalar(out=ngsc[:, tt, :], in0=gsc[:, tt, :], scalar1=-1.0, scalar2=None, op0=ALU.mult), reads=[P + 'gsc'], writes=[P + 'ngsc'])
            S.op('dve', lambda e: e.tensor_copy(out=cs[:, :, 0], in_=top[:, :, 0, 0]), reads=allt, writes=[P + 'cs'])
            S.op('dve', lambda e: e.scalar_tensor_tensor(out=cs[:, :, 1], in0=c16[:, :, 15], scalar=-DELTA, in1=top[:, :, 0, 0], op0=ALU.add, op1=ALU.subtract),
                 reads=allc + allt, writes=[P + 'cs'])
            S.op('dve', lambda e: e.tensor_tensor(out=E[:, tt, :, :], in0=s_sb, in1=cs[:].rearrange("p h q -> p (h q)").unsqueeze(2).to_broadcast([128, 16, 128]), op=ALU.subtract),
                 reads=[P + 's_sb', P + 'cs'], writes=[P + 'E'])
            S.op('act', lambda e: e.activation(out=E[:, tt, :, :].rearrange("p a b -> p (a b)"), in_=E[:, tt, :, :].rearrange("p a b -> p (a b)"), func=AF.Exp),
                 reads=[P + 'E'], writes=[P + 'E'])
            Ea = E[:, tt, :, :].rearrange("p (h q) n -> p h q n", q=2)[:, :, 0, :]
            S.op('dve', lambda e: e.tensor_tensor(out=Ea, in0=Ea, in1=gsc[:, tt, :].unsqueeze(2).to_broadcast([128, 8, 128]), op=ALU.mult),
                 reads=[P + 'E', P + 'gsc'], writes=[P + 'E'])
        S.barrier()
        its = [(k, tt) for k in range(32) for tt in range(4)]
        NI = len(its)

        def st_P1(n):
            k, tt = its[n]; z = n % 2; q = k % 2
            for c in range(8):
                S.op('pe', lambda e: e.matmul(B[z][:, :], xmT[:, c, tt * 128:(tt + 1) * 128], uT[q][:, c, :], start=(c == 0), stop=(c == 7)),
                     reads=[P + 'xmT', P + 'uT%d' % q], writes=['B%d' % z])
            S.op('act', lambda e: e.activation(out=gel[z][:], in_=B[z][:, :], func=AF.Gelu), reads=['B%d' % z], writes=[P + 'gel%d' % z])

        def st_D1a(n):
            k, tt = its[n]; z = n % 2
            Ev = E[:, tt, :, :].rearrange("p (h q) n -> p h q n", q=2)
            HD = 8 - N_ACT_HEADS
            S.op('dve', lambda e: e.tensor_tensor(out=W8[z][:, 0:HD, :, :], in0=Ev[:, 0:HD, 0, k * 4:(k + 1) * 4].unsqueeze(3).to_broadcast([128, HD, 4, 128]),
                                                  in1=Ev[:, 0:HD, 1, :].unsqueeze(2).to_broadcast([128, HD, 4, 128]), op=ALU.mult),
                 reads=[P + 'E'], writes=[P + 'W8_%d' % z])
            for h in range(HD, 8):
                for a in range(4):
                    S.op('act', lambda e: e.activation(out=W8[z][:, h, a, :], in_=Ev[:, h, 1, :], func=AF.Copy, scale=Ev[:, h, 0, k * 4 + a:k * 4 + a + 1]),
                         reads=[P + 'E'], writes=[P + 'W8a_%d' % z])

        def st_SG(n):
            k, tt = its[n]; z = n % 2
            W8h = W8[z][:].rearrange("p h a b -> p h (a b)")
            for h in range(8):
                S.op('act', lambda e: e.activation(out=Sg[z][:, h, :], in_=W8h[:, h, :], func=AF.Sign, bias=ngsc[:, tt, h:h + 1], scale=1.0),
                     reads=[P + 'W8_%d' % z, P + 'W8a_%d' % z, P + 'ngsc'], writes=[P + 'Sg%d_%d' % (z, h)])

        def st_D1b(n):
            k, tt = its[n]; z = n % 2
            kS = [P + 'Sg%d_%d' % (z, h) for h in range(8)]
            Sf = Sg[z][:].rearrange("p h n -> p (h n)")
            S.op('dve', lambda e: e.scalar_tensor_tensor(out=Sf, in0=Sf, scalar=1.0, in1=W8[z][:].rearrange("p h a b -> p (h a b)"), op0=ALU.add, op1=ALU.mult),
                 reads=kS + [P + 'W8_%d' % z, P + 'W8a_%d' % z], writes=kS)
            S.op('dve', lambda e: e.tensor_tensor(out=Sg[z][:, 0:4, :], in0=Sg[z][:, 0:4, :], in1=Sg[z][:, 4:8, :], op=ALU.add), reads=kS, writes=kS)
            S.op('dve', lambda e: e.tensor_tensor(out=Sg[z][:, 0:2, :], in0=Sg[z][:, 0:2, :], in1=Sg[z][:, 2:4, :], op=ALU.add), reads=kS, writes=kS)
            S.op('dve', lambda e: e.tensor_tensor(out=G[:], in0=Sg[z][:, 0, :], in1=Sg[z][:, 1, :], op=ALU.add), reads=kS, writes=[P + 'G'])
            S.op('dve', lambda e: e.scalar_tensor_tensor(out=A[z][:], in0=G[:], scalar=0.5, in1=gel[z][:], op0=ALU.mult, op1=ALU.mult),
                 reads=[P + 'gel%d' % z, P + 'G'], writes=[P + 'A%d' % z])

        def st_P2(n):
            z = n % 2
            bt = 2 + z
            pb = B[bt][:].bitcast(BF16)
            for a in range(4):
                S.op('pe', lambda e: e.transpose(pb[:, a * 128:(a + 1) * 128], A[z][:, a * 128:(a + 1) * 128], C.ident[:]),
                     reads=[P + 'A%d' % z, 'ident'], writes=['B%d' % bt])
            S.op('act', lambda e: e.copy(out=AT[z][:].rearrange("p a t -> p (a t)"), in_=pb[:, 0:512]), reads=['B%d' % bt], writes=[P + 'AT%d' % z])

        def st_P3(n):
            k, tt = its[n]; z = n % 2; q = k % 2
            for hlf in range(2):
                bo = 4 + z * 2 + hlf
                for a in range(4):
                    S.op('pe', lambda e: e.matmul(B[bo][:, :], AT[z][:, a, :], vv[q][:, a, hlf * 512:(hlf + 1) * 512], start=(a == 0), stop=(a == 3)),
                         reads=[P + 'AT%d' % z, P + 'vv%d' % q], writes=['B%d' % bo])

        def st_D2(n):
            k, tt = its[n]; z = n % 2
            for hlf in range(2):
                bo = 4 + z * 2 + hlf
                if k == 0:
                    S.op('act', lambda e: e.copy(out=acc[:, tt, hlf * 512:(hlf + 1) * 512], in_=B[bo][:, :]), reads=['B%d' % bo], writes=[P + 'acc%d' % tt])
                else:
                    S.op('act', lambda e: e.copy(out=tmpo[z][:, hlf * 512:(hlf + 1) * 512], in_=B[bo][:, :]), reads=['B%d' % bo], writes=[P + 'tmpo%d' % z])
            if k > 0:
                S.dma('pool', acc[:, tt, :], tmpo[z][:], reads=[P + 'tmpo%d' % z, P + 'acc%d' % tt], writes=[P + 'acc%d' % tt], accum_op=ALU.add)

        st_P1(0)
        st_D1a(0)
        st_SG(0)
        for n in range(NI + 1):
            if n + 1 < NI:
                st_P1(n + 1)
                st_D1a(n + 1)
                st_SG(n + 1)
            if n < NI:
                st_D1b(n)
            if n >= 1:
                st_P3(n - 1)
                k_prev, tt_prev = its[n - 1]
                if tt_prev == 3 and k_prev + 2 < 32:
                    load_chunk(k_prev + 2)
            if n < NI:
                st_P2(n)
            if n >= 1:
                st_D2(n - 1)
        S.barrier()
        for tt in range(4):
            i = sti * 4 + tt
            S.dma('sp', xr[:], xm_d[i * 128:(i + 1) * 128, :], writes=[P + 'xr'])
            S.op('dve', lambda e: e.scalar_tensor_tensor(out=pre[:], in0=xr[:], scalar=DN_ALPHA, in1=acc[:, tt, :], op0=ALU.mult, op1=ALU.add),
                 reads=[P + 'xr', P + 'acc%d' % tt], writes=['pre'])
            ln_epilogue(C, pre, 'pre', lng, lnb, xf_d, xfT_d, i, 'P%d' % l, 3)
    st.close()


def phase_B(C, xf_d, xfT_d, xm1_d, xm1T_d):
    S, B, dram, nc = C.S, C.B, C.dram, C.nc
    st = ExitStack()
    sb = lambda name, shape, dt=F32: st.enter_context(nc.sbuf_tensor("b_" + name, list(shape), dt))
    C.sb_save = C.sb
    C.sb = sb
    bw_d = dram('b_w_in', [D, B_W])
    kv_d = dram('shared_w_kv', [D, 1536])
    catT_d = dram('catT1', [768, SEQ], BF16, "Internal")
    XF = sb("XF", [128, 8, SEQ], BF16)
    for c in range(8):
        S.dma('sp' if c % 2 == 0 else 'act', XF[:, c, :], xfT_d[c * 128:(c + 1) * 128, :], writes=['XF'])
    st1 = ExitStack()
    sb_outer = sb
    sb = lambda name, shape, dt=F32: st1.enter_context(nc.sbuf_tensor("b1_" + name, list(shape), dt))
    acc = sb("acc", [128, 2, SEQ])
    mixT = sb("mixT", [128, SEQ], BF16)
    KT = sb("KT", [128, SEQ], BF16)
    QT = [sb("QT%d" % q, [128, SEQ], BF16) for q in range(2)]
    V = [sb("V%d" % q, [128, 32, 128], BF16) for q in range(2)]
    Wq = sb("Wq", [128, 8, 3, 128], BF16)
    Wk = sb("Wk", [128, 8, 128], BF16)
    Wva = sb("Wva", [128, 8, 768], BF16)
    vsb = [sb("vsb%d" % q, [128, 768], BF16) for q in range(2)]
    PT = [sb("PT%d" % q, [128, 2, 128], BF16) for q in range(2)]
    SC = 128 ** -0.5
    DIL = (1, 4, 16)
    Vd = dram('b_Vd', [SEQ, 768], BF16, "Internal")

    for c in range(8):
        S.dma('pool', Wva[:, c, :], kv_d[c * 128:(c + 1) * 128, 768:1536], writes=['b_Wva'])
    for i in range(NT):
        z = i % 2
        for (bk, c0, n) in ((4 + 2 * z, 0, 512), (5 + 2 * z, 512, 256)):
            for c in range(8):
                S.op('pe', lambda e: e.matmul(B[bk][:, 0:n], XF[:, c, i * 128:(i + 1) * 128], Wva[:, c, c0:c0 + n], start=(c == 0), stop=(c == 7)),
                     reads=['XF', 'b_Wva'], writes=['B%d' % bk])
            S.op('act', lambda e: e.copy(out=vsb[z][:, c0:c0 + n], in_=B[bk][:, 0:n]), reads=['B%d' % bk], writes=['b_vsb%d' % z])
        S.dma('sp', Vd[i * 128:(i + 1) * 128, :], vsb[z][:], reads=['b_vsb%d' % z], writes=['b_Vd'])

    def proj_T(dst, key_dst, wsel, key_w):
        for tg in range(8):
            bk = 4 + tg % 4
            for c in range(8):
                S.op('pe', lambda e: e.matmul(B[bk][:, :], wsel(c), XF[:, c, tg * 512:(tg + 1) * 512], start=(c == 0), stop=(c == 7)),
                     reads=[key_w, 'XF'], writes=['B%d' % bk])
            S.op('act', lambda e: e.copy(out=dst[:, tg * 512:(tg + 1) * 512], in_=B[bk][:, :]), reads=['B%d' % bk], writes=[key_dst])

    def load_V(s_, g, q):
        d = DIL[g]
        nb = 32 // d
        src = Vd.rearrange("(n p r) f -> p r n f", p=128, r=d)[:, :, :, s_ * 128:(s_ + 1) * 128]
        S.dma('act', V[q][:].rearrange("p (r n) e -> p r n e", r=d), src, reads=['b_Vd'], writes=['b_V%d' % q])

    sg = [(s_, g) for s_ in range(6) for g in range(3)]
    load_V(0, 0, 0)
    for idx, (s_, g) in enumerate(sg):
        vq = idx % 2
        if idx + 1 < len(sg):
            load_V(sg[idx + 1][0], sg[idx + 1][1], (idx + 1) % 2)
        if g == 0:
            for c in range(8):
                S.dma('pool', Wq[:, c, :, :], bw_d[c * 128:(c + 1) * 128, 0:2304].rearrange("p (g s e) -> p g s e", g=3, s=6)[:, :, s_, :], writes=['b_Wq'])
                S.dma('pool', Wk[:, c, :], kv_d[c * 128:(c + 1) * 128, s_ * 128:(s_ + 1) * 128], writes=['b_Wk'])
            proj_T(KT, 'b_KT', lambda c: Wk[:, c, :], 'b_Wk')
        d = DIL[g]
        nb = 32 // d
        QTg = QT[idx % 2]
        kQ = 'b_QT%d' % (idx % 2)
        proj_T(QTg, kQ, lambda c: Wq[:, c, g, :], 'b_Wq')
        Vg = V[vq]
        kV = 'b_V%d' % vq

        def blk_info(blk):
            r, n = blk // nb, blk % nb
            tq = d * 128 * n + r
            return r, n, slice(tq, tq + 127 * d + 1, d), ([1] if n == 0 else [0, 1])

        def stg1(blk):
            r, n, qs, kbs = blk_info(blk)
            z = blk % 2
            for kb in kbs:
                tk = d * 128 * (n - 1 + kb) + r
                S.op('pe', lambda e: e.matmul(B[z][:, kb * 128:(kb + 1) * 128], KT[:, tk:tk + 127 * d + 1:d], QTg[:, qs], start=True, stop=True),
                     reads=['b_KT', kQ], writes=['B%d' % z])
            lo = kbs[0]
            S.op('act', lambda e: e.activation(out=PT[z][:, lo:2, :].rearrange("p a b -> p (a b)"), in_=B[z][:, lo * 128:256], func=AF.Exp, scale=SC),
                 reads=['B%d' % z], writes=['b_PT%d' % z])
            S.op('dve', lambda e: e.tensor_tensor(out=PT[z][:, lo:2, :], in0=PT[z][:, lo:2, :], in1=C.dmask[:, lo:2, :], op=ALU.mult),
                 reads=['b_PT%d' % z, 'dmask'], writes=['b_PT%d' % z])

        def stg2(blk):
            r, n, qs, kbs = blk_info(blk)
            z = blk % 2
            bo = 2 + z
            for which in range(2):
                for kb in kbs:
                    lhs = Vg[:, blk - 1 + kb, :] if which == 0 else C.ones_b[:]
                    S.op('pe', lambda e: e.matmul(B[bo][:, which * 128:(which + 1) * 128], lhs, PT[z][:, kb, :], start=(kb == kbs[0]), stop=(kb == 1)),
                         reads=[kV, 'ones_b', 'b_PT%d' % z], writes=['B%d' % bo])
            pv = B[bo][:, 0:256].rearrange("p (a b) -> p a b", a=2)
            if g == 0:
                S.op('act', lambda e: e.copy(out=acc[:, :, qs], in_=pv), reads=['B%d' % bo], writes=['b_acc'])
            else:
                S.op('dve', lambda e: e.tensor_tensor(out=acc[:, :, qs], in0=acc[:, :, qs], in1=pv, op=ALU.add), reads=['B%d' % bo, 'b_acc'], writes=['b_acc'])

        stg1(0)
        for blk in range(32):
            if blk + 1 < 32:
                stg1(blk + 1)
            stg2(blk)
        if g == 2:
            S.op('dve', lambda e: e.reciprocal(out=acc[:, 1, :], in_=acc[:, 1, :]), reads=['b_acc'], writes=['b_acc'])
            S.op('dve', lambda e: e.tensor_tensor(out=mixT[:], in0=acc[:, 0, :], in1=acc[:, 1, :], op=ALU.mult), reads=['b_acc'], writes=['b_mixT'])
            S.dma('sp', catT_d[s_ * 128:(s_ + 1) * 128, :], mixT[:], reads=['b_mixT'], writes=['catT1_d'])
    S.barrier()
    st1.close()
    sb = sb_outer
    Wm = sb("Wm", [128, 8, 256], BF16)
    for c in range(8):
        S.dma('pool', Wm[:, c, :], bw_d[c * 128:(c + 1) * 128, 2304:2560], writes=['b_Wm'])
    wout = load_w_bf(C, "wout1", dram('w_out1', [D, D]), D, 'wout1')
    lng = load_rep(C, "lng", dram('ln_mix_g1', [1, D]), D, 'lnp')
    lnb = load_rep(C, "lnb", dram('ln_mix_b1', [1, D]), D, 'lnp')
    kmT, vmx = mem_kv(C, dram('memT', [D, 256]) if not hasattr(C, 'memT_d') else C.memT_d, dram('w_mem_kv1', [D, 512]), '1')
    alloc_ln(C)
    C.ma_pT = sb("ma_pT", [128, 1024], BF16)
    C.ma_rd = sb("ma_rd", [128, 4])
    C.catT = sb("catT", [128, 8, 128], BF16)
    qmT = sb("qmT", [64, 4, 128], BF16)
    cat = sb("cat", [128, 1024], BF16)
    pre = sb("pre", [128, 1024])
    XR = [sb("XR%d" % j, [128, 1024]) for j in range(2)]
    pres = [pre, sb("pre2", [128, 1024])]

    def tile_X(i):
        j = i % 2
        S.dma('sp', XR[j][:], xf_d[i * 128:(i + 1) * 128, :], writes=['bXR%d' % j])
        S.dma('sp', C.catT[:, 0:6, :], catT_d[:, i * 128:(i + 1) * 128].rearrange("(c p) t -> p c t", p=128), reads=['catT1_d'], writes=['catT'])
        for h in range(4):
            for c in range(8):
                S.op('pe', lambda e: e.matmul(B[6][0:64, h * 128:(h + 1) * 128], Wm[:, c, h * 64:(h + 1) * 64], XF[:, c, i * 128:(i + 1) * 128], start=(c == 0), stop=(c == 7)),
                     reads=['b_Wm', 'XF'], writes=['B6'])
        S.op('act', lambda e: e.copy(out=qmT[:].rearrange("p h t -> p (h t)"), in_=B[6][0:64, :]), reads=['B6'], writes=['b_qmT'])
        mem_attn_tile(C, qmT, 'b_qmT', kmT, vmx, '1', cat, 'b_cat', (2, 3), 4)
        outproj_tile(C, cat, 'b_cat', wout, 'wout1', XR[j], 'bXR%d' % j, 5, (7, 0), pres[j], chunks=(6, 7), kpre='b_pre%d' % j)

    tile_X(0)
    for i in range(NT):
        if i + 1 < NT:
            tile_X(i + 1)
        ln_epilogue(C, pres[i % 2], 'b_pre%d' % (i % 2), lng, lnb, xm1_d, xm1T_d, i, 'B', 1)
    C.sb = C.sb_save
    st.close()


_PROG_CACHE = {}


def _prep_inputs(inputs, b):
    g = lambda k: np.asarray(inputs[k], dtype=np.float32)
    m = {}
    m['x'] = np.ascontiguousarray(g('x')[b])
    m['xT'] = np.ascontiguousarray(g('x')[b].T)
    m['memT'] = np.ascontiguousarray(g('mem')[b].T)
    m['a_w_in'] = np.ascontiguousarray(g('a_w_in')[0])
    m['a_w_gate2'] = np.ascontiguousarray(g('a_w_gate2')[0])
    m['a_b_gate'] = np.ascontiguousarray(g('a_b_gate')[0][None, :])
    m['a_norm_g'] = np.ascontiguousarray(g('a_norm_g')[0][None, :])
    for l in range(2):
        m['w_mem_kv%d' % l] = np.ascontiguousarray(g('w_mem_kv')[l])
        m['w_out%d' % l] = np.ascontiguousarray(g('w_out')[l])
        for nm in ('ln_mix_g', 'ln_mix_b', 'ln_ffn_g', 'ln_ffn_b'):
            m['%s%d' % (nm, l)] = np.ascontiguousarray(g(nm)[l][None, :])
    m['b_w_in'] = np.ascontiguousarray(g('b_w_in')[0])
    m['shared_w_kv'] = np.ascontiguousarray(g('shared_w_kv'))
    for l in range(2):
        m['peer_w_q%d' % l] = np.ascontiguousarray(g('peer_w_q')[l])
        m['peer_keysT%d' % l] = np.ascontiguousarray(g('peer_sub_keys')[l].transpose(0, 2, 1))
        m['peer_uT%d' % l] = np.ascontiguousarray(g('peer_u')[l].T)
        m['peer_v%d' % l] = np.ascontiguousarray(g('peer_v')[l])
    m.update(_consts_host())
    return m


def kernel(**inputs):
    if 'nc' not in _PROG_CACHE:
        _PROG_CACHE['nc'] = build()
    nc = _PROG_CACHE['nc']
    in_maps = [_prep_inputs(inputs, b) for b in range(8)]
    res = run_bass_kernel_spmd(nc, in_maps, core_ids=list(range(8)))
    out = np.stack([np.asarray(r['out'], dtype=np.float32) for r in res.results], axis=0)
    return out
```
